# Optimizing a Trainium2 kernel written in Bass

```python
import functools
import jax
import jax.numpy as jnp
from jax import lax
import numpy as np

D_MODEL = 1024
BATCH = 8
SEQ = 4096
DEPTH = 2

GRID_W = 64
CTX_LEN = 256
N_EVEN = (DEPTH + 1) // 2
N_ODD = DEPTH // 2
N_MOD = 6
EPS = 1e-6
ROPE_THETA = 10000.0
BLOCK = 128
NEG_INF = -1e30

CONV_A_DIM = D_MODEL // 2
CONV_A_WIDTH = 31
WIN_HEADS = 8
WIN_KV_HEADS = 2
WIN_HEAD_DIM = 64
WINDOW = 128
LRU_DIM = D_MODEL // 2
LRU_BLOCKS = 8
LRU_BLOCK_DIM = LRU_DIM // LRU_BLOCKS
LRU_CONV_WIDTH = 4
LRU_C = 8.0
MLA_HEADS = 8
MLA_Q_RANK = 384
MLA_KV_RANK = 256
MLA_NOPE = 64
MLA_ROPE = 32
MLA_V = 64
FFN_DIM = 2816
FFN_CONV_WIDTH = 3

AB_IN = 2 * CONV_A_DIM + (WIN_HEADS + 2 * WIN_KV_HEADS) * WIN_HEAD_DIM
AB_OUT = CONV_A_DIM + WIN_HEADS * WIN_HEAD_DIM
CD_IN = 2 * LRU_DIM + MLA_Q_RANK + MLA_KV_RANK + MLA_ROPE
CD_OUT = LRU_DIM + MLA_HEADS * MLA_V

kernel_name = "hybrid_prefix_dit_block"


def rms_norm(x, g):
    x32 = x.astype(jnp.float32)
    y = x32 * lax.rsqrt(jnp.mean(x32 * x32, axis=-1, keepdims=True) + EPS)
    return (y * g.astype(jnp.float32)).astype(x.dtype)


def layer_norm(x, g, b):
    x32 = x.astype(jnp.float32)
    mu = jnp.mean(x32, axis=-1, keepdims=True)
    var = jnp.mean(jnp.square(x32 - mu), axis=-1, keepdims=True)
    y = (x32 - mu) * lax.rsqrt(var + EPS)
    return (y * g.astype(jnp.float32) + b.astype(jnp.float32)).astype(x.dtype)


def modulate(h, shift, scale):
    return h * (1 + scale) + shift


def dwconv(x, w, b, pad_left, pad_right):
    y = lax.conv_general_dilated(x, w[:, None, :].astype(x.dtype), window_strides=(1,),
                                 padding=[(pad_left, pad_right)],
                                 dimension_numbers=('NWC', 'WIO', 'NWC'),
                                 feature_group_count=x.shape[-1])
    return y + b.astype(x.dtype)


def axial_rope_tables(rows, dim):
    row = jnp.repeat(jnp.arange(rows), GRID_W).astype(jnp.float32)
    col = jnp.tile(jnp.arange(GRID_W), rows).astype(jnp.float32)
    nf = dim // 4
    inv = ROPE_THETA ** (-jnp.arange(nf, dtype=jnp.float32) / nf)
    ar = row[:, None] * inv
    ac = col[:, None] * inv
    ang = jnp.concatenate([ar, ar, ac, ac], axis=-1)
    return jnp.cos(ang), jnp.sin(ang)


def apply_rope(x, cos, sin):
    q = x.shape[-1] // 4
    x1, x2, x3, x4 = x[..., :q], x[..., q:2 * q], x[..., 2 * q:3 * q], x[..., 3 * q:]
    partner = jnp.concatenate([-x2, x1, -x4, x3], axis=-1)
    shape = (cos.shape[0],) + (1,) * (x.ndim - 3) + (cos.shape[1],)
    return (x * cos.reshape(shape) + partner * sin.reshape(shape)).astype(x.dtype)


def joint_softmax(logits_list):
    m = functools.reduce(jnp.maximum, [jnp.max(l, axis=-1, keepdims=True) for l in logits_list])
    e = [jnp.exp(l - m) for l in logits_list]
    denom = functools.reduce(jnp.add, [jnp.sum(z, axis=-1, keepdims=True) for z in e])
    return [z / denom for z in e]


def conformer_conv(z, conv_w, conv_b, ln_g, ln_b):
    val, gate = z[..., :CONV_A_DIM], z[..., CONV_A_DIM:]
    u = val * jax.nn.sigmoid(gate)
    half = CONV_A_WIDTH // 2
    u = dwconv(u, conv_w, conv_b, half, half)
    u = layer_norm(u, ln_g, ln_b)
    return jax.nn.silu(u)


def window_gqa_latent(q, q_raw, k, v, kc, vc, sink):
    b, t, h, d = q.shape
    kvh = k.shape[2]
    g = h // kvh
    nb = t // BLOCK
    scale = d ** -0.5
    qb = q.reshape(b, nb, BLOCK, kvh, g, d)
    qrb = q_raw.reshape(b, nb, BLOCK, kvh, g, d)

    def band(z):
        zp = jnp.pad(z, ((0, 0), (BLOCK, BLOCK), (0, 0), (0, 0))).reshape(b, nb + 2, BLOCK, kvh, d)
        return jnp.concatenate([zp[:, :-2], zp[:, 1:-1], zp[:, 2:]], axis=2)

    kb, vb = band(k), band(v)
    s_lat = jnp.einsum('bnqkgd,bnskd->bnkgqs', qb, kb, preferred_element_type=jnp.float32) * scale
    s_ctx = jnp.einsum('bnqkgd,bckd->bnkgqc', qrb, kc, preferred_element_type=jnp.float32) * scale
    qpos = jnp.arange(nb)[:, None, None] * BLOCK + jnp.arange(BLOCK)[None, :, None]
    kpos = jnp.arange(nb)[:, None, None] * BLOCK - BLOCK + jnp.arange(3 * BLOCK)[None, None, :]
    valid = (jnp.abs(qpos - kpos) <= WINDOW) & (kpos >= 0) & (kpos < t)
    s_lat = jnp.where(valid[None, :, None, None], s_lat, NEG_INF)
    sink_l = sink.astype(jnp.float32).reshape(1, 1, kvh, g, 1, 1)
    p_lat, p_ctx, _ = joint_softmax([s_lat, s_ctx, sink_l])
    out = (jnp.einsum('bnkgqs,bnskd->bnqkgd', p_lat.astype(v.dtype), vb)
           + jnp.einsum('bnkgqc,bckd->bnqkgd', p_ctx.astype(vc.dtype), vc))
    return out.reshape(b, t, h * d)


def context_gqa(qc, kc, vc, sink):
    b, l, h, d = qc.shape
    kvh = kc.shape[2]
    g = h // kvh
    qg = qc.reshape(b, l, kvh, g, d)
    s = jnp.einsum('bqkgd,bskd->bkgqs', qg, kc, preferred_element_type=jnp.float32) * d ** -0.5
    p, _ = joint_softmax([s, sink.astype(jnp.float32).reshape(1, kvh, g, 1, 1)])
    out = jnp.einsum('bkgqs,bskd->bqkgd', p.astype(vc.dtype), vc)
    return out.reshape(b, l, h * d)


def mixer_ab(hl, hc, w_in, conv_w, conv_b, ln_g, ln_b, sink, w_out, rope, need_ctx):
    cos, sin = rope
    n_glu = 2 * CONV_A_DIM
    n_q = WIN_HEADS * WIN_HEAD_DIM
    n_kv = WIN_KV_HEADS * WIN_HEAD_DIM

    def split(z):
        lead = z.shape[:-1]
        glu = z[..., :n_glu]
        q = z[..., n_glu:n_glu + n_q].reshape(*lead, WIN_HEADS, WIN_HEAD_DIM)
        k = z[..., n_glu + n_q:n_glu + n_q + n_kv].reshape(*lead, WIN_KV_HEADS, WIN_HEAD_DIM)
        v = z[..., n_glu + n_q + n_kv:].reshape(*lead, WIN_KV_HEADS, WIN_HEAD_DIM)
        return glu, q, k, v

    glu_l, q_l, k_l, v_l = split(hl @ w_in)
    glu_c, q_c, k_c, v_c = split(hc @ w_in)
    a_l = conformer_conv(glu_l, conv_w, conv_b, ln_g, ln_b)
    b_l = window_gqa_latent(apply_rope(q_l, cos, sin), q_l, apply_rope(k_l, cos, sin), v_l, k_c, v_c, sink)
    y_l = jnp.concatenate([a_l, b_l], axis=-1) @ w_out
    if not need_ctx:
        return y_l, None
    a_c = conformer_conv(glu_c, conv_w, conv_b, ln_g, ln_b)
    b_c = context_gqa(q_c, k_c, v_c, sink)
    y_c = jnp.concatenate([a_c, b_c], axis=-1) @ w_out
    return y_l, y_c


def block_diag(x, w, b):
    xr = x.reshape(*x.shape[:-1], LRU_BLOCKS, LRU_BLOCK_DIM)
    return jnp.einsum('btnh,nhk->btnk', xr, w).reshape(x.shape) + b


def linear_scan(a, bx, h0, reverse):
    def combine(left, right):
        a1, b1 = left
        a2, b2 = right
        return a1 * a2, a2 * b1 + b2
    a_cum, b_cum = lax.associative_scan(combine, (a, bx), axis=1, reverse=reverse)
    return a_cum * h0[:, None, :] + b_cum


def rglru_direction(u_lat, u_ctx, conv_w, conv_b, gate_w, gate_b, lam, reverse):
    pad = (0, LRU_CONV_WIDTH - 1) if reverse else (LRU_CONV_WIDTH - 1, 0)
    neg_sp = jax.nn.softplus(-lam.astype(jnp.float32))

    def coeffs(u):
        xc = dwconv(u, conv_w, conv_b, pad[0], pad[1])
        r = jax.nn.sigmoid(block_diag(xc, gate_w[0], gate_b[0]).astype(jnp.float32))
        i = jax.nn.sigmoid(block_diag(xc, gate_w[1], gate_b[1]).astype(jnp.float32))
        log_a = -LRU_C * r * neg_sp
        bx = jnp.sqrt(-jnp.expm1(2.0 * log_a)) * i * xc.astype(jnp.float32)
        return jnp.exp(log_a), bx

    a_c, b_c = coeffs(u_ctx)
    h_ctx = linear_scan(a_c, b_c, jnp.zeros((u_ctx.shape[0], u_ctx.shape[-1]), jnp.float32), reverse)
    h0 = h_ctx[:, 0] if reverse else h_ctx[:, -1]
    a_l, b_l = coeffs(u_lat)
    h_lat = linear_scan(a_l, b_l, h0, reverse)
    return h_lat, h_ctx


def mla_q(cq, q_norm, w_uq):
    q = (rms_norm(cq, q_norm) @ w_uq).reshape(*cq.shape[:-1], MLA_HEADS, MLA_NOPE + MLA_ROPE)
    return q[..., :MLA_NOPE], q[..., MLA_NOPE:]


def mla_kv(ckv, kv_norm, w_ukv):
    kv = (rms_norm(ckv, kv_norm) @ w_ukv).reshape(*ckv.shape[:-1], MLA_HEADS, MLA_NOPE + MLA_V)
    return kv[..., :MLA_NOPE], kv[..., MLA_NOPE:]


def mla_latent(q_nope, q_rope, q_rope_raw, k_nope, k_rope, v, kc_nope, kc_rope, vc):
    b, t, h, _ = q_nope.shape
    nb = t // BLOCK
    scale = (MLA_NOPE + MLA_ROPE) ** -0.5

    def to_blocks(z):
        return jnp.moveaxis(z.reshape(b, nb, BLOCK, *z.shape[2:]), 1, 0)

    def one_block(args):
        qn, qr, qrr = args
        s_lat = (jnp.einsum('bqhd,bshd->bhqs', qn, k_nope, preferred_element_type=jnp.float32)
                 + jnp.einsum('bqhr,bsr->bhqs', qr, k_rope, preferred_element_type=jnp.float32)) * scale
        s_ctx = (jnp.einsum('bqhd,bshd->bhqs', qn, kc_nope, preferred_element_type=jnp.float32)
                 + jnp.einsum('bqhr,bsr->bhqs', qrr, kc_rope, preferred_element_type=jnp.float32)) * scale
        p_lat, p_ctx = joint_softmax([s_lat, s_ctx])
        return (jnp.einsum('bhqs,bshd->bqhd', p_lat.astype(v.dtype), v)
                + jnp.einsum('bhqs,bshd->bqhd', p_ctx.astype(vc.dtype), vc))

    out = lax.map(one_block, (to_blocks(q_nope), to_blocks(q_rope), to_blocks(q_rope_raw)))
    return jnp.moveaxis(out, 0, 1).reshape(b, t, h * MLA_V)


def mla_context(qn, qr, kn, kr, v):
    scale = (MLA_NOPE + MLA_ROPE) ** -0.5
    s = (jnp.einsum('bqhd,bshd->bhqs', qn, kn, preferred_element_type=jnp.float32)
         + jnp.einsum('bqhr,bsr->bhqs', qr, kr, preferred_element_type=jnp.float32)) * scale
    p = jax.nn.softmax(s, axis=-1)
    out = jnp.einsum('bhqs,bshd->bqhd', p.astype(v.dtype), v)
    return out.reshape(*qn.shape[:2], MLA_HEADS * MLA_V)


def mixer_cd(hl, hc, w_in, lru_conv_w, lru_conv_b, lru_gate_w, lru_gate_b, lru_lambda,
             q_norm, w_uq, kv_norm, w_ukv, w_out, rope, need_ctx):
    cos, sin = rope
    o1 = LRU_DIM
    o2 = 2 * LRU_DIM
    o3 = o2 + MLA_Q_RANK
    o4 = o3 + MLA_KV_RANK

    def split(z):
        return z[..., :o1], z[..., o1:o2], z[..., o2:o3], z[..., o3:o4], z[..., o4:]

    xb_l, gt_l, cq_l, ckv_l, kr_l = split(hl @ w_in)
    xb_c, gt_c, cq_c, ckv_c, kr_c = split(hc @ w_in)
    hf_l, hf_c = rglru_direction(xb_l, xb_c, lru_conv_w[0], lru_conv_b[0], lru_gate_w[0], lru_gate_b[0],
                                 lru_lambda[0], False)
    hb_l, hb_c = rglru_direction(xb_l, xb_c, lru_conv_w[1], lru_conv_b[1], lru_gate_w[1], lru_gate_b[1],
                                 lru_lambda[1], True)
    c_l = ((hf_l + hb_l) * jax.nn.gelu(gt_l.astype(jnp.float32))).astype(hl.dtype)
    qn_l, qr_l = mla_q(cq_l, q_norm, w_uq)
    kn_l, v_l = mla_kv(ckv_l, kv_norm, w_ukv)
    kn_c, v_c = mla_kv(ckv_c, kv_norm, w_ukv)
    d_l = mla_latent(qn_l, apply_rope(qr_l, cos, sin), qr_l, kn_l, apply_rope(kr_l, cos, sin), v_l,
                     kn_c, kr_c, v_c)
    y_l = jnp.concatenate([c_l, d_l], axis=-1) @ w_out
    if not need_ctx:
        return y_l, None
    c_c = ((hf_c + hb_c) * jax.nn.gelu(gt_c.astype(jnp.float32))).astype(hc.dtype)
    qn_c, qr_c = mla_q(cq_c, q_norm, w_uq)
    d_c = mla_context(qn_c, qr_c, kn_c, kr_c, v_c)
    y_c = jnp.concatenate([c_c, d_c], axis=-1) @ w_out
    return y_l, y_c


def conv_glu_ffn(h, w_up, conv_w, conv_b, w_down):
    z = h @ w_up
    g, u = z[..., :FFN_DIM], z[..., FFN_DIM:]
    half = FFN_CONV_WIDTH // 2
    g = dwconv(g, conv_w, conv_b, half, half)
    return (jax.nn.gelu(g) * u) @ w_down


def setup_inputs(seed: int = 0) -> dict:
    key = jax.random.key(seed)
    ks = iter(jax.random.split(key, 40))
    f32 = jnp.float32

    def nrm(shape, fan_in, scale=1.0):
        return jax.random.normal(next(ks), shape, f32) * (scale * fan_in ** -0.5)

    def gain(shape):
        return 1.0 + 0.05 * jax.random.normal(next(ks), shape, f32)

    def small(shape):
        return 0.01 * jax.random.normal(next(ks), shape, f32)

    x = jax.random.normal(next(ks), (BATCH, SEQ, D_MODEL), f32)
    c = jax.random.normal(next(ks), (BATCH, D_MODEL), f32)
    ctx = jax.random.normal(next(ks), (BATCH, CTX_LEN, D_MODEL), f32)
    c_ctx = jax.random.normal(next(ks), (D_MODEL,), f32)
    w_mod = nrm((DEPTH, D_MODEL, N_MOD * D_MODEL), D_MODEL, 0.5)
    b_mod = small((DEPTH, N_MOD * D_MODEL))
    norm_g = gain((DEPTH, 4, D_MODEL))
    ffn_w_up = nrm((DEPTH, D_MODEL, 2 * FFN_DIM), D_MODEL)
    ffn_conv_w = nrm((DEPTH, FFN_CONV_WIDTH, FFN_DIM), FFN_CONV_WIDTH)
    ffn_conv_b = small((DEPTH, FFN_DIM))
    ffn_w_down = nrm((DEPTH, FFN_DIM, D_MODEL), FFN_DIM)
    ab_w_in = nrm((N_EVEN, D_MODEL, AB_IN), D_MODEL)
    a_conv_w = nrm((N_EVEN, CONV_A_WIDTH, CONV_A_DIM), CONV_A_WIDTH)
    a_conv_b = small((N_EVEN, CONV_A_DIM))
    a_ln_g = gain((N_EVEN, CONV_A_DIM))
    a_ln_b = small((N_EVEN, CONV_A_DIM))
    b_sink = 0.5 * jax.random.normal(next(ks), (N_EVEN, WIN_HEADS), f32)
    ab_w_out = nrm((N_EVEN, AB_OUT, D_MODEL), AB_OUT)
    cd_w_in = nrm((N_ODD, D_MODEL, CD_IN), D_MODEL)
    lru_conv_w = nrm((N_ODD, 2, LRU_CONV_WIDTH, LRU_DIM), LRU_CONV_WIDTH)
    lru_conv_b = small((N_ODD, 2, LRU_DIM))
    lru_gate_w = nrm((N_ODD, 2, 2, LRU_BLOCKS, LRU_BLOCK_DIM, LRU_BLOCK_DIM), LRU_BLOCK_DIM)
    lru_gate_b = small((N_ODD, 2, 2, LRU_DIM))
    a_pow_c = jax.random.uniform(next(ks), (N_ODD, 2, LRU_DIM), f32, 0.9, 0.999)
    a0 = a_pow_c ** (1.0 / LRU_C)
    lru_lambda = jnp.log(a0) - jnp.log1p(-a0)
    mla_q_norm = gain((N_ODD, MLA_Q_RANK))
    mla_w_uq = nrm((N_ODD, MLA_Q_RANK, MLA_HEADS * (MLA_NOPE + MLA_ROPE)), MLA_Q_RANK)
    mla_kv_norm = gain((N_ODD, MLA_KV_RANK))
    mla_w_ukv = nrm((N_ODD, MLA_KV_RANK, MLA_HEADS * (MLA_NOPE + MLA_V)), MLA_KV_RANK)
    cd_w_out = nrm((N_ODD, CD_OUT, D_MODEL), CD_OUT)
    return {"x": x, "c": c, "ctx": ctx, "c_ctx": c_ctx, "w_mod": w_mod, "b_mod": b_mod,
            "norm_g": norm_g, "ffn_w_up": ffn_w_up, "ffn_conv_w": ffn_conv_w, "ffn_conv_b": ffn_conv_b,
            "ffn_w_down": ffn_w_down, "ab_w_in": ab_w_in, "a_conv_w": a_conv_w, "a_conv_b": a_conv_b,
            "a_ln_g": a_ln_g, "a_ln_b": a_ln_b, "b_sink": b_sink, "ab_w_out": ab_w_out,
            "cd_w_in": cd_w_in, "lru_conv_w": lru_conv_w, "lru_conv_b": lru_conv_b,
            "lru_gate_w": lru_gate_w, "lru_gate_b": lru_gate_b, "lru_lambda": lru_lambda,
            "mla_q_norm": mla_q_norm, "mla_w_uq": mla_w_uq, "mla_kv_norm": mla_kv_norm,
            "mla_w_ukv": mla_w_ukv, "cd_w_out": cd_w_out}


def reference(x, c, ctx, c_ctx, w_mod, b_mod, norm_g, ffn_w_up, ffn_conv_w, ffn_conv_b, ffn_w_down,
              ab_w_in, a_conv_w, a_conv_b, a_ln_g, a_ln_b, b_sink, ab_w_out,
              cd_w_in, lru_conv_w, lru_conv_b, lru_gate_w, lru_gate_b, lru_lambda,
              mla_q_norm, mla_w_uq, mla_kv_norm, mla_w_ukv, cd_w_out):
    n_tokens = x.shape[1]
    rows = n_tokens // GRID_W
    rope_win = axial_rope_tables(rows, WIN_HEAD_DIM)
    rope_mla = axial_rope_tables(rows, MLA_ROPE)
    xl, xc = x, ctx
    for layer in range(DEPTH):
        last = layer == DEPTH - 1
        j = layer // 2
        ml = (jax.nn.silu(c) @ w_mod[layer] + b_mod[layer]).reshape(c.shape[0], 1, N_MOD, D_MODEL)
        mc = (jax.nn.silu(c_ctx) @ w_mod[layer] + b_mod[layer]).reshape(N_MOD, D_MODEL)
        g = norm_g[layer]
        hl = modulate(rms_norm(xl, g[0]), ml[:, :, 0], ml[:, :, 1])
        hc = modulate(rms_norm(xc, g[0]), mc[0], mc[1])
        if layer % 2 == 0:
            yl, yc = mixer_ab(hl, hc, ab_w_in[j], a_conv_w[j], a_conv_b[j], a_ln_g[j], a_ln_b[j],
                              b_sink[j], ab_w_out[j], rope_win, not last)
        else:
            yl, yc = mixer_cd(hl, hc, cd_w_in[j], lru_conv_w[j], lru_conv_b[j], lru_gate_w[j],
                              lru_gate_b[j], lru_lambda[j], mla_q_norm[j], mla_w_uq[j], mla_kv_norm[j],
                              mla_w_ukv[j], cd_w_out[j], rope_mla, not last)
        xl = xl + ml[:, :, 2] * rms_norm(yl, g[1])
        hl = modulate(rms_norm(xl, g[2]), ml[:, :, 3], ml[:, :, 4])
        xl = xl + ml[:, :, 5] * rms_norm(
            conv_glu_ffn(hl, ffn_w_up[layer], ffn_conv_w[layer], ffn_conv_b[layer], ffn_w_down[layer]), g[3])
        if not last:
            xc = xc + mc[2] * rms_norm(yc, g[1])
            hc = modulate(rms_norm(xc, g[2]), mc[3], mc[4])
            xc = xc + mc[5] * rms_norm(
                conv_glu_ffn(hc, ffn_w_up[layer], ffn_conv_w[layer], ffn_conv_b[layer], ffn_w_down[layer]), g[3])
    return xl
```

```python
import numpy as np
import ml_dtypes
from contextlib import ExitStack
import concourse.bass as bass
import concourse.mybir as mybir
from concourse.bass_utils import run_bass_kernel_spmd

F32 = mybir.dt.float32
BF16 = mybir.dt.bfloat16
AF = mybir.ActivationFunctionType
ALU = mybir.AluOpType

D = 1024
CTX = 256
EPS = 1e-6
FFN = 2816
NJ = FFN // 128


class Buf:
    __slots__ = ("w", "r")

    def __init__(self):
        self.w = None
        self.r = {}


class Op:
    __slots__ = ("eng", "chan", "seq", "fn", "waits", "signal", "clock", "val", "isdma")


class Sched:
    COMPUTE = ("pe", "act", "dve", "pool")

    def __init__(self, nc, es):
        self.nc = nc
        self.eobj = dict(pe=nc.tensor, act=nc.scalar, dve=nc.vector, pool=nc.gpsimd, sp=nc.sync)
        self.sem = {}
        for e in self.COMPUTE:
            self.sem[e] = es.enter_context(nc.semaphore("sem_" + e))
        self.nslot = {"sp": 12, "pool": 8}
        for q, n in self.nslot.items():
            for k in range(n):
                self.sem[(q, k)] = es.enter_context(nc.semaphore("dq_%s%d" % (q, k)))
        self.clock = {e: {} for e in self.eobj}
        self.seq = {e: 0 for e in self.COMPUTE}
        self.sigcount = {e: 0 for e in self.COMPUTE}
        self.dcount = {q: 0 for q in self.nslot}
        self.slot_last = {}
        self.last = {}
        self.pending = []
        self.bar = None
        self.nops = 0
        self.nwaits = 0
        self.dummy = es.enter_context(nc.sbuf_tensor("sched_dummy", [128, 8], F32))

    def _add(self, eng, chan, seq, fn, r, w, isdma, extra=()):
        op = Op()
        op.eng, op.chan, op.seq, op.fn, op.isdma = eng, chan, seq, fn, isdma
        op.signal = isdma
        op.val = 16 * (seq + 1) if isdma else None
        deps = {}

        def need(d):
            if d is None:
                return
            cur = deps.get(d.chan)
            if cur is None or cur.seq < d.seq:
                deps[d.chan] = d

        r = list(r)
        if self.bar is not None:
            r.append(self.bar)
        for b in r:
            need(b.w)
        for b in w:
            need(b.w)
            for d in b.r.values():
                need(d)
        for d in extra:
            need(d)
        clk = self.clock[eng]
        waits = []
        for d in deps.values():
            if d.chan == "pe" and eng == "pe":
                continue
            if clk.get(d.chan, -1) >= d.seq:
                continue
            waits.append(d)
            d.signal = True
            for k, v in d.clock.items():
                if clk.get(k, -1) < v:
                    clk[k] = v
            if clk.get(d.chan, -1) < d.seq:
                clk[d.chan] = d.seq
        op.waits = waits
        op.clock = dict(clk)
        for b in r:
            cur = b.r.get(chan)
            if cur is None or cur.seq < seq:
                b.r[chan] = op
        for b in w:
            b.w = op
            b.r = {}
        self.last[chan] = op
        self.pending.append(op)
        self.nops += 1
        self.nwaits += len(waits)
        return op

    def op(self, eng, fn, r=(), w=()):
        s = self.seq[eng]
        self.seq[eng] = s + 1
        return self._add(eng, eng, s, fn, r, w, False)

    def dma(self, q, out, in_, r=(), w=(), **kw):
        i = self.dcount[q]
        self.dcount[q] = i + 1
        n = self.nslot[q]
        slot, gen = i % n, i // n
        chan = (q, slot)
        prev = self.slot_last.get(chan)
        extra = [prev] if prev is not None else []
        op = self._add(q, chan, gen, lambda e: e.dma_start(out=out, in_=in_, **kw), r, w, True, extra)
        self.slot_last[chan] = op
        return op

    def barrier(self):
        b = Buf()
        extra = list(self.last.values())
        dummy = self.dummy
        s = self.seq["pool"]
        self.seq["pool"] = s + 1
        m = self._add("pool", "pool", s, lambda e: e.memset(dummy[:], 0.0), [], [b], False, extra)
        m.signal = True
        self.bar = b
        self.flush()

    def flush(self):
        for op in self.pending:
            e = self.eobj[op.eng]
            for d in op.waits:
                e.wait_ge(self.sem[d.chan], d.val)
            if op.signal and not op.isdma:
                self.sigcount[op.chan] += 1
                op.val = self.sigcount[op.chan]
            ins = op.fn(e)
            if op.signal:
                ins.then_inc(self.sem[op.chan], 16 if op.isdma else 1)
            op.fn = None
        self.pending = []


class Ring:
    uid = 0

    def __init__(self, nc, es, name, shape, dtype, n):
        Ring.uid += 1
        self.t = [es.enter_context(nc.sbuf_tensor("%s_%d_%d" % (name, Ring.uid, i), shape, dtype)) for i in range(n)]
        self.b = [Buf() for _ in range(n)]
        self.i = 0

    def next(self):
        k = self.i % len(self.t)
        self.i += 1
        return self.t[k], self.b[k]


class Rot:
    def __init__(self, items):
        self.items = list(items)
        self.i = 0

    def next(self):
        v = self.items[self.i % len(self.items)]
        self.i += 1
        return v


def _rope_tables(T, dim):
    rows = T // 64
    row = np.repeat(np.arange(rows), 64).astype(np.float32)
    col = np.tile(np.arange(64), rows).astype(np.float32)
    nf = dim // 4
    inv = (np.float32(10000.0) ** (-np.arange(nf, dtype=np.float32) / np.float32(nf))).astype(np.float32)
    ar = row[:, None] * inv
    ac = col[:, None] * inv
    ang = np.concatenate([ar, ar, ac, ac], axis=-1)
    return np.ascontiguousarray(np.cos(ang).T.astype(np.float32)), np.ascontiguousarray(np.sin(ang).T.astype(np.float32))


def _rot_T(dim):
    q = dim // 4
    R = np.zeros((dim, dim), np.float32)
    for i in range(q):
        R[i, q + i] = -1.0
        R[q + i, i] = 1.0
        R[2 * q + i, 3 * q + i] = -1.0
        R[3 * q + i, 2 * q + i] = 1.0
    return np.ascontiguousarray(R.T)


def make_consts(T):
    cw, sw = _rope_tables(T, 64)
    cm, sm = _rope_tables(T, 32)
    ident = np.eye(128, dtype=np.float32)
    sel = np.zeros((128, 64), np.float32)
    sel[64, :] = 1.0
    b = np.arange(128)[:, None]
    a = np.arange(128)[None, :]
    m1 = (b <= a).astype(np.float32)
    m2 = (a <= b).astype(np.float32)
    r64 = _rot_T(64)
    r32 = _rot_T(32)
    return {
        "k_ident": ident, "k_sel": sel, "k_m1": m1, "k_m2": m2,
        "k_r64": np.concatenate([r64, r64], 0), "k_r32": np.concatenate([r32, r32], 0),
        "k_cw": cw, "k_sw": sw, "k_cm": cm, "k_sm": sm,
    }


WEIGHT_SHAPES = {
    "w_mod": [2, 1024, 6144], "b_mod": [2, 6144], "norm_g": [2, 4, 1024], "ffn_w_up": [2, 1024, 5632],
    "ffn_conv_w": [2, 3, 2816], "ffn_conv_b": [2, 2816], "ffn_w_down": [2, 2816, 1024],
    "ab_w_in": [1, 1024, 1792], "a_conv_w": [1, 31, 512], "a_conv_b": [1, 512], "a_ln_g": [1, 512],
    "a_ln_b": [1, 512], "b_sink": [1, 8], "ab_w_out": [1, 1024, 1024], "cd_w_in": [1, 1024, 1696],
    "lru_conv_w": [1, 2, 4, 512], "lru_conv_b": [1, 2, 512], "lru_gate_w": [1, 2, 2, 8, 64, 64],
    "lru_gate_b": [1, 2, 2, 512], "lru_lambda": [1, 2, 512], "mla_q_norm": [1, 384],
    "mla_w_uq": [1, 384, 768], "mla_kv_norm": [1, 256], "mla_w_ukv": [1, 256, 1024],
    "cd_w_out": [1, 1024, 1024],
}


def build(T=4096, dbg=(), stop=None):
    nc = bass.Bass("TRN2", target_bir_lowering=False)
    TT = T + CTX
    NT = T // 512
    NBL = T // 128
    NB = NBL + CTX // 128
    tiles = [(i * 512, 512, False) for i in range(NT)] + [(T, CTX, True)]
    lat_tiles = tiles[:NT]

    def din(name, shape):
        return nc.dram_tensor(name, list(shape), F32, kind="ExternalInput").ap()

    x_in = din("x", [T, D])
    ctx_in = din("ctx", [CTX, D])
    c_in = din("c", [8, 128])
    cctx_in = din("c_ctx", [8, 128])
    W = {k: din(k, s) for k, s in WEIGHT_SHAPES.items()}
    KC = {k: din(k, v.shape) for k, v in make_consts(T).items()}
    y_out = nc.dram_tensor("y", [T, D], F32, kind="ExternalOutput").ap()

    def scratch(name, shape, dt):
        kind = "ExternalOutput" if name in dbg else "Internal"
        return nc.dram_tensor(name, list(shape), dt, kind=kind).ap()

    XT = scratch("XT", [D, TT], F32)
    U0 = scratch("U0", [512, TT], BF16)
    A0 = scratch("A0", [512, TT], BF16)
    B0 = scratch("B0", [512, TT], BF16)
    GS = scratch("GS", [FFN, TT], BF16)
    US = scratch("US", [FFN, TT], BF16)
    XBS = scratch("XBS", [512, TT], F32)
    GTS = scratch("GTS", [512, T], F32)
    HFS = scratch("HFS", [512, T], F32)
    C1 = scratch("C1", [512, T], BF16)
    D1 = scratch("D1", [512, T], BF16)
    DBGV = scratch("DBGV", [128, 512], F32)
    QS = scratch("QS", [8, 128, TT], BF16)
    XTv = XT.rearrange("(c p) t -> p c t", p=128)
    XTB = [Buf() for _ in tiles]
    U0B = [Buf() for _ in tiles]
    A0B = [Buf() for _ in tiles]
    B0B = [Buf() for _ in tiles]
    GSB = [Buf() for _ in tiles]
    USB = [Buf() for _ in tiles]
    XBSB = [Buf() for _ in tiles]
    GTSB = [Buf() for _ in tiles]
    HFSB = [Buf() for _ in tiles]
    C1B = [Buf() for _ in tiles]
    D1B = [Buf() for _ in tiles]

    ges = ExitStack()
    S = Sched(nc, ges)
    PS = ges.enter_context(nc.psum_tensor("PS", [128, 8, 512], F32))
    PB = [Buf() for _ in range(8)]

    def sb(es, name, shape, dt):
        Ring.uid += 1
        return es.enter_context(nc.sbuf_tensor("%s_%d" % (name, Ring.uid), list(shape), dt))

    def MM(out, lhsT, rhs, st, sp, r, w):
        S.op("pe", lambda e: e.matmul(out, lhsT, rhs, start=st, stop=sp), r, w)

    def TR(out, in_, ident, r, w):
        S.op("pe", lambda e: e.transpose(out, in_, ident), r, w)

    def ACT(out, in_, func, r, w, bias=None, scale=None):
        kw = {}
        if bias is not None:
            kw["bias"] = bias
        if scale is not None:
            kw["scale"] = scale
        S.op("act", lambda e: e.activation(out, in_, func, **kw), r, w)

    def CP(eng, out, in_, r, w):
        if eng == "act":
            S.op("act", lambda e: e.copy(out, in_), r, w)
        else:
            S.op(eng, lambda e: e.tensor_copy(out, in_), r, w)

    def TTo(eng, out, a, b, op, r, w):
        S.op(eng, lambda e: e.tensor_tensor(out, a, b, op), r, w)

    def TS(eng, out, a, s1, s2, op0, op1, r, w):
        if s2 is None:
            S.op(eng, lambda e: e.tensor_scalar(out, a, s1, None, op0), r, w)
        else:
            S.op(eng, lambda e: e.tensor_scalar(out, a, s1, s2, op0, op1), r, w)

    def STT(out, in0, scalar, in1, op0, op1, r, w):
        S.op("dve", lambda e: e.scalar_tensor_tensor(out, in0, scalar, in1, op0, op1), r, w)

    def RCP(out, in_, r, w):
        S.op("dve", lambda e: e.reciprocal(out, in_), r, w)

    def MSET(eng, ap, val, w):
        S.op(eng, lambda e: e.memset(ap, val), [], w)

    identF = sb(ges, "identF", [128, 128], F32)
    identB = sb(ges, "identB", [128, 128], BF16)
    onesB = sb(ges, "onesB", [128, 128], BF16)
    onesF = sb(ges, "onesF", [128, 128], F32)
    selF = sb(ges, "selF", [128, 64], F32)
    m1B = sb(ges, "m1B", [128, 128], BF16)
    m2B = sb(ges, "m2B", [128, 128], BF16)
    r64B = sb(ges, "r64B", [128, 64], BF16)
    r32B = sb(ges, "r32B", [64, 32], BF16)
    CB = Buf()
    S.dma("sp", identF[:], KC["k_ident"], w=[CB])
    S.dma("sp", selF[:], KC["k_sel"], w=[CB])
    S.dma("pool", m1B[:], KC["k_m1"], w=[CB])
    S.dma("pool", m2B[:], KC["k_m2"], w=[CB])
    S.dma("pool", r64B[:], KC["k_r64"], w=[CB])
    S.dma("pool", r32B[:], KC["k_r32"], w=[CB])
    CP("dve", identB[:], identF[:], [CB], [CB])
    MSET("dve", onesB[:], 1.0, [CB])
    MSET("dve", onesF[:], 1.0, [CB])

    cols = {}
    colspec = {
        "g": (W["norm_g"], 64), "bm": (W["b_mod"], 96), "fcb": (W["ffn_conv_b"], 44),
        "fcw0": (W["ffn_conv_w"][0], 66), "fcw1": (W["ffn_conv_w"][1], 66),
        "acb": (W["a_conv_b"], 4), "alg": (W["a_ln_g"], 4), "alb": (W["a_ln_b"], 4),
        "acw": (W["a_conv_w"], 124), "lcw": (W["lru_conv_w"], 32), "lcb": (W["lru_conv_b"], 8),
        "lgb": (W["lru_gate_b"], 16), "lam": (W["lru_lambda"], 8), "qn": (W["mla_q_norm"], 3),
        "kvn": (W["mla_kv_norm"], 2), "c": (c_in, 8), "cc": (cctx_in, 8),
    }
    for name, (src, n) in colspec.items():
        cols[name] = sb(ges, "col_" + name, [128, n], F32)
    esink = sb(ges, "esink", [64, 8], F32)
    scT = sb(ges, "scT", [128, 8, 2], F32)
    MODT = sb(ges, "MODT", [128, 2, 2, 48], F32)
    A1 = sb(ges, "A1", [128, 2, 2, 8], F32)
    G1 = sb(ges, "G1", [128, 2, 2, 8], F32)
    A2 = sb(ges, "A2", [128, 2, 2, 8], F32)
    G2 = sb(ges, "G2", [128, 2, 2, 8], F32)
    epsT = sb(ges, "epsT", [128, 1], F32)
    cch = sb(ges, "cch", [128, 2, 8], F32)
    pre = ExitStack()
    rows_ring = Ring(nc, pre, "rows", [128, 128], F32, 2)
    for i, (name, (src, n)) in enumerate(colspec.items()):
        dst = cols[name]
        nd = len(src.shape)
        if nd == 1:
            s2 = src.rearrange("(r p) -> r p", p=128)
        elif nd == 2 and src.shape[1] == 128:
            s2 = src
        else:
            names = " ".join("a%d" % k for k in range(nd - 1))
            s2 = src.rearrange("%s (r p) -> (%s r) p" % (names, names), p=128)
        rt, rb = rows_ring.next()
        S.dma("sp", rt[0:n, :], s2, w=[rb])
        bank = 6 + (i % 2)
        TR(PS[:, bank, 0:n], rt[0:n, :], identF[0:n, 0:n], [rb, CB], [PB[bank]])
        CP("dve", dst[:], PS[:, bank, 0:n], [PB[bank]], [CB])
    sk = sb(pre, "sk", [1, 8], F32)
    skb = Buf()
    S.dma("sp", sk[:], W["b_sink"], w=[skb])
    MM(PS[0:64, 5, 0:8], onesF[0:1, 0:64], sk[0:1, :], True, True, [skb, CB], [PB[5]])
    ACT(esink[:], PS[0:64, 5, 0:8], AF.Exp, [PB[5]], [CB])
    ACT(scT[:, :, 0], cols["c"][:], AF.Silu, [CB], [CB])
    ACT(scT[:, :, 1], cols["cc"][:], AF.Silu, [CB], [CB])
    S.barrier()
    pre.close()

    gc = cols["g"]

    def phase_mod(l):
        with ExitStack() as es:
            wring = Ring(nc, es, "wmod", [128, 8, 768], F32, 2)
            wsrc = W["w_mod"][l].rearrange("(kc p) n -> p kc n", p=128)
            for jb in range(8):
                wt, wb = wring.next()
                S.dma("sp", wt[:], wsrc[:, :, jb * 768:(jb + 1) * 768], w=[wb])
                for jj in range(6):
                    j = jb * 6 + jj
                    for kc in range(8):
                        MM(PS[:, 6, 2 * j:2 * j + 2], wt[:, kc, jj * 128:(jj + 1) * 128], scT[:, kc, :],
                           kc == 0, kc == 7, [wb, CB], [PB[6]])
            pv = PS[:, 6, 0:96].rearrange("p (j s) -> p j s", s=2)
            for s in range(2):
                TTo("dve", MODT[:, l, s, :], pv[:, :, s], cols["bm"][:, l * 48:(l + 1) * 48], ALU.add, [PB[6], CB], [CB])
                STT(A1[:, l, s, :], MODT[:, l, s, 8:16], 1.0, gc[:, l * 32:l * 32 + 8], ALU.add, ALU.mult, [CB], [CB])
                TTo("dve", G1[:, l, s, :], MODT[:, l, s, 16:24], gc[:, l * 32 + 8:l * 32 + 16], ALU.mult, [CB], [CB])
                STT(A2[:, l, s, :], MODT[:, l, s, 32:40], 1.0, gc[:, l * 32 + 16:l * 32 + 24], ALU.add, ALU.mult, [CB], [CB])
                TTo("dve", G2[:, l, s, :], MODT[:, l, s, 40:48], gc[:, l * 32 + 24:l * 32 + 32], ALU.mult, [CB], [CB])
            S.barrier()

    def phase_tin():
        with ExitStack() as es:
            xin_ring = Ring(nc, es, "xin", [128, D], F32, 3)
            xt_ring = Ring(nc, es, "xtt", [128, 8, 512], F32, 2)
            for j, (t0, n, isc) in enumerate(tiles):
                src = ctx_in if isc else x_in
                s0 = 0 if isc else t0
                for b in range(n // 128):
                    xin, xb_ = xin_ring.next()
                    S.dma("sp", xin[:], src[s0 + b * 128:s0 + (b + 1) * 128, :], w=[xb_])
                    for fc in range(8):
                        TR(PS[:, fc, b * 128:(b + 1) * 128], xin[:, fc * 128:(fc + 1) * 128], identF[:], [xb_, CB], [PB[fc]])
                xt, xtb = xt_ring.next()
                for fc in range(8):
                    CP("act" if fc % 2 else "dve", xt[:, fc, 0:n], PS[:, fc, 0:n], [PB[fc]], [xtb])
                S.dma("pool", XTv[:, :, t0:t0 + n], xt[:, :, 0:n], r=[xtb], w=[XTB[j]])
            S.barrier()

    def stat_rstd(es_rings, src, srcb, nch, n, dim, bank):
        for c in range(nch):
            MM(PS[:, bank, 0:n], onesB[:], src[:, c, 0:n], c == 0, c == nch - 1, [srcb, CB], [PB[bank]])
        rs, rsb = es_rings["rs"].next()
        ACT(rs[:, 0:n], PS[:, bank, 0:n], AF.Sqrt, [PB[bank]], [rsb], bias=epsT[:, 0:1], scale=1.0 / dim)
        RCP(rs[:, 0:n], rs[:, 0:n], [rsb], [rsb])
        return rs, rsb

    MSET("dve", epsT[:], EPS, [CB])

    def prenorm(rings, xt, xb, n, Acol, SHcol, bank):
        sq, sqb = rings["sq"].next()
        ACT(sq[:, :, 0:n], xt[:, :, 0:n], AF.Square, [xb], [sqb])
        rs, rsb = stat_rstd(rings, sq, sqb, 8, n, D, bank)
        TTo("dve", xt[:, :, 0:n], xt[:, :, 0:n], rs[:, 0:n].unsqueeze(1).to_broadcast([128, 8, n]), ALU.mult, [xb, rsb], [xb])
        h, hb = rings["h"].next()
        for c in range(8):
            ACT(h[:, c, 0:n], xt[:, c, 0:n], AF.Identity, [xb, CB], [hb], bias=SHcol[:, c:c + 1], scale=Acol[:, c:c + 1])
        return h, hb

    def postnorm_residual(rings, ysb, yb, xt, xb, n, Gcol, bank):
        sq, sqb = rings["sq"].next()
        TTo("pool", sq[:, :, 0:n], ysb[:, :, 0:n], ysb[:, :, 0:n], ALU.mult, [yb], [sqb])
        rs, rsb = stat_rstd(rings, sq, sqb, 8, n, D, bank)
        TTo("dve", ysb[:, :, 0:n], ysb[:, :, 0:n], rs[:, 0:n].unsqueeze(1).to_broadcast([128, 8, n]), ALU.mult, [yb, rsb], [yb])
        for c in range(8):
            STT(xt[:, c, 0:n], ysb[:, c, 0:n], Gcol[:, c:c + 1], xt[:, c, 0:n], ALU.mult, ALU.add, [yb, xb, CB], [xb])

    def norm_rings(es, with_h=True, nsq=2):
        rings = {
            "sq": Ring(nc, es, "sq", [128, 8, 512], BF16, nsq),
            "rs": Ring(nc, es, "rs", [128, 512], F32, 2),
        }
        if with_h:
            rings["h"] = Ring(nc, es, "h", [128, 8, 512], BF16, 2)
        return rings

    def cast_load(dst, src, wb):
        S.dma("pool", dst, src, w=[wb])

    L0 = ExitStack()
    Klat = sb(L0, "Klat", [64, 2, T], BF16)
    Kctx = sb(L0, "Kctx", [128, 2, CTX], BF16)
    Vt = sb(L0, "Vt", [128, NB, 2, 66], BF16)
    QSB = [[Buf() for _ in tiles] for _ in range(8)]
    KLB = [Buf() for _ in tiles]
    KCB = Buf()
    VB = [Buf() for _ in tiles]
    cwv, swv = KC["k_cw"], KC["k_sw"]
    cmv, smv = KC["k_cm"], KC["k_sm"]

    def phase_p1_l0():
        l = 0
        with ExitStack() as es:
            NCOL = 1024 + 1024 + 256 + 128
            Wt = sb(es, "Wt0", [128, 8, NCOL], BF16)
            WB = [Buf() for _ in range(5)]
            wsrc = W["ab_w_in"][0].rearrange("(kc p) n -> p kc n", p=128)
            cast_load(Wt[:, :, 0:1024], wsrc[:, :, 0:1024], WB[0])
            qd = Wt[:, :, 1024:2048].rearrange("p k (h two d) -> p k h two d", two=2, d=64)
            qs = wsrc[:, :, 1024:1536].rearrange("p k (h d) -> p k h d", d=64)
            for dup in range(2):
                for kc in range(8):
                    cast_load(qd[:, kc, :, dup, :], qs[:, kc, :, :], WB[1 + dup])
            kd = Wt[:, :, 2048:2304].rearrange("p k (h two d) -> p k h two d", two=2, d=64)
            ks = wsrc[:, :, 1536:1664].rearrange("p k (h d) -> p k h d", d=64)
            for dup in range(2):
                for kc in range(8):
                    cast_load(kd[:, kc, :, dup, :], ks[:, kc, :, :], WB[3])
            cast_load(Wt[:, :, 2304:2432], wsrc[:, :, 1664:1792], WB[4])
            MSET("pool", Vt[:, :, :, 64:66], 1.0, VB)
            rings = norm_rings(es)
            x_ring = Ring(nc, es, "xt", [128, 8, 512], F32, 2)
            sg_ring = Ring(nc, es, "sg", [128, 512], F32, 2)
            ust_ring = Ring(nc, es, "ust", [128, 4, 512], BF16, 2)
            cs_ring = Ring(nc, es, "cs", [64, 2, 512], F32, 2)
            t1_ring = Ring(nc, es, "t1", [64, 512], F32, 2)
            t2_ring = Ring(nc, es, "t2", [64, 512], F32, 2)
            kraw_ring = Ring(nc, es, "kraw", [128, 512], BF16, 2)
            qst_ring = Ring(nc, es, "qst", [128, 512], BF16, 3)
            banks = Rot([0, 1, 2, 3, 4])
            rbanks = Rot([5, 6])
            loads = {}

            def issue_load(j):
                t0, n, isc = tiles[j]
                xt, xb = x_ring.next()
                S.dma("sp", xt[:, :, 0:n], XTv[:, :, t0:t0 + n], r=[XTB[j]], w=[xb])
                cs, csb = cs_ring.next()
                if not isc:
                    S.dma("sp", cs[:, 0, :], cwv[:, t0:t0 + n], w=[csb])
                    S.dma("sp", cs[:, 1, :], swv[:, t0:t0 + n], w=[csb])
                loads[j] = (xt, xb, cs, csb)

            issue_load(0)
            for j, (t0, n, isc) in enumerate(tiles):
                if j + 1 < len(tiles):
                    issue_load(j + 1)
                xt, xb, cs, csb = loads.pop(j)
                s = 1 if isc else 0
                h, hb = prenorm(rings, xt, xb, n, A1[:, l, s, :], MODT[:, l, s, 0:8], 7)

                def proj(col0, bank, M=128):
                    for kc in range(8):
                        MM(PS[0:M, bank, 0:n], Wt[:, kc, col0:col0 + M], h[:, kc, 0:n], kc == 0, kc == 7, [hb] + WB, [PB[bank]])

                ust, ustb = ust_ring.next()
                for i in range(4):
                    bg = banks.next()
                    proj(512 + 128 * i, bg)
                    sg, sgb = sg_ring.next()
                    ACT(sg[:, 0:n], PS[:, bg, 0:n], AF.Sigmoid, [PB[bg]], [sgb])
                    bv = banks.next()
                    proj(128 * i, bv)
                    TTo("dve", ust[:, i, 0:n], PS[:, bv, 0:n], sg[:, 0:n], ALU.mult, [PB[bv], sgb], [ustb])
                S.dma("pool", U0.rearrange("(c p) t -> p c t", p=128)[:, :, t0:t0 + n], ust[:, :, 0:n], r=[ustb], w=[U0B[j]])

                def rope(bank, rawsrc, rawb, dst, dstb):
                    rbk = rbanks.next()
                    MM(PS[0:64, rbk, 0:n], r64B[64:128, :], rawsrc, True, True, [rawb, CB], [PB[rbk]])
                    t1, t1b = t1_ring.next()
                    t2, t2b = t2_ring.next()
                    TTo("dve", t1[:, 0:n], PS[0:64, bank, 0:n], cs[:, 0, 0:n], ALU.mult, [PB[bank], csb], [t1b])
                    TTo("dve", t2[:, 0:n], PS[0:64, rbk, 0:n], cs[:, 1, 0:n], ALU.mult, [PB[rbk], csb], [t2b])
                    TTo("pool", dst, t1[:, 0:n], t2[:, 0:n], ALU.add, [t1b, t2b], [dstb])

                for hh in range(8):
                    bq = banks.next()
                    proj(1024 + 128 * hh, bq)
                    qst, qstb = qst_ring.next()
                    CP("act", qst[64:128, 0:n], PS[64:128, bq, 0:n], [PB[bq]], [qstb])
                    if not isc:
                        rope(bq, qst[64:128, 0:n], qstb, qst[0:64, 0:n], qstb)
                        S.dma("pool", QS[hh, :, t0:t0 + n], qst[:, 0:n], r=[qstb], w=[QSB[hh][j]])
                    else:
                        S.dma("pool", QS[hh, 64:128, t0:t0 + n], qst[64:128, 0:n], r=[qstb], w=[QSB[hh][j]])
                for g in range(2):
                    bk = banks.next()
                    proj(2048 + 128 * g, bk)
                    if isc:
                        CP("act", Kctx[64:128, g, :], PS[64:128, bk, 0:n], [PB[bk]], [KCB])
                    else:
                        kr, krb = kraw_ring.next()
                        CP("act", kr[64:128, 0:n], PS[64:128, bk, 0:n], [PB[bk]], [krb])
                        rope(bk, kr[64:128, 0:n], krb, Klat[0:64, g, t0:t0 + n], KLB[j])
                for b in range(n // 128):
                    bv = banks.next()
                    for kc in range(8):
                        MM(PS[:, bv, 0:128], h[:, kc, b * 128:(b + 1) * 128], Wt[:, kc, 2304:2432], kc == 0, kc == 7, [hb] + WB, [PB[bv]])
                    blk = (t0 // 128) + b
                    CP("act" if b % 2 else "dve", Vt[:, blk, :, 0:64], PS[:, bv, 0:128].rearrange("p (g d) -> p g d", g=2), [PB[bv]], [VB[j]])
            S.barrier()

    def phase_conva():
        with ExitStack() as es:
            Dg = sb(es, "DgA", [128, 4, 31, 128], BF16)
            DgB = Buf()
            for c in range(4):
                for k in range(31):
                    col = cols["acw"][:, k * 4 + c:k * 4 + c + 1]
                    TS("dve" if (k % 2) else "pool", Dg[:, c, k, :], identB[:], col, None, ALU.mult, None, [CB], [DgB])
            up_ring = Ring(nc, es, "up", [128, 4, 512 + 30], BF16, 2)
            ucv_ring = Ring(nc, es, "ucv", [128, 4, 512], F32, 2)
            usq_ring = Ring(nc, es, "usq", [128, 4, 512], F32, 2)
            st_ring = Ring(nc, es, "lnst", [128, 3, 512], F32, 2)
            tt_ring = Ring(nc, es, "lntt", [128, 512], F32, 2)
            ao_ring = Ring(nc, es, "ao", [128, 4, 512], BF16, 2)
            U0v = U0.rearrange("(c p) t -> p c t", p=128)
            A0v = A0.rearrange("(c p) t -> p c t", p=128)
            banks = Rot([0, 1, 2, 3])
            for j, (t0, n, isc) in enumerate(tiles):
                seg0, seg1 = (T, TT) if isc else (0, T)
                lo, hi = max(t0 - 15, seg0), min(t0 + n + 15, seg1)
                up, upb = up_ring.next()
                rd = [U0B[j]]
                if j > 0 and not isc:
                    rd.append(U0B[j - 1])
                if j + 1 < NT:
                    rd.append(U0B[j + 1])
                if lo > t0 - 15:
                    MSET("pool", up[:, :, 0:15], 0.0, [upb])
                if hi < t0 + n + 15:
                    MSET("pool", up[:, :, n + 15:n + 30], 0.0, [upb])
                S.dma("sp", up[:, :, lo - (t0 - 15):hi - (t0 - 15)], U0v[:, :, lo:hi], r=rd, w=[upb])
                ucv, ucvb = ucv_ring.next()
                usq, usqb = usq_ring.next()
                for c in range(4):
                    bk = banks.next()
                    for k in range(31):
                        MM(PS[:, bk, 0:n], Dg[:, c, k, :], up[:, c, k:k + n], k == 0, k == 30, [upb, DgB], [PB[bk]])
                    ACT(ucv[:, c, 0:n], PS[:, bk, 0:n], AF.Identity, [PB[bk], CB], [ucvb], bias=cols["acb"][:, c:c + 1])
                    ACT(usq[:, c, 0:n], PS[:, bk, 0:n], AF.Square, [PB[bk], CB], [usqb], bias=cols["acb"][:, c:c + 1])
                for c in range(4):
                    MM(PS[:, 4, 0:n], onesF[:], ucv[:, c, 0:n], c == 0, c == 3, [ucvb, CB], [PB[4]])
                for c in range(4):
                    MM(PS[:, 5, 0:n], onesF[:], usq[:, c, 0:n], c == 0, c == 3, [usqb, CB], [PB[5]])
                st, stb = st_ring.next()
                TS("dve", st[:, 0, 0:n], PS[:, 4, 0:n], 1.0 / 512, None, ALU.mult, None, [PB[4]], [stb])
                TTo("dve", st[:, 1, 0:n], st[:, 0, 0:n], st[:, 0, 0:n], ALU.mult, [stb], [stb])
                STT(st[:, 2, 0:n], PS[:, 5, 0:n], 1.0 / 512, st[:, 1, 0:n], ALU.mult, ALU.subtract, [PB[5], stb], [stb])
                ACT(st[:, 2, 0:n], st[:, 2, 0:n], AF.Sqrt, [stb, CB], [stb], bias=epsT[:, 0:1])
                RCP(st[:, 2, 0:n], st[:, 2, 0:n], [stb], [stb])
                ao, aob = ao_ring.next()
                for c in range(4):
                    tt, ttb = tt_ring.next()
                    TTo("dve", tt[:, 0:n], ucv[:, c, 0:n], st[:, 0, 0:n], ALU.subtract, [ucvb, stb], [ttb])
                    TTo("dve", tt[:, 0:n], tt[:, 0:n], st[:, 2, 0:n], ALU.mult, [ttb, stb], [ttb])
                    ACT(ao[:, c, 0:n], tt[:, 0:n], AF.Silu, [ttb, CB], [aob], bias=cols["alb"][:, c:c + 1], scale=cols["alg"][:, c:c + 1])
                S.dma("pool", A0v[:, :, t0:t0 + n], ao[:, :, 0:n], r=[aob], w=[A0B[j]])
            S.barrier()

    def attn_finalize(rings, acc, n, extra_col, dst_dram, dstb, dbank):
        osb, ob = rings["osb"].next()
        CP("act", osb[0:65, 0:n], PS[0:65, acc, 0:n], [PB[acc]], [ob])
        MM(PS[0:64, dbank, 0:n], selF[0:65, 0:64], osb[0:65, 0:n], True, True, [ob, CB], [PB[dbank]])
        rd, rdb = rings["rd"].next()
        if extra_col is not None:
            TS("dve", rd[0:64, 0:n], PS[0:64, dbank, 0:n], extra_col, None, ALU.add, None, [PB[dbank], CB], [rdb])
            RCP(rd[0:64, 0:n], rd[0:64, 0:n], [rdb], [rdb])
        else:
            RCP(rd[0:64, 0:n], PS[0:64, dbank, 0:n], [PB[dbank]], [rdb])
        bt, btb = rings["bt"].next()
        TTo("dve", bt[0:64, 0:n], osb[0:64, 0:n], rd[0:64, 0:n], ALU.mult, [ob, rdb], [btb])
        S.dma("pool", dst_dram, bt[0:64, 0:n], r=[btb], w=[dstb])

    def attn_rings(es):
        return {
            "osb": Ring(nc, es, "osb", [128, 512], F32, 2),
            "rd": Ring(nc, es, "rd", [64, 512], F32, 2),
            "bt": Ring(nc, es, "bt", [64, 512], BF16, 2),
        }

    def phase_attn0():
        with ExitStack() as es:
            rings = attn_rings(es)
            pt_ring = Ring(nc, es, "pt", [128, 512], BF16, 4)
            sbanks = Rot([0, 1, 2, 3])
            abanks = Rot([4, 5])
            dbanks = Rot([6, 7])
            qt_ring = Ring(nc, es, "qt", [128, 512], BF16, 3)
            for hh in range(8):
                g = hh // 4
                for j, (t0, n, isc) in enumerate(tiles):
                    qt, qtb = qt_ring.next()
                    if isc:
                        S.dma("sp", qt[64:128, 0:n], QS[hh, 64:128, t0:t0 + n], r=[QSB[hh][j]], w=[qtb])
                    else:
                        S.dma("sp", qt[:, 0:n], QS[hh, :, t0:t0 + n], r=[QSB[hh][j]], w=[qtb])
                    steps = []
                    for cc in range(CTX // 128):
                        steps.append((Kctx[64:128, g, cc * 128:(cc + 1) * 128], qt[64:128, 0:n],
                                      [KCB, qtb], NBL + cc, 0, n, []))
                    if not isc:
                        i4 = t0 // 128
                        for jb in range(i4 - 1, i4 + 5):
                            if jb < 0 or jb >= NBL:
                                continue
                            qb0, qb1 = max(jb - 1, i4), min(jb + 1, i4 + 3)
                            c0, c1 = (qb0 - i4) * 128, (qb1 - i4 + 1) * 128
                            masks = []
                            for qb in range(qb0, qb1 + 1):
                                if qb == jb - 1:
                                    masks.append(((qb - qb0) * 128, m1B))
                                elif qb == jb + 1:
                                    masks.append(((qb - qb0) * 128, m2B))
                            steps.append((Klat[0:64, g, jb * 128:(jb + 1) * 128], qt[0:64, c0:c1],
                                          [KLB[jb // 4], qtb], jb, c0, c1, masks))
                    acc = abanks.next()
                    for si, (lhsT, rhs, rdb_, vblk, c0, c1, masks) in enumerate(steps):
                        m = c1 - c0
                        sbk = sbanks.next()
                        MM(PS[:, sbk, 0:m], lhsT, rhs, True, True, rdb_, [PB[sbk]])
                        pt, ptb = pt_ring.next()
                        ACT(pt[:, 0:m], PS[:, sbk, 0:m], AF.Exp, [PB[sbk]], [ptb], scale=0.125)
                        for (mo, mk) in masks:
                            TTo("pool", pt[:, mo:mo + 128], pt[:, mo:mo + 128], mk[:], ALU.mult, [ptb, CB], [ptb])
                        MM(PS[0:65, acc, c0:c1], Vt[:, vblk, g, 0:65], pt[:, 0:m], si == 0, si == len(steps) - 1,
                           [ptb, VB[min(vblk // 4, NT)]], [PB[acc]])
                    attn_finalize(rings, acc, n, esink[0:64, hh:hh + 1], B0[hh * 64:(hh + 1) * 64, t0:t0 + n], B0B[j], dbanks.next())
            S.barrier()

    def phase_wout(l, Wsrc, Asrc, ASB, Bsrc, BSB, tl):
        with ExitStack() as es:
            Wa = sb(es, "Wa", [128, 4, D], BF16)
            Wb = sb(es, "Wb", [64, 8, D], BF16)
            WB = Buf()
            cast_load(Wa[:], Wsrc[0:512, :].rearrange("(c p) n -> p c n", p=128), WB)
            cast_load(Wb[:], Wsrc[512:1024, :].rearrange("(h d) n -> d h n", d=64), WB)
            rings = norm_rings(es, with_h=False)
            x_ring = Ring(nc, es, "xt", [128, 8, 512], F32, 2)
            a_ring = Ring(nc, es, "at", [128, 4, 512], BF16, 2)
            b_ring = Ring(nc, es, "bt2", [64, 8, 512], BF16, 2)
            y_ring = Ring(nc, es, "ysb", [128, 8, 512], F32, 2)
            Av = Asrc.rearrange("(c p) t -> p c t", p=128)
            Bv = Bsrc.rearrange("(h d) t -> d h t", d=64)
            banks = Rot([0, 1, 2, 3])
            loads = {}

            def issue_load(ji):
                j = tl[ji]
                t0, n, isc = tiles[j]
                xt, xb = x_ring.next()
                S.dma("sp", xt[:, :, 0:n], XTv[:, :, t0:t0 + n], r=[XTB[j]], w=[xb])
                at, ab = a_ring.next()
                S.dma("sp", at[:, :, 0:n], Av[:, :, t0:t0 + n], r=[ASB[j]], w=[ab])
                bt, bb = b_ring.next()
                S.dma("sp", bt[:, :, 0:n], Bv[:, :, t0:t0 + n], r=[BSB[j]], w=[bb])
                loads[ji] = (xt, xb, at, ab, bt, bb)

            issue_load(0)
            for ji, j in enumerate(tl):
                t0, n, isc = tiles[j]
                if ji + 1 < len(tl):
                    issue_load(ji + 1)
                xt, xb, at, ab, bt, bb = loads.pop(ji)
                s = 1 if isc else 0
                ysb, yb = y_ring.next()
                for fc in range(8):
                    bk = banks.next()
                    for c in range(4):
                        MM(PS[:, bk, 0:n], Wa[:, c, fc * 128:(fc + 1) * 128], at[:, c, 0:n], c == 0, False, [WB, ab], [PB[bk]])
                    for hh in range(8):
                        MM(PS[:, bk, 0:n], Wb[0:64, hh, fc * 128:(fc + 1) * 128], bt[0:64, hh, 0:n], False, hh == 7, [WB, bb], [PB[bk]])
                    CP("act", ysb[:, fc, 0:n], PS[:, bk, 0:n], [PB[bk]], [yb])
                postnorm_residual(rings, ysb, yb, xt, xb, n, G1[:, l, s, :], 7)
                S.dma("pool", XTv[:, :, t0:t0 + n], xt[:, :, 0:n], r=[xb], w=[XTB[j]])
            S.barrier()

    def phase_ffna(l, tl):
        with ExitStack() as es:
            Wu = sb(es, "Wu", [128, 8, 2 * FFN], BF16)
            WB = [Buf() for _ in range(8)]
            wsrc = W["ffn_w_up"][l].rearrange("(kc p) n -> p kc n", p=128)
            for blk in range(8):
                c0 = blk * 704
                cast_load(Wu[:, :, c0:c0 + 704], wsrc[:, :, c0:c0 + 704], WB[blk])
            rings = norm_rings(es)
            x_ring = Ring(nc, es, "xt", [128, 8, 512], F32, 2)
            st_ring = Ring(nc, es, "gst", [128, 4, 512], BF16, 3)
            banks = Rot([0, 1, 2, 3, 4, 5])
            GSv = GS.rearrange("(c p) t -> p c t", p=128)
            USv = US.rearrange("(c p) t -> p c t", p=128)
            loads = {}

            def issue_load(ji):
                j = tl[ji]
                t0, n, isc = tiles[j]
                xt, xb = x_ring.next()
                S.dma("sp", xt[:, :, 0:n], XTv[:, :, t0:t0 + n], r=[XTB[j]], w=[xb])
                loads[ji] = (xt, xb)

            issue_load(0)
            for ji, j in enumerate(tl):
                t0, n, isc = tiles[j]
                if ji + 1 < len(tl):
                    issue_load(ji + 1)
                xt, xb = loads.pop(ji)
                s = 1 if isc else 0
                h, hb = prenorm(rings, xt, xb, n, A2[:, l, s, :], MODT[:, l, s, 24:32], 7)
                for part, (dstv, dstB) in enumerate(((GSv, GSB), (USv, USB))):
                    k = 0
                    while k < NJ:
                        m = min(4, NJ - k)
                        st, stb = st_ring.next()
                        for q in range(m):
                            fc = part * NJ + k + q
                            bk = banks.next()
                            wb = WB[(fc * 128) // 704]
                            wb2 = WB[(fc * 128 + 127) // 704]
                            for kc in range(8):
                                MM(PS[:, bk, 0:n], Wu[:, kc, fc * 128:(fc + 1) * 128], h[:, kc, 0:n], kc == 0, kc == 7, [hb, wb, wb2], [PB[bk]])
                            CP("act" if (q % 2) else "dve", st[:, q, 0:n], PS[:, bk, 0:n], [PB[bk]], [stb])
                        S.dma("pool", dstv[:, k:k + m, t0:t0 + n], st[:, 0:m, 0:n], r=[stb], w=[dstB[j]])
                        k += m
            S.barrier()

    def phase_ffnb(l, tl):
        with ExitStack() as es:
            Wd = sb(es, "Wd", [128, NJ, D], BF16)
            WB = [Buf() for _ in range(2)]
            wsrc = W["ffn_w_down"][l].rearrange("(j p) n -> p j n", p=128)
            cast_load(Wd[:, 0:11, :], wsrc[:, 0:11, :], WB[0])
            cast_load(Wd[:, 11:22, :], wsrc[:, 11:22, :], WB[1])
            Dg = sb(es, "DgF", [128, NJ, 3, 128], BF16)
            DgB = Buf()
            fcw = cols["fcw%d" % l]
            for jj in range(NJ):
                for k in range(3):
                    TS("dve" if (k % 2) else "pool", Dg[:, jj, k, :], identB[:], fcw[:, k * NJ + jj:k * NJ + jj + 1], None, ALU.mult, None, [CB], [DgB])
            rings = norm_rings(es, with_h=False, nsq=1)
            x_ring = Ring(nc, es, "xt", [128, 8, 512], F32, 1)
            gH = [sb(es, "gtH%d" % i, [128, 11, 514], BF16) for i in range(2)]
            uH = [sb(es, "utH%d" % i, [128, 11, 512], BF16) for i in range(2)]
            gHB = [Buf(), Buf()]
            uHB = [Buf(), Buf()]
            ga_ring = Ring(nc, es, "ga", [128, 512], BF16, 3)
            act_ring = Ring(nc, es, "actt", [128, NJ, 512], BF16, 1)
            y_ring = Ring(nc, es, "ysb", [128, 8, 512], F32, 1)
            GSv = GS.rearrange("(c p) t -> p c t", p=128)
            USv = US.rearrange("(c p) t -> p c t", p=128)
            cbanks = Rot([0, 1, 2, 3])
            dbanks = Rot([4, 5, 6])
            loads = {}

            def load_gu(ji):
                j = tl[ji]
                t0, n, isc = tiles[j]
                seg0, seg1 = (T, TT) if isc else (0, T)
                lo, hi = max(t0 - 1, seg0), min(t0 + n + 1, seg1)
                rd = [GSB[j]]
                if j > 0 and not isc:
                    rd.append(GSB[j - 1])
                if j + 1 < NT:
                    rd.append(GSB[j + 1])
                for half in range(2):
                    gt, gb = gH[half], gHB[half]
                    if lo > t0 - 1:
                        MSET("pool", gt[:, :, 0:1], 0.0, [gb])
                    if hi < t0 + n + 1:
                        MSET("pool", gt[:, :, n + 1:n + 2], 0.0, [gb])
                    S.dma("sp", gt[:, :, lo - (t0 - 1):hi - (t0 - 1)], GSv[:, half * 11:(half + 1) * 11, lo:hi], r=rd, w=[gb])
                    S.dma("sp", uH[half][:, :, 0:n], USv[:, half * 11:(half + 1) * 11, t0:t0 + n], r=[USB[j]], w=[uHB[half]])

            def load_x(ji):
                j = tl[ji]
                t0, n, isc = tiles[j]
                xt, xb = x_ring.next()
                S.dma("sp", xt[:, :, 0:n], XTv[:, :, t0:t0 + n], r=[XTB[j]], w=[xb])
                loads[ji] = (xt, xb)

            load_gu(0)
            load_x(0)
            for ji, j in enumerate(tl):
                t0, n, isc = tiles[j]
                xt, xb = loads.pop(ji)
                s = 1 if isc else 0
                actt, actb = act_ring.next()
                for jj in range(NJ):
                    bk = cbanks.next()
                    gt, gb, ut, ub = gH[jj // 11], gHB[jj // 11], uH[jj // 11], uHB[jj // 11]
                    for k in range(3):
                        MM(PS[:, bk, 0:n], Dg[:, jj, k, :], gt[:, jj % 11, k:k + n], k == 0, k == 2, [gb, DgB], [PB[bk]])
                    ga, gab = ga_ring.next()
                    ACT(ga[:, 0:n], PS[:, bk, 0:n], AF.Gelu_apprx_tanh, [PB[bk], CB], [gab], bias=cols["fcb"][:, l * NJ + jj:l * NJ + jj + 1])
                    TTo("dve" if (jj % 2) else "pool", actt[:, jj, 0:n], ga[:, 0:n], ut[:, jj % 11, 0:n], ALU.mult, [gab, ub], [actb])
                if ji + 1 < len(tl):
                    load_gu(ji + 1)
                ysb, yb = y_ring.next()
                for fc in range(8):
                    bk = dbanks.next()
                    for jj in range(NJ):
                        MM(PS[:, bk, 0:n], Wd[:, jj, fc * 128:(fc + 1) * 128], actt[:, jj, 0:n], jj == 0, jj == NJ - 1, [actb] + WB, [PB[bk]])
                    CP("act", ysb[:, fc, 0:n], PS[:, bk, 0:n], [PB[bk]], [yb])
                postnorm_residual(rings, ysb, yb, xt, xb, n, G2[:, l, s, :], 7)
                S.dma("pool", XTv[:, :, t0:t0 + n], xt[:, :, 0:n], r=[xb], w=[XTB[j]])
                if ji + 1 < len(tl):
                    load_x(ji + 1)
            S.barrier()

    def phase_tout():
        with ExitStack() as es:
            x_ring = Ring(nc, es, "xt", [128, 8, 512], F32, 2)
            o_ring = Ring(nc, es, "ot", [128, D], F32, 3)
            banks = Rot([(0, 1), (2, 3), (4, 5), (6, 7)])
            for j, (t0, n, isc) in enumerate(lat_tiles):
                xt, xb = x_ring.next()
                S.dma("sp", xt[:, :, 0:n], XTv[:, :, t0:t0 + n], r=[XTB[j]], w=[xb])
                for b in range(n // 128):
                    b0, b1 = banks.next()
                    for fc in range(8):
                        bk = b0 if fc < 4 else b1
                        TR(PS[:, bk, (fc % 4) * 128:(fc % 4 + 1) * 128], xt[:, fc, b * 128:(b + 1) * 128], identF[:], [xb, CB], [PB[bk]])
                    ot, ob = o_ring.next()
                    CP("act", ot[:, 0:512], PS[:, b0, :], [PB[b0]], [ob])
                    CP("dve", ot[:, 512:1024], PS[:, b1, :], [PB[b1]], [ob])
                    S.dma("pool", y_out[t0 + b * 128:t0 + (b + 1) * 128, :], ot[:], r=[ob], w=[Buf()])
            S.barrier()

    L1 = ExitStack()
    L1T = {}

    def alloc_l1():
        L1T["CQN"] = sb(L1, "CQN", [128, 3, T], BF16)
        L1T["CKVN"] = sb(L1, "CKVN", [128, 2, TT], BF16)
        L1T["KRb"] = sb(L1, "KRb", [64, TT], BF16)

    CQB = [Buf() for _ in tiles]
    CKB = [Buf() for _ in tiles]
    KRB = [Buf() for _ in tiles]

    def phase_p1_l1():
        l = 1
        CQN, CKVN, KRb = L1T["CQN"], L1T["CKVN"], L1T["KRb"]
        with ExitStack() as es:
            Wt = sb(es, "Wt1", [128, 8, 1728], BF16)
            WB = [Buf() for _ in range(3)]
            wsrc = W["cd_w_in"][0].rearrange("(kc p) n -> p kc n", p=128)
            cast_load(Wt[:, :, 0:1024], wsrc[:, :, 0:1024], WB[0])
            cast_load(Wt[:, :, 1024:1664], wsrc[:, :, 1024:1664], WB[1])
            cast_load(Wt[:, :, 1664:1696], wsrc[:, :, 1664:1696], WB[2])
            cast_load(Wt[:, :, 1696:1728], wsrc[:, :, 1664:1696], WB[2])
            MSET("pool", KRb[32:64, 0:T], 0.0, KRB[:NT])
            MSET("pool", KRb[0:32, T:TT], 0.0, [KRB[NT]])
            rings = norm_rings(es)
            x_ring = Ring(nc, es, "xt", [128, 8, 512], F32, 2)
            xst_ring = Ring(nc, es, "xst", [128, 4, 512], F32, 1)
            gst_ring = Ring(nc, es, "gst1", [128, 4, 512], F32, 1)
            cqs_ring = Ring(nc, es, "cqs", [128, 3, 512], F32, 1)
            cs_ring = Ring(nc, es, "csm", [32, 2, 512], F32, 2)
            t1_ring = Ring(nc, es, "t1m", [32, 512], F32, 2)
            t2_ring = Ring(nc, es, "t2m", [32, 512], F32, 2)
            krs_ring = Ring(nc, es, "krs", [64, 512], BF16, 2)
            banks = Rot([0, 1, 2, 3, 4])
            XBv = XBS.rearrange("(c p) t -> p c t", p=128)
            GTv = GTS.rearrange("(c p) t -> p c t", p=128)
            loads = {}

            def issue_load(j):
                t0, n, isc = tiles[j]
                xt, xb = x_ring.next()
                S.dma("sp", xt[:, :, 0:n], XTv[:, :, t0:t0 + n], r=[XTB[j]], w=[xb])
                cs, csb = cs_ring.next()
                if not isc:
                    S.dma("sp", cs[:, 0, :], cmv[:, t0:t0 + n], w=[csb])
                    S.dma("sp", cs[:, 1, :], smv[:, t0:t0 + n], w=[csb])
                loads[j] = (xt, xb, cs, csb)

            issue_load(0)
            for j, (t0, n, isc) in enumerate(tiles):
                if j + 1 < len(tiles):
                    issue_load(j + 1)
                xt, xb, cs, csb = loads.pop(j)
                s = 1 if isc else 0
                h, hb = prenorm(rings, xt, xb, n, A1[:, l, s, :], MODT[:, l, s, 0:8], 7)

                def proj(col0, bank, M=128):
                    for kc in range(8):
                        MM(PS[0:M, bank, 0:n], Wt[:, kc, col0:col0 + M], h[:, kc, 0:n], kc == 0, kc == 7, [hb] + WB, [PB[bank]])

                xst, xstb = xst_ring.next()
                for c in range(4):
                    bk = banks.next()
                    proj(128 * c, bk)
                    CP("act" if c % 2 else "dve", xst[:, c, 0:n], PS[:, bk, 0:n], [PB[bk]], [xstb])
                S.dma("pool", XBv[:, :, t0:t0 + n], xst[:, :, 0:n], r=[xstb], w=[XBSB[j]])
                if not isc:
                    gst, gstb = gst_ring.next()
                    for c in range(4):
                        bk = banks.next()
                        proj(512 + 128 * c, bk)
                        ACT(gst[:, c, 0:n], PS[:, bk, 0:n], AF.Gelu_apprx_tanh, [PB[bk]], [gstb])
                    S.dma("pool", GTv[:, :, t0:t0 + n], gst[:, :, 0:n], r=[gstb], w=[GTSB[j]])

                def lowrank_norm(col0, nch, dim, gcol, dst, dstb):
                    cqs, cqsb = cqs_ring.next()
                    for c in range(nch):
                        bk = banks.next()
                        proj(col0 + 128 * c, bk)
                        CP("act" if c % 2 else "dve", cqs[:, c, 0:n], PS[:, bk, 0:n], [PB[bk]], [cqsb])
                    sq, sqb = rings["sq"].next()
                    TTo("pool", sq[:, 0:nch, 0:n], cqs[:, 0:nch, 0:n], cqs[:, 0:nch, 0:n], ALU.mult, [cqsb], [sqb])
                    rs, rsb = stat_rstd(rings, sq, sqb, nch, n, dim, 7)
                    TTo("dve", cqs[:, 0:nch, 0:n], cqs[:, 0:nch, 0:n], rs[:, 0:n].unsqueeze(1).to_broadcast([128, nch, n]), ALU.mult, [cqsb, rsb], [cqsb])
                    for c in range(nch):
                        ACT(dst[:, c, t0:t0 + n], cqs[:, c, 0:n], AF.Identity, [cqsb, CB], [dstb], scale=gcol[:, c:c + 1])

                if not isc:
                    lowrank_norm(1024, 3, 384, cols["qn"], CQN, CQB[j])
                lowrank_norm(1408, 2, 256, cols["kvn"], CKVN, CKB[j])
                bk = banks.next()
                proj(1664, bk, M=64)
                if isc:
                    CP("act", KRb[32:64, t0:t0 + n], PS[32:64, bk, 0:n], [PB[bk]], [KRB[j]])
                else:
                    krs, krsb = krs_ring.next()
                    CP("act", krs[32:64, 0:n], PS[32:64, bk, 0:n], [PB[bk]], [krsb])
                    rbk = 5
                    MM(PS[0:32, rbk, 0:n], r32B[32:64, :], krs[32:64, 0:n], True, True, [krsb, CB], [PB[rbk]])
                    t1, t1b = t1_ring.next()
                    t2, t2b = t2_ring.next()
                    TTo("dve", t1[:, 0:n], PS[0:32, bk, 0:n], cs[:, 0, 0:n], ALU.mult, [PB[bk], csb], [t1b])
                    TTo("dve", t2[:, 0:n], PS[0:32, rbk, 0:n], cs[:, 1, 0:n], ALU.mult, [PB[rbk], csb], [t2b])
                    TTo("pool", KRb[0:32, t0:t0 + n], t1[:, 0:n], t2[:, 0:n], ALU.add, [t1b, t2b], [KRB[j]])
            S.barrier()

    def phase_lru():
        with ExitStack() as es:
            GW = sb(es, "GW", [128, 2, 2, 4, 128], BF16)
            GWB = Buf()
            MSET("pool", GW[:], 0.0, [GWB])
            for d in range(2):
                for gate in range(2):
                    for nb in range(8):
                        p0 = (nb % 2) * 64
                        cast_load(GW[p0:p0 + 64, d, gate, nb // 2, p0:p0 + 64], W["lru_gate_w"][0, d, gate, nb], GWB)
            ytmp = sb(es, "ytmp", [128, 8], F32)
            yb_ = Buf()
            ACT(ytmp[:], cols["lam"][:], AF.Exp, [CB], [yb_], scale=-1.0)
            ACT(ytmp[:], ytmp[:], AF.Ln, [yb_, CB], [yb_], bias=onesF[:, 0:1])
            TS("dve", cch[:, 0, :], ytmp[:], -8.0, None, ALU.mult, None, [yb_], [CB])
            TS("dve", cch[:, 1, :], ytmp[:], -16.0, None, ALU.mult, None, [yb_], [CB])
            xb_ring = Ring(nc, es, "xbt", [128, 4, 515], F32, 2)
            xc_ring = Ring(nc, es, "xc", [128, 4, 512], F32, 1)
            xcb_ring = Ring(nc, es, "xcb", [128, 4, 512], BF16, 2)
            rg_ring = Ring(nc, es, "rg", [128, 4, 512], F32, 1)
            ig_ring = Ring(nc, es, "ig", [128, 4, 512], F32, 1)
            av_ring = Ring(nc, es, "av", [128, 4, 512], F32, 1)
            e2_ring = Ring(nc, es, "e2", [128, 4, 512], F32, 2)
            hv_ring = Ring(nc, es, "hv", [128, 4, 512], F32, 2)
            hf_ring = Ring(nc, es, "hf", [128, 4, 512], F32, 1)
            gg_ring = Ring(nc, es, "gg", [128, 4, 512], F32, 1)
            cl_ring = Ring(nc, es, "cl", [128, 4, 512], BF16, 2)
            XBv = XBS.rearrange("(c p) t -> p c t", p=128)
            GTv = GTS.rearrange("(c p) t -> p c t", p=128)
            HFv = HFS.rearrange("(c p) t -> p c t", p=128)
            C1v = C1.rearrange("(c p) t -> p c t", p=128)
            banks = Rot([0, 1, 2, 3, 4, 5])
            for d in range(2):
                order = [NT] + (list(range(NT)) if d == 0 else list(range(NT - 1, -1, -1)))
                prev = None
                for j in order:
                    t0, n, isc = tiles[j]
                    seg0, seg1 = (T, TT) if isc else (0, T)
                    xbt, xbb = xb_ring.next()
                    rd = [XBSB[j]]
                    if d == 0:
                        lo, hi = max(t0 - 3, seg0), t0 + n
                        if lo > t0 - 3:
                            MSET("pool", xbt[:, :, 0:3], 0.0, [xbb])
                        elif j > 0:
                            rd.append(XBSB[j - 1])
                        S.dma("sp", xbt[:, :, lo - (t0 - 3):n + 3], XBv[:, :, lo:hi], r=rd, w=[xbb])
                    else:
                        lo, hi = t0, min(t0 + n + 3, seg1)
                        if hi < t0 + n + 3:
                            MSET("pool", xbt[:, :, n:n + 3], 0.0, [xbb])
                        elif j + 1 < NT:
                            rd.append(XBSB[j + 1])
                        S.dma("sp", xbt[:, :, 0:hi - lo], XBv[:, :, lo:hi], r=rd, w=[xbb])
                    if d == 1 and not isc:
                        hf, hfb = hf_ring.next()
                        S.dma("sp", hf[:, :, 0:n], HFv[:, :, t0:t0 + n], r=[HFSB[j]], w=[hfb])
                        gg, ggb = gg_ring.next()
                        S.dma("sp", gg[:, :, 0:n], GTv[:, :, t0:t0 + n], r=[GTSB[j]], w=[ggb])
                    xc, xcb_ = xc_ring.next()
                    for c in range(4):
                        wc = lambda k: cols["lcw"][:, d * 16 + k * 4 + c:d * 16 + k * 4 + c + 1]
                        TS("dve", xc[:, c, 0:n], xbt[:, c, 0:n], wc(0), cols["lcb"][:, d * 4 + c:d * 4 + c + 1], ALU.mult, ALU.add, [xbb, CB], [xcb_])
                        for k in range(1, 4):
                            STT(xc[:, c, 0:n], xbt[:, c, k:k + n], wc(k), xc[:, c, 0:n], ALU.mult, ALU.add, [xbb, xcb_, CB], [xcb_])
                    xcb, xcbb = xcb_ring.next()
                    CP("pool", xcb[:, :, 0:n], xc[:, :, 0:n], [xcb_], [xcbb])
                    rg, rgb = rg_ring.next()
                    ig, igb = ig_ring.next()
                    av, avb = av_ring.next()
                    e2, e2b = e2_ring.next()
                    for c in range(4):
                        b0 = banks.next()
                        MM(PS[:, b0, 0:n], GW[:, d, 0, c, :], xcb[:, c, 0:n], True, True, [xcbb, GWB], [PB[b0]])
                        ACT(rg[:, c, 0:n], PS[:, b0, 0:n], AF.Sigmoid, [PB[b0], CB], [rgb], bias=cols["lgb"][:, d * 8 + c:d * 8 + c + 1])
                        b1 = banks.next()
                        MM(PS[:, b1, 0:n], GW[:, d, 1, c, :], xcb[:, c, 0:n], True, True, [xcbb, GWB], [PB[b1]])
                        ACT(ig[:, c, 0:n], PS[:, b1, 0:n], AF.Sigmoid, [PB[b1], CB], [igb], bias=cols["lgb"][:, d * 8 + 4 + c:d * 8 + 4 + c + 1])
                    for c in range(4):
                        ACT(av[:, c, 0:n], rg[:, c, 0:n], AF.Exp, [rgb, CB], [avb], scale=cch[:, 0, d * 4 + c:d * 4 + c + 1])
                        ACT(e2[:, c, 0:n], rg[:, c, 0:n], AF.Exp, [rgb, CB], [e2b], scale=cch[:, 1, d * 4 + c:d * 4 + c + 1])
                    ACT(e2[:, :, 0:n], e2[:, :, 0:n], AF.Sqrt, [e2b, CB], [e2b], bias=onesF[:, 0:1], scale=-1.0)
                    TTo("dve", e2[:, :, 0:n], e2[:, :, 0:n], ig[:, :, 0:n], ALU.mult, [e2b, igb], [e2b])
                    TTo("dve", e2[:, :, 0:n], e2[:, :, 0:n], xc[:, :, 0:n], ALU.mult, [e2b, xcb_], [e2b])
                    hv, hvb = hv_ring.next()
                    for c in range(4):
                        if prev is None:
                            init, rdp = 0.0, []
                        else:
                            ph, phb, pn = prev
                            init = ph[:, c, pn - 1:pn] if d == 0 else ph[:, c, 0:1]
                            rdp = [phb]
                        if d == 0:
                            o_, a_, b_ = hv[:, c, 0:n], av[:, c, 0:n], e2[:, c, 0:n]
                        else:
                            o_, a_, b_ = hv[:, c, 0:n][:, ::-1], av[:, c, 0:n][:, ::-1], e2[:, c, 0:n][:, ::-1]
                        S.op("dve", lambda e, o_=o_, a_=a_, b_=b_, init=init: e.tensor_tensor_scan(o_, a_, b_, init, ALU.mult, ALU.add),
                             [avb, e2b] + rdp, [hvb])
                    prev = (hv, hvb, n)
                    if not isc:
                        if d == 0:
                            S.dma("pool", HFv[:, :, t0:t0 + n], hv[:, :, 0:n], r=[hvb], w=[HFSB[j]])
                        else:
                            cl, clb = cl_ring.next()
                            TTo("dve", hf[:, :, 0:n], hf[:, :, 0:n], hv[:, :, 0:n], ALU.add, [hfb, hvb], [hfb])
                            TTo("dve", cl[:, :, 0:n], hf[:, :, 0:n], gg[:, :, 0:n], ALU.mult, [hfb, ggb], [clb])
                            S.dma("pool", C1v[:, :, t0:t0 + n], cl[:, :, 0:n], r=[clb], w=[C1B[j]])
            S.barrier()

    def phase_mla():
        CQN, CKVN, KRb = L1T["CQN"], L1T["CKVN"], L1T["KRb"]
        with ExitStack() as es:
            Wq = sb(es, "Wq", [128, 3, 8, 128], BF16)
            Wk = sb(es, "Wk", [128, 2, 8, 64], BF16)
            Wv = sb(es, "Wv", [128, 2, 8, 64], BF16)
            WB = Buf()
            qsrc = W["mla_w_uq"][0].rearrange("(kc p) (h e) -> p kc h e", p=128, e=96)
            ksrc = W["mla_w_ukv"][0].rearrange("(kc p) (h e) -> p kc h e", p=128, e=128)
            for kc in range(3):
                cast_load(Wq[:, kc, :, 64:128], qsrc[:, kc, :, 0:64], WB)
                cast_load(Wq[:, kc, :, 0:32], qsrc[:, kc, :, 64:96], WB)
                cast_load(Wq[:, kc, :, 32:64], qsrc[:, kc, :, 64:96], WB)
            for kc in range(2):
                cast_load(Wk[:, kc, :, :], ksrc[:, kc, :, 0:64], WB)
                cast_load(Wv[:, kc, :, :], ksrc[:, kc, :, 64:128], WB)
            Va = sb(es, "Va", [128, NB, 8, 66], BF16)
            VaB = Buf()
            MSET("pool", Va[:, :, :, 64:66], 1.0, [VaB])
            rings = attn_rings(es)
            k_ring = Ring(nc, es, "Kh", [128, TT], BF16, 2)
            q_ring = Ring(nc, es, "Qh", [128, T], BF16, 2)
            pt_ring = Ring(nc, es, "ptm", [128, 2, 512], BF16, 3)
            cs_ring = Ring(nc, es, "csq", [32, 2, 512], F32, 2)
            t1_ring = Ring(nc, es, "t1q", [32, 512], F32, 2)
            t2_ring = Ring(nc, es, "t2q", [32, 512], F32, 2)
            mbanks = Rot([6, 7])
            sbanks = Rot([0, 2])
            abanks = Rot([4, 5])
            for blk in range(NB):
                bk = mbanks.next()
                for kc in range(2):
                    MM(PS[:, bk, 0:512], CKVN[:, kc, blk * 128:(blk + 1) * 128], Wv[:, kc, :, :].rearrange("p h d -> p (h d)"),
                       kc == 0, kc == 1, [CKB[min(blk // 4, NT)], WB], [PB[bk]])
                CP("act" if blk % 2 else "dve", Va[:, blk, :, 0:64], PS[:, bk, 0:512].rearrange("p (h d) -> p h d", d=64), [PB[bk]], [VaB])
            sc = float(96 ** -0.5)
            for hh in range(8):
                Kh, KhB = k_ring.next()
                Qh, QhB = q_ring.next()
                CP("pool", Kh[0:64, :], KRb[0:64, :], KRB, [KhB])
                for j, (t0, n, isc) in enumerate(tiles):
                    bk = mbanks.next()
                    for kc in range(2):
                        MM(PS[64:128, bk, 0:n], Wk[:, kc, hh, :], CKVN[:, kc, t0:t0 + n], kc == 0, kc == 1, [CKB[j], WB], [PB[bk]])
                    CP("act", Kh[64:128, t0:t0 + n], PS[64:128, bk, 0:n], [PB[bk]], [KhB])
                for j, (t0, n, isc) in enumerate(lat_tiles):
                    cs, csb = cs_ring.next()
                    S.dma("sp", cs[:, 0, :], cmv[:, t0:t0 + n], w=[csb])
                    S.dma("sp", cs[:, 1, :], smv[:, t0:t0 + n], w=[csb])
                    bk = mbanks.next()
                    for kc in range(3):
                        MM(PS[:, bk, 0:n], Wq[:, kc, hh, :], CQN[:, kc, t0:t0 + n], kc == 0, kc == 2, [CQB[j], WB], [PB[bk]])
                    CP("act", Qh[32:64, t0:t0 + n], PS[32:64, bk, 0:n], [PB[bk]], [QhB])
                    CP("dve", Qh[64:128, t0:t0 + n], PS[64:128, bk, 0:n], [PB[bk]], [QhB])
                    rbk = mbanks.next()
                    MM(PS[0:32, rbk, 0:n], r32B[32:64, :], Qh[32:64, t0:t0 + n], True, True, [QhB, CB], [PB[rbk]])
                    t1, t1b = t1_ring.next()
                    t2, t2b = t2_ring.next()
                    TTo("dve", t1[:, 0:n], PS[0:32, bk, 0:n], cs[:, 0, 0:n], ALU.mult, [PB[bk], csb], [t1b])
                    TTo("dve", t2[:, 0:n], PS[0:32, rbk, 0:n], cs[:, 1, 0:n], ALU.mult, [PB[rbk], csb], [t2b])
                    TTo("pool", Qh[0:32, t0:t0 + n], t1[:, 0:n], t2[:, 0:n], ALU.add, [t1b, t2b], [QhB])
                for j, (t0, n, isc) in enumerate(lat_tiles):
                    acc = abanks.next()
                    ngrp = (NB + 1) // 2
                    for gi in range(ngrp):
                        kbs = [kb for kb in (2 * gi, 2 * gi + 1) if kb < NB]
                        sb0 = sbanks.next()
                        for qi, kb in enumerate(kbs):
                            MM(PS[:, sb0 + qi, 0:n], Kh[:, kb * 128:(kb + 1) * 128], Qh[:, t0:t0 + n], True, True, [KhB, QhB], [PB[sb0 + qi]])
                        pt, ptb = pt_ring.next()
                        m = len(kbs)
                        ACT(pt[:, 0:m, 0:n], PS[:, sb0:sb0 + m, 0:n], AF.Exp, [PB[sb0 + q_] for q_ in range(m)], [ptb], scale=sc)
                        for qi, kb in enumerate(kbs):
                            MM(PS[0:65, acc, 0:n], Va[:, kb, hh, 0:65], pt[:, qi, 0:n], gi == 0 and qi == 0, kb == NB - 1, [ptb, VaB], [PB[acc]])
                    attn_finalize(rings, acc, n, None, D1[hh * 64:(hh + 1) * 64, t0:t0 + n], D1B[j], mbanks.next())
            S.barrier()

    def layer1():
        alloc_l1()
        phase_mod(1)
        phase_p1_l1()
        if stop == "p1_l1":
            return
        phase_lru()
        if stop == "lru":
            return
        phase_mla()
        if stop == "mla":
            return
        L1.close()
        phase_wout(1, W["cd_w_out"][0], C1, C1B, D1, D1B, lat_t)
        if stop == "wout1":
            return
        phase_ffna(1, lat_t)
        phase_ffnb(1, lat_t)

    all_t = list(range(len(tiles)))
    lat_t = list(range(NT))

    def dump_small(parts):
        off = 0
        for t, w in parts:
            S.dma("sp", DBGV[:, off:off + w], t, r=[CB], w=[Buf()])
            off += w
        S.barrier()

    def run():
        phase_mod(0)
        if stop == "mod0":
            dump_small([(MODT[:, 0, :, :].rearrange("p s m -> p (s m)"), 96), (A1[:, 0].rearrange("p s m -> p (s m)"), 16),
                        (G1[:, 0].rearrange("p s m -> p (s m)"), 16), (A2[:, 0].rearrange("p s m -> p (s m)"), 16),
                        (G2[:, 0].rearrange("p s m -> p (s m)"), 16)])
            return
        phase_tin()
        if stop == "tin":
            return
        phase_p1_l0()
        if stop == "p1_l0":
            return
        phase_conva()
        if stop == "conva":
            return
        phase_attn0()
        if stop == "attn0":
            return
        L0.close()
        phase_wout(0, W["ab_w_out"][0], A0, A0B, B0, B0B, all_t)
        if stop == "wout0":
            return
        phase_ffna(0, all_t)
        if stop == "ffna0":
            return
        phase_ffnb(0, all_t)
        if stop == "ffnb0":
            return
        layer1()
        phase_tout()

    run()
    L1.close()
    L0.close()
    ges.close()
    build.stats = (S.nops, S.nwaits)
    return nc


def make_in_maps(inputs, T):
    consts = make_consts(T)
    f = lambda a: np.ascontiguousarray(np.asarray(a, dtype=np.float32))
    shared = {k: f(inputs[k]) for k in WEIGHT_SHAPES}
    shared.update(consts)
    shared["c_ctx"] = f(inputs["c_ctx"]).reshape(8, 128)
    x, c, ctx = f(inputs["x"]), f(inputs["c"]), f(inputs["ctx"])
    maps = []
    for b in range(x.shape[0]):
        m = dict(shared)
        m["x"] = np.ascontiguousarray(x[b])
        m["ctx"] = np.ascontiguousarray(ctx[b])
        m["c"] = np.ascontiguousarray(c[b]).reshape(8, 128)
        maps.append(m)
    return maps


def kernel(**inputs):
    T = int(np.asarray(inputs["x"]).shape[1])
    nc = build(T)
    in_maps = make_in_maps(inputs, T)
    res = run_bass_kernel_spmd(nc, in_maps, core_ids=list(range(len(in_maps))))
    return np.stack([np.asarray(r["y"], dtype=np.float32) for r in res.results], axis=0)
```

```python
import numpy as np
import ml_dtypes
from contextlib import ExitStack
import concourse.bass as bass
import concourse.mybir as mybir
from concourse.bass_utils import run_bass_kernel_spmd

F32 = mybir.dt.float32
BF16 = mybir.dt.bfloat16
AF = mybir.ActivationFunctionType
ALU = mybir.AluOpType

D = 1024
CTX = 256
EPS = 1e-6
FFN = 2816
NJ = FFN // 128


class Buf:
    __slots__ = ("w", "r")

    def __init__(self):
        self.w = None
        self.r = {}


class Op:
    __slots__ = ("eng", "chan", "seq", "fn", "waits", "signal", "clock", "val", "isdma")


class Sched:
    COMPUTE = ("pe", "act", "dve", "pool")

    def __init__(self, nc, es):
        self.nc = nc
        self.eobj = dict(pe=nc.tensor, act=nc.scalar, dve=nc.vector, pool=nc.gpsimd, sp=nc.sync)
        self.sem = {}
        for e in self.COMPUTE:
            self.sem[e] = es.enter_context(nc.semaphore("sem_" + e))
        self.nslot = {"sp": 12, "pool": 8}
        for q, n in self.nslot.items():
            for k in range(n):
                self.sem[(q, k)] = es.enter_context(nc.semaphore("dq_%s%d" % (q, k)))
        self.clock = {e: {} for e in self.eobj}
        self.seq = {e: 0 for e in self.COMPUTE}
        self.sigcount = {e: 0 for e in self.COMPUTE}
        self.dcount = {q: 0 for q in self.nslot}
        self.slot_last = {}
        self.last = {}
        self.pending = []
        self.bar = None
        self.nops = 0
        self.nwaits = 0
        self.dummy = es.enter_context(nc.sbuf_tensor("sched_dummy", [128, 8], F32))

    def _add(self, eng, chan, seq, fn, r, w, isdma, extra=()):
        op = Op()
        op.eng, op.chan, op.seq, op.fn, op.isdma = eng, chan, seq, fn, isdma
        op.signal = isdma
        op.val = 16 * (seq + 1) if isdma else None
        deps = {}

        def need(d):
            if d is None:
                return
            cur = deps.get(d.chan)
            if cur is None or cur.seq < d.seq:
                deps[d.chan] = d

        r = list(r)
        if self.bar is not None:
            r.append(self.bar)
        for b in r:
            need(b.w)
        for b in w:
            need(b.w)
            for d in b.r.values():
                need(d)
        for d in extra:
            need(d)
        clk = self.clock[eng]
        waits = []
        for d in deps.values():
            if d.chan == "pe" and eng == "pe":
                continue
            if clk.get(d.chan, -1) >= d.seq:
                continue
            waits.append(d)
            d.signal = True
            for k, v in d.clock.items():
                if clk.get(k, -1) < v:
                    clk[k] = v
            if clk.get(d.chan, -1) < d.seq:
                clk[d.chan] = d.seq
        op.waits = waits
        op.clock = dict(clk)
        for b in r:
            cur = b.r.get(chan)
            if cur is None or cur.seq < seq:
                b.r[chan] = op
        for b in w:
            b.w = op
            b.r = {}
        self.last[chan] = op
        self.pending.append(op)
        self.nops += 1
        self.nwaits += len(waits)
        return op

    def op(self, eng, fn, r=(), w=()):
        s = self.seq[eng]
        self.seq[eng] = s + 1
        return self._add(eng, eng, s, fn, r, w, False)

    def dma(self, q, out, in_, r=(), w=(), **kw):
        i = self.dcount[q]
        self.dcount[q] = i + 1
        n = self.nslot[q]
        slot, gen = i % n, i // n
        chan = (q, slot)
        prev = self.slot_last.get(chan)
        extra = [prev] if prev is not None else []
        op = self._add(q, chan, gen, lambda e: e.dma_start(out=out, in_=in_, **kw), r, w, True, extra)
        self.slot_last[chan] = op
        return op

    def barrier(self):
        b = Buf()
        extra = list(self.last.values())
        dummy = self.dummy
        s = self.seq["pool"]
        self.seq["pool"] = s + 1
        m = self._add("pool", "pool", s, lambda e: e.memset(dummy[:], 0.0), [], [b], False, extra)
        m.signal = True
        self.bar = b
        self.flush()

    def flush(self):
        for op in self.pending:
            e = self.eobj[op.eng]
            for d in op.waits:
                e.wait_ge(self.sem[d.chan], d.val)
            if op.signal and not op.isdma:
                self.sigcount[op.chan] += 1
                op.val = self.sigcount[op.chan]
            ins = op.fn(e)
            if op.signal:
                ins.then_inc(self.sem[op.chan], 16 if op.isdma else 1)
            op.fn = None
        self.pending = []


class Ring:
    uid = 0

    def __init__(self, nc, es, name, shape, dtype, n):
        Ring.uid += 1
        self.t = [es.enter_context(nc.sbuf_tensor("%s_%d_%d" % (name, Ring.uid, i), shape, dtype)) for i in range(n)]
        self.b = [Buf() for _ in range(n)]
        self.i = 0

    def next(self):
        k = self.i % len(self.t)
        self.i += 1
        return self.t[k], self.b[k]


class Rot:
    def __init__(self, items):
        self.items = list(items)
        self.i = 0

    def next(self):
        v = self.items[self.i % len(self.items)]
        self.i += 1
        return v


def _rope_tables(T, dim):
    rows = T // 64
    row = np.repeat(np.arange(rows), 64).astype(np.float32)
    col = np.tile(np.arange(64), rows).astype(np.float32)
    nf = dim // 4
    inv = (np.float32(10000.0) ** (-np.arange(nf, dtype=np.float32) / np.float32(nf))).astype(np.float32)
    ar = row[:, None] * inv
    ac = col[:, None] * inv
    ang = np.concatenate([ar, ar, ac, ac], axis=-1)
    return np.ascontiguousarray(np.cos(ang).T.astype(np.float32)), np.ascontiguousarray(np.sin(ang).T.astype(np.float32))


def _rot_T(dim):
    q = dim // 4
    R = np.zeros((dim, dim), np.float32)
    for i in range(q):
        R[i, q + i] = -1.0
        R[q + i, i] = 1.0
        R[2 * q + i, 3 * q + i] = -1.0
        R[3 * q + i, 2 * q + i] = 1.0
    return np.ascontiguousarray(R.T)


def make_consts(T):
    cw, sw = _rope_tables(T, 64)
    cm, sm = _rope_tables(T, 32)
    ident = np.eye(128, dtype=np.float32)
    sel = np.zeros((128, 64), np.float32)
    sel[64, :] = 1.0
    b = np.arange(128)[:, None]
    a = np.arange(128)[None, :]
    m1 = (b <= a).astype(np.float32)
    m2 = (a <= b).astype(np.float32)
    r64 = _rot_T(64)
    r32 = _rot_T(32)
    return {
        "k_ident": ident, "k_sel": sel, "k_m1": m1, "k_m2": m2,
        "k_r64": np.concatenate([r64, r64], 0), "k_r32": np.concatenate([r32, r32], 0),
        "k_cw": cw, "k_sw": sw, "k_cm": cm, "k_sm": sm,
    }


WEIGHT_SHAPES = {
    "w_mod": [2, 1024, 6144], "b_mod": [2, 6144], "norm_g": [2, 4, 1024], "ffn_w_up": [2, 1024, 5632],
    "ffn_conv_w": [2, 3, 2816], "ffn_conv_b": [2, 2816], "ffn_w_down": [2, 2816, 1024],
    "ab_w_in": [1, 1024, 1792], "a_conv_w": [1, 31, 512], "a_conv_b": [1, 512], "a_ln_g": [1, 512],
    "a_ln_b": [1, 512], "b_sink": [1, 8], "ab_w_out": [1, 1024, 1024], "cd_w_in": [1, 1024, 1696],
    "lru_conv_w": [1, 2, 4, 512], "lru_conv_b": [1, 2, 512], "lru_gate_w": [1, 2, 2, 8, 64, 64],
    "lru_gate_b": [1, 2, 2, 512], "lru_lambda": [1, 2, 512], "mla_q_norm": [1, 384],
    "mla_w_uq": [1, 384, 768], "mla_kv_norm": [1, 256], "mla_w_ukv": [1, 256, 1024],
    "cd_w_out": [1, 1024, 1024],
}


def build(T=4096, dbg=(), stop=None):
    nc = bass.Bass("TRN2", target_bir_lowering=False)
    TT = T + CTX
    NT = T // 512
    NBL = T // 128
    NB = NBL + CTX // 128
    tiles = [(i * 512, 512, False) for i in range(NT)] + [(T, CTX, True)]
    lat_tiles = tiles[:NT]

    def din(name, shape):
        return nc.dram_tensor(name, list(shape), F32, kind="ExternalInput").ap()

    x_in = din("x", [T, D])
    ctx_in = din("ctx", [CTX, D])
    c_in = din("c", [8, 128])
    cctx_in = din("c_ctx", [8, 128])
    W = {k: din(k, s) for k, s in WEIGHT_SHAPES.items()}
    KC = {k: din(k, v.shape) for k, v in make_consts(T).items()}
    y_out = nc.dram_tensor("y", [T, D], F32, kind="ExternalOutput").ap()

    def scratch(name, shape, dt):
        kind = "ExternalOutput" if name in dbg else "Internal"
        return nc.dram_tensor(name, list(shape), dt, kind=kind).ap()

    XT = scratch("XT", [D, TT], F32)
    U0 = scratch("U0", [512, TT], BF16)
    A0 = scratch("A0", [512, TT], BF16)
    B0 = scratch("B0", [512, TT], BF16)
    GS = scratch("GS", [FFN, TT], BF16)
    US = scratch("US", [FFN, TT], BF16)
    XBS = scratch("XBS", [512, TT], F32)
    GTS = scratch("GTS", [512, T], F32)
    HFS = scratch("HFS", [512, T], F32)
    HBS = scratch("HBS", [512, T], F32)
    C1 = scratch("C1", [512, T], BF16)
    D1 = scratch("D1", [512, T], BF16)
    DBGV = scratch("DBGV", [128, 512], F32)
    QS = scratch("QS", [8, 128, TT], BF16)
    XTv = XT.rearrange("(c p) t -> p c t", p=128)
    XTB = [Buf() for _ in tiles]
    U0B = [Buf() for _ in tiles]
    A0B = [Buf() for _ in tiles]
    B0B = [Buf() for _ in tiles]
    GSB = [Buf() for _ in tiles]
    USB = [Buf() for _ in tiles]
    XBSB = [Buf() for _ in tiles]
    GTSB = [Buf() for _ in tiles]
    HFSB = [Buf() for _ in tiles]
    HBSB = [Buf() for _ in tiles]
    C1B = [Buf() for _ in tiles]
    D1B = [Buf() for _ in tiles]

    ges = ExitStack()
    S = Sched(nc, ges)
    PS = ges.enter_context(nc.psum_tensor("PS", [128, 8, 512], F32))
    PB = [Buf() for _ in range(8)]

    def sb(es, name, shape, dt):
        Ring.uid += 1
        return es.enter_context(nc.sbuf_tensor("%s_%d" % (name, Ring.uid), list(shape), dt))

    def MM(out, lhsT, rhs, st, sp, r, w):
        S.op("pe", lambda e: e.matmul(out, lhsT, rhs, start=st, stop=sp), r, w)

    def TR(out, in_, ident, r, w):
        S.op("pe", lambda e: e.transpose(out, in_, ident), r, w)

    def ACT(out, in_, func, r, w, bias=None, scale=None):
        kw = {}
        if bias is not None:
            kw["bias"] = bias
        if scale is not None:
            kw["scale"] = scale
        S.op("act", lambda e: e.activation(out, in_, func, **kw), r, w)

    def CP(eng, out, in_, r, w):
        if eng == "act":
            S.op("act", lambda e: e.copy(out, in_), r, w)
        else:
            S.op(eng, lambda e: e.tensor_copy(out, in_), r, w)

    def TTo(eng, out, a, b, op, r, w):
        S.op(eng, lambda e: e.tensor_tensor(out, a, b, op), r, w)

    def TS(eng, out, a, s1, s2, op0, op1, r, w):
        if s2 is None:
            S.op(eng, lambda e: e.tensor_scalar(out, a, s1, None, op0), r, w)
        else:
            S.op(eng, lambda e: e.tensor_scalar(out, a, s1, s2, op0, op1), r, w)

    def STT(out, in0, scalar, in1, op0, op1, r, w):
        S.op("dve", lambda e: e.scalar_tensor_tensor(out, in0, scalar, in1, op0, op1), r, w)

    def RCP(out, in_, r, w):
        S.op("dve", lambda e: e.reciprocal(out, in_), r, w)

    def MSET(eng, ap, val, w):
        S.op(eng, lambda e: e.memset(ap, val), [], w)

    identF = sb(ges, "identF", [128, 128], F32)
    identB = sb(ges, "identB", [128, 128], BF16)
    onesB = sb(ges, "onesB", [128, 128], BF16)
    onesF = sb(ges, "onesF", [128, 128], F32)
    selF = sb(ges, "selF", [128, 64], F32)
    m1B = sb(ges, "m1B", [128, 128], BF16)
    m2B = sb(ges, "m2B", [128, 128], BF16)
    r64B = sb(ges, "r64B", [128, 64], BF16)
    r32B = sb(ges, "r32B", [64, 32], BF16)
    CB = Buf()
    S.dma("sp", identF[:], KC["k_ident"], w=[CB])
    S.dma("sp", selF[:], KC["k_sel"], w=[CB])
    S.dma("pool", m1B[:], KC["k_m1"], w=[CB])
    S.dma("pool", m2B[:], KC["k_m2"], w=[CB])
    S.dma("pool", r64B[:], KC["k_r64"], w=[CB])
    S.dma("pool", r32B[:], KC["k_r32"], w=[CB])
    CP("dve", identB[:], identF[:], [CB], [CB])
    MSET("dve", onesB[:], 1.0, [CB])
    MSET("dve", onesF[:], 1.0, [CB])

    cols = {}
    colspec = {
        "g": (W["norm_g"], 64), "bm": (W["b_mod"], 96), "fcb": (W["ffn_conv_b"], 44),
        "fcw0": (W["ffn_conv_w"][0], 66), "fcw1": (W["ffn_conv_w"][1], 66),
        "acb": (W["a_conv_b"], 4), "alg": (W["a_ln_g"], 4), "alb": (W["a_ln_b"], 4),
        "acw": (W["a_conv_w"], 124), "lcw": (W["lru_conv_w"], 32), "lcb": (W["lru_conv_b"], 8),
        "lgb": (W["lru_gate_b"], 16), "lam": (W["lru_lambda"], 8), "qn": (W["mla_q_norm"], 3),
        "kvn": (W["mla_kv_norm"], 2), "c": (c_in, 8), "cc": (cctx_in, 8),
    }
    for name, (src, n) in colspec.items():
        cols[name] = sb(ges, "col_" + name, [128, n], F32)
    esink = sb(ges, "esink", [64, 8], F32)
    scT = sb(ges, "scT", [128, 8, 2], F32)
    MODT = sb(ges, "MODT", [128, 2, 2, 48], F32)
    A1 = sb(ges, "A1", [128, 2, 2, 8], F32)
    G1 = sb(ges, "G1", [128, 2, 2, 8], F32)
    A2 = sb(ges, "A2", [128, 2, 2, 8], F32)
    G2 = sb(ges, "G2", [128, 2, 2, 8], F32)
    epsT = sb(ges, "epsT", [128, 1], F32)
    cch = sb(ges, "cch", [128, 2, 8], F32)
    pre = ExitStack()
    rows_ring = Ring(nc, pre, "rows", [128, 128], F32, 2)
    for i, (name, (src, n)) in enumerate(colspec.items()):
        dst = cols[name]
        nd = len(src.shape)
        if nd == 1:
            s2 = src.rearrange("(r p) -> r p", p=128)
        elif nd == 2 and src.shape[1] == 128:
            s2 = src
        else:
            names = " ".join("a%d" % k for k in range(nd - 1))
            s2 = src.rearrange("%s (r p) -> (%s r) p" % (names, names), p=128)
        rt, rb = rows_ring.next()
        S.dma("sp", rt[0:n, :], s2, w=[rb])
        bank = 6 + (i % 2)
        TR(PS[:, bank, 0:n], rt[0:n, :], identF[0:n, 0:n], [rb, CB], [PB[bank]])
        CP("dve", dst[:], PS[:, bank, 0:n], [PB[bank]], [CB])
    sk = sb(pre, "sk", [1, 8], F32)
    skb = Buf()
    S.dma("sp", sk[:], W["b_sink"], w=[skb])
    MM(PS[0:64, 5, 0:8], onesF[0:1, 0:64], sk[0:1, :], True, True, [skb, CB], [PB[5]])
    ACT(esink[:], PS[0:64, 5, 0:8], AF.Exp, [PB[5]], [CB])
    ACT(scT[:, :, 0], cols["c"][:], AF.Silu, [CB], [CB])
    ACT(scT[:, :, 1], cols["cc"][:], AF.Silu, [CB], [CB])
    S.barrier()
    pre.close()

    gc = cols["g"]

    def phase_mod(l):
        with ExitStack() as es:
            wring = Ring(nc, es, "wmod", [128, 8, 768], F32, 2)
            wsrc = W["w_mod"][l].rearrange("(kc p) n -> p kc n", p=128)
            for jb in range(8):
                wt, wb = wring.next()
                S.dma("sp", wt[:], wsrc[:, :, jb * 768:(jb + 1) * 768], w=[wb])
                for jj in range(6):
                    j = jb * 6 + jj
                    for kc in range(8):
                        MM(PS[:, 6, 2 * j:2 * j + 2], wt[:, kc, jj * 128:(jj + 1) * 128], scT[:, kc, :],
                           kc == 0, kc == 7, [wb, CB], [PB[6]])
            pv = PS[:, 6, 0:96].rearrange("p (j s) -> p j s", s=2)
            for s in range(2):
                TTo("dve", MODT[:, l, s, :], pv[:, :, s], cols["bm"][:, l * 48:(l + 1) * 48], ALU.add, [PB[6], CB], [CB])
                STT(A1[:, l, s, :], MODT[:, l, s, 8:16], 1.0, gc[:, l * 32:l * 32 + 8], ALU.add, ALU.mult, [CB], [CB])
                TTo("dve", G1[:, l, s, :], MODT[:, l, s, 16:24], gc[:, l * 32 + 8:l * 32 + 16], ALU.mult, [CB], [CB])
                STT(A2[:, l, s, :], MODT[:, l, s, 32:40], 1.0, gc[:, l * 32 + 16:l * 32 + 24], ALU.add, ALU.mult, [CB], [CB])
                TTo("dve", G2[:, l, s, :], MODT[:, l, s, 40:48], gc[:, l * 32 + 24:l * 32 + 32], ALU.mult, [CB], [CB])
            S.barrier()

    def phase_tin():
        with ExitStack() as es:
            xin_ring = Ring(nc, es, "xin", [128, D], F32, 3)
            xt_ring = Ring(nc, es, "xtt", [128, 8, 512], F32, 2)
            for j, (t0, n, isc) in enumerate(tiles):
                src = ctx_in if isc else x_in
                s0 = 0 if isc else t0
                for b in range(n // 128):
                    xin, xb_ = xin_ring.next()
                    S.dma("sp", xin[:], src[s0 + b * 128:s0 + (b + 1) * 128, :], w=[xb_])
                    for fc in range(8):
                        TR(PS[:, fc, b * 128:(b + 1) * 128], xin[:, fc * 128:(fc + 1) * 128], identF[:], [xb_, CB], [PB[fc]])
                xt, xtb = xt_ring.next()
                for fc in range(8):
                    CP("act" if fc % 2 else "dve", xt[:, fc, 0:n], PS[:, fc, 0:n], [PB[fc]], [xtb])
                S.dma("pool", XTv[:, :, t0:t0 + n], xt[:, :, 0:n], r=[xtb], w=[XTB[j]])
            S.barrier()

    def stat_rstd(es_rings, src, srcb, nch, n, dim, bank):
        for c in range(nch):
            MM(PS[:, bank, 0:n], onesB[:], src[:, c, 0:n], c == 0, c == nch - 1, [srcb, CB], [PB[bank]])
        rs, rsb = es_rings["rs"].next()
        ACT(rs[:, 0:n], PS[:, bank, 0:n], AF.Sqrt, [PB[bank]], [rsb], bias=epsT[:, 0:1], scale=1.0 / dim)
        RCP(rs[:, 0:n], rs[:, 0:n], [rsb], [rsb])
        return rs, rsb

    MSET("dve", epsT[:], EPS, [CB])

    def prenorm(rings, xt, xb, n, Acol, SHcol, bank):
        sq, sqb = rings["sq"].next()
        ACT(sq[:, :, 0:n], xt[:, :, 0:n], AF.Square, [xb], [sqb])
        rs, rsb = stat_rstd(rings, sq, sqb, 8, n, D, bank)
        TTo("dve", xt[:, :, 0:n], xt[:, :, 0:n], rs[:, 0:n].unsqueeze(1).to_broadcast([128, 8, n]), ALU.mult, [xb, rsb], [xb])
        h, hb = rings["h"].next()
        for c in range(8):
            ACT(h[:, c, 0:n], xt[:, c, 0:n], AF.Identity, [xb, CB], [hb], bias=SHcol[:, c:c + 1], scale=Acol[:, c:c + 1])
        return h, hb

    def postnorm_residual(rings, ysb, yb, xt, xb, n, Gcol, bank):
        sq, sqb = rings["sq"].next()
        TTo("pool", sq[:, :, 0:n], ysb[:, :, 0:n], ysb[:, :, 0:n], ALU.mult, [yb], [sqb])
        rs, rsb = stat_rstd(rings, sq, sqb, 8, n, D, bank)
        TTo("dve", ysb[:, :, 0:n], ysb[:, :, 0:n], rs[:, 0:n].unsqueeze(1).to_broadcast([128, 8, n]), ALU.mult, [yb, rsb], [yb])
        for c in range(8):
            STT(xt[:, c, 0:n], ysb[:, c, 0:n], Gcol[:, c:c + 1], xt[:, c, 0:n], ALU.mult, ALU.add, [yb, xb, CB], [xb])

    def norm_rings(es, with_h=True, nsq=2):
        rings = {
            "sq": Ring(nc, es, "sq", [128, 8, 512], BF16, nsq),
            "rs": Ring(nc, es, "rs", [128, 512], F32, 2),
        }
        if with_h:
            rings["h"] = Ring(nc, es, "h", [128, 8, 512], BF16, 2)
        return rings

    def cast_load(dst, src, wb):
        S.dma("pool", dst, src, w=[wb])

    L0 = ExitStack()
    Klat = sb(L0, "Klat", [64, 2, T], BF16)
    Kctx = sb(L0, "Kctx", [128, 2, CTX], BF16)
    Vt = sb(L0, "Vt", [128, NB, 2, 66], BF16)
    QSB = [[Buf() for _ in tiles] for _ in range(8)]
    KLB = [Buf() for _ in tiles]
    KCB = Buf()
    VB = [Buf() for _ in tiles]
    cwv, swv = KC["k_cw"], KC["k_sw"]
    cmv, smv = KC["k_cm"], KC["k_sm"]

    def phase_p1_l0():
        l = 0
        with ExitStack() as es:
            NCOL = 1024 + 1024 + 256 + 128
            Wt = sb(es, "Wt0", [128, 8, NCOL], BF16)
            WB = [Buf() for _ in range(5)]
            wsrc = W["ab_w_in"][0].rearrange("(kc p) n -> p kc n", p=128)
            cast_load(Wt[:, :, 0:1024], wsrc[:, :, 0:1024], WB[0])
            qd = Wt[:, :, 1024:2048].rearrange("p k (h two d) -> p k h two d", two=2, d=64)
            qs = wsrc[:, :, 1024:1536].rearrange("p k (h d) -> p k h d", d=64)
            for dup in range(2):
                for kc in range(8):
                    cast_load(qd[:, kc, :, dup, :], qs[:, kc, :, :], WB[1 + dup])
            kd = Wt[:, :, 2048:2304].rearrange("p k (h two d) -> p k h two d", two=2, d=64)
            ks = wsrc[:, :, 1536:1664].rearrange("p k (h d) -> p k h d", d=64)
            for dup in range(2):
                for kc in range(8):
                    cast_load(kd[:, kc, :, dup, :], ks[:, kc, :, :], WB[3])
            cast_load(Wt[:, :, 2304:2432], wsrc[:, :, 1664:1792], WB[4])
            MSET("pool", Vt[:, :, :, 64:66], 1.0, VB)
            rings = norm_rings(es)
            x_ring = Ring(nc, es, "xt", [128, 8, 512], F32, 2)
            sg_ring = Ring(nc, es, "sg", [128, 512], F32, 2)
            ust_ring = Ring(nc, es, "ust", [128, 4, 512], BF16, 2)
            cs_ring = Ring(nc, es, "cs", [64, 2, 512], F32, 3)
            t1_ring = Ring(nc, es, "t1", [64, 512], F32, 2)
            t2_ring = Ring(nc, es, "t2", [64, 512], F32, 2)
            kraw_ring = Ring(nc, es, "kraw", [128, 512], BF16, 2)
            qst_ring = Ring(nc, es, "qst", [128, 512], BF16, 3)
            banks = Rot([0, 1, 2, 3, 4])
            rbanks = Rot([5, 6])
            loads = {}

            def issue_load(j):
                t0, n, isc = tiles[j]
                xt, xb = x_ring.next()
                S.dma("sp", xt[:, :, 0:n], XTv[:, :, t0:t0 + n], r=[XTB[j]], w=[xb])
                cs, csb = cs_ring.next()
                if not isc:
                    S.dma("sp", cs[:, 0, :], cwv[:, t0:t0 + n], w=[csb])
                    S.dma("sp", cs[:, 1, :], swv[:, t0:t0 + n], w=[csb])
                loads[j] = (xt, xb, cs, csb)

            TL = list(range(len(tiles)))
            ACOL, SH0, SPLIT = A1, 0, 6

            def body(j, hcur_):
                t0, n, isc = tiles[j]
                xt, xb, cs, csb = loads[j]
                h, hb = hcur_

                def proj(col0, bank, M=128):
                    for kc in range(8):
                        MM(PS[0:M, bank, 0:n], Wt[:, kc, col0:col0 + M], h[:, kc, 0:n], kc == 0, kc == 7, [hb] + WB, [PB[bank]])

                ust, ustb = ust_ring.next()
                for i in range(4):
                    bg = banks.next()
                    proj(512 + 128 * i, bg)
                    sg, sgb = sg_ring.next()
                    ACT(sg[:, 0:n], PS[:, bg, 0:n], AF.Sigmoid, [PB[bg]], [sgb])
                    bv = banks.next()
                    proj(128 * i, bv)
                    TTo("dve", ust[:, i, 0:n], PS[:, bv, 0:n], sg[:, 0:n], ALU.mult, [PB[bv], sgb], [ustb])
                    yield
                S.dma("pool", U0.rearrange("(c p) t -> p c t", p=128)[:, :, t0:t0 + n], ust[:, :, 0:n], r=[ustb], w=[U0B[j]])

                def rope(bank, rawsrc, rawb, dst, dstb):
                    rbk = rbanks.next()
                    MM(PS[0:64, rbk, 0:n], r64B[64:128, :], rawsrc, True, True, [rawb, CB], [PB[rbk]])
                    t1, t1b = t1_ring.next()
                    t2, t2b = t2_ring.next()
                    TTo("dve", t1[:, 0:n], PS[0:64, bank, 0:n], cs[:, 0, 0:n], ALU.mult, [PB[bank], csb], [t1b])
                    TTo("dve", t2[:, 0:n], PS[0:64, rbk, 0:n], cs[:, 1, 0:n], ALU.mult, [PB[rbk], csb], [t2b])
                    TTo("pool", dst, t1[:, 0:n], t2[:, 0:n], ALU.add, [t1b, t2b], [dstb])

                for hh in range(8):
                    bq = banks.next()
                    proj(1024 + 128 * hh, bq)
                    qst, qstb = qst_ring.next()
                    CP("act", qst[64:128, 0:n], PS[64:128, bq, 0:n], [PB[bq]], [qstb])
                    if not isc:
                        rope(bq, qst[64:128, 0:n], qstb, qst[0:64, 0:n], qstb)
                        S.dma("pool", QS[hh, :, t0:t0 + n], qst[:, 0:n], r=[qstb], w=[QSB[hh][j]])
                    else:
                        S.dma("pool", QS[hh, 64:128, t0:t0 + n], qst[64:128, 0:n], r=[qstb], w=[QSB[hh][j]])
                    yield
                for g in range(2):
                    bk = banks.next()
                    proj(2048 + 128 * g, bk)
                    if isc:
                        CP("act", Kctx[64:128, g, :], PS[64:128, bk, 0:n], [PB[bk]], [KCB])
                    else:
                        kr, krb = kraw_ring.next()
                        CP("act", kr[64:128, 0:n], PS[64:128, bk, 0:n], [PB[bk]], [krb])
                        rope(bk, kr[64:128, 0:n], krb, Klat[0:64, g, t0:t0 + n], KLB[j])
                for b in range(n // 128):
                    bv = banks.next()
                    for kc in range(8):
                        MM(PS[:, bv, 0:128], h[:, kc, b * 128:(b + 1) * 128], Wt[:, kc, 2304:2432], kc == 0, kc == 7, [hb] + WB, [PB[bv]])
                    blk = (t0 // 128) + b
                    CP("act" if b % 2 else "dve", Vt[:, blk, :, 0:64], PS[:, bv, 0:128].rearrange("p (g d) -> p g d", g=2), [PB[bv]], [VB[j]])
            def do_prenorm(jj):
                t0_, n_, isc_ = tiles[TL[jj]]
                s_ = 1 if isc_ else 0
                return prenorm(rings, loads[jj][0], loads[jj][1], n_, ACOL[:, l, s_, :], MODT[:, l, s_, SH0:SH0 + 8], 7)

            issue_load(0)
            if len(TL) > 1:
                issue_load(1)
            hcur = do_prenorm(0)
            for ji in range(len(TL)):
                gen = body(ji, hcur)
                k = 0
                done = ji + 1 >= len(TL)
                for _ in gen:
                    k += 1
                    if k == SPLIT and not done:
                        hcur = do_prenorm(ji + 1)
                        if ji + 2 < len(TL):
                            issue_load(ji + 2)
                        done = True
                if not done:
                    hcur = do_prenorm(ji + 1)
                    if ji + 2 < len(TL):
                        issue_load(ji + 2)
                loads.pop(ji)
            S.barrier()

    def phase_conva():
        with ExitStack() as es:
            Dg = sb(es, "DgA", [128, 4, 31, 128], BF16)
            DgB = Buf()
            for c in range(4):
                for k in range(31):
                    col = cols["acw"][:, k * 4 + c:k * 4 + c + 1]
                    TS("dve", Dg[:, c, k, :], identB[:], col, None, ALU.mult, None, [CB], [DgB])
            up_ring = Ring(nc, es, "up", [128, 4, 512 + 30], BF16, 2)
            ucv_ring = Ring(nc, es, "ucv", [128, 4, 512], F32, 2)
            usq_ring = Ring(nc, es, "usq", [128, 4, 512], F32, 2)
            st_ring = Ring(nc, es, "lnst", [128, 3, 512], F32, 2)
            tt_ring = Ring(nc, es, "lntt", [128, 512], F32, 2)
            ao_ring = Ring(nc, es, "ao", [128, 4, 512], BF16, 2)
            U0v = U0.rearrange("(c p) t -> p c t", p=128)
            A0v = A0.rearrange("(c p) t -> p c t", p=128)
            banks = Rot([0, 1, 2, 3])
            for j, (t0, n, isc) in enumerate(tiles):
                seg0, seg1 = (T, TT) if isc else (0, T)
                lo, hi = max(t0 - 15, seg0), min(t0 + n + 15, seg1)
                up, upb = up_ring.next()
                rd = [U0B[j]]
                if j > 0 and not isc:
                    rd.append(U0B[j - 1])
                if j + 1 < NT:
                    rd.append(U0B[j + 1])
                if lo > t0 - 15:
                    MSET("pool", up[:, :, 0:15], 0.0, [upb])
                if hi < t0 + n + 15:
                    MSET("pool", up[:, :, n + 15:n + 30], 0.0, [upb])
                S.dma("sp", up[:, :, lo - (t0 - 15):hi - (t0 - 15)], U0v[:, :, lo:hi], r=rd, w=[upb])
                ucv, ucvb = ucv_ring.next()
                usq, usqb = usq_ring.next()
                for c in range(4):
                    bk = banks.next()
                    for k in range(31):
                        MM(PS[:, bk, 0:n], Dg[:, c, k, :], up[:, c, k:k + n], k == 0, k == 30, [upb, DgB], [PB[bk]])
                    ACT(ucv[:, c, 0:n], PS[:, bk, 0:n], AF.Identity, [PB[bk], CB], [ucvb], bias=cols["acb"][:, c:c + 1])
                    ACT(usq[:, c, 0:n], PS[:, bk, 0:n], AF.Square, [PB[bk], CB], [usqb], bias=cols["acb"][:, c:c + 1])
                for c in range(4):
                    MM(PS[:, 4, 0:n], onesF[:], ucv[:, c, 0:n], c == 0, c == 3, [ucvb, CB], [PB[4]])
                for c in range(4):
                    MM(PS[:, 5, 0:n], onesF[:], usq[:, c, 0:n], c == 0, c == 3, [usqb, CB], [PB[5]])
                st, stb = st_ring.next()
                TS("dve", st[:, 0, 0:n], PS[:, 4, 0:n], 1.0 / 512, None, ALU.mult, None, [PB[4]], [stb])
                TTo("dve", st[:, 1, 0:n], st[:, 0, 0:n], st[:, 0, 0:n], ALU.mult, [stb], [stb])
                STT(st[:, 2, 0:n], PS[:, 5, 0:n], 1.0 / 512, st[:, 1, 0:n], ALU.mult, ALU.subtract, [PB[5], stb], [stb])
                ACT(st[:, 2, 0:n], st[:, 2, 0:n], AF.Sqrt, [stb, CB], [stb], bias=epsT[:, 0:1])
                RCP(st[:, 2, 0:n], st[:, 2, 0:n], [stb], [stb])
                ao, aob = ao_ring.next()
                for c in range(4):
                    tt, ttb = tt_ring.next()
                    TTo("dve", tt[:, 0:n], ucv[:, c, 0:n], st[:, 0, 0:n], ALU.subtract, [ucvb, stb], [ttb])
                    TTo("dve", tt[:, 0:n], tt[:, 0:n], st[:, 2, 0:n], ALU.mult, [ttb, stb], [ttb])
                    ACT(ao[:, c, 0:n], tt[:, 0:n], AF.Silu, [ttb, CB], [aob], bias=cols["alb"][:, c:c + 1], scale=cols["alg"][:, c:c + 1])
                S.dma("pool", A0v[:, :, t0:t0 + n], ao[:, :, 0:n], r=[aob], w=[A0B[j]])
            S.barrier()

    def attn_finalize(rings, acc, n, extra_col, dst_dram, dstb, dbank):
        osb, ob = rings["osb"].next()
        CP("act", osb[0:65, 0:n], PS[0:65, acc, 0:n], [PB[acc]], [ob])
        MM(PS[0:64, dbank, 0:n], selF[0:65, 0:64], osb[0:65, 0:n], True, True, [ob, CB], [PB[dbank]])
        rd, rdb = rings["rd"].next()
        if extra_col is not None:
            TS("dve", rd[0:64, 0:n], PS[0:64, dbank, 0:n], extra_col, None, ALU.add, None, [PB[dbank], CB], [rdb])
            RCP(rd[0:64, 0:n], rd[0:64, 0:n], [rdb], [rdb])
        else:
            RCP(rd[0:64, 0:n], PS[0:64, dbank, 0:n], [PB[dbank]], [rdb])
        bt, btb = rings["bt"].next()
        TTo("dve", bt[0:64, 0:n], osb[0:64, 0:n], rd[0:64, 0:n], ALU.mult, [ob, rdb], [btb])
        S.dma("pool", dst_dram, bt[0:64, 0:n], r=[btb], w=[dstb])

    def attn_rings(es):
        return {
            "osb": Ring(nc, es, "osb", [128, 512], F32, 2),
            "rd": Ring(nc, es, "rd", [64, 512], F32, 2),
            "bt": Ring(nc, es, "bt", [64, 512], BF16, 2),
        }

    def phase_attn0():
        with ExitStack() as es:
            rings = attn_rings(es)
            pt_ring = Ring(nc, es, "pt", [128, 512], BF16, 4)
            sbanks = Rot([0, 1, 2, 3])
            abanks = Rot([4, 5])
            dbanks = Rot([6, 7])
            qt_ring = Ring(nc, es, "qt", [128, 512], BF16, 3)
            pend = []
            for hh in range(8):
                g = hh // 4
                for j, (t0, n, isc) in enumerate(tiles):
                    qt, qtb = qt_ring.next()
                    if isc:
                        S.dma("sp", qt[64:128, 0:n], QS[hh, 64:128, t0:t0 + n], r=[QSB[hh][j]], w=[qtb])
                    else:
                        S.dma("sp", qt[:, 0:n], QS[hh, :, t0:t0 + n], r=[QSB[hh][j]], w=[qtb])
                    steps = []
                    for cc in range(CTX // 128):
                        steps.append((Kctx[64:128, g, cc * 128:(cc + 1) * 128], qt[64:128, 0:n],
                                      [KCB, qtb], NBL + cc, 0, n, []))
                    if not isc:
                        i4 = t0 // 128
                        for jb in range(i4 - 1, i4 + 5):
                            if jb < 0 or jb >= NBL:
                                continue
                            qb0, qb1 = max(jb - 1, i4), min(jb + 1, i4 + 3)
                            c0, c1 = (qb0 - i4) * 128, (qb1 - i4 + 1) * 128
                            masks = []
                            for qb in range(qb0, qb1 + 1):
                                if qb == jb - 1:
                                    masks.append(((qb - qb0) * 128, m1B))
                                elif qb == jb + 1:
                                    masks.append(((qb - qb0) * 128, m2B))
                            steps.append((Klat[0:64, g, jb * 128:(jb + 1) * 128], qt[0:64, c0:c1],
                                          [KLB[jb // 4], qtb], jb, c0, c1, masks))
                    acc = abanks.next()
                    for si, (lhsT, rhs, rdb_, vblk, c0, c1, masks) in enumerate(steps):
                        m = c1 - c0
                        sbk = sbanks.next()
                        MM(PS[:, sbk, 0:m], lhsT, rhs, True, True, rdb_, [PB[sbk]])
                        pt, ptb = pt_ring.next()
                        ACT(pt[:, 0:m], PS[:, sbk, 0:m], AF.Exp, [PB[sbk]], [ptb], scale=0.125)
                        for (mo, mk) in masks:
                            TTo("pool", pt[:, mo:mo + 128], pt[:, mo:mo + 128], mk[:], ALU.mult, [ptb, CB], [ptb])
                        if pend:
                            pend.pop()()

                        def later(acc=acc, c0=c0, c1=c1, vblk=vblk, g=g, pt=pt, ptb=ptb, m=m, si=si, ns=len(steps), n=n, hh=hh, t0=t0, j=j):
                            MM(PS[0:65, acc, c0:c1], Vt[:, vblk, g, 0:65], pt[:, 0:m], si == 0, si == ns - 1,
                               [ptb, VB[min(vblk // 4, NT)]], [PB[acc]])
                            if si == ns - 1:
                                attn_finalize(rings, acc, n, esink[0:64, hh:hh + 1], B0[hh * 64:(hh + 1) * 64, t0:t0 + n], B0B[j], dbanks.next())
                        pend.append(later)
            if pend:
                pend.pop()()
            S.barrier()

    def phase_wout(l, Wsrc, Asrc, ASB, Bsrc, BSB, tl):
        with ExitStack() as es:
            Wa = sb(es, "Wa", [128, 4, D], BF16)
            Wb = sb(es, "Wb", [64, 8, D], BF16)
            WB = Buf()
            cast_load(Wa[:], Wsrc[0:512, :].rearrange("(c p) n -> p c n", p=128), WB)
            cast_load(Wb[:], Wsrc[512:1024, :].rearrange("(h d) n -> d h n", d=64), WB)
            rings = norm_rings(es, with_h=False)
            x_ring = Ring(nc, es, "xt", [128, 8, 512], F32, 2)
            a_ring = Ring(nc, es, "at", [128, 4, 512], BF16, 2)
            b_ring = Ring(nc, es, "bt2", [64, 8, 512], BF16, 2)
            y_ring = Ring(nc, es, "ysb", [128, 8, 512], F32, 2)
            Av = Asrc.rearrange("(c p) t -> p c t", p=128)
            Bv = Bsrc.rearrange("(h d) t -> d h t", d=64)
            banks = Rot([0, 1, 2, 3])
            loads = {}

            def issue_load(ji):
                j = tl[ji]
                t0, n, isc = tiles[j]
                xt, xb = x_ring.next()
                S.dma("sp", xt[:, :, 0:n], XTv[:, :, t0:t0 + n], r=[XTB[j]], w=[xb])
                at, ab = a_ring.next()
                S.dma("sp", at[:, :, 0:n], Av[:, :, t0:t0 + n], r=[ASB[j]], w=[ab])
                bt, bb = b_ring.next()
                S.dma("sp", bt[:, :, 0:n], Bv[:, :, t0:t0 + n], r=[BSB[j]], w=[bb])
                loads[ji] = (xt, xb, at, ab, bt, bb)

            issue_load(0)
            for ji, j in enumerate(tl):
                t0, n, isc = tiles[j]
                if ji + 1 < len(tl):
                    issue_load(ji + 1)
                xt, xb, at, ab, bt, bb = loads.pop(ji)
                s = 1 if isc else 0
                ysb, yb = y_ring.next()
                for fc in range(8):
                    bk = banks.next()
                    for c in range(4):
                        MM(PS[:, bk, 0:n], Wa[:, c, fc * 128:(fc + 1) * 128], at[:, c, 0:n], c == 0, False, [WB, ab], [PB[bk]])
                    for hh in range(8):
                        MM(PS[:, bk, 0:n], Wb[0:64, hh, fc * 128:(fc + 1) * 128], bt[0:64, hh, 0:n], False, hh == 7, [WB, bb], [PB[bk]])
                    CP("act", ysb[:, fc, 0:n], PS[:, bk, 0:n], [PB[bk]], [yb])
                postnorm_residual(rings, ysb, yb, xt, xb, n, G1[:, l, s, :], 7)
                S.dma("pool", XTv[:, :, t0:t0 + n], xt[:, :, 0:n], r=[xb], w=[XTB[j]])
            S.barrier()

    def phase_ffna(l, tl):
        with ExitStack() as es:
            Wu = sb(es, "Wu", [128, 8, 2 * FFN], BF16)
            WB = [Buf() for _ in range(8)]
            wsrc = W["ffn_w_up"][l].rearrange("(kc p) n -> p kc n", p=128)
            for blk in range(8):
                c0 = blk * 704
                cast_load(Wu[:, :, c0:c0 + 704], wsrc[:, :, c0:c0 + 704], WB[blk])
            rings = norm_rings(es)
            x_ring = Ring(nc, es, "xt", [128, 8, 512], F32, 2)
            st_ring = Ring(nc, es, "gst", [128, 4, 512], BF16, 3)
            banks = Rot([0, 1, 2, 3, 4, 5])
            GSv = GS.rearrange("(c p) t -> p c t", p=128)
            USv = US.rearrange("(c p) t -> p c t", p=128)
            loads = {}

            def issue_load(ji):
                j = tl[ji]
                t0, n, isc = tiles[j]
                xt, xb = x_ring.next()
                S.dma("sp", xt[:, :, 0:n], XTv[:, :, t0:t0 + n], r=[XTB[j]], w=[xb])
                loads[ji] = (xt, xb)

            TL = tl
            ACOL, SH0, SPLIT = A2, 24, 5

            def body(ji, hcur_):
                j = tl[ji]
                t0, n, isc = tiles[j]
                xt, xb = loads[ji]
                h, hb = hcur_
                for part, (dstv, dstB) in enumerate(((GSv, GSB), (USv, USB))):
                    k = 0
                    while k < NJ:
                        m = min(4, NJ - k)
                        st, stb = st_ring.next()
                        for q in range(m):
                            fc = part * NJ + k + q
                            bk = banks.next()
                            wb = WB[(fc * 128) // 704]
                            wb2 = WB[(fc * 128 + 127) // 704]
                            for kc in range(8):
                                MM(PS[:, bk, 0:n], Wu[:, kc, fc * 128:(fc + 1) * 128], h[:, kc, 0:n], kc == 0, kc == 7, [hb, wb, wb2], [PB[bk]])
                            CP("act" if (q % 2) else "dve", st[:, q, 0:n], PS[:, bk, 0:n], [PB[bk]], [stb])
                        S.dma("pool", dstv[:, k:k + m, t0:t0 + n], st[:, 0:m, 0:n], r=[stb], w=[dstB[j]])
                        yield
                        k += m
            def do_prenorm(jj):
                t0_, n_, isc_ = tiles[TL[jj]]
                s_ = 1 if isc_ else 0
                return prenorm(rings, loads[jj][0], loads[jj][1], n_, ACOL[:, l, s_, :], MODT[:, l, s_, SH0:SH0 + 8], 7)

            issue_load(0)
            if len(TL) > 1:
                issue_load(1)
            hcur = do_prenorm(0)
            for ji in range(len(TL)):
                gen = body(ji, hcur)
                k = 0
                done = ji + 1 >= len(TL)
                for _ in gen:
                    k += 1
                    if k == SPLIT and not done:
                        hcur = do_prenorm(ji + 1)
                        if ji + 2 < len(TL):
                            issue_load(ji + 2)
                        done = True
                if not done:
                    hcur = do_prenorm(ji + 1)
                    if ji + 2 < len(TL):
                        issue_load(ji + 2)
                loads.pop(ji)
            S.barrier()

    def phase_ffnb(l, tl):
        with ExitStack() as es:
            Wd = sb(es, "Wd", [128, NJ, D], BF16)
            WB = [Buf() for _ in range(2)]
            wsrc = W["ffn_w_down"][l].rearrange("(j p) n -> p j n", p=128)
            cast_load(Wd[:, 0:11, :], wsrc[:, 0:11, :], WB[0])
            cast_load(Wd[:, 11:22, :], wsrc[:, 11:22, :], WB[1])
            Dg = sb(es, "DgF", [128, NJ, 3, 128], BF16)
            DgB = Buf()
            fcw = cols["fcw%d" % l]
            for jj in range(NJ):
                for k in range(3):
                    TS("dve", Dg[:, jj, k, :], identB[:], fcw[:, k * NJ + jj:k * NJ + jj + 1], None, ALU.mult, None, [CB], [DgB])
            rings = norm_rings(es, with_h=False, nsq=1)
            x_ring = Ring(nc, es, "xt", [128, 8, 512], F32, 1)
            gH = [sb(es, "gtH%d" % i, [128, 11, 514], BF16) for i in range(2)]
            uH = [sb(es, "utH%d" % i, [128, 11, 512], BF16) for i in range(2)]
            gHB = [Buf(), Buf()]
            uHB = [Buf(), Buf()]
            ga_ring = Ring(nc, es, "ga", [128, 512], BF16, 3)
            act_ring = Ring(nc, es, "actt", [128, NJ, 512], BF16, 1)
            y_ring = Ring(nc, es, "ysb", [128, 8, 512], F32, 1)
            GSv = GS.rearrange("(c p) t -> p c t", p=128)
            USv = US.rearrange("(c p) t -> p c t", p=128)
            cbanks = Rot([0, 1, 2, 3])
            dbanks = Rot([4, 5, 6])
            loads = {}

            def load_gu(ji):
                j = tl[ji]
                t0, n, isc = tiles[j]
                seg0, seg1 = (T, TT) if isc else (0, T)
                lo, hi = max(t0 - 1, seg0), min(t0 + n + 1, seg1)
                rd = [GSB[j]]
                if j > 0 and not isc:
                    rd.append(GSB[j - 1])
                if j + 1 < NT:
                    rd.append(GSB[j + 1])
                for half in range(2):
                    gt, gb = gH[half], gHB[half]
                    if lo > t0 - 1:
                        MSET("pool", gt[:, :, 0:1], 0.0, [gb])
                    if hi < t0 + n + 1:
                        MSET("pool", gt[:, :, n + 1:n + 2], 0.0, [gb])
                    S.dma("sp", gt[:, :, lo - (t0 - 1):hi - (t0 - 1)], GSv[:, half * 11:(half + 1) * 11, lo:hi], r=rd, w=[gb])
                    S.dma("sp", uH[half][:, :, 0:n], USv[:, half * 11:(half + 1) * 11, t0:t0 + n], r=[USB[j]], w=[uHB[half]])

            def load_x(ji):
                j = tl[ji]
                t0, n, isc = tiles[j]
                xt, xb = x_ring.next()
                S.dma("sp", xt[:, :, 0:n], XTv[:, :, t0:t0 + n], r=[XTB[j]], w=[xb])
                loads[ji] = (xt, xb)

            load_gu(0)
            load_x(0)
            for ji, j in enumerate(tl):
                t0, n, isc = tiles[j]
                xt, xb = loads.pop(ji)
                s = 1 if isc else 0
                actt, actb = act_ring.next()
                for jj in range(NJ):
                    bk = cbanks.next()
                    gt, gb, ut, ub = gH[jj // 11], gHB[jj // 11], uH[jj // 11], uHB[jj // 11]
                    for k in range(3):
                        MM(PS[:, bk, 0:n], Dg[:, jj, k, :], gt[:, jj % 11, k:k + n], k == 0, k == 2, [gb, DgB], [PB[bk]])
                    ga, gab = ga_ring.next()
                    ACT(ga[:, 0:n], PS[:, bk, 0:n], AF.Gelu_apprx_tanh, [PB[bk], CB], [gab], bias=cols["fcb"][:, l * NJ + jj:l * NJ + jj + 1])
                    TTo("dve" if (jj % 2) else "pool", actt[:, jj, 0:n], ga[:, 0:n], ut[:, jj % 11, 0:n], ALU.mult, [gab, ub], [actb])
                if ji + 1 < len(tl):
                    load_gu(ji + 1)
                ysb, yb = y_ring.next()
                for fc in range(8):
                    bk = dbanks.next()
                    for jj in range(NJ):
                        MM(PS[:, bk, 0:n], Wd[:, jj, fc * 128:(fc + 1) * 128], actt[:, jj, 0:n], jj == 0, jj == NJ - 1, [actb] + WB, [PB[bk]])
                    CP("act", ysb[:, fc, 0:n], PS[:, bk, 0:n], [PB[bk]], [yb])
                postnorm_residual(rings, ysb, yb, xt, xb, n, G2[:, l, s, :], 7)
                S.dma("pool", XTv[:, :, t0:t0 + n], xt[:, :, 0:n], r=[xb], w=[XTB[j]])
                if ji + 1 < len(tl):
                    load_x(ji + 1)
            S.barrier()

    def phase_tout():
        with ExitStack() as es:
            x_ring = Ring(nc, es, "xt", [128, 8, 512], F32, 2)
            o_ring = Ring(nc, es, "ot", [128, D], F32, 3)
            banks = Rot([(0, 1), (2, 3), (4, 5), (6, 7)])
            for j, (t0, n, isc) in enumerate(lat_tiles):
                xt, xb = x_ring.next()
                S.dma("sp", xt[:, :, 0:n], XTv[:, :, t0:t0 + n], r=[XTB[j]], w=[xb])
                for b in range(n // 128):
                    b0, b1 = banks.next()
                    for fc in range(8):
                        bk = b0 if fc < 4 else b1
                        TR(PS[:, bk, (fc % 4) * 128:(fc % 4 + 1) * 128], xt[:, fc, b * 128:(b + 1) * 128], identF[:], [xb, CB], [PB[bk]])
                    ot, ob = o_ring.next()
                    CP("act", ot[:, 0:512], PS[:, b0, :], [PB[b0]], [ob])
                    CP("dve", ot[:, 512:1024], PS[:, b1, :], [PB[b1]], [ob])
                    S.dma("pool", y_out[t0 + b * 128:t0 + (b + 1) * 128, :], ot[:], r=[ob], w=[Buf()])
            S.barrier()

    L1 = ExitStack()
    L1T = {}

    def alloc_l1():
        L1T["CQN"] = sb(L1, "CQN", [128, 3, T], BF16)
        L1T["CKVN"] = sb(L1, "CKVN", [128, 2, TT], BF16)
        L1T["KRb"] = sb(L1, "KRb", [64, TT], BF16)

    CQB = [Buf() for _ in tiles]
    CKB = [Buf() for _ in tiles]
    KRB = [Buf() for _ in tiles]

    def phase_p1_l1():
        l = 1
        CQN, CKVN, KRb = L1T["CQN"], L1T["CKVN"], L1T["KRb"]
        with ExitStack() as es:
            Wt = sb(es, "Wt1", [128, 8, 1728], BF16)
            WB = [Buf() for _ in range(3)]
            wsrc = W["cd_w_in"][0].rearrange("(kc p) n -> p kc n", p=128)
            cast_load(Wt[:, :, 0:1024], wsrc[:, :, 0:1024], WB[0])
            cast_load(Wt[:, :, 1024:1664], wsrc[:, :, 1024:1664], WB[1])
            cast_load(Wt[:, :, 1664:1696], wsrc[:, :, 1664:1696], WB[2])
            cast_load(Wt[:, :, 1696:1728], wsrc[:, :, 1664:1696], WB[2])
            MSET("pool", KRb[32:64, 0:T], 0.0, KRB[:NT])
            MSET("pool", KRb[0:32, T:TT], 0.0, [KRB[NT]])
            rings = norm_rings(es)
            x_ring = Ring(nc, es, "xt", [128, 8, 512], F32, 2)
            xst_ring = Ring(nc, es, "xst", [128, 4, 512], F32, 1)
            gst_ring = Ring(nc, es, "gst1", [128, 4, 512], F32, 1)
            cqs_ring = Ring(nc, es, "cqs", [128, 3, 512], F32, 1)
            cs_ring = Ring(nc, es, "csm", [32, 2, 512], F32, 3)
            t1_ring = Ring(nc, es, "t1m", [32, 512], F32, 2)
            t2_ring = Ring(nc, es, "t2m", [32, 512], F32, 2)
            krs_ring = Ring(nc, es, "krs", [64, 512], BF16, 2)
            banks = Rot([0, 1, 2, 3, 4])
            XBv = XBS.rearrange("(c p) t -> p c t", p=128)
            GTv = GTS.rearrange("(c p) t -> p c t", p=128)
            loads = {}

            def issue_load(j):
                t0, n, isc = tiles[j]
                xt, xb = x_ring.next()
                S.dma("sp", xt[:, :, 0:n], XTv[:, :, t0:t0 + n], r=[XTB[j]], w=[xb])
                cs, csb = cs_ring.next()
                if not isc:
                    S.dma("sp", cs[:, 0, :], cmv[:, t0:t0 + n], w=[csb])
                    S.dma("sp", cs[:, 1, :], smv[:, t0:t0 + n], w=[csb])
                loads[j] = (xt, xb, cs, csb)

            TL = list(range(len(tiles)))
            ACOL, SH0, SPLIT = A1, 0, 4

            def body(j, hcur_):
                t0, n, isc = tiles[j]
                xt, xb, cs, csb = loads[j]
                h, hb = hcur_

                def proj(col0, bank, M=128):
                    for kc in range(8):
                        MM(PS[0:M, bank, 0:n], Wt[:, kc, col0:col0 + M], h[:, kc, 0:n], kc == 0, kc == 7, [hb] + WB, [PB[bank]])

                xst, xstb = xst_ring.next()
                for c in range(4):
                    bk = banks.next()
                    proj(128 * c, bk)
                    CP("act" if c % 2 else "dve", xst[:, c, 0:n], PS[:, bk, 0:n], [PB[bk]], [xstb])
                    yield
                S.dma("pool", XBv[:, :, t0:t0 + n], xst[:, :, 0:n], r=[xstb], w=[XBSB[j]])
                if not isc:
                    gst, gstb = gst_ring.next()
                    for c in range(4):
                        bk = banks.next()
                        proj(512 + 128 * c, bk)
                        ACT(gst[:, c, 0:n], PS[:, bk, 0:n], AF.Gelu_apprx_tanh, [PB[bk]], [gstb])
                        yield
                    S.dma("pool", GTv[:, :, t0:t0 + n], gst[:, :, 0:n], r=[gstb], w=[GTSB[j]])

                def lowrank_norm(col0, nch, dim, gcol, dst, dstb):
                    cqs, cqsb = cqs_ring.next()
                    for c in range(nch):
                        bk = banks.next()
                        proj(col0 + 128 * c, bk)
                        CP("act" if c % 2 else "dve", cqs[:, c, 0:n], PS[:, bk, 0:n], [PB[bk]], [cqsb])
                    sq, sqb = rings["sq"].next()
                    TTo("pool", sq[:, 0:nch, 0:n], cqs[:, 0:nch, 0:n], cqs[:, 0:nch, 0:n], ALU.mult, [cqsb], [sqb])
                    rs, rsb = stat_rstd(rings, sq, sqb, nch, n, dim, 7)
                    TTo("dve", cqs[:, 0:nch, 0:n], cqs[:, 0:nch, 0:n], rs[:, 0:n].unsqueeze(1).to_broadcast([128, nch, n]), ALU.mult, [cqsb, rsb], [cqsb])
                    for c in range(nch):
                        ACT(dst[:, c, t0:t0 + n], cqs[:, c, 0:n], AF.Identity, [cqsb, CB], [dstb], scale=gcol[:, c:c + 1])

                if not isc:
                    lowrank_norm(1024, 3, 384, cols["qn"], CQN, CQB[j])
                lowrank_norm(1408, 2, 256, cols["kvn"], CKVN, CKB[j])
                bk = banks.next()
                proj(1664, bk, M=64)
                if isc:
                    CP("act", KRb[32:64, t0:t0 + n], PS[32:64, bk, 0:n], [PB[bk]], [KRB[j]])
                else:
                    krs, krsb = krs_ring.next()
                    CP("act", krs[32:64, 0:n], PS[32:64, bk, 0:n], [PB[bk]], [krsb])
                    rbk = 5
                    MM(PS[0:32, rbk, 0:n], r32B[32:64, :], krs[32:64, 0:n], True, True, [krsb, CB], [PB[rbk]])
                    t1, t1b = t1_ring.next()
                    t2, t2b = t2_ring.next()
                    TTo("dve", t1[:, 0:n], PS[0:32, bk, 0:n], cs[:, 0, 0:n], ALU.mult, [PB[bk], csb], [t1b])
                    TTo("dve", t2[:, 0:n], PS[0:32, rbk, 0:n], cs[:, 1, 0:n], ALU.mult, [PB[rbk], csb], [t2b])
                    TTo("pool", KRb[0:32, t0:t0 + n], t1[:, 0:n], t2[:, 0:n], ALU.add, [t1b, t2b], [KRB[j]])
            def do_prenorm(jj):
                t0_, n_, isc_ = tiles[TL[jj]]
                s_ = 1 if isc_ else 0
                return prenorm(rings, loads[jj][0], loads[jj][1], n_, ACOL[:, l, s_, :], MODT[:, l, s_, SH0:SH0 + 8], 7)

            issue_load(0)
            if len(TL) > 1:
                issue_load(1)
            hcur = do_prenorm(0)
            for ji in range(len(TL)):
                gen = body(ji, hcur)
                k = 0
                done = ji + 1 >= len(TL)
                for _ in gen:
                    k += 1
                    if k == SPLIT and not done:
                        hcur = do_prenorm(ji + 1)
                        if ji + 2 < len(TL):
                            issue_load(ji + 2)
                        done = True
                if not done:
                    hcur = do_prenorm(ji + 1)
                    if ji + 2 < len(TL):
                        issue_load(ji + 2)
                loads.pop(ji)
            S.barrier()

    def phase_lru():
        with ExitStack() as es:
            GW = sb(es, "GW", [128, 2, 2, 4, 128], BF16)
            GWB = Buf()
            MSET("pool", GW[:], 0.0, [GWB])
            for d in range(2):
                for gate in range(2):
                    for nb in range(8):
                        p0 = (nb % 2) * 64
                        cast_load(GW[p0:p0 + 64, d, gate, nb // 2, p0:p0 + 64], W["lru_gate_w"][0, d, gate, nb], GWB)
            ytmp = sb(es, "ytmp", [128, 8], F32)
            yb_ = Buf()
            ACT(ytmp[:], cols["lam"][:], AF.Exp, [CB], [yb_], scale=-1.0)
            ACT(ytmp[:], ytmp[:], AF.Ln, [yb_, CB], [yb_], bias=onesF[:, 0:1])
            TS("dve", cch[:, 0, :], ytmp[:], -8.0, None, ALU.mult, None, [yb_], [CB])
            TS("dve", cch[:, 1, :], ytmp[:], -16.0, None, ALU.mult, None, [yb_], [CB])
            xb_ring = Ring(nc, es, "xbt", [128, 4, 515], F32, 2)
            xc_ring = Ring(nc, es, "xc", [128, 4, 512], F32, 2)
            xcb_ring = Ring(nc, es, "xcb", [128, 4, 512], BF16, 2)
            rg_ring = Ring(nc, es, "rg", [128, 4, 512], F32, 2)
            ig_ring = Ring(nc, es, "ig", [128, 4, 512], F32, 2)
            av_ring = Ring(nc, es, "av", [128, 4, 512], F32, 2)
            e2_ring = Ring(nc, es, "e2", [128, 4, 512], F32, 2)
            hv_rings = [Ring(nc, es, "hv%d" % d_, [128, 4, 512], F32, 2) for d_ in range(2)]
            XBv = XBS.rearrange("(c p) t -> p c t", p=128)
            GTv = GTS.rearrange("(c p) t -> p c t", p=128)
            HFv = HFS.rearrange("(c p) t -> p c t", p=128)
            C1v = C1.rearrange("(c p) t -> p c t", p=128)
            banks = Rot([0, 1, 2, 3, 4, 5])
            HBv = HBS.rearrange("(c p) t -> p c t", p=128)
            orders = [[NT] + list(range(NT)), [NT] + list(range(NT - 1, -1, -1))]
            prevs = [None, None]
            for step in range(NT + 1):
                for d in range(2):
                    j = orders[d][step]
                    prev = prevs[d]
                    t0, n, isc = tiles[j]
                    seg0, seg1 = (T, TT) if isc else (0, T)
                    xbt, xbb = xb_ring.next()
                    rd = [XBSB[j]]
                    if d == 0:
                        lo, hi = max(t0 - 3, seg0), t0 + n
                        if lo > t0 - 3:
                            MSET("pool", xbt[:, :, 0:3], 0.0, [xbb])
                        elif j > 0:
                            rd.append(XBSB[j - 1])
                        S.dma("sp", xbt[:, :, lo - (t0 - 3):n + 3], XBv[:, :, lo:hi], r=rd, w=[xbb])
                    else:
                        lo, hi = t0, min(t0 + n + 3, seg1)
                        if hi < t0 + n + 3:
                            MSET("pool", xbt[:, :, n:n + 3], 0.0, [xbb])
                        elif j + 1 < NT:
                            rd.append(XBSB[j + 1])
                        S.dma("sp", xbt[:, :, 0:hi - lo], XBv[:, :, lo:hi], r=rd, w=[xbb])
                    xc, xcb_ = xc_ring.next()
                    for c in range(4):
                        wc = lambda k: cols["lcw"][:, d * 16 + k * 4 + c:d * 16 + k * 4 + c + 1]
                        TS("dve", xc[:, c, 0:n], xbt[:, c, 0:n], wc(0), cols["lcb"][:, d * 4 + c:d * 4 + c + 1], ALU.mult, ALU.add, [xbb, CB], [xcb_])
                        for k in range(1, 4):
                            STT(xc[:, c, 0:n], xbt[:, c, k:k + n], wc(k), xc[:, c, 0:n], ALU.mult, ALU.add, [xbb, xcb_, CB], [xcb_])
                    xcb, xcbb = xcb_ring.next()
                    CP("pool", xcb[:, :, 0:n], xc[:, :, 0:n], [xcb_], [xcbb])
                    rg, rgb = rg_ring.next()
                    ig, igb = ig_ring.next()
                    av, avb = av_ring.next()
                    e2, e2b = e2_ring.next()
                    for c in range(4):
                        b0 = banks.next()
                        MM(PS[:, b0, 0:n], GW[:, d, 0, c, :], xcb[:, c, 0:n], True, True, [xcbb, GWB], [PB[b0]])
                        ACT(rg[:, c, 0:n], PS[:, b0, 0:n], AF.Sigmoid, [PB[b0], CB], [rgb], bias=cols["lgb"][:, d * 8 + c:d * 8 + c + 1])
                        b1 = banks.next()
                        MM(PS[:, b1, 0:n], GW[:, d, 1, c, :], xcb[:, c, 0:n], True, True, [xcbb, GWB], [PB[b1]])
                        ACT(ig[:, c, 0:n], PS[:, b1, 0:n], AF.Sigmoid, [PB[b1], CB], [igb], bias=cols["lgb"][:, d * 8 + 4 + c:d * 8 + 4 + c + 1])
                    for c in range(4):
                        ACT(av[:, c, 0:n], rg[:, c, 0:n], AF.Exp, [rgb, CB], [avb], scale=cch[:, 0, d * 4 + c:d * 4 + c + 1])
                        ACT(e2[:, c, 0:n], rg[:, c, 0:n], AF.Exp, [rgb, CB], [e2b], scale=cch[:, 1, d * 4 + c:d * 4 + c + 1])
                    ACT(e2[:, :, 0:n], e2[:, :, 0:n], AF.Sqrt, [e2b, CB], [e2b], bias=onesF[:, 0:1], scale=-1.0)
                    TTo("dve", e2[:, :, 0:n], e2[:, :, 0:n], ig[:, :, 0:n], ALU.mult, [e2b, igb], [e2b])
                    TTo("dve", e2[:, :, 0:n], e2[:, :, 0:n], xc[:, :, 0:n], ALU.mult, [e2b, xcb_], [e2b])
                    hv, hvb = hv_rings[d].next()
                    for c in range(4):
                        if prev is None:
                            init, rdp = 0.0, []
                        else:
                            ph, phb, pn = prev
                            init = ph[:, c, pn - 1:pn] if d == 0 else ph[:, c, 0:1]
                            rdp = [phb]
                        if d == 0:
                            o_, a_, b_ = hv[:, c, 0:n], av[:, c, 0:n], e2[:, c, 0:n]
                        else:
                            o_, a_, b_ = hv[:, c, 0:n][:, ::-1], av[:, c, 0:n][:, ::-1], e2[:, c, 0:n][:, ::-1]
                        S.op("dve", lambda e, o_=o_, a_=a_, b_=b_, init=init: e.tensor_tensor_scan(o_, a_, b_, init, ALU.mult, ALU.add),
                             [avb, e2b] + rdp, [hvb])
                    prevs[d] = (hv, hvb, n)
                    if not isc:
                        if d == 0:
                            S.dma("pool", HFv[:, :, t0:t0 + n], hv[:, :, 0:n], r=[hvb], w=[HFSB[j]])
                        else:
                            S.dma("pool", HBv[:, :, t0:t0 + n], hv[:, :, 0:n], r=[hvb], w=[HBSB[j]])
            S.barrier()

    def phase_lru_combine():
        with ExitStack() as es:
            hf_ring = Ring(nc, es, "hf", [128, 4, 512], F32, 2)
            hb_ring = Ring(nc, es, "hb", [128, 4, 512], F32, 2)
            gg_ring = Ring(nc, es, "gg", [128, 4, 512], F32, 2)
            cl_ring = Ring(nc, es, "cl", [128, 4, 512], BF16, 2)
            GTv = GTS.rearrange("(c p) t -> p c t", p=128)
            HFv = HFS.rearrange("(c p) t -> p c t", p=128)
            HBv = HBS.rearrange("(c p) t -> p c t", p=128)
            C1v = C1.rearrange("(c p) t -> p c t", p=128)
            for j, (t0, n, isc) in enumerate(lat_tiles):
                hf, hfb = hf_ring.next()
                S.dma("sp", hf[:, :, 0:n], HFv[:, :, t0:t0 + n], r=[HFSB[j]], w=[hfb])
                hb, hbb = hb_ring.next()
                S.dma("sp", hb[:, :, 0:n], HBv[:, :, t0:t0 + n], r=[HBSB[j]], w=[hbb])
                gg, ggb = gg_ring.next()
                S.dma("sp", gg[:, :, 0:n], GTv[:, :, t0:t0 + n], r=[GTSB[j]], w=[ggb])
                cl, clb = cl_ring.next()
                TTo("dve", hf[:, :, 0:n], hf[:, :, 0:n], hb[:, :, 0:n], ALU.add, [hfb, hbb], [hfb])
                TTo("pool", cl[:, :, 0:n], hf[:, :, 0:n], gg[:, :, 0:n], ALU.mult, [hfb, ggb], [clb])
                S.dma("pool", C1v[:, :, t0:t0 + n], cl[:, :, 0:n], r=[clb], w=[C1B[j]])
            S.barrier()

    def phase_mla():
        CQN, CKVN, KRb = L1T["CQN"], L1T["CKVN"], L1T["KRb"]
        with ExitStack() as es:
            Wq = sb(es, "Wq", [128, 3, 8, 128], BF16)
            Wk = sb(es, "Wk", [128, 2, 8, 64], BF16)
            Wv = sb(es, "Wv", [128, 2, 8, 64], BF16)
            WB = Buf()
            qsrc = W["mla_w_uq"][0].rearrange("(kc p) (h e) -> p kc h e", p=128, e=96)
            ksrc = W["mla_w_ukv"][0].rearrange("(kc p) (h e) -> p kc h e", p=128, e=128)
            for kc in range(3):
                cast_load(Wq[:, kc, :, 64:128], qsrc[:, kc, :, 0:64], WB)
                cast_load(Wq[:, kc, :, 0:32], qsrc[:, kc, :, 64:96], WB)
                cast_load(Wq[:, kc, :, 32:64], qsrc[:, kc, :, 64:96], WB)
            for kc in range(2):
                cast_load(Wk[:, kc, :, :], ksrc[:, kc, :, 0:64], WB)
                cast_load(Wv[:, kc, :, :], ksrc[:, kc, :, 64:128], WB)
            Va = sb(es, "Va", [128, NB, 8, 66], BF16)
            VaB = Buf()
            MSET("pool", Va[:, :, :, 64:66], 1.0, [VaB])
            rings = attn_rings(es)
            k_ring = Ring(nc, es, "Kh", [128, TT], BF16, 2)
            q_ring = Ring(nc, es, "Qh", [128, T], BF16, 2)
            pt_ring = Ring(nc, es, "ptm", [128, 2, 512], BF16, 3)
            cs_ring = Ring(nc, es, "csq", [32, 2, 512], F32, 2)
            t1_ring = Ring(nc, es, "t1q", [32, 512], F32, 2)
            t2_ring = Ring(nc, es, "t2q", [32, 512], F32, 2)
            mbanks = Rot([6, 7])
            sbanks = Rot([0, 2])
            abanks = Rot([4, 5])
            for blk in range(NB):
                bk = mbanks.next()
                for kc in range(2):
                    MM(PS[:, bk, 0:512], CKVN[:, kc, blk * 128:(blk + 1) * 128], Wv[:, kc, :, :].rearrange("p h d -> p (h d)"),
                       kc == 0, kc == 1, [CKB[min(blk // 4, NT)], WB], [PB[bk]])
                CP("act" if blk % 2 else "dve", Va[:, blk, :, 0:64], PS[:, bk, 0:512].rearrange("p (h d) -> p h d", d=64), [PB[bk]], [VaB])
            sc = float(96 ** -0.5)
            pend = []
            for hh in range(8):
                Kh, KhB = k_ring.next()
                Qh, QhB = q_ring.next()
                CP("pool", Kh[0:64, :], KRb[0:64, :], KRB, [KhB])
                for j, (t0, n, isc) in enumerate(tiles):
                    bk = mbanks.next()
                    for kc in range(2):
                        MM(PS[64:128, bk, 0:n], Wk[:, kc, hh, :], CKVN[:, kc, t0:t0 + n], kc == 0, kc == 1, [CKB[j], WB], [PB[bk]])
                    CP("act", Kh[64:128, t0:t0 + n], PS[64:128, bk, 0:n], [PB[bk]], [KhB])
                for j, (t0, n, isc) in enumerate(lat_tiles):
                    cs, csb = cs_ring.next()
                    S.dma("sp", cs[:, 0, :], cmv[:, t0:t0 + n], w=[csb])
                    S.dma("sp", cs[:, 1, :], smv[:, t0:t0 + n], w=[csb])
                    bk = mbanks.next()
                    for kc in range(3):
                        MM(PS[:, bk, 0:n], Wq[:, kc, hh, :], CQN[:, kc, t0:t0 + n], kc == 0, kc == 2, [CQB[j], WB], [PB[bk]])
                    CP("act", Qh[32:64, t0:t0 + n], PS[32:64, bk, 0:n], [PB[bk]], [QhB])
                    CP("dve", Qh[64:128, t0:t0 + n], PS[64:128, bk, 0:n], [PB[bk]], [QhB])
                    rbk = mbanks.next()
                    MM(PS[0:32, rbk, 0:n], r32B[32:64, :], Qh[32:64, t0:t0 + n], True, True, [QhB, CB], [PB[rbk]])
                    t1, t1b = t1_ring.next()
                    t2, t2b = t2_ring.next()
                    TTo("dve", t1[:, 0:n], PS[0:32, bk, 0:n], cs[:, 0, 0:n], ALU.mult, [PB[bk], csb], [t1b])
                    TTo("dve", t2[:, 0:n], PS[0:32, rbk, 0:n], cs[:, 1, 0:n], ALU.mult, [PB[rbk], csb], [t2b])
                    TTo("pool", Qh[0:32, t0:t0 + n], t1[:, 0:n], t2[:, 0:n], ALU.add, [t1b, t2b], [QhB])
                for j, (t0, n, isc) in enumerate(lat_tiles):
                    acc = abanks.next()
                    ngrp = (NB + 1) // 2
                    for gi in range(ngrp):
                        kbs = [kb for kb in (2 * gi, 2 * gi + 1) if kb < NB]
                        sb0 = sbanks.next()
                        for qi, kb in enumerate(kbs):
                            MM(PS[:, sb0 + qi, 0:n], Kh[:, kb * 128:(kb + 1) * 128], Qh[:, t0:t0 + n], True, True, [KhB, QhB], [PB[sb0 + qi]])
                        pt, ptb = pt_ring.next()
                        m = len(kbs)
                        ACT(pt[:, 0:m, 0:n], PS[:, sb0:sb0 + m, 0:n], AF.Exp, [PB[sb0 + q_] for q_ in range(m)], [ptb], scale=sc)
                        if pend:
                            pend.pop()()

                        def later(kbs=kbs, acc=acc, n=n, pt=pt, ptb=ptb, gi=gi, hh=hh, t0=t0, j=j, last=(gi == ngrp - 1)):
                            for qi, kb in enumerate(kbs):
                                MM(PS[0:65, acc, 0:n], Va[:, kb, hh, 0:65], pt[:, qi, 0:n], gi == 0 and qi == 0, kb == NB - 1, [ptb, VaB], [PB[acc]])
                            if last:
                                attn_finalize(rings, acc, n, None, D1[hh * 64:(hh + 1) * 64, t0:t0 + n], D1B[j], mbanks.next())
                        pend.append(later)
            if pend:
                pend.pop()()
            S.barrier()

    def layer1():
        alloc_l1()
        phase_mod(1)
        phase_p1_l1()
        if stop == "p1_l1":
            return
        phase_lru()
        phase_lru_combine()
        if stop == "lru":
            return
        phase_mla()
        if stop == "mla":
            return
        L1.close()
        phase_wout(1, W["cd_w_out"][0], C1, C1B, D1, D1B, lat_t)
        if stop == "wout1":
            return
        phase_ffna(1, lat_t)
        phase_ffnb(1, lat_t)

    all_t = list(range(len(tiles)))
    lat_t = list(range(NT))

    def dump_small(parts):
        off = 0
        for t, w in parts:
            S.dma("sp", DBGV[:, off:off + w], t, r=[CB], w=[Buf()])
            off += w
        S.barrier()

    def run():
        phase_mod(0)
        if stop == "mod0":
            dump_small([(MODT[:, 0, :, :].rearrange("p s m -> p (s m)"), 96), (A1[:, 0].rearrange("p s m -> p (s m)"), 16),
                        (G1[:, 0].rearrange("p s m -> p (s m)"), 16), (A2[:, 0].rearrange("p s m -> p (s m)"), 16),
                        (G2[:, 0].rearrange("p s m -> p (s m)"), 16)])
            return
        phase_tin()
        if stop == "tin":
            return
        phase_p1_l0()
        if stop == "p1_l0":
            return
        phase_conva()
        if stop == "conva":
            return
        phase_attn0()
        if stop == "attn0":
            return
        L0.close()
        phase_wout(0, W["ab_w_out"][0], A0, A0B, B0, B0B, all_t)
        if stop == "wout0":
            return
        phase_ffna(0, all_t)
        if stop == "ffna0":
            return
        phase_ffnb(0, all_t)
        if stop == "ffnb0":
            return
        layer1()
        phase_tout()

    run()
    L1.close()
    L0.close()
    ges.close()
    build.stats = (S.nops, S.nwaits)
    return nc


def make_in_maps(inputs, T):
    consts = make_consts(T)
    f = lambda a: np.ascontiguousarray(np.asarray(a, dtype=np.float32))
    shared = {k: f(inputs[k]) for k in WEIGHT_SHAPES}
    shared.update(consts)
    shared["c_ctx"] = f(inputs["c_ctx"]).reshape(8, 128)
    x, c, ctx = f(inputs["x"]), f(inputs["c"]), f(inputs["ctx"])
    maps = []
    for b in range(x.shape[0]):
        m = dict(shared)
        m["x"] = np.ascontiguousarray(x[b])
        m["ctx"] = np.ascontiguousarray(ctx[b])
        m["c"] = np.ascontiguousarray(c[b]).reshape(8, 128)
        maps.append(m)
    return maps


def kernel(**inputs):
    T = int(np.asarray(inputs["x"]).shape[1])
    nc = build(T)
    in_maps = make_in_maps(inputs, T)
    res = run_bass_kernel_spmd(nc, in_maps, core_ids=list(range(len(in_maps))))
    return np.stack([np.asarray(r["y"], dtype=np.float32) for r in res.results], axis=0)
```

```python
import numpy as np
import ml_dtypes
from contextlib import ExitStack
import concourse.bass as bass
import concourse.mybir as mybir
from concourse.bass_utils import run_bass_kernel_spmd

F32 = mybir.dt.float32
BF16 = mybir.dt.bfloat16
AF = mybir.ActivationFunctionType
ALU = mybir.AluOpType

D = 1024
CTX = 256
EPS = 1e-6
FFN = 2816
NJ = FFN // 128


class Buf:
    __slots__ = ("w", "r")

    def __init__(self):
        self.w = None
        self.r = {}


class Op:
    __slots__ = ("eng", "chan", "seq", "fn", "waits", "signal", "clock", "val", "isdma")


class Sched:
    COMPUTE = ("pe", "act", "dve", "pool")

    def __init__(self, nc, es):
        self.nc = nc
        self.eobj = dict(pe=nc.tensor, act=nc.scalar, dve=nc.vector, pool=nc.gpsimd, sp=nc.sync)
        self.sem = {}
        for e in self.COMPUTE:
            self.sem[e] = es.enter_context(nc.semaphore("sem_" + e))
        self.nslot = {"sp": 12, "pool": 8}
        for q, n in self.nslot.items():
            for k in range(n):
                self.sem[(q, k)] = es.enter_context(nc.semaphore("dq_%s%d" % (q, k)))
        self.clock = {e: {} for e in self.eobj}
        self.seq = {e: 0 for e in self.COMPUTE}
        self.sigcount = {e: 0 for e in self.COMPUTE}
        self.dcount = {q: 0 for q in self.nslot}
        self.slot_last = {}
        self.last = {}
        self.pending = []
        self.bar = None
        self.nops = 0
        self.nwaits = 0
        self.dummy = es.enter_context(nc.sbuf_tensor("sched_dummy", [128, 8], F32))

    def _add(self, eng, chan, seq, fn, r, w, isdma, extra=()):
        op = Op()
        op.eng, op.chan, op.seq, op.fn, op.isdma = eng, chan, seq, fn, isdma
        op.signal = isdma
        op.val = 16 * (seq + 1) if isdma else None
        deps = {}

        def need(d):
            if d is None:
                return
            cur = deps.get(d.chan)
            if cur is None or cur.seq < d.seq:
                deps[d.chan] = d

        r = list(r)
        if self.bar is not None:
            r.append(self.bar)
        for b in r:
            need(b.w)
        for b in w:
            need(b.w)
            for d in b.r.values():
                need(d)
        for d in extra:
            need(d)
        clk = self.clock[eng]
        waits = []
        for d in deps.values():
            if d.chan == "pe" and eng == "pe":
                continue
            if clk.get(d.chan, -1) >= d.seq:
                continue
            waits.append(d)
            d.signal = True
            for k, v in d.clock.items():
                if clk.get(k, -1) < v:
                    clk[k] = v
            if clk.get(d.chan, -1) < d.seq:
                clk[d.chan] = d.seq
        op.waits = waits
        op.clock = dict(clk)
        for b in r:
            cur = b.r.get(chan)
            if cur is None or cur.seq < seq:
                b.r[chan] = op
        for b in w:
            b.w = op
            b.r = {}
        self.last[chan] = op
        self.pending.append(op)
        self.nops += 1
        self.nwaits += len(waits)
        return op

    def op(self, eng, fn, r=(), w=()):
        s = self.seq[eng]
        self.seq[eng] = s + 1
        return self._add(eng, eng, s, fn, r, w, False)

    def dma(self, q, out, in_, r=(), w=(), **kw):
        i = self.dcount[q]
        self.dcount[q] = i + 1
        n = self.nslot[q]
        slot, gen = i % n, i // n
        chan = (q, slot)
        prev = self.slot_last.get(chan)
        extra = [prev] if prev is not None else []
        op = self._add(q, chan, gen, lambda e: e.dma_start(out=out, in_=in_, **kw), r, w, True, extra)
        self.slot_last[chan] = op
        return op

    def barrier(self):
        b = Buf()
        extra = list(self.last.values())
        dummy = self.dummy
        s = self.seq["pool"]
        self.seq["pool"] = s + 1
        m = self._add("pool", "pool", s, lambda e: e.memset(dummy[:], 0.0), [], [b], False, extra)
        m.signal = True
        self.bar = b
        self.flush()

    def flush(self):
        for op in self.pending:
            e = self.eobj[op.eng]
            for d in op.waits:
                e.wait_ge(self.sem[d.chan], d.val)
            if op.signal and not op.isdma:
                self.sigcount[op.chan] += 1
                op.val = self.sigcount[op.chan]
            ins = op.fn(e)
            if op.signal:
                ins.then_inc(self.sem[op.chan], 16 if op.isdma else 1)
            op.fn = None
        self.pending = []


class Ring:
    uid = 0

    def __init__(self, nc, es, name, shape, dtype, n):
        Ring.uid += 1
        self.t = [es.enter_context(nc.sbuf_tensor("%s_%d_%d" % (name, Ring.uid, i), shape, dtype)) for i in range(n)]
        self.b = [Buf() for _ in range(n)]
        self.i = 0

    def next(self):
        k = self.i % len(self.t)
        self.i += 1
        return self.t[k], self.b[k]


class Rot:
    def __init__(self, items):
        self.items = list(items)
        self.i = 0

    def next(self):
        v = self.items[self.i % len(self.items)]
        self.i += 1
        return v


def _rope_tables(T, dim):
    rows = T // 64
    row = np.repeat(np.arange(rows), 64).astype(np.float32)
    col = np.tile(np.arange(64), rows).astype(np.float32)
    nf = dim // 4
    inv = (np.float32(10000.0) ** (-np.arange(nf, dtype=np.float32) / np.float32(nf))).astype(np.float32)
    ar = row[:, None] * inv
    ac = col[:, None] * inv
    ang = np.concatenate([ar, ar, ac, ac], axis=-1)
    return np.ascontiguousarray(np.cos(ang).T.astype(np.float32)), np.ascontiguousarray(np.sin(ang).T.astype(np.float32))


def _rot_T(dim):
    q = dim // 4
    R = np.zeros((dim, dim), np.float32)
    for i in range(q):
        R[i, q + i] = -1.0
        R[q + i, i] = 1.0
        R[2 * q + i, 3 * q + i] = -1.0
        R[3 * q + i, 2 * q + i] = 1.0
    return np.ascontiguousarray(R.T)


def make_consts(T):
    cw, sw = _rope_tables(T, 64)
    cm, sm = _rope_tables(T, 32)
    ident = np.eye(128, dtype=np.float32)
    sel = np.zeros((128, 64), np.float32)
    sel[64, :] = 1.0
    b = np.arange(128)[:, None]
    a = np.arange(128)[None, :]
    m1 = (b <= a).astype(np.float32)
    m2 = (a <= b).astype(np.float32)
    r64 = _rot_T(64)
    r32 = _rot_T(32)
    return {
        "k_ident": ident, "k_sel": sel, "k_m1": m1, "k_m2": m2,
        "k_r64": np.concatenate([r64, r64], 0), "k_r32": np.concatenate([r32, r32], 0),
        "k_cw": cw, "k_sw": sw, "k_cm": cm, "k_sm": sm,
    }


WEIGHT_SHAPES = {
    "w_mod": [2, 1024, 6144], "b_mod": [2, 6144], "norm_g": [2, 4, 1024], "ffn_w_up": [2, 1024, 5632],
    "ffn_conv_w": [2, 3, 2816], "ffn_conv_b": [2, 2816], "ffn_w_down": [2, 2816, 1024],
    "ab_w_in": [1, 1024, 1792], "a_conv_w": [1, 31, 512], "a_conv_b": [1, 512], "a_ln_g": [1, 512],
    "a_ln_b": [1, 512], "b_sink": [1, 8], "ab_w_out": [1, 1024, 1024], "cd_w_in": [1, 1024, 1696],
    "lru_conv_w": [1, 2, 4, 512], "lru_conv_b": [1, 2, 512], "lru_gate_w": [1, 2, 2, 8, 64, 64],
    "lru_gate_b": [1, 2, 2, 512], "lru_lambda": [1, 2, 512], "mla_q_norm": [1, 384],
    "mla_w_uq": [1, 384, 768], "mla_kv_norm": [1, 256], "mla_w_ukv": [1, 256, 1024],
    "cd_w_out": [1, 1024, 1024],
}


def build(T=4096, dbg=(), stop=None):
    nc = bass.Bass("TRN2", target_bir_lowering=False)
    TT = T + CTX
    NT = T // 512
    NBL = T // 128
    NB = NBL + CTX // 128
    tiles = [(i * 512, 512, False) for i in range(NT)] + [(T, CTX, True)]
    lat_tiles = tiles[:NT]

    def din(name, shape):
        return nc.dram_tensor(name, list(shape), F32, kind="ExternalInput").ap()

    x_in = din("x", [T, D])
    ctx_in = din("ctx", [CTX, D])
    c_in = din("c", [8, 128])
    cctx_in = din("c_ctx", [8, 128])
    W = {k: din(k, s) for k, s in WEIGHT_SHAPES.items()}
    KC = {k: din(k, v.shape) for k, v in make_consts(T).items()}
    y_out = nc.dram_tensor("y", [T, D], F32, kind="ExternalOutput").ap()

    def scratch(name, shape, dt):
        kind = "ExternalOutput" if name in dbg else "Internal"
        return nc.dram_tensor(name, list(shape), dt, kind=kind).ap()

    XT = scratch("XT", [D, TT], F32)
    U0 = scratch("U0", [512, TT], BF16)
    A0 = scratch("A0", [512, TT], BF16)
    B0 = scratch("B0", [512, TT], BF16)
    GS = scratch("GS", [FFN, TT], BF16)
    US = scratch("US", [FFN, TT], BF16)
    XBS = scratch("XBS", [512, TT], F32)
    GTS = scratch("GTS", [512, T], F32)
    HFS = scratch("HFS", [512, T], F32)
    HBS = scratch("HBS", [512, T], F32)
    C1 = scratch("C1", [512, T], BF16)
    D1 = scratch("D1", [512, T], BF16)
    DBGV = scratch("DBGV", [128, 512], F32)
    QS = scratch("QS", [8, 128, TT], BF16)
    XTv = XT.rearrange("(c p) t -> p c t", p=128)
    XTB = [Buf() for _ in tiles]
    U0B = [Buf() for _ in tiles]
    A0B = [Buf() for _ in tiles]
    B0B = [Buf() for _ in tiles]
    GSB = [Buf() for _ in tiles]
    USB = [Buf() for _ in tiles]
    XBSB = [Buf() for _ in tiles]
    GTSB = [Buf() for _ in tiles]
    HFSB = [Buf() for _ in tiles]
    HBSB = [Buf() for _ in tiles]
    C1B = [Buf() for _ in tiles]
    D1B = [Buf() for _ in tiles]

    ges = ExitStack()
    S = Sched(nc, ges)
    PS = ges.enter_context(nc.psum_tensor("PS", [128, 8, 512], F32))
    PB = [Buf() for _ in range(8)]

    def sb(es, name, shape, dt):
        Ring.uid += 1
        return es.enter_context(nc.sbuf_tensor("%s_%d" % (name, Ring.uid), list(shape), dt))

    def MM(out, lhsT, rhs, st, sp, r, w):
        S.op("pe", lambda e: e.matmul(out, lhsT, rhs, start=st, stop=sp), r, w)

    def TR(out, in_, ident, r, w):
        S.op("pe", lambda e: e.transpose(out, in_, ident), r, w)

    def ACT(out, in_, func, r, w, bias=None, scale=None):
        kw = {}
        if bias is not None:
            kw["bias"] = bias
        if scale is not None:
            kw["scale"] = scale
        S.op("act", lambda e: e.activation(out, in_, func, **kw), r, w)

    def CP(eng, out, in_, r, w):
        if eng == "act":
            S.op("act", lambda e: e.copy(out, in_), r, w)
        else:
            S.op(eng, lambda e: e.tensor_copy(out, in_), r, w)

    def TTo(eng, out, a, b, op, r, w):
        S.op(eng, lambda e: e.tensor_tensor(out, a, b, op), r, w)

    def TS(eng, out, a, s1, s2, op0, op1, r, w):
        if s2 is None:
            S.op(eng, lambda e: e.tensor_scalar(out, a, s1, None, op0), r, w)
        else:
            S.op(eng, lambda e: e.tensor_scalar(out, a, s1, s2, op0, op1), r, w)

    def STT(out, in0, scalar, in1, op0, op1, r, w):
        S.op("dve", lambda e: e.scalar_tensor_tensor(out, in0, scalar, in1, op0, op1), r, w)

    def RCP(out, in_, r, w):
        S.op("dve", lambda e: e.reciprocal(out, in_), r, w)

    def MSET(eng, ap, val, w):
        S.op(eng, lambda e: e.memset(ap, val), [], w)

    identF = sb(ges, "identF", [128, 128], F32)
    identB = sb(ges, "identB", [128, 128], BF16)
    onesB = sb(ges, "onesB", [128, 128], BF16)
    onesF = sb(ges, "onesF", [128, 128], F32)
    selF = sb(ges, "selF", [128, 64], F32)
    m1B = sb(ges, "m1B", [128, 128], BF16)
    m2B = sb(ges, "m2B", [128, 128], BF16)
    r64B = sb(ges, "r64B", [128, 64], BF16)
    r32B = sb(ges, "r32B", [64, 32], BF16)
    CB = Buf()
    S.dma("sp", identF[:], KC["k_ident"], w=[CB])
    S.dma("sp", selF[:], KC["k_sel"], w=[CB])
    S.dma("pool", m1B[:], KC["k_m1"], w=[CB])
    S.dma("pool", m2B[:], KC["k_m2"], w=[CB])
    S.dma("pool", r64B[:], KC["k_r64"], w=[CB])
    S.dma("pool", r32B[:], KC["k_r32"], w=[CB])
    CP("dve", identB[:], identF[:], [CB], [CB])
    MSET("dve", onesB[:], 1.0, [CB])
    MSET("dve", onesF[:], 1.0, [CB])

    cols = {}
    colspec = {
        "g": (W["norm_g"], 64), "bm": (W["b_mod"], 96), "fcb": (W["ffn_conv_b"], 44),
        "fcw0": (W["ffn_conv_w"][0], 66), "fcw1": (W["ffn_conv_w"][1], 66),
        "acb": (W["a_conv_b"], 4), "alg": (W["a_ln_g"], 4), "alb": (W["a_ln_b"], 4),
        "acw": (W["a_conv_w"], 124), "lcw": (W["lru_conv_w"], 32), "lcb": (W["lru_conv_b"], 8),
        "lgb": (W["lru_gate_b"], 16), "lam": (W["lru_lambda"], 8), "qn": (W["mla_q_norm"], 3),
        "kvn": (W["mla_kv_norm"], 2), "c": (c_in, 8), "cc": (cctx_in, 8),
    }
    for name, (src, n) in colspec.items():
        cols[name] = sb(ges, "col_" + name, [128, n], F32)
    esink = sb(ges, "esink", [64, 8], F32)
    scT = sb(ges, "scT", [128, 8, 2], F32)
    MODT = sb(ges, "MODT", [128, 2, 2, 48], F32)
    A1 = sb(ges, "A1", [128, 2, 2, 8], F32)
    G1 = sb(ges, "G1", [128, 2, 2, 8], F32)
    A2 = sb(ges, "A2", [128, 2, 2, 8], F32)
    G2 = sb(ges, "G2", [128, 2, 2, 8], F32)
    epsT = sb(ges, "epsT", [128, 1], F32)
    cch = sb(ges, "cch", [128, 2, 8], F32)
    pre = ExitStack()
    rows_ring = Ring(nc, pre, "rows", [128, 128], F32, 2)
    for i, (name, (src, n)) in enumerate(colspec.items()):
        dst = cols[name]
        nd = len(src.shape)
        if nd == 1:
            s2 = src.rearrange("(r p) -> r p", p=128)
        elif nd == 2 and src.shape[1] == 128:
            s2 = src
        else:
            names = " ".join("a%d" % k for k in range(nd - 1))
            s2 = src.rearrange("%s (r p) -> (%s r) p" % (names, names), p=128)
        rt, rb = rows_ring.next()
        S.dma("sp", rt[0:n, :], s2, w=[rb])
        bank = 6 + (i % 2)
        TR(PS[:, bank, 0:n], rt[0:n, :], identF[0:n, 0:n], [rb, CB], [PB[bank]])
        CP("dve", dst[:], PS[:, bank, 0:n], [PB[bank]], [CB])
    sk = sb(pre, "sk", [1, 8], F32)
    skb = Buf()
    S.dma("sp", sk[:], W["b_sink"], w=[skb])
    MM(PS[0:64, 5, 0:8], onesF[0:1, 0:64], sk[0:1, :], True, True, [skb, CB], [PB[5]])
    ACT(esink[:], PS[0:64, 5, 0:8], AF.Exp, [PB[5]], [CB])
    ACT(scT[:, :, 0], cols["c"][:], AF.Silu, [CB], [CB])
    ACT(scT[:, :, 1], cols["cc"][:], AF.Silu, [CB], [CB])
    S.barrier()
    pre.close()

    gc = cols["g"]

    def mod_load(l, jb, wring):
        wsrc = W["w_mod"][l].rearrange("(kc p) n -> p kc n", p=128)
        wt, wb = wring.next()
        S.dma("sp", wt[:], wsrc[:, :, jb * 768:(jb + 1) * 768], w=[wb])
        return wt, wb

    def mod_mm(l, jb, wt, wb):
        for jj in range(6):
            j = jb * 6 + jj
            for kc in range(8):
                MM(PS[:, 6, 2 * j:2 * j + 2], wt[:, kc, jj * 128:(jj + 1) * 128], scT[:, kc, :],
                   kc == 0, kc == 7, [wb, CB], [PB[6]])

    def mod_block(l, jb, wring):
        wt, wb = mod_load(l, jb, wring)
        mod_mm(l, jb, wt, wb)

    def mod_finish(l):
        pv = PS[:, 6, 0:96].rearrange("p (j s) -> p j s", s=2)
        for s in range(2):
            TTo("dve", MODT[:, l, s, :], pv[:, :, s], cols["bm"][:, l * 48:(l + 1) * 48], ALU.add, [PB[6], CB], [CB])
            STT(A1[:, l, s, :], MODT[:, l, s, 8:16], 1.0, gc[:, l * 32:l * 32 + 8], ALU.add, ALU.mult, [CB], [CB])
            TTo("dve", G1[:, l, s, :], MODT[:, l, s, 16:24], gc[:, l * 32 + 8:l * 32 + 16], ALU.mult, [CB], [CB])
            STT(A2[:, l, s, :], MODT[:, l, s, 32:40], 1.0, gc[:, l * 32 + 16:l * 32 + 24], ALU.add, ALU.mult, [CB], [CB])
            TTo("dve", G2[:, l, s, :], MODT[:, l, s, 40:48], gc[:, l * 32 + 24:l * 32 + 32], ALU.mult, [CB], [CB])

    def phase_mod(l):
        with ExitStack() as es:
            wring = Ring(nc, es, "wmod", [128, 8, 768], F32, 2)
            for jb in range(8):
                mod_block(l, jb, wring)
            mod_finish(l)
            S.barrier()

    def phase_tin():
        with ExitStack() as es:
            xin_ring = Ring(nc, es, "xin", [128, D], F32, 3)
            xt_ring = Ring(nc, es, "xtt", [128, 8, 512], F32, 2)
            for j, (t0, n, isc) in enumerate(tiles):
                src = ctx_in if isc else x_in
                s0 = 0 if isc else t0
                for b in range(n // 128):
                    xin, xb_ = xin_ring.next()
                    S.dma("sp", xin[:], src[s0 + b * 128:s0 + (b + 1) * 128, :], w=[xb_])
                    for fc in range(8):
                        TR(PS[:, fc, b * 128:(b + 1) * 128], xin[:, fc * 128:(fc + 1) * 128], identF[:], [xb_, CB], [PB[fc]])
                xt, xtb = xt_ring.next()
                for fc in range(8):
                    CP("act" if fc % 2 else "dve", xt[:, fc, 0:n], PS[:, fc, 0:n], [PB[fc]], [xtb])
                S.dma("pool", XTv[:, :, t0:t0 + n], xt[:, :, 0:n], r=[xtb], w=[XTB[j]])
            S.barrier()

    def stat_rstd(es_rings, src, srcb, nch, n, dim, bank):
        for c in range(nch):
            MM(PS[:, bank, 0:n], onesB[:], src[:, c, 0:n], c == 0, c == nch - 1, [srcb, CB], [PB[bank]])
        rs, rsb = es_rings["rs"].next()
        ACT(rs[:, 0:n], PS[:, bank, 0:n], AF.Sqrt, [PB[bank]], [rsb], bias=epsT[:, 0:1], scale=1.0 / dim)
        RCP(rs[:, 0:n], rs[:, 0:n], [rsb], [rsb])
        return rs, rsb

    MSET("dve", epsT[:], EPS, [CB])

    def prenorm(rings, xt, xb, n, Acol, SHcol, bank):
        sq, sqb = rings["sq"].next()
        ACT(sq[:, :, 0:n], xt[:, :, 0:n], AF.Square, [xb], [sqb])
        rs, rsb = stat_rstd(rings, sq, sqb, 8, n, D, bank)
        TTo("dve", xt[:, :, 0:n], xt[:, :, 0:n], rs[:, 0:n].unsqueeze(1).to_broadcast([128, 8, n]), ALU.mult, [xb, rsb], [xb])
        h, hb = rings["h"].next()
        for c in range(8):
            ACT(h[:, c, 0:n], xt[:, c, 0:n], AF.Identity, [xb, CB], [hb], bias=SHcol[:, c:c + 1], scale=Acol[:, c:c + 1])
        return h, hb

    def postnorm_residual(rings, ysb, yb, xt, xb, n, Gcol, bank):
        sq, sqb = rings["sq"].next()
        TTo("pool", sq[:, :, 0:n], ysb[:, :, 0:n], ysb[:, :, 0:n], ALU.mult, [yb], [sqb])
        rs, rsb = stat_rstd(rings, sq, sqb, 8, n, D, bank)
        TTo("dve", ysb[:, :, 0:n], ysb[:, :, 0:n], rs[:, 0:n].unsqueeze(1).to_broadcast([128, 8, n]), ALU.mult, [yb, rsb], [yb])
        for c in range(8):
            STT(xt[:, c, 0:n], ysb[:, c, 0:n], Gcol[:, c:c + 1], xt[:, c, 0:n], ALU.mult, ALU.add, [yb, xb, CB], [xb])

    def norm_rings(es, with_h=True, nsq=2):
        rings = {
            "sq": Ring(nc, es, "sq", [128, 8, 512], BF16, nsq),
            "rs": Ring(nc, es, "rs", [128, 512], F32, 2),
        }
        if with_h:
            rings["h"] = Ring(nc, es, "h", [128, 8, 512], BF16, 2)
        return rings

    def cast_load(dst, src, wb):
        S.dma("pool", dst, src, w=[wb])

    L0 = ExitStack()
    Klat = sb(L0, "Klat", [64, 2, T], BF16)
    Kctx = sb(L0, "Kctx", [128, 2, CTX], BF16)
    Vt = sb(L0, "Vt", [128, NB, 2, 66], BF16)
    QSB = [[Buf() for _ in tiles] for _ in range(8)]
    KLB = [Buf() for _ in tiles]
    KCB = Buf()
    VB = [Buf() for _ in tiles]
    cwv, swv = KC["k_cw"], KC["k_sw"]
    cmv, smv = KC["k_cm"], KC["k_sm"]

    def phase_p1_l0():
        l = 0
        with ExitStack() as es:
            NCOL = 1024 + 1024 + 256 + 128
            Wt = sb(es, "Wt0", [128, 8, NCOL], BF16)
            WB = [Buf() for _ in range(5)]
            wsrc = W["ab_w_in"][0].rearrange("(kc p) n -> p kc n", p=128)
            cast_load(Wt[:, :, 0:1024], wsrc[:, :, 0:1024], WB[0])
            qd = Wt[:, :, 1024:2048].rearrange("p k (h two d) -> p k h two d", two=2, d=64)
            qs = wsrc[:, :, 1024:1536].rearrange("p k (h d) -> p k h d", d=64)
            for dup in range(2):
                for kc in range(8):
                    cast_load(qd[:, kc, :, dup, :], qs[:, kc, :, :], WB[1 + dup])
            kd = Wt[:, :, 2048:2304].rearrange("p k (h two d) -> p k h two d", two=2, d=64)
            ks = wsrc[:, :, 1536:1664].rearrange("p k (h d) -> p k h d", d=64)
            for dup in range(2):
                for kc in range(8):
                    cast_load(kd[:, kc, :, dup, :], ks[:, kc, :, :], WB[3])
            cast_load(Wt[:, :, 2304:2432], wsrc[:, :, 1664:1792], WB[4])
            MSET("pool", Vt[:, :, :, 64:66], 1.0, VB)
            rings = norm_rings(es)
            x_ring = Ring(nc, es, "xt", [128, 8, 512], F32, 2)
            sg_ring = Ring(nc, es, "sg", [128, 512], F32, 2)
            ust_ring = Ring(nc, es, "ust", [128, 4, 512], BF16, 2)
            cs_ring = Ring(nc, es, "cs", [64, 2, 512], F32, 3)
            t1_ring = Ring(nc, es, "t1", [64, 512], F32, 2)
            t2_ring = Ring(nc, es, "t2", [64, 512], F32, 2)
            kraw_ring = Ring(nc, es, "kraw", [128, 512], BF16, 2)
            qst_ring = Ring(nc, es, "qst", [128, 512], BF16, 3)
            banks = Rot([0, 1, 2, 3, 4])
            rbanks = Rot([5, 6])
            loads = {}

            def issue_load(j):
                t0, n, isc = tiles[j]
                xt, xb = x_ring.next()
                S.dma("sp", xt[:, :, 0:n], XTv[:, :, t0:t0 + n], r=[XTB[j]], w=[xb])
                cs, csb = cs_ring.next()
                if not isc:
                    S.dma("sp", cs[:, 0, :], cwv[:, t0:t0 + n], w=[csb])
                    S.dma("sp", cs[:, 1, :], swv[:, t0:t0 + n], w=[csb])
                loads[j] = (xt, xb, cs, csb)

            TL = list(range(len(tiles)))
            ACOL, SH0, SPLIT = A1, 0, 6

            def body(j, hcur_):
                t0, n, isc = tiles[j]
                xt, xb, cs, csb = loads[j]
                h, hb = hcur_

                def proj(col0, bank, M=128):
                    wdep = [WB[0]] if col0 < 1024 else ([WB[1], WB[2]] if col0 < 2048 else [WB[3]])
                    for kc in range(8):
                        MM(PS[0:M, bank, 0:n], Wt[:, kc, col0:col0 + M], h[:, kc, 0:n], kc == 0, kc == 7, [hb] + wdep, [PB[bank]])

                ust, ustb = ust_ring.next()
                for i in range(4):
                    bg = banks.next()
                    proj(512 + 128 * i, bg)
                    sg, sgb = sg_ring.next()
                    ACT(sg[:, 0:n], PS[:, bg, 0:n], AF.Sigmoid, [PB[bg]], [sgb])
                    bv = banks.next()
                    proj(128 * i, bv)
                    TTo("dve", ust[:, i, 0:n], PS[:, bv, 0:n], sg[:, 0:n], ALU.mult, [PB[bv], sgb], [ustb])
                    yield
                S.dma("pool", U0.rearrange("(c p) t -> p c t", p=128)[:, :, t0:t0 + n], ust[:, :, 0:n], r=[ustb], w=[U0B[j]])

                def rope(bank, rawsrc, rawb, dst, dstb):
                    rbk = rbanks.next()
                    MM(PS[0:64, rbk, 0:n], r64B[64:128, :], rawsrc, True, True, [rawb, CB], [PB[rbk]])
                    t1, t1b = t1_ring.next()
                    t2, t2b = t2_ring.next()
                    TTo("dve", t1[:, 0:n], PS[0:64, bank, 0:n], cs[:, 0, 0:n], ALU.mult, [PB[bank], csb], [t1b])
                    TTo("dve", t2[:, 0:n], PS[0:64, rbk, 0:n], cs[:, 1, 0:n], ALU.mult, [PB[rbk], csb], [t2b])
                    TTo("pool", dst, t1[:, 0:n], t2[:, 0:n], ALU.add, [t1b, t2b], [dstb])

                for hh in range(8):
                    bq = banks.next()
                    proj(1024 + 128 * hh, bq)
                    qst, qstb = qst_ring.next()
                    CP("act", qst[64:128, 0:n], PS[64:128, bq, 0:n], [PB[bq]], [qstb])
                    if not isc:
                        rope(bq, qst[64:128, 0:n], qstb, qst[0:64, 0:n], qstb)
                        S.dma("pool", QS[hh, :, t0:t0 + n], qst[:, 0:n], r=[qstb], w=[QSB[hh][j]])
                    else:
                        S.dma("pool", QS[hh, 64:128, t0:t0 + n], qst[64:128, 0:n], r=[qstb], w=[QSB[hh][j]])
                    yield
                for g in range(2):
                    bk = banks.next()
                    proj(2048 + 128 * g, bk)
                    if isc:
                        CP("act", Kctx[64:128, g, :], PS[64:128, bk, 0:n], [PB[bk]], [KCB])
                    else:
                        kr, krb = kraw_ring.next()
                        CP("act", kr[64:128, 0:n], PS[64:128, bk, 0:n], [PB[bk]], [krb])
                        rope(bk, kr[64:128, 0:n], krb, Klat[0:64, g, t0:t0 + n], KLB[j])
                for b in range(n // 128):
                    bv = banks.next()
                    for kc in range(8):
                        MM(PS[:, bv, 0:128], h[:, kc, b * 128:(b + 1) * 128], Wt[:, kc, 2304:2432], kc == 0, kc == 7, [hb, WB[4]], [PB[bv]])
                    blk = (t0 // 128) + b
                    CP("act" if b % 2 else "dve", Vt[:, blk, :, 0:64], PS[:, bv, 0:128].rearrange("p (g d) -> p g d", g=2), [PB[bv]], [VB[j]])
            def do_prenorm(jj):
                t0_, n_, isc_ = tiles[TL[jj]]
                s_ = 1 if isc_ else 0
                return prenorm(rings, loads[jj][0], loads[jj][1], n_, ACOL[:, l, s_, :], MODT[:, l, s_, SH0:SH0 + 8], 7)

            issue_load(0)
            if len(TL) > 1:
                issue_load(1)
            hcur = do_prenorm(0)
            for ji in range(len(TL)):
                gen = body(ji, hcur)
                k = 0
                done = ji + 1 >= len(TL)
                for _ in gen:
                    k += 1
                    if k == SPLIT and not done:
                        hcur = do_prenorm(ji + 1)
                        if ji + 2 < len(TL):
                            issue_load(ji + 2)
                        done = True
                if not done:
                    hcur = do_prenorm(ji + 1)
                    if ji + 2 < len(TL):
                        issue_load(ji + 2)
                loads.pop(ji)
            S.barrier()

    def phase_conva():
        with ExitStack() as es:
            Dg = sb(es, "DgA", [128, 4, 31, 128], BF16)
            DgB = Buf()
            for c in range(4):
                for k in range(31):
                    col = cols["acw"][:, k * 4 + c:k * 4 + c + 1]
                    TS("dve", Dg[:, c, k, :], identB[:], col, None, ALU.mult, None, [CB], [DgB])
            up_ring = Ring(nc, es, "up", [128, 4, 512 + 30], BF16, 2)
            ucv_ring = Ring(nc, es, "ucv", [128, 4, 512], F32, 2)
            usq_ring = Ring(nc, es, "usq", [128, 4, 512], F32, 2)
            st_ring = Ring(nc, es, "lnst", [128, 3, 512], F32, 2)
            tt_ring = Ring(nc, es, "lntt", [128, 512], F32, 2)
            ao_ring = Ring(nc, es, "ao", [128, 4, 512], BF16, 2)
            U0v = U0.rearrange("(c p) t -> p c t", p=128)
            A0v = A0.rearrange("(c p) t -> p c t", p=128)
            banks = Rot([0, 1, 2, 3])
            wring = Ring(nc, es, "wmod", [128, 8, 768], F32, 2)
            mod_jb = [0]
            mod_q = []

            def mod_step():
                k = mod_jb[0]
                if k > 8:
                    return
                if k < 8:
                    mod_q.append((k,) + mod_load(1, k, wring))
                if k >= 1:
                    kk, wt_, wb_ = mod_q.pop(0)
                    mod_mm(1, kk, wt_, wb_)
                mod_jb[0] += 1

            for j, (t0, n, isc) in enumerate(tiles):
                mod_step()
                seg0, seg1 = (T, TT) if isc else (0, T)
                lo, hi = max(t0 - 15, seg0), min(t0 + n + 15, seg1)
                up, upb = up_ring.next()
                rd = [U0B[j]]
                if j > 0 and not isc:
                    rd.append(U0B[j - 1])
                if j + 1 < NT:
                    rd.append(U0B[j + 1])
                if lo > t0 - 15:
                    MSET("pool", up[:, :, 0:15], 0.0, [upb])
                if hi < t0 + n + 15:
                    MSET("pool", up[:, :, n + 15:n + 30], 0.0, [upb])
                S.dma("sp", up[:, :, lo - (t0 - 15):hi - (t0 - 15)], U0v[:, :, lo:hi], r=rd, w=[upb])
                ucv, ucvb = ucv_ring.next()
                usq, usqb = usq_ring.next()
                for c in range(4):
                    bk = banks.next()
                    for k in range(31):
                        MM(PS[:, bk, 0:n], Dg[:, c, k, :], up[:, c, k:k + n], k == 0, k == 30, [upb, DgB], [PB[bk]])
                    ACT(ucv[:, c, 0:n], PS[:, bk, 0:n], AF.Identity, [PB[bk], CB], [ucvb], bias=cols["acb"][:, c:c + 1])
                    ACT(usq[:, c, 0:n], PS[:, bk, 0:n], AF.Square, [PB[bk], CB], [usqb], bias=cols["acb"][:, c:c + 1])
                for c in range(4):
                    MM(PS[:, 4, 0:n], onesF[:], ucv[:, c, 0:n], c == 0, c == 3, [ucvb, CB], [PB[4]])
                for c in range(4):
                    MM(PS[:, 5, 0:n], onesF[:], usq[:, c, 0:n], c == 0, c == 3, [usqb, CB], [PB[5]])
                st, stb = st_ring.next()
                TS("dve", st[:, 0, 0:n], PS[:, 4, 0:n], 1.0 / 512, None, ALU.mult, None, [PB[4]], [stb])
                TTo("dve", st[:, 1, 0:n], st[:, 0, 0:n], st[:, 0, 0:n], ALU.mult, [stb], [stb])
                STT(st[:, 2, 0:n], PS[:, 5, 0:n], 1.0 / 512, st[:, 1, 0:n], ALU.mult, ALU.subtract, [PB[5], stb], [stb])
                ACT(st[:, 2, 0:n], st[:, 2, 0:n], AF.Sqrt, [stb, CB], [stb], bias=epsT[:, 0:1])
                RCP(st[:, 2, 0:n], st[:, 2, 0:n], [stb], [stb])
                ao, aob = ao_ring.next()
                for c in range(4):
                    tt, ttb = tt_ring.next()
                    TTo("dve", tt[:, 0:n], ucv[:, c, 0:n], st[:, 0, 0:n], ALU.subtract, [ucvb, stb], [ttb])
                    TTo("dve", tt[:, 0:n], tt[:, 0:n], st[:, 2, 0:n], ALU.mult, [ttb, stb], [ttb])
                    ACT(ao[:, c, 0:n], tt[:, 0:n], AF.Silu, [ttb, CB], [aob], bias=cols["alb"][:, c:c + 1], scale=cols["alg"][:, c:c + 1])
                S.dma("pool", A0v[:, :, t0:t0 + n], ao[:, :, 0:n], r=[aob], w=[A0B[j]])
            while mod_jb[0] <= 8:
                mod_step()
            mod_finish(1)
            S.barrier()

    def attn_finalize(rings, acc, n, extra_col, dst_dram, dstb, dbank):
        osb, ob = rings["osb"].next()
        CP("act", osb[0:65, 0:n], PS[0:65, acc, 0:n], [PB[acc]], [ob])
        MM(PS[0:64, dbank, 0:n], selF[0:65, 0:64], osb[0:65, 0:n], True, True, [ob, CB], [PB[dbank]])
        rd, rdb = rings["rd"].next()
        if extra_col is not None:
            TS("dve", rd[0:64, 0:n], PS[0:64, dbank, 0:n], extra_col, None, ALU.add, None, [PB[dbank], CB], [rdb])
            RCP(rd[0:64, 0:n], rd[0:64, 0:n], [rdb], [rdb])
        else:
            RCP(rd[0:64, 0:n], PS[0:64, dbank, 0:n], [PB[dbank]], [rdb])
        bt, btb = rings["bt"].next()
        TTo("dve", bt[0:64, 0:n], osb[0:64, 0:n], rd[0:64, 0:n], ALU.mult, [ob, rdb], [btb])
        S.dma("pool", dst_dram, bt[0:64, 0:n], r=[btb], w=[dstb])

    def attn_rings(es):
        return {
            "osb": Ring(nc, es, "osb", [128, 512], F32, 2),
            "rd": Ring(nc, es, "rd", [64, 512], F32, 2),
            "bt": Ring(nc, es, "bt", [64, 512], BF16, 2),
        }

    def phase_attn0():
        with ExitStack() as es:
            rings = attn_rings(es)
            pt_ring = Ring(nc, es, "pt", [128, 512], BF16, 4)
            sbanks = Rot([0, 1, 2, 3])
            abanks = Rot([4, 5])
            dbanks = Rot([6, 7])
            qt_ring = Ring(nc, es, "qt", [128, 512], BF16, 3)
            pend = []
            for hh in range(8):
                g = hh // 4
                for j, (t0, n, isc) in enumerate(tiles):
                    qt, qtb = qt_ring.next()
                    if isc:
                        S.dma("sp", qt[64:128, 0:n], QS[hh, 64:128, t0:t0 + n], r=[QSB[hh][j]], w=[qtb])
                    else:
                        S.dma("sp", qt[:, 0:n], QS[hh, :, t0:t0 + n], r=[QSB[hh][j]], w=[qtb])
                    steps = []
                    for cc in range(CTX // 128):
                        steps.append((Kctx[64:128, g, cc * 128:(cc + 1) * 128], qt[64:128, 0:n],
                                      [KCB, qtb], NBL + cc, 0, n, []))
                    if not isc:
                        i4 = t0 // 128
                        for jb in range(i4 - 1, i4 + 5):
                            if jb < 0 or jb >= NBL:
                                continue
                            qb0, qb1 = max(jb - 1, i4), min(jb + 1, i4 + 3)
                            c0, c1 = (qb0 - i4) * 128, (qb1 - i4 + 1) * 128
                            masks = []
                            for qb in range(qb0, qb1 + 1):
                                if qb == jb - 1:
                                    masks.append(((qb - qb0) * 128, m1B))
                                elif qb == jb + 1:
                                    masks.append(((qb - qb0) * 128, m2B))
                            steps.append((Klat[0:64, g, jb * 128:(jb + 1) * 128], qt[0:64, c0:c1],
                                          [KLB[jb // 4], qtb], jb, c0, c1, masks))
                    acc = abanks.next()
                    for si, (lhsT, rhs, rdb_, vblk, c0, c1, masks) in enumerate(steps):
                        m = c1 - c0
                        sbk = sbanks.next()
                        MM(PS[:, sbk, 0:m], lhsT, rhs, True, True, rdb_, [PB[sbk]])
                        pt, ptb = pt_ring.next()
                        ACT(pt[:, 0:m], PS[:, sbk, 0:m], AF.Exp, [PB[sbk]], [ptb], scale=0.125)
                        for (mo, mk) in masks:
                            TTo("dve", pt[:, mo:mo + 128], pt[:, mo:mo + 128], mk[:], ALU.mult, [ptb, CB], [ptb])
                        if len(pend) >= 2:
                            pend.pop(0)()

                        def later(acc=acc, c0=c0, c1=c1, vblk=vblk, g=g, pt=pt, ptb=ptb, m=m, si=si, ns=len(steps), n=n, hh=hh, t0=t0, j=j):
                            MM(PS[0:65, acc, c0:c1], Vt[:, vblk, g, 0:65], pt[:, 0:m], si == 0, si == ns - 1,
                               [ptb, VB[min(vblk // 4, NT)]], [PB[acc]])
                            if si == ns - 1:
                                attn_finalize(rings, acc, n, esink[0:64, hh:hh + 1], B0[hh * 64:(hh + 1) * 64, t0:t0 + n], B0B[j], dbanks.next())
                        pend.append(later)
            while pend:
                pend.pop(0)()
            S.barrier()

    def phase_wout(l, Wsrc, Asrc, ASB, Bsrc, BSB, tl):
        with ExitStack() as es:
            Wa = sb(es, "Wa", [128, 4, D], BF16)
            Wb = sb(es, "Wb", [64, 8, D], BF16)
            WB = Buf()
            cast_load(Wa[:], Wsrc[0:512, :].rearrange("(c p) n -> p c n", p=128), WB)
            cast_load(Wb[:], Wsrc[512:1024, :].rearrange("(h d) n -> d h n", d=64), WB)
            rings = norm_rings(es, with_h=False)
            x_ring = Ring(nc, es, "xt", [128, 8, 512], F32, 2)
            a_ring = Ring(nc, es, "at", [128, 4, 512], BF16, 2)
            b_ring = Ring(nc, es, "bt2", [64, 8, 512], BF16, 2)
            y_ring = Ring(nc, es, "ysb", [128, 8, 512], F32, 2)
            Av = Asrc.rearrange("(c p) t -> p c t", p=128)
            Bv = Bsrc.rearrange("(h d) t -> d h t", d=64)
            banks = Rot([0, 1, 2, 3])
            loads = {}

            def issue_load(ji):
                j = tl[ji]
                t0, n, isc = tiles[j]
                xt, xb = x_ring.next()
                S.dma("sp", xt[:, :, 0:n], XTv[:, :, t0:t0 + n], r=[XTB[j]], w=[xb])
                at, ab = a_ring.next()
                S.dma("sp", at[:, :, 0:n], Av[:, :, t0:t0 + n], r=[ASB[j]], w=[ab])
                bt, bb = b_ring.next()
                S.dma("sp", bt[:, :, 0:n], Bv[:, :, t0:t0 + n], r=[BSB[j]], w=[bb])
                loads[ji] = (xt, xb, at, ab, bt, bb)

            issue_load(0)
            for ji, j in enumerate(tl):
                t0, n, isc = tiles[j]
                if ji + 1 < len(tl):
                    issue_load(ji + 1)
                xt, xb, at, ab, bt, bb = loads.pop(ji)
                s = 1 if isc else 0
                ysb, yb = y_ring.next()
                for fc in range(8):
                    bk = banks.next()
                    for c in range(4):
                        MM(PS[:, bk, 0:n], Wa[:, c, fc * 128:(fc + 1) * 128], at[:, c, 0:n], c == 0, False, [WB, ab], [PB[bk]])
                    for hh in range(8):
                        MM(PS[:, bk, 0:n], Wb[0:64, hh, fc * 128:(fc + 1) * 128], bt[0:64, hh, 0:n], False, hh == 7, [WB, bb], [PB[bk]])
                    CP("act", ysb[:, fc, 0:n], PS[:, bk, 0:n], [PB[bk]], [yb])
                postnorm_residual(rings, ysb, yb, xt, xb, n, G1[:, l, s, :], 7)
                S.dma("pool", XTv[:, :, t0:t0 + n], xt[:, :, 0:n], r=[xb], w=[XTB[j]])
            S.barrier()

    def phase_ffna(l, tl):
        with ExitStack() as es:
            Wu = sb(es, "Wu", [128, 8, 2 * FFN], BF16)
            WB = [Buf() for _ in range(8)]
            wsrc = W["ffn_w_up"][l].rearrange("(kc p) n -> p kc n", p=128)
            for blk in range(8):
                c0 = blk * 704
                cast_load(Wu[:, :, c0:c0 + 704], wsrc[:, :, c0:c0 + 704], WB[blk])
            rings = norm_rings(es)
            x_ring = Ring(nc, es, "xt", [128, 8, 512], F32, 2)
            st_ring = Ring(nc, es, "gst", [128, 4, 512], BF16, 3)
            banks = Rot([0, 1, 2, 3, 4, 5])
            GSv = GS.rearrange("(c p) t -> p c t", p=128)
            USv = US.rearrange("(c p) t -> p c t", p=128)
            loads = {}

            def issue_load(ji):
                j = tl[ji]
                t0, n, isc = tiles[j]
                xt, xb = x_ring.next()
                S.dma("sp", xt[:, :, 0:n], XTv[:, :, t0:t0 + n], r=[XTB[j]], w=[xb])
                loads[ji] = (xt, xb)

            TL = tl
            ACOL, SH0, SPLIT = A2, 24, 5

            def body(ji, hcur_):
                j = tl[ji]
                t0, n, isc = tiles[j]
                xt, xb = loads[ji]
                h, hb = hcur_
                for part, (dstv, dstB) in enumerate(((GSv, GSB), (USv, USB))):
                    k = 0
                    while k < NJ:
                        m = min(4, NJ - k)
                        st, stb = st_ring.next()
                        for q in range(m):
                            fc = part * NJ + k + q
                            bk = banks.next()
                            wb = WB[(fc * 128) // 704]
                            wb2 = WB[(fc * 128 + 127) // 704]
                            for kc in range(8):
                                MM(PS[:, bk, 0:n], Wu[:, kc, fc * 128:(fc + 1) * 128], h[:, kc, 0:n], kc == 0, kc == 7, [hb, wb, wb2], [PB[bk]])
                            CP("act" if (q % 2) else "dve", st[:, q, 0:n], PS[:, bk, 0:n], [PB[bk]], [stb])
                        S.dma("pool", dstv[:, k:k + m, t0:t0 + n], st[:, 0:m, 0:n], r=[stb], w=[dstB[j]])
                        yield
                        k += m
            def do_prenorm(jj):
                t0_, n_, isc_ = tiles[TL[jj]]
                s_ = 1 if isc_ else 0
                return prenorm(rings, loads[jj][0], loads[jj][1], n_, ACOL[:, l, s_, :], MODT[:, l, s_, SH0:SH0 + 8], 7)

            issue_load(0)
            if len(TL) > 1:
                issue_load(1)
            hcur = do_prenorm(0)
            for ji in range(len(TL)):
                gen = body(ji, hcur)
                k = 0
                done = ji + 1 >= len(TL)
                for _ in gen:
                    k += 1
                    if k == SPLIT and not done:
                        hcur = do_prenorm(ji + 1)
                        if ji + 2 < len(TL):
                            issue_load(ji + 2)
                        done = True
                if not done:
                    hcur = do_prenorm(ji + 1)
                    if ji + 2 < len(TL):
                        issue_load(ji + 2)
                loads.pop(ji)
            S.barrier()

    def phase_ffnb(l, tl):
        with ExitStack() as es:
            Wd = sb(es, "Wd", [128, NJ, D], BF16)
            WB = [Buf() for _ in range(2)]
            wsrc = W["ffn_w_down"][l].rearrange("(j p) n -> p j n", p=128)
            cast_load(Wd[:, 0:11, :], wsrc[:, 0:11, :], WB[0])
            cast_load(Wd[:, 11:22, :], wsrc[:, 11:22, :], WB[1])
            Dg = sb(es, "DgF", [128, NJ, 3, 128], BF16)
            DgB = Buf()
            fcw = cols["fcw%d" % l]
            for jj in range(NJ):
                for k in range(3):
                    TS("dve", Dg[:, jj, k, :], identB[:], fcw[:, k * NJ + jj:k * NJ + jj + 1], None, ALU.mult, None, [CB], [DgB])
            rings = norm_rings(es, with_h=False, nsq=1)
            x_ring = Ring(nc, es, "xt", [128, 8, 512], F32, 1)
            gH = [sb(es, "gtH%d" % i, [128, 11, 514], BF16) for i in range(2)]
            uH = [sb(es, "utH%d" % i, [128, 11, 512], BF16) for i in range(2)]
            gHB = [Buf(), Buf()]
            uHB = [Buf(), Buf()]
            ga_ring = Ring(nc, es, "ga", [128, 512], BF16, 3)
            act_ring = Ring(nc, es, "actt", [128, NJ, 512], BF16, 1)
            y_ring = Ring(nc, es, "ysb", [128, 8, 512], F32, 1)
            GSv = GS.rearrange("(c p) t -> p c t", p=128)
            USv = US.rearrange("(c p) t -> p c t", p=128)
            cbanks = Rot([0, 1, 2, 3])
            dbanks = Rot([4, 5, 6])
            loads = {}

            def load_gu(ji):
                j = tl[ji]
                t0, n, isc = tiles[j]
                seg0, seg1 = (T, TT) if isc else (0, T)
                lo, hi = max(t0 - 1, seg0), min(t0 + n + 1, seg1)
                rd = [GSB[j]]
                if j > 0 and not isc:
                    rd.append(GSB[j - 1])
                if j + 1 < NT:
                    rd.append(GSB[j + 1])
                for half in range(2):
                    gt, gb = gH[half], gHB[half]
                    if lo > t0 - 1:
                        MSET("pool", gt[:, :, 0:1], 0.0, [gb])
                    if hi < t0 + n + 1:
                        MSET("pool", gt[:, :, n + 1:n + 2], 0.0, [gb])
                    S.dma("sp", gt[:, :, lo - (t0 - 1):hi - (t0 - 1)], GSv[:, half * 11:(half + 1) * 11, lo:hi], r=rd, w=[gb])
                    S.dma("sp", uH[half][:, :, 0:n], USv[:, half * 11:(half + 1) * 11, t0:t0 + n], r=[USB[j]], w=[uHB[half]])

            def load_x(ji):
                j = tl[ji]
                t0, n, isc = tiles[j]
                xt, xb = x_ring.next()
                S.dma("sp", xt[:, :, 0:n], XTv[:, :, t0:t0 + n], r=[XTB[j]], w=[xb])
                loads[ji] = (xt, xb)

            load_gu(0)
            load_x(0)
            for ji, j in enumerate(tl):
                t0, n, isc = tiles[j]
                xt, xb = loads.pop(ji)
                s = 1 if isc else 0
                actt, actb = act_ring.next()
                for jj in range(NJ):
                    bk = cbanks.next()
                    gt, gb, ut, ub = gH[jj // 11], gHB[jj // 11], uH[jj // 11], uHB[jj // 11]
                    for k in range(3):
                        MM(PS[:, bk, 0:n], Dg[:, jj, k, :], gt[:, jj % 11, k:k + n], k == 0, k == 2, [gb, DgB], [PB[bk]])
                    ga, gab = ga_ring.next()
                    ACT(ga[:, 0:n], PS[:, bk, 0:n], AF.Gelu_apprx_tanh, [PB[bk], CB], [gab], bias=cols["fcb"][:, l * NJ + jj:l * NJ + jj + 1])
                    TTo("dve" if (jj % 2) else "pool", actt[:, jj, 0:n], ga[:, 0:n], ut[:, jj % 11, 0:n], ALU.mult, [gab, ub], [actb])
                if ji + 1 < len(tl):
                    load_gu(ji + 1)
                ysb, yb = y_ring.next()
                for fc in range(8):
                    bk = dbanks.next()
                    for jj in range(NJ):
                        MM(PS[:, bk, 0:n], Wd[:, jj, fc * 128:(fc + 1) * 128], actt[:, jj, 0:n], jj == 0, jj == NJ - 1, [actb] + WB, [PB[bk]])
                    CP("act", ysb[:, fc, 0:n], PS[:, bk, 0:n], [PB[bk]], [yb])
                postnorm_residual(rings, ysb, yb, xt, xb, n, G2[:, l, s, :], 7)
                S.dma("pool", XTv[:, :, t0:t0 + n], xt[:, :, 0:n], r=[xb], w=[XTB[j]])
                if ji + 1 < len(tl):
                    load_x(ji + 1)
            S.barrier()

    def phase_tout():
        with ExitStack() as es:
            x_ring = Ring(nc, es, "xt", [128, 8, 512], F32, 2)
            o_ring = Ring(nc, es, "ot", [128, D], F32, 3)
            banks = Rot([(0, 1), (2, 3), (4, 5), (6, 7)])
            for j, (t0, n, isc) in enumerate(lat_tiles):
                xt, xb = x_ring.next()
                S.dma("sp", xt[:, :, 0:n], XTv[:, :, t0:t0 + n], r=[XTB[j]], w=[xb])
                for b in range(n // 128):
                    b0, b1 = banks.next()
                    for fc in range(8):
                        bk = b0 if fc < 4 else b1
                        TR(PS[:, bk, (fc % 4) * 128:(fc % 4 + 1) * 128], xt[:, fc, b * 128:(b + 1) * 128], identF[:], [xb, CB], [PB[bk]])
                    ot, ob = o_ring.next()
                    CP("act", ot[:, 0:512], PS[:, b0, :], [PB[b0]], [ob])
                    CP("dve", ot[:, 512:1024], PS[:, b1, :], [PB[b1]], [ob])
                    S.dma("pool", y_out[t0 + b * 128:t0 + (b + 1) * 128, :], ot[:], r=[ob], w=[Buf()])
            S.barrier()

    L1 = ExitStack()
    L1T = {}

    def alloc_l1():
        L1T["CQN"] = sb(L1, "CQN", [128, 3, T], BF16)
        L1T["CKVN"] = sb(L1, "CKVN", [128, 2, TT], BF16)
        L1T["KRb"] = sb(L1, "KRb", [64, TT], BF16)

    CQB = [Buf() for _ in tiles]
    CKB = [Buf() for _ in tiles]
    KRB = [Buf() for _ in tiles]

    def phase_p1_l1():
        l = 1
        CQN, CKVN, KRb = L1T["CQN"], L1T["CKVN"], L1T["KRb"]
        with ExitStack() as es:
            Wt = sb(es, "Wt1", [128, 8, 1728], BF16)
            WB = [Buf() for _ in range(3)]
            wsrc = W["cd_w_in"][0].rearrange("(kc p) n -> p kc n", p=128)
            cast_load(Wt[:, :, 0:1024], wsrc[:, :, 0:1024], WB[0])
            cast_load(Wt[:, :, 1024:1664], wsrc[:, :, 1024:1664], WB[1])
            cast_load(Wt[:, :, 1664:1696], wsrc[:, :, 1664:1696], WB[2])
            cast_load(Wt[:, :, 1696:1728], wsrc[:, :, 1664:1696], WB[2])
            MSET("pool", KRb[32:64, 0:T], 0.0, KRB[:NT])
            MSET("pool", KRb[0:32, T:TT], 0.0, [KRB[NT]])
            rings = norm_rings(es)
            x_ring = Ring(nc, es, "xt", [128, 8, 512], F32, 2)
            xst_ring = Ring(nc, es, "xst", [128, 4, 512], F32, 1)
            gst_ring = Ring(nc, es, "gst1", [128, 4, 512], F32, 1)
            cqs_ring = Ring(nc, es, "cqs", [128, 3, 512], F32, 1)
            cs_ring = Ring(nc, es, "csm", [32, 2, 512], F32, 3)
            t1_ring = Ring(nc, es, "t1m", [32, 512], F32, 2)
            t2_ring = Ring(nc, es, "t2m", [32, 512], F32, 2)
            krs_ring = Ring(nc, es, "krs", [64, 512], BF16, 2)
            banks = Rot([0, 1, 2, 3, 4])
            XBv = XBS.rearrange("(c p) t -> p c t", p=128)
            GTv = GTS.rearrange("(c p) t -> p c t", p=128)
            loads = {}

            def issue_load(j):
                t0, n, isc = tiles[j]
                xt, xb = x_ring.next()
                S.dma("sp", xt[:, :, 0:n], XTv[:, :, t0:t0 + n], r=[XTB[j]], w=[xb])
                cs, csb = cs_ring.next()
                if not isc:
                    S.dma("sp", cs[:, 0, :], cmv[:, t0:t0 + n], w=[csb])
                    S.dma("sp", cs[:, 1, :], smv[:, t0:t0 + n], w=[csb])
                loads[j] = (xt, xb, cs, csb)

            TL = list(range(len(tiles)))
            ACOL, SH0, SPLIT = A1, 0, 4

            def body(j, hcur_):
                t0, n, isc = tiles[j]
                xt, xb, cs, csb = loads[j]
                h, hb = hcur_

                def proj(col0, bank, M=128):
                    wdep = [WB[0]] if col0 < 1024 else ([WB[1]] if col0 < 1664 else [WB[2]])
                    for kc in range(8):
                        MM(PS[0:M, bank, 0:n], Wt[:, kc, col0:col0 + M], h[:, kc, 0:n], kc == 0, kc == 7, [hb] + wdep, [PB[bank]])

                xst, xstb = xst_ring.next()
                for c in range(4):
                    bk = banks.next()
                    proj(128 * c, bk)
                    CP("act" if c % 2 else "dve", xst[:, c, 0:n], PS[:, bk, 0:n], [PB[bk]], [xstb])
                    yield
                S.dma("pool", XBv[:, :, t0:t0 + n], xst[:, :, 0:n], r=[xstb], w=[XBSB[j]])
                if not isc:
                    gst, gstb = gst_ring.next()
                    for c in range(4):
                        bk = banks.next()
                        proj(512 + 128 * c, bk)
                        ACT(gst[:, c, 0:n], PS[:, bk, 0:n], AF.Gelu_apprx_tanh, [PB[bk]], [gstb])
                        yield
                    S.dma("pool", GTv[:, :, t0:t0 + n], gst[:, :, 0:n], r=[gstb], w=[GTSB[j]])

                def lowrank_norm(col0, nch, dim, gcol, dst, dstb):
                    cqs, cqsb = cqs_ring.next()
                    for c in range(nch):
                        bk = banks.next()
                        proj(col0 + 128 * c, bk)
                        CP("act" if c % 2 else "dve", cqs[:, c, 0:n], PS[:, bk, 0:n], [PB[bk]], [cqsb])
                    sq, sqb = rings["sq"].next()
                    TTo("pool", sq[:, 0:nch, 0:n], cqs[:, 0:nch, 0:n], cqs[:, 0:nch, 0:n], ALU.mult, [cqsb], [sqb])
                    rs, rsb = stat_rstd(rings, sq, sqb, nch, n, dim, 7)
                    TTo("dve", cqs[:, 0:nch, 0:n], cqs[:, 0:nch, 0:n], rs[:, 0:n].unsqueeze(1).to_broadcast([128, nch, n]), ALU.mult, [cqsb, rsb], [cqsb])
                    for c in range(nch):
                        ACT(dst[:, c, t0:t0 + n], cqs[:, c, 0:n], AF.Identity, [cqsb, CB], [dstb], scale=gcol[:, c:c + 1])

                if not isc:
                    lowrank_norm(1024, 3, 384, cols["qn"], CQN, CQB[j])
                lowrank_norm(1408, 2, 256, cols["kvn"], CKVN, CKB[j])
                bk = banks.next()
                proj(1664, bk, M=64)
                if isc:
                    CP("act", KRb[32:64, t0:t0 + n], PS[32:64, bk, 0:n], [PB[bk]], [KRB[j]])
                else:
                    krs, krsb = krs_ring.next()
                    CP("act", krs[32:64, 0:n], PS[32:64, bk, 0:n], [PB[bk]], [krsb])
                    rbk = 5
                    MM(PS[0:32, rbk, 0:n], r32B[32:64, :], krs[32:64, 0:n], True, True, [krsb, CB], [PB[rbk]])
                    t1, t1b = t1_ring.next()
                    t2, t2b = t2_ring.next()
                    TTo("dve", t1[:, 0:n], PS[0:32, bk, 0:n], cs[:, 0, 0:n], ALU.mult, [PB[bk], csb], [t1b])
                    TTo("dve", t2[:, 0:n], PS[0:32, rbk, 0:n], cs[:, 1, 0:n], ALU.mult, [PB[rbk], csb], [t2b])
                    TTo("pool", KRb[0:32, t0:t0 + n], t1[:, 0:n], t2[:, 0:n], ALU.add, [t1b, t2b], [KRB[j]])
            def do_prenorm(jj):
                t0_, n_, isc_ = tiles[TL[jj]]
                s_ = 1 if isc_ else 0
                return prenorm(rings, loads[jj][0], loads[jj][1], n_, ACOL[:, l, s_, :], MODT[:, l, s_, SH0:SH0 + 8], 7)

            issue_load(0)
            if len(TL) > 1:
                issue_load(1)
            hcur = do_prenorm(0)
            for ji in range(len(TL)):
                gen = body(ji, hcur)
                k = 0
                done = ji + 1 >= len(TL)
                for _ in gen:
                    k += 1
                    if k == SPLIT and not done:
                        hcur = do_prenorm(ji + 1)
                        if ji + 2 < len(TL):
                            issue_load(ji + 2)
                        done = True
                if not done:
                    hcur = do_prenorm(ji + 1)
                    if ji + 2 < len(TL):
                        issue_load(ji + 2)
                loads.pop(ji)
            S.barrier()

    def phase_lru():
        with ExitStack() as es:
            GW = sb(es, "GW", [128, 2, 2, 4, 128], BF16)
            GWB = Buf()
            MSET("pool", GW[:], 0.0, [GWB])
            for d in range(2):
                for gate in range(2):
                    for nb in range(8):
                        p0 = (nb % 2) * 64
                        cast_load(GW[p0:p0 + 64, d, gate, nb // 2, p0:p0 + 64], W["lru_gate_w"][0, d, gate, nb], GWB)
            ytmp = sb(es, "ytmp", [128, 8], F32)
            yb_ = Buf()
            ACT(ytmp[:], cols["lam"][:], AF.Exp, [CB], [yb_], scale=-1.0)
            ACT(ytmp[:], ytmp[:], AF.Ln, [yb_, CB], [yb_], bias=onesF[:, 0:1])
            TS("dve", cch[:, 0, :], ytmp[:], -8.0, None, ALU.mult, None, [yb_], [CB])
            TS("dve", cch[:, 1, :], ytmp[:], -16.0, None, ALU.mult, None, [yb_], [CB])
            xb_ring = Ring(nc, es, "xbt", [128, 4, 515], F32, 2)
            xc_ring = Ring(nc, es, "xc", [128, 4, 512], F32, 2)
            xcb_ring = Ring(nc, es, "xcb", [128, 4, 512], BF16, 2)
            rg_ring = Ring(nc, es, "rg", [128, 4, 512], F32, 2)
            ig_ring = Ring(nc, es, "ig", [128, 4, 512], F32, 2)
            av_ring = Ring(nc, es, "av", [128, 4, 512], F32, 2)
            e2_ring = Ring(nc, es, "e2", [128, 4, 512], F32, 2)
            hv_rings = [Ring(nc, es, "hv%d" % d_, [128, 4, 512], F32, 2) for d_ in range(2)]
            XBv = XBS.rearrange("(c p) t -> p c t", p=128)
            GTv = GTS.rearrange("(c p) t -> p c t", p=128)
            HFv = HFS.rearrange("(c p) t -> p c t", p=128)
            C1v = C1.rearrange("(c p) t -> p c t", p=128)
            banks = Rot([0, 1, 2, 3, 4, 5])
            HBv = HBS.rearrange("(c p) t -> p c t", p=128)
            orders = [[NT] + list(range(NT)), [NT] + list(range(NT - 1, -1, -1))]
            prevs = [None, None]

            def stageA(d, j):
                t0, n, isc = tiles[j]
                seg0, seg1 = (T, TT) if isc else (0, T)
                xbt, xbb = xb_ring.next()
                rd = [XBSB[j]]
                if d == 0:
                    lo, hi = max(t0 - 3, seg0), t0 + n
                    if lo > t0 - 3:
                        MSET("pool", xbt[:, :, 0:3], 0.0, [xbb])
                    elif j > 0:
                        rd.append(XBSB[j - 1])
                    S.dma("sp", xbt[:, :, lo - (t0 - 3):n + 3], XBv[:, :, lo:hi], r=rd, w=[xbb])
                else:
                    lo, hi = t0, min(t0 + n + 3, seg1)
                    if hi < t0 + n + 3:
                        MSET("pool", xbt[:, :, n:n + 3], 0.0, [xbb])
                    elif j + 1 < NT:
                        rd.append(XBSB[j + 1])
                    S.dma("sp", xbt[:, :, 0:hi - lo], XBv[:, :, lo:hi], r=rd, w=[xbb])
                xc, xcb_ = xc_ring.next()
                for c in range(4):
                    wc = lambda k: cols["lcw"][:, d * 16 + k * 4 + c:d * 16 + k * 4 + c + 1]
                    TS("dve", xc[:, c, 0:n], xbt[:, c, 0:n], wc(0), cols["lcb"][:, d * 4 + c:d * 4 + c + 1], ALU.mult, ALU.add, [xbb, CB], [xcb_])
                    for k in range(1, 4):
                        STT(xc[:, c, 0:n], xbt[:, c, k:k + n], wc(k), xc[:, c, 0:n], ALU.mult, ALU.add, [xbb, xcb_, CB], [xcb_])
                xcb, xcbb = xcb_ring.next()
                CP("act", xcb[:, :, 0:n], xc[:, :, 0:n], [xcb_], [xcbb])
                rg, rgb = rg_ring.next()
                ig, igb = ig_ring.next()
                av, avb = av_ring.next()
                e2, e2b = e2_ring.next()
                for c in range(4):
                    b0 = banks.next()
                    MM(PS[:, b0, 0:n], GW[:, d, 0, c, :], xcb[:, c, 0:n], True, True, [xcbb, GWB], [PB[b0]])
                    ACT(rg[:, c, 0:n], PS[:, b0, 0:n], AF.Sigmoid, [PB[b0], CB], [rgb], bias=cols["lgb"][:, d * 8 + c:d * 8 + c + 1])
                    b1 = banks.next()
                    MM(PS[:, b1, 0:n], GW[:, d, 1, c, :], xcb[:, c, 0:n], True, True, [xcbb, GWB], [PB[b1]])
                    ACT(ig[:, c, 0:n], PS[:, b1, 0:n], AF.Sigmoid, [PB[b1], CB], [igb], bias=cols["lgb"][:, d * 8 + 4 + c:d * 8 + 4 + c + 1])
                for c in range(4):
                    ACT(av[:, c, 0:n], rg[:, c, 0:n], AF.Exp, [rgb, CB], [avb], scale=cch[:, 0, d * 4 + c:d * 4 + c + 1])
                    ACT(e2[:, c, 0:n], rg[:, c, 0:n], AF.Exp, [rgb, CB], [e2b], scale=cch[:, 1, d * 4 + c:d * 4 + c + 1])
                ACT(e2[:, :, 0:n], e2[:, :, 0:n], AF.Sqrt, [e2b, CB], [e2b], bias=onesF[:, 0:1], scale=-1.0)
                return (d, j, xc, xcb_, ig, igb, av, avb, e2, e2b)

            def stageB(ctx_):
                d, j, xc, xcb_, ig, igb, av, avb, e2, e2b = ctx_
                t0, n, isc = tiles[j]
                prev = prevs[d]
                TTo("dve", e2[:, :, 0:n], e2[:, :, 0:n], ig[:, :, 0:n], ALU.mult, [e2b, igb], [e2b])
                TTo("dve", e2[:, :, 0:n], e2[:, :, 0:n], xc[:, :, 0:n], ALU.mult, [e2b, xcb_], [e2b])
                hv, hvb = hv_rings[d].next()
                for c in range(4):
                    if prev is None:
                        init, rdp = 0.0, []
                    else:
                        ph, phb, pn = prev
                        init = ph[:, c, pn - 1:pn] if d == 0 else ph[:, c, 0:1]
                        rdp = [phb]
                    if d == 0:
                        o_, a_, b_ = hv[:, c, 0:n], av[:, c, 0:n], e2[:, c, 0:n]
                    else:
                        o_, a_, b_ = hv[:, c, 0:n][:, ::-1], av[:, c, 0:n][:, ::-1], e2[:, c, 0:n][:, ::-1]
                    S.op("dve", lambda e, o_=o_, a_=a_, b_=b_, init=init: e.tensor_tensor_scan(o_, a_, b_, init, ALU.mult, ALU.add),
                         [avb, e2b] + rdp, [hvb])
                prevs[d] = (hv, hvb, n)
                if not isc:
                    if d == 0:
                        S.dma("pool", HFv[:, :, t0:t0 + n], hv[:, :, 0:n], r=[hvb], w=[HFSB[j]])
                    else:
                        S.dma("pool", HBv[:, :, t0:t0 + n], hv[:, :, 0:n], r=[hvb], w=[HBSB[j]])

            items = [(d, orders[d][step]) for step in range(NT + 1) for d in range(2)]
            pend_ctx = None
            for (d, j) in items:
                ctx_ = stageA(d, j)
                if pend_ctx is not None:
                    stageB(pend_ctx)
                pend_ctx = ctx_
            stageB(pend_ctx)
            S.barrier()

    def phase_lru_combine():
        with ExitStack() as es:
            hf_ring = Ring(nc, es, "hf", [128, 4, 512], F32, 2)
            hb_ring = Ring(nc, es, "hb", [128, 4, 512], F32, 2)
            gg_ring = Ring(nc, es, "gg", [128, 4, 512], F32, 2)
            cl_ring = Ring(nc, es, "cl", [128, 4, 512], BF16, 2)
            GTv = GTS.rearrange("(c p) t -> p c t", p=128)
            HFv = HFS.rearrange("(c p) t -> p c t", p=128)
            HBv = HBS.rearrange("(c p) t -> p c t", p=128)
            C1v = C1.rearrange("(c p) t -> p c t", p=128)
            for j, (t0, n, isc) in enumerate(lat_tiles):
                hf, hfb = hf_ring.next()
                S.dma("sp", hf[:, :, 0:n], HFv[:, :, t0:t0 + n], r=[HFSB[j]], w=[hfb])
                hb, hbb = hb_ring.next()
                S.dma("sp", hb[:, :, 0:n], HBv[:, :, t0:t0 + n], r=[HBSB[j]], w=[hbb])
                gg, ggb = gg_ring.next()
                S.dma("sp", gg[:, :, 0:n], GTv[:, :, t0:t0 + n], r=[GTSB[j]], w=[ggb])
                cl, clb = cl_ring.next()
                TTo("dve", hf[:, :, 0:n], hf[:, :, 0:n], hb[:, :, 0:n], ALU.add, [hfb, hbb], [hfb])
                TTo("pool", cl[:, :, 0:n], hf[:, :, 0:n], gg[:, :, 0:n], ALU.mult, [hfb, ggb], [clb])
                S.dma("pool", C1v[:, :, t0:t0 + n], cl[:, :, 0:n], r=[clb], w=[C1B[j]])
            S.barrier()

    def phase_mla():
        CQN, CKVN, KRb = L1T["CQN"], L1T["CKVN"], L1T["KRb"]
        with ExitStack() as es:
            Wq = sb(es, "Wq", [128, 3, 8, 128], BF16)
            Wk = sb(es, "Wk", [128, 2, 8, 64], BF16)
            Wv = sb(es, "Wv", [128, 2, 8, 64], BF16)
            WB = Buf()
            qsrc = W["mla_w_uq"][0].rearrange("(kc p) (h e) -> p kc h e", p=128, e=96)
            ksrc = W["mla_w_ukv"][0].rearrange("(kc p) (h e) -> p kc h e", p=128, e=128)
            for kc in range(3):
                cast_load(Wq[:, kc, :, 64:128], qsrc[:, kc, :, 0:64], WB)
                cast_load(Wq[:, kc, :, 0:32], qsrc[:, kc, :, 64:96], WB)
                cast_load(Wq[:, kc, :, 32:64], qsrc[:, kc, :, 64:96], WB)
            for kc in range(2):
                cast_load(Wk[:, kc, :, :], ksrc[:, kc, :, 0:64], WB)
                cast_load(Wv[:, kc, :, :], ksrc[:, kc, :, 64:128], WB)
            Va = sb(es, "Va", [128, NB, 8, 66], BF16)
            VaB = Buf()
            MSET("pool", Va[:, :, :, 64:66], 1.0, [VaB])
            rings = attn_rings(es)
            k_ring = Ring(nc, es, "Kh", [128, TT], BF16, 2)
            q_ring = Ring(nc, es, "Qh", [128, T], BF16, 2)
            pt_ring = Ring(nc, es, "ptm", [128, 2, 512], BF16, 3)
            cs_ring = Ring(nc, es, "csq", [32, 2, 512], F32, 2)
            t1_ring = Ring(nc, es, "t1q", [32, 512], F32, 2)
            t2_ring = Ring(nc, es, "t2q", [32, 512], F32, 2)
            mbanks = Rot([6, 7])
            sbanks = Rot([0, 2])
            abanks = Rot([4, 5])
            for blk in range(NB):
                bk = mbanks.next()
                for kc in range(2):
                    MM(PS[:, bk, 0:512], CKVN[:, kc, blk * 128:(blk + 1) * 128], Wv[:, kc, :, :].rearrange("p h d -> p (h d)"),
                       kc == 0, kc == 1, [CKB[min(blk // 4, NT)], WB], [PB[bk]])
                CP("act" if blk % 2 else "dve", Va[:, blk, :, 0:64], PS[:, bk, 0:512].rearrange("p (h d) -> p h d", d=64), [PB[bk]], [VaB])
            sc = float(96 ** -0.5)
            pend = []
            hbufs = {}

            def get_bufs(h_):
                if h_ not in hbufs:
                    Kh_, KhB_ = k_ring.next()
                    Qh_, QhB_ = q_ring.next()
                    CP("pool", Kh_[0:64, :], KRb[0:64, :], KRB, [KhB_])
                    hbufs[h_] = (Kh_, KhB_, Qh_, QhB_)
                return hbufs[h_]

            def prod_k(h_, j):
                Kh_, KhB_, Qh_, QhB_ = get_bufs(h_)
                t0, n, isc = tiles[j]
                bk = mbanks.next()
                for kc in range(2):
                    MM(PS[64:128, bk, 0:n], Wk[:, kc, h_, :], CKVN[:, kc, t0:t0 + n], kc == 0, kc == 1, [CKB[j], WB], [PB[bk]])
                CP("act", Kh_[64:128, t0:t0 + n], PS[64:128, bk, 0:n], [PB[bk]], [KhB_])

            def prod_q(h_, j):
                Kh_, KhB_, Qh_, QhB_ = get_bufs(h_)
                t0, n, isc = tiles[j]
                cs, csb = cs_ring.next()
                S.dma("sp", cs[:, 0, :], cmv[:, t0:t0 + n], w=[csb])
                S.dma("sp", cs[:, 1, :], smv[:, t0:t0 + n], w=[csb])
                bk = mbanks.next()
                for kc in range(3):
                    MM(PS[:, bk, 0:n], Wq[:, kc, h_, :], CQN[:, kc, t0:t0 + n], kc == 0, kc == 2, [CQB[j], WB], [PB[bk]])
                CP("act", Qh_[32:64, t0:t0 + n], PS[32:64, bk, 0:n], [PB[bk]], [QhB_])
                CP("dve", Qh_[64:128, t0:t0 + n], PS[64:128, bk, 0:n], [PB[bk]], [QhB_])
                rbk = mbanks.next()
                MM(PS[0:32, rbk, 0:n], r32B[32:64, :], Qh_[32:64, t0:t0 + n], True, True, [QhB_, CB], [PB[rbk]])
                t1, t1b = t1_ring.next()
                t2, t2b = t2_ring.next()
                TTo("dve", t1[:, 0:n], PS[0:32, bk, 0:n], cs[:, 0, 0:n], ALU.mult, [PB[bk], csb], [t1b])
                TTo("dve", t2[:, 0:n], PS[0:32, rbk, 0:n], cs[:, 1, 0:n], ALU.mult, [PB[rbk], csb], [t2b])
                TTo("pool", Qh_[0:32, t0:t0 + n], t1[:, 0:n], t2[:, 0:n], ALU.add, [t1b, t2b], [QhB_])

            for j in range(len(tiles)):
                prod_k(0, j)
            for j in range(NT):
                prod_q(0, j)
            for hh in range(8):
                Kh, KhB, Qh, QhB = get_bufs(hh)
                for j, (t0, n, isc) in enumerate(lat_tiles):
                    acc = abanks.next()
                    ngrp = (NB + 1) // 2
                    for gi in range(ngrp):
                        kbs = [kb for kb in (2 * gi, 2 * gi + 1) if kb < NB]
                        sb0 = sbanks.next()
                        for qi, kb in enumerate(kbs):
                            MM(PS[:, sb0 + qi, 0:n], Kh[:, kb * 128:(kb + 1) * 128], Qh[:, t0:t0 + n], True, True, [KhB, QhB], [PB[sb0 + qi]])
                        pt, ptb = pt_ring.next()
                        m = len(kbs)
                        ACT(pt[:, 0:m, 0:n], PS[:, sb0:sb0 + m, 0:n], AF.Exp, [PB[sb0 + q_] for q_ in range(m)], [ptb], scale=sc)
                        if pend:
                            pend.pop()()

                        def later(kbs=kbs, acc=acc, n=n, pt=pt, ptb=ptb, gi=gi, hh=hh, t0=t0, j=j, last=(gi == ngrp - 1)):
                            for qi, kb in enumerate(kbs):
                                MM(PS[0:65, acc, 0:n], Va[:, kb, hh, 0:65], pt[:, qi, 0:n], gi == 0 and qi == 0, kb == NB - 1, [ptb, VaB], [PB[acc]])
                            if last:
                                attn_finalize(rings, acc, n, None, D1[hh * 64:(hh + 1) * 64, t0:t0 + n], D1B[j], mbanks.next())
                        pend.append(later)
                    if hh + 1 < 8:
                        prod_k(hh + 1, j)
                        prod_q(hh + 1, j)
                        if j == NT - 1:
                            prod_k(hh + 1, NT)
            if pend:
                pend.pop()()
            S.barrier()

    def layer1():
        alloc_l1()
        phase_p1_l1()
        if stop == "p1_l1":
            return
        phase_lru()
        phase_lru_combine()
        if stop == "lru":
            return
        phase_mla()
        if stop == "mla":
            return
        L1.close()
        phase_wout(1, W["cd_w_out"][0], C1, C1B, D1, D1B, lat_t)
        if stop == "wout1":
            return
        phase_ffna(1, lat_t)
        phase_ffnb(1, lat_t)

    all_t = list(range(len(tiles)))
    lat_t = list(range(NT))

    def dump_small(parts):
        off = 0
        for t, w in parts:
            S.dma("sp", DBGV[:, off:off + w], t, r=[CB], w=[Buf()])
            off += w
        S.barrier()

    def run():
        phase_mod(0)
        if stop == "mod0":
            dump_small([(MODT[:, 0, :, :].rearrange("p s m -> p (s m)"), 96), (A1[:, 0].rearrange("p s m -> p (s m)"), 16),
                        (G1[:, 0].rearrange("p s m -> p (s m)"), 16), (A2[:, 0].rearrange("p s m -> p (s m)"), 16),
                        (G2[:, 0].rearrange("p s m -> p (s m)"), 16)])
            return
        phase_tin()
        if stop == "tin":
            return
        phase_p1_l0()
        if stop == "p1_l0":
            return
        phase_conva()
        if stop == "conva":
            return
        phase_attn0()
        if stop == "attn0":
            return
        L0.close()
        phase_wout(0, W["ab_w_out"][0], A0, A0B, B0, B0B, all_t)
        if stop == "wout0":
            return
        phase_ffna(0, all_t)
        if stop == "ffna0":
            return
        phase_ffnb(0, all_t)
        if stop == "ffnb0":
            return
        layer1()
        phase_tout()

    run()
    L1.close()
    L0.close()
    ges.close()
    build.stats = (S.nops, S.nwaits)
    return nc


def make_in_maps(inputs, T):
    consts = make_consts(T)
    f = lambda a: np.ascontiguousarray(np.asarray(a, dtype=np.float32))
    shared = {k: f(inputs[k]) for k in WEIGHT_SHAPES}
    shared.update(consts)
    shared["c_ctx"] = f(inputs["c_ctx"]).reshape(8, 128)
    x, c, ctx = f(inputs["x"]), f(inputs["c"]), f(inputs["ctx"])
    maps = []
    for b in range(x.shape[0]):
        m = dict(shared)
        m["x"] = np.ascontiguousarray(x[b])
        m["ctx"] = np.ascontiguousarray(ctx[b])
        m["c"] = np.ascontiguousarray(c[b]).reshape(8, 128)
        maps.append(m)
    return maps


def kernel(**inputs):
    T = int(np.asarray(inputs["x"]).shape[1])
    nc = build(T)
    in_maps = make_in_maps(inputs, T)
    res = run_bass_kernel_spmd(nc, in_maps, core_ids=list(range(len(in_maps))))
    return np.stack([np.asarray(r["y"], dtype=np.float32) for r in res.results], axis=0)
```

```python
import numpy as np
import ml_dtypes
from contextlib import ExitStack
import concourse.bass as bass
import concourse.mybir as mybir
from concourse.bass_utils import run_bass_kernel_spmd

F32 = mybir.dt.float32
BF16 = mybir.dt.bfloat16
AF = mybir.ActivationFunctionType
ALU = mybir.AluOpType

D = 1024
CTX = 256
EPS = 1e-6
FFN = 2816
NJ = FFN // 128


class Buf:
    __slots__ = ("w", "r")

    def __init__(self):
        self.w = None
        self.r = {}


class Op:
    __slots__ = ("eng", "chan", "seq", "fn", "waits", "signal", "clock", "val", "isdma")


class Sched:
    COMPUTE = ("pe", "act", "dve", "pool")

    def __init__(self, nc, es):
        self.nc = nc
        self.eobj = dict(pe=nc.tensor, act=nc.scalar, dve=nc.vector, pool=nc.gpsimd, sp=nc.sync)
        self.sem = {}
        for e in self.COMPUTE:
            self.sem[e] = es.enter_context(nc.semaphore("sem_" + e))
        self.nslot = {"sp": 12, "pool": 8}
        for q, n in self.nslot.items():
            for k in range(n):
                self.sem[(q, k)] = es.enter_context(nc.semaphore("dq_%s%d" % (q, k)))
        self.clock = {e: {} for e in self.eobj}
        self.seq = {e: 0 for e in self.COMPUTE}
        self.sigcount = {e: 0 for e in self.COMPUTE}
        self.dcount = {q: 0 for q in self.nslot}
        self.slot_last = {}
        self.last = {}
        self.pending = []
        self.bar = None
        self.nops = 0
        self.nwaits = 0
        self.dummy = es.enter_context(nc.sbuf_tensor("sched_dummy", [128, 8], F32))

    def _add(self, eng, chan, seq, fn, r, w, isdma, extra=()):
        op = Op()
        op.eng, op.chan, op.seq, op.fn, op.isdma = eng, chan, seq, fn, isdma
        op.signal = isdma
        op.val = 16 * (seq + 1) if isdma else None
        deps = {}

        def need(d):
            if d is None:
                return
            cur = deps.get(d.chan)
            if cur is None or cur.seq < d.seq:
                deps[d.chan] = d

        r = list(r)
        if self.bar is not None:
            r.append(self.bar)
        for b in r:
            need(b.w)
        for b in w:
            need(b.w)
            for d in b.r.values():
                need(d)
        for d in extra:
            need(d)
        clk = self.clock[eng]
        waits = []
        for d in deps.values():
            if d.chan == "pe" and eng == "pe":
                continue
            if clk.get(d.chan, -1) >= d.seq:
                continue
            waits.append(d)
            d.signal = True
            for k, v in d.clock.items():
                if clk.get(k, -1) < v:
                    clk[k] = v
            if clk.get(d.chan, -1) < d.seq:
                clk[d.chan] = d.seq
        op.waits = waits
        op.clock = dict(clk)
        for b in r:
            cur = b.r.get(chan)
            if cur is None or cur.seq < seq:
                b.r[chan] = op
        for b in w:
            b.w = op
            b.r = {}
        self.last[chan] = op
        self.pending.append(op)
        self.nops += 1
        self.nwaits += len(waits)
        return op

    def op(self, eng, fn, r=(), w=()):
        s = self.seq[eng]
        self.seq[eng] = s + 1
        return self._add(eng, eng, s, fn, r, w, False)

    def dma(self, q, out, in_, r=(), w=(), **kw):
        i = self.dcount[q]
        self.dcount[q] = i + 1
        n = self.nslot[q]
        slot, gen = i % n, i // n
        chan = (q, slot)
        prev = self.slot_last.get(chan)
        extra = [prev] if prev is not None else []
        op = self._add(q, chan, gen, lambda e: e.dma_start(out=out, in_=in_, **kw), r, w, True, extra)
        self.slot_last[chan] = op
        return op

    def barrier(self):
        b = Buf()
        extra = list(self.last.values())
        dummy = self.dummy
        s = self.seq["pool"]
        self.seq["pool"] = s + 1
        m = self._add("pool", "pool", s, lambda e: e.memset(dummy[:], 0.0), [], [b], False, extra)
        m.signal = True
        self.bar = b
        self.flush()

    def flush(self):
        for op in self.pending:
            e = self.eobj[op.eng]
            for d in op.waits:
                e.wait_ge(self.sem[d.chan], d.val)
            if op.signal and not op.isdma:
                self.sigcount[op.chan] += 1
                op.val = self.sigcount[op.chan]
            ins = op.fn(e)
            if op.signal:
                ins.then_inc(self.sem[op.chan], 16 if op.isdma else 1)
            op.fn = None
        self.pending = []


class Ring:
    uid = 0

    def __init__(self, nc, es, name, shape, dtype, n):
        Ring.uid += 1
        self.t = [es.enter_context(nc.sbuf_tensor("%s_%d_%d" % (name, Ring.uid, i), shape, dtype)) for i in range(n)]
        self.b = [Buf() for _ in range(n)]
        self.i = 0

    def next(self):
        k = self.i % len(self.t)
        self.i += 1
        return self.t[k], self.b[k]


class Rot:
    def __init__(self, items):
        self.items = list(items)
        self.i = 0

    def next(self):
        v = self.items[self.i % len(self.items)]
        self.i += 1
        return v


def _rope_tables(T, dim):
    rows = T // 64
    row = np.repeat(np.arange(rows), 64).astype(np.float32)
    col = np.tile(np.arange(64), rows).astype(np.float32)
    nf = dim // 4
    inv = (np.float32(10000.0) ** (-np.arange(nf, dtype=np.float32) / np.float32(nf))).astype(np.float32)
    ar = row[:, None] * inv
    ac = col[:, None] * inv
    ang = np.concatenate([ar, ar, ac, ac], axis=-1)
    return np.ascontiguousarray(np.cos(ang).T.astype(np.float32)), np.ascontiguousarray(np.sin(ang).T.astype(np.float32))


def _rot_T(dim):
    q = dim // 4
    R = np.zeros((dim, dim), np.float32)
    for i in range(q):
        R[i, q + i] = -1.0
        R[q + i, i] = 1.0
        R[2 * q + i, 3 * q + i] = -1.0
        R[3 * q + i, 2 * q + i] = 1.0
    return np.ascontiguousarray(R.T)


def make_consts(T):
    cw, sw = _rope_tables(T, 64)
    cm, sm = _rope_tables(T, 32)
    ident = np.eye(128, dtype=np.float32)
    sel = np.zeros((128, 64), np.float32)
    sel[64, :] = 1.0
    b = np.arange(128)[:, None]
    a = np.arange(128)[None, :]
    m1 = (b <= a).astype(np.float32)
    m2 = (a <= b).astype(np.float32)
    r64 = _rot_T(64)
    r32 = _rot_T(32)
    return {
        "k_ident": ident, "k_sel": sel, "k_m1": m1, "k_m2": m2,
        "k_r64": np.concatenate([r64, r64], 0), "k_r32": np.concatenate([r32, r32], 0),
        "k_cw": cw, "k_sw": sw, "k_cm": cm, "k_sm": sm,
    }


WEIGHT_SHAPES = {
    "w_mod": [2, 1024, 6144], "b_mod": [2, 6144], "norm_g": [2, 4, 1024], "ffn_w_up": [2, 1024, 5632],
    "ffn_conv_w": [2, 3, 2816], "ffn_conv_b": [2, 2816], "ffn_w_down": [2, 2816, 1024],
    "ab_w_in": [1, 1024, 1792], "a_conv_w": [1, 31, 512], "a_conv_b": [1, 512], "a_ln_g": [1, 512],
    "a_ln_b": [1, 512], "b_sink": [1, 8], "ab_w_out": [1, 1024, 1024], "cd_w_in": [1, 1024, 1696],
    "lru_conv_w": [1, 2, 4, 512], "lru_conv_b": [1, 2, 512], "lru_gate_w": [1, 2, 2, 8, 64, 64],
    "lru_gate_b": [1, 2, 2, 512], "lru_lambda": [1, 2, 512], "mla_q_norm": [1, 384],
    "mla_w_uq": [1, 384, 768], "mla_kv_norm": [1, 256], "mla_w_ukv": [1, 256, 1024],
    "cd_w_out": [1, 1024, 1024],
}


def build(T=4096, dbg=(), stop=None):
    nc = bass.Bass("TRN2", target_bir_lowering=False)
    TT = T + CTX
    NT = T // 512
    NBL = T // 128
    NB = NBL + CTX // 128
    tiles = [(i * 512, 512, False) for i in range(NT)] + [(T, CTX, True)]
    lat_tiles = tiles[:NT]

    def din(name, shape):
        return nc.dram_tensor(name, list(shape), F32, kind="ExternalInput").ap()

    x_in = din("x", [T, D])
    ctx_in = din("ctx", [CTX, D])
    c_in = din("c", [8, 128])
    cctx_in = din("c_ctx", [8, 128])
    W = {k: din(k, s) for k, s in WEIGHT_SHAPES.items()}
    KC = {k: din(k, v.shape) for k, v in make_consts(T).items()}
    y_out = nc.dram_tensor("y", [T, D], F32, kind="ExternalOutput").ap()

    def scratch(name, shape, dt):
        kind = "ExternalOutput" if name in dbg else "Internal"
        return nc.dram_tensor(name, list(shape), dt, kind=kind).ap()

    XT = scratch("XT", [D, TT], F32)
    U0 = scratch("U0", [512, TT], BF16)
    A0 = scratch("A0", [512, TT], BF16)
    B0 = scratch("B0", [512, TT], BF16)
    GS = scratch("GS", [FFN, TT], BF16)
    US = scratch("US", [FFN, TT], BF16)
    XBS = scratch("XBS", [512, TT], F32)
    GTS = scratch("GTS", [512, T], F32)
    HFS = scratch("HFS", [512, T], F32)
    HBS = scratch("HBS", [512, T], F32)
    C1 = scratch("C1", [512, T], BF16)
    D1 = scratch("D1", [512, T], BF16)
    DBGV = scratch("DBGV", [128, 512], F32)
    QS = scratch("QS", [8, 128, TT], BF16)
    XTv = XT.rearrange("(c p) t -> p c t", p=128)
    XTB = [Buf() for _ in tiles]
    U0B = [Buf() for _ in tiles]
    A0B = [Buf() for _ in tiles]
    B0B = [Buf() for _ in tiles]
    GSB = [Buf() for _ in tiles]
    USB = [Buf() for _ in tiles]
    XBSB = [Buf() for _ in tiles]
    GTSB = [Buf() for _ in tiles]
    HFSB = [Buf() for _ in tiles]
    HBSB = [Buf() for _ in tiles]
    C1B = [Buf() for _ in tiles]
    D1B = [Buf() for _ in tiles]

    ges = ExitStack()
    S = Sched(nc, ges)
    PS = ges.enter_context(nc.psum_tensor("PS", [128, 8, 512], F32))
    PB = [Buf() for _ in range(8)]

    def sb(es, name, shape, dt):
        Ring.uid += 1
        return es.enter_context(nc.sbuf_tensor("%s_%d" % (name, Ring.uid), list(shape), dt))

    def MM(out, lhsT, rhs, st, sp, r, w):
        S.op("pe", lambda e: e.matmul(out, lhsT, rhs, start=st, stop=sp), r, w)

    def TR(out, in_, ident, r, w):
        S.op("pe", lambda e: e.transpose(out, in_, ident), r, w)

    def ACT(out, in_, func, r, w, bias=None, scale=None):
        kw = {}
        if bias is not None:
            kw["bias"] = bias
        if scale is not None:
            kw["scale"] = scale
        S.op("act", lambda e: e.activation(out, in_, func, **kw), r, w)

    def CP(eng, out, in_, r, w):
        if eng == "act":
            S.op("act", lambda e: e.copy(out, in_), r, w)
        else:
            S.op(eng, lambda e: e.tensor_copy(out, in_), r, w)

    def TTo(eng, out, a, b, op, r, w):
        S.op(eng, lambda e: e.tensor_tensor(out, a, b, op), r, w)

    def TS(eng, out, a, s1, s2, op0, op1, r, w):
        if s2 is None:
            S.op(eng, lambda e: e.tensor_scalar(out, a, s1, None, op0), r, w)
        else:
            S.op(eng, lambda e: e.tensor_scalar(out, a, s1, s2, op0, op1), r, w)

    def STT(out, in0, scalar, in1, op0, op1, r, w):
        S.op("dve", lambda e: e.scalar_tensor_tensor(out, in0, scalar, in1, op0, op1), r, w)

    def RCP(out, in_, r, w):
        S.op("dve", lambda e: e.reciprocal(out, in_), r, w)

    def MSET(eng, ap, val, w):
        S.op(eng, lambda e: e.memset(ap, val), [], w)

    identF = sb(ges, "identF", [128, 128], F32)
    identB = sb(ges, "identB", [128, 128], BF16)
    onesB = sb(ges, "onesB", [128, 128], BF16)
    onesF = sb(ges, "onesF", [128, 128], F32)
    selF = sb(ges, "selF", [128, 64], F32)
    m1B = sb(ges, "m1B", [128, 128], BF16)
    m2B = sb(ges, "m2B", [128, 128], BF16)
    r64B = sb(ges, "r64B", [128, 64], BF16)
    r32B = sb(ges, "r32B", [64, 32], BF16)
    CB = Buf()
    S.dma("sp", identF[:], KC["k_ident"], w=[CB])
    S.dma("sp", selF[:], KC["k_sel"], w=[CB])
    S.dma("pool", m1B[:], KC["k_m1"], w=[CB])
    S.dma("pool", m2B[:], KC["k_m2"], w=[CB])
    S.dma("pool", r64B[:], KC["k_r64"], w=[CB])
    S.dma("pool", r32B[:], KC["k_r32"], w=[CB])
    CP("dve", identB[:], identF[:], [CB], [CB])
    MSET("dve", onesB[:], 1.0, [CB])
    MSET("dve", onesF[:], 1.0, [CB])

    cols = {}
    colspec = {
        "g": (W["norm_g"], 64), "bm": (W["b_mod"], 96), "fcb": (W["ffn_conv_b"], 44),
        "fcw0": (W["ffn_conv_w"][0], 66), "fcw1": (W["ffn_conv_w"][1], 66),
        "acb": (W["a_conv_b"], 4), "alg": (W["a_ln_g"], 4), "alb": (W["a_ln_b"], 4),
        "acw": (W["a_conv_w"], 124), "lcw": (W["lru_conv_w"], 32), "lcb": (W["lru_conv_b"], 8),
        "lgb": (W["lru_gate_b"], 16), "lam": (W["lru_lambda"], 8), "qn": (W["mla_q_norm"], 3),
        "kvn": (W["mla_kv_norm"], 2), "c": (c_in, 8), "cc": (cctx_in, 8),
    }
    for name, (src, n) in colspec.items():
        cols[name] = sb(ges, "col_" + name, [128, n], F32)
    esink = sb(ges, "esink", [64, 8], F32)
    scT = sb(ges, "scT", [128, 8, 2], F32)
    MODT = sb(ges, "MODT", [128, 2, 2, 48], F32)
    A1 = sb(ges, "A1", [128, 2, 2, 8], F32)
    G1 = sb(ges, "G1", [128, 2, 2, 8], F32)
    A2 = sb(ges, "A2", [128, 2, 2, 8], F32)
    G2 = sb(ges, "G2", [128, 2, 2, 8], F32)
    epsT = sb(ges, "epsT", [128, 1], F32)
    cch = sb(ges, "cch", [128, 2, 8], F32)
    pre = ExitStack()
    rows_ring = Ring(nc, pre, "rows", [128, 128], F32, 2)
    for i, (name, (src, n)) in enumerate(colspec.items()):
        dst = cols[name]
        nd = len(src.shape)
        if nd == 1:
            s2 = src.rearrange("(r p) -> r p", p=128)
        elif nd == 2 and src.shape[1] == 128:
            s2 = src
        else:
            names = " ".join("a%d" % k for k in range(nd - 1))
            s2 = src.rearrange("%s (r p) -> (%s r) p" % (names, names), p=128)
        rt, rb = rows_ring.next()
        S.dma("sp", rt[0:n, :], s2, w=[rb])
        bank = 6 + (i % 2)
        TR(PS[:, bank, 0:n], rt[0:n, :], identF[0:n, 0:n], [rb, CB], [PB[bank]])
        CP("dve", dst[:], PS[:, bank, 0:n], [PB[bank]], [CB])
    sk = sb(pre, "sk", [1, 8], F32)
    skb = Buf()
    S.dma("sp", sk[:], W["b_sink"], w=[skb])
    MM(PS[0:64, 5, 0:8], onesF[0:1, 0:64], sk[0:1, :], True, True, [skb, CB], [PB[5]])
    ACT(esink[:], PS[0:64, 5, 0:8], AF.Exp, [PB[5]], [CB])
    ACT(scT[:, :, 0], cols["c"][:], AF.Silu, [CB], [CB])
    ACT(scT[:, :, 1], cols["cc"][:], AF.Silu, [CB], [CB])
    S.barrier()
    pre.close()

    gc = cols["g"]

    def mod_load(l, jb, wring):
        wsrc = W["w_mod"][l].rearrange("(kc p) n -> p kc n", p=128)
        wt, wb = wring.next()
        S.dma("sp", wt[:], wsrc[:, :, jb * 768:(jb + 1) * 768], w=[wb])
        return wt, wb

    def mod_mm(l, jb, wt, wb):
        for jj in range(6):
            j = jb * 6 + jj
            for kc in range(8):
                MM(PS[:, 6, 2 * j:2 * j + 2], wt[:, kc, jj * 128:(jj + 1) * 128], scT[:, kc, :],
                   kc == 0, kc == 7, [wb, CB], [PB[6]])

    def mod_block(l, jb, wring):
        wt, wb = mod_load(l, jb, wring)
        mod_mm(l, jb, wt, wb)

    def mod_finish(l):
        pv = PS[:, 6, 0:96].rearrange("p (j s) -> p j s", s=2)
        for s in range(2):
            TTo("dve", MODT[:, l, s, :], pv[:, :, s], cols["bm"][:, l * 48:(l + 1) * 48], ALU.add, [PB[6], CB], [CB])
            STT(A1[:, l, s, :], MODT[:, l, s, 8:16], 1.0, gc[:, l * 32:l * 32 + 8], ALU.add, ALU.mult, [CB], [CB])
            TTo("dve", G1[:, l, s, :], MODT[:, l, s, 16:24], gc[:, l * 32 + 8:l * 32 + 16], ALU.mult, [CB], [CB])
            STT(A2[:, l, s, :], MODT[:, l, s, 32:40], 1.0, gc[:, l * 32 + 16:l * 32 + 24], ALU.add, ALU.mult, [CB], [CB])
            TTo("dve", G2[:, l, s, :], MODT[:, l, s, 40:48], gc[:, l * 32 + 24:l * 32 + 32], ALU.mult, [CB], [CB])

    def phase_mod(l):
        with ExitStack() as es:
            wring = Ring(nc, es, "wmod", [128, 8, 768], F32, 2)
            for jb in range(8):
                mod_block(l, jb, wring)
            mod_finish(l)
            S.barrier()

    def phase_tin():
        with ExitStack() as es:
            xin_ring = Ring(nc, es, "xin", [128, D], F32, 3)
            xt_ring = Ring(nc, es, "xtt", [128, 8, 512], F32, 2)
            for j, (t0, n, isc) in enumerate(tiles):
                src = ctx_in if isc else x_in
                s0 = 0 if isc else t0
                for b in range(n // 128):
                    xin, xb_ = xin_ring.next()
                    S.dma("sp", xin[:], src[s0 + b * 128:s0 + (b + 1) * 128, :], w=[xb_])
                    for fc in range(8):
                        TR(PS[:, fc, b * 128:(b + 1) * 128], xin[:, fc * 128:(fc + 1) * 128], identF[:], [xb_, CB], [PB[fc]])
                xt, xtb = xt_ring.next()
                for fc in range(8):
                    CP("act" if fc % 2 else "dve", xt[:, fc, 0:n], PS[:, fc, 0:n], [PB[fc]], [xtb])
                S.dma("pool", XTv[:, :, t0:t0 + n], xt[:, :, 0:n], r=[xtb], w=[XTB[j]])
            S.barrier()

    def stat_rstd(es_rings, src, srcb, nch, n, dim, bank):
        for c in range(nch):
            MM(PS[:, bank, 0:n], onesB[:], src[:, c, 0:n], c == 0, c == nch - 1, [srcb, CB], [PB[bank]])
        rs, rsb = es_rings["rs"].next()
        ACT(rs[:, 0:n], PS[:, bank, 0:n], AF.Sqrt, [PB[bank]], [rsb], bias=epsT[:, 0:1], scale=1.0 / dim)
        RCP(rs[:, 0:n], rs[:, 0:n], [rsb], [rsb])
        return rs, rsb

    MSET("dve", epsT[:], EPS, [CB])

    def prenorm(rings, xt, xb, n, Acol, SHcol, bank):
        sq, sqb = rings["sq"].next()
        ACT(sq[:, :, 0:n], xt[:, :, 0:n], AF.Square, [xb], [sqb])
        rs, rsb = stat_rstd(rings, sq, sqb, 8, n, D, bank)
        TTo("dve", xt[:, :, 0:n], xt[:, :, 0:n], rs[:, 0:n].unsqueeze(1).to_broadcast([128, 8, n]), ALU.mult, [xb, rsb], [xb])
        h, hb = rings["h"].next()
        for c in range(8):
            ACT(h[:, c, 0:n], xt[:, c, 0:n], AF.Identity, [xb, CB], [hb], bias=SHcol[:, c:c + 1], scale=Acol[:, c:c + 1])
        return h, hb

    def postnorm_residual(rings, ysb, yb, xt, xb, n, Gcol, bank):
        sq, sqb = rings["sq"].next()
        ACT(sq[:, :, 0:n], ysb[:, :, 0:n], AF.Square, [yb], [sqb])
        rs, rsb = stat_rstd(rings, sq, sqb, 8, n, D, bank)
        TTo("dve", ysb[:, :, 0:n], ysb[:, :, 0:n], rs[:, 0:n].unsqueeze(1).to_broadcast([128, 8, n]), ALU.mult, [yb, rsb], [yb])
        for c in range(8):
            STT(xt[:, c, 0:n], ysb[:, c, 0:n], Gcol[:, c:c + 1], xt[:, c, 0:n], ALU.mult, ALU.add, [yb, xb, CB], [xb])

    def norm_rings(es, with_h=True, nsq=2):
        rings = {
            "sq": Ring(nc, es, "sq", [128, 8, 512], BF16, nsq),
            "rs": Ring(nc, es, "rs", [128, 512], F32, 2),
        }
        if with_h:
            rings["h"] = Ring(nc, es, "h", [128, 8, 512], BF16, 2)
        return rings

    def cast_load(dst, src, wb):
        S.dma("pool", dst, src, w=[wb])

    L0 = ExitStack()
    Klat = sb(L0, "Klat", [64, 2, T], BF16)
    Kctx = sb(L0, "Kctx", [128, 2, CTX], BF16)
    Vt = sb(L0, "Vt", [128, NB, 2, 66], BF16)
    QSB = [[Buf() for _ in tiles] for _ in range(8)]
    KLB = [Buf() for _ in tiles]
    KCB = Buf()
    VB = [Buf() for _ in tiles]
    cwv, swv = KC["k_cw"], KC["k_sw"]
    cmv, smv = KC["k_cm"], KC["k_sm"]

    def phase_p1_l0():
        l = 0
        with ExitStack() as es:
            NCOL = 1024 + 1024 + 256 + 128
            Wt = sb(es, "Wt0", [128, 8, NCOL], BF16)
            WB = [Buf() for _ in range(5)]
            wsrc = W["ab_w_in"][0].rearrange("(kc p) n -> p kc n", p=128)
            cast_load(Wt[:, :, 0:1024], wsrc[:, :, 0:1024], WB[0])
            qd = Wt[:, :, 1024:2048].rearrange("p k (h two d) -> p k h two d", two=2, d=64)
            qs = wsrc[:, :, 1024:1536].rearrange("p k (h d) -> p k h d", d=64)
            for dup in range(2):
                for kc in range(8):
                    cast_load(qd[:, kc, :, dup, :], qs[:, kc, :, :], WB[1 + dup])
            kd = Wt[:, :, 2048:2304].rearrange("p k (h two d) -> p k h two d", two=2, d=64)
            ks = wsrc[:, :, 1536:1664].rearrange("p k (h d) -> p k h d", d=64)
            for dup in range(2):
                for kc in range(8):
                    cast_load(kd[:, kc, :, dup, :], ks[:, kc, :, :], WB[3])
            cast_load(Wt[:, :, 2304:2432], wsrc[:, :, 1664:1792], WB[4])
            MSET("pool", Vt[:, :, :, 64:66], 1.0, VB)
            rings = norm_rings(es)
            x_ring = Ring(nc, es, "xt", [128, 8, 512], F32, 2)
            sg_ring = Ring(nc, es, "sg", [128, 512], F32, 2)
            ust_ring = Ring(nc, es, "ust", [128, 4, 512], BF16, 2)
            cs_ring = Ring(nc, es, "cs", [64, 2, 512], F32, 3)
            t1_ring = Ring(nc, es, "t1", [64, 512], F32, 2)
            t2_ring = Ring(nc, es, "t2", [64, 512], F32, 2)
            kraw_ring = Ring(nc, es, "kraw", [128, 512], BF16, 2)
            qst_ring = Ring(nc, es, "qst", [128, 512], BF16, 3)
            banks = Rot([0, 1, 2, 3, 4])
            rbanks = Rot([5, 6])
            loads = {}

            def issue_load(j):
                t0, n, isc = tiles[j]
                xt, xb = x_ring.next()
                S.dma("sp", xt[:, :, 0:n], XTv[:, :, t0:t0 + n], r=[XTB[j]], w=[xb])
                cs, csb = cs_ring.next()
                if not isc:
                    S.dma("sp", cs[:, 0, :], cwv[:, t0:t0 + n], w=[csb])
                    S.dma("sp", cs[:, 1, :], swv[:, t0:t0 + n], w=[csb])
                loads[j] = (xt, xb, cs, csb)

            TL = list(range(len(tiles)))
            ACOL, SH0, SPLIT = A1, 0, 6

            def body(j, hcur_):
                t0, n, isc = tiles[j]
                xt, xb, cs, csb = loads[j]
                h, hb = hcur_

                def proj(col0, bank, M=128):
                    wdep = [WB[0]] if col0 < 1024 else ([WB[1], WB[2]] if col0 < 2048 else [WB[3]])
                    for kc in range(8):
                        MM(PS[0:M, bank, 0:n], Wt[:, kc, col0:col0 + M], h[:, kc, 0:n], kc == 0, kc == 7, [hb] + wdep, [PB[bank]])

                ust, ustb = ust_ring.next()
                for i in range(4):
                    bg = banks.next()
                    proj(512 + 128 * i, bg)
                    sg, sgb = sg_ring.next()
                    ACT(sg[:, 0:n], PS[:, bg, 0:n], AF.Sigmoid, [PB[bg]], [sgb])
                    bv = banks.next()
                    proj(128 * i, bv)
                    TTo("dve", ust[:, i, 0:n], PS[:, bv, 0:n], sg[:, 0:n], ALU.mult, [PB[bv], sgb], [ustb])
                    yield
                S.dma("pool", U0.rearrange("(c p) t -> p c t", p=128)[:, :, t0:t0 + n], ust[:, :, 0:n], r=[ustb], w=[U0B[j]])

                def rope(bank, rawsrc, rawb, dst, dstb):
                    rbk = rbanks.next()
                    MM(PS[0:64, rbk, 0:n], r64B[64:128, :], rawsrc, True, True, [rawb, CB], [PB[rbk]])
                    t1, t1b = t1_ring.next()
                    t2, t2b = t2_ring.next()
                    TTo("dve", t1[:, 0:n], PS[0:64, bank, 0:n], cs[:, 0, 0:n], ALU.mult, [PB[bank], csb], [t1b])
                    TTo("dve", t2[:, 0:n], PS[0:64, rbk, 0:n], cs[:, 1, 0:n], ALU.mult, [PB[rbk], csb], [t2b])
                    TTo("pool", dst, t1[:, 0:n], t2[:, 0:n], ALU.add, [t1b, t2b], [dstb])

                for hh in range(8):
                    bq = banks.next()
                    proj(1024 + 128 * hh, bq)
                    qst, qstb = qst_ring.next()
                    CP("act", qst[64:128, 0:n], PS[64:128, bq, 0:n], [PB[bq]], [qstb])
                    if not isc:
                        rope(bq, qst[64:128, 0:n], qstb, qst[0:64, 0:n], qstb)
                        S.dma("pool", QS[hh, :, t0:t0 + n], qst[:, 0:n], r=[qstb], w=[QSB[hh][j]])
                    else:
                        S.dma("pool", QS[hh, 64:128, t0:t0 + n], qst[64:128, 0:n], r=[qstb], w=[QSB[hh][j]])
                    yield
                for g in range(2):
                    bk = banks.next()
                    proj(2048 + 128 * g, bk)
                    if isc:
                        CP("act", Kctx[64:128, g, :], PS[64:128, bk, 0:n], [PB[bk]], [KCB])
                    else:
                        kr, krb = kraw_ring.next()
                        CP("act", kr[64:128, 0:n], PS[64:128, bk, 0:n], [PB[bk]], [krb])
                        rope(bk, kr[64:128, 0:n], krb, Klat[0:64, g, t0:t0 + n], KLB[j])
                for b in range(n // 128):
                    bv = banks.next()
                    for kc in range(8):
                        MM(PS[:, bv, 0:128], h[:, kc, b * 128:(b + 1) * 128], Wt[:, kc, 2304:2432], kc == 0, kc == 7, [hb, WB[4]], [PB[bv]])
                    blk = (t0 // 128) + b
                    CP("act" if b % 2 else "dve", Vt[:, blk, :, 0:64], PS[:, bv, 0:128].rearrange("p (g d) -> p g d", g=2), [PB[bv]], [VB[j]])
            def do_prenorm(jj):
                t0_, n_, isc_ = tiles[TL[jj]]
                s_ = 1 if isc_ else 0
                return prenorm(rings, loads[jj][0], loads[jj][1], n_, ACOL[:, l, s_, :], MODT[:, l, s_, SH0:SH0 + 8], 7)

            issue_load(0)
            if len(TL) > 1:
                issue_load(1)
            hcur = do_prenorm(0)
            for ji in range(len(TL)):
                gen = body(ji, hcur)
                k = 0
                done = ji + 1 >= len(TL)
                for _ in gen:
                    k += 1
                    if k == SPLIT and not done:
                        hcur = do_prenorm(ji + 1)
                        if ji + 2 < len(TL):
                            issue_load(ji + 2)
                        done = True
                if not done:
                    hcur = do_prenorm(ji + 1)
                    if ji + 2 < len(TL):
                        issue_load(ji + 2)
                loads.pop(ji)
            S.barrier()

    def phase_conva():
        with ExitStack() as es:
            Dg = sb(es, "DgA", [128, 4, 31, 128], BF16)
            DgB = Buf()
            for c in range(4):
                for k in range(31):
                    col = cols["acw"][:, k * 4 + c:k * 4 + c + 1]
                    TS("dve", Dg[:, c, k, :], identB[:], col, None, ALU.mult, None, [CB], [DgB])
            up_ring = Ring(nc, es, "up", [128, 4, 512 + 30], BF16, 2)
            ucv_ring = Ring(nc, es, "ucv", [128, 4, 512], F32, 2)
            usq_ring = Ring(nc, es, "usq", [128, 4, 512], F32, 2)
            st_ring = Ring(nc, es, "lnst", [128, 3, 512], F32, 2)
            tt_ring = Ring(nc, es, "lntt", [128, 512], F32, 2)
            ao_ring = Ring(nc, es, "ao", [128, 4, 512], BF16, 2)
            U0v = U0.rearrange("(c p) t -> p c t", p=128)
            A0v = A0.rearrange("(c p) t -> p c t", p=128)
            banks = Rot([0, 1, 2, 3])
            wring = Ring(nc, es, "wmod", [128, 8, 768], F32, 2)
            mod_jb = [0]
            mod_q = []

            def mod_step():
                k = mod_jb[0]
                if k > 8:
                    return
                if k < 8:
                    mod_q.append((k,) + mod_load(1, k, wring))
                if k >= 1:
                    kk, wt_, wb_ = mod_q.pop(0)
                    mod_mm(1, kk, wt_, wb_)
                mod_jb[0] += 1

            for j, (t0, n, isc) in enumerate(tiles):
                mod_step()
                seg0, seg1 = (T, TT) if isc else (0, T)
                lo, hi = max(t0 - 15, seg0), min(t0 + n + 15, seg1)
                up, upb = up_ring.next()
                rd = [U0B[j]]
                if j > 0 and not isc:
                    rd.append(U0B[j - 1])
                if j + 1 < NT:
                    rd.append(U0B[j + 1])
                if lo > t0 - 15:
                    MSET("pool", up[:, :, 0:15], 0.0, [upb])
                if hi < t0 + n + 15:
                    MSET("pool", up[:, :, n + 15:n + 30], 0.0, [upb])
                S.dma("sp", up[:, :, lo - (t0 - 15):hi - (t0 - 15)], U0v[:, :, lo:hi], r=rd, w=[upb])
                ucv, ucvb = ucv_ring.next()
                usq, usqb = usq_ring.next()
                for c in range(4):
                    bk = banks.next()
                    for k in range(31):
                        MM(PS[:, bk, 0:n], Dg[:, c, k, :], up[:, c, k:k + n], k == 0, k == 30, [upb, DgB], [PB[bk]])
                    ACT(ucv[:, c, 0:n], PS[:, bk, 0:n], AF.Identity, [PB[bk], CB], [ucvb], bias=cols["acb"][:, c:c + 1])
                    ACT(usq[:, c, 0:n], PS[:, bk, 0:n], AF.Square, [PB[bk], CB], [usqb], bias=cols["acb"][:, c:c + 1])
                for c in range(4):
                    MM(PS[:, 4, 0:n], onesF[:], ucv[:, c, 0:n], c == 0, c == 3, [ucvb, CB], [PB[4]])
                for c in range(4):
                    MM(PS[:, 5, 0:n], onesF[:], usq[:, c, 0:n], c == 0, c == 3, [usqb, CB], [PB[5]])
                st, stb = st_ring.next()
                TS("dve", st[:, 0, 0:n], PS[:, 4, 0:n], 1.0 / 512, None, ALU.mult, None, [PB[4]], [stb])
                TTo("dve", st[:, 1, 0:n], st[:, 0, 0:n], st[:, 0, 0:n], ALU.mult, [stb], [stb])
                STT(st[:, 2, 0:n], PS[:, 5, 0:n], 1.0 / 512, st[:, 1, 0:n], ALU.mult, ALU.subtract, [PB[5], stb], [stb])
                ACT(st[:, 2, 0:n], st[:, 2, 0:n], AF.Sqrt, [stb, CB], [stb], bias=epsT[:, 0:1])
                RCP(st[:, 2, 0:n], st[:, 2, 0:n], [stb], [stb])
                ao, aob = ao_ring.next()
                for c in range(4):
                    tt, ttb = tt_ring.next()
                    TTo("dve", tt[:, 0:n], ucv[:, c, 0:n], st[:, 0, 0:n], ALU.subtract, [ucvb, stb], [ttb])
                    TTo("dve", tt[:, 0:n], tt[:, 0:n], st[:, 2, 0:n], ALU.mult, [ttb, stb], [ttb])
                    ACT(ao[:, c, 0:n], tt[:, 0:n], AF.Silu, [ttb, CB], [aob], bias=cols["alb"][:, c:c + 1], scale=cols["alg"][:, c:c + 1])
                S.dma("pool", A0v[:, :, t0:t0 + n], ao[:, :, 0:n], r=[aob], w=[A0B[j]])
            while mod_jb[0] <= 8:
                mod_step()
            mod_finish(1)
            S.barrier()

    def attn_finalize_a(rings, acc, n, cp_eng="act"):
        osb, ob = rings["osb"].next()
        CP(cp_eng, osb[0:65, 0:n], PS[0:65, acc, 0:n], [PB[acc]], [ob])
        return osb, ob

    def attn_finalize_b(rings, osb, ob, n, extra_col, dst_dram, dstb, dbank):
        MM(PS[0:64, dbank, 0:n], selF[0:65, 0:64], osb[0:65, 0:n], True, True, [ob, CB], [PB[dbank]])
        rd, rdb = rings["rd"].next()
        if extra_col is not None:
            TS("dve", rd[0:64, 0:n], PS[0:64, dbank, 0:n], extra_col, None, ALU.add, None, [PB[dbank], CB], [rdb])
            RCP(rd[0:64, 0:n], rd[0:64, 0:n], [rdb], [rdb])
        else:
            RCP(rd[0:64, 0:n], PS[0:64, dbank, 0:n], [PB[dbank]], [rdb])
        bt, btb = rings["bt"].next()
        TTo("dve", bt[0:64, 0:n], osb[0:64, 0:n], rd[0:64, 0:n], ALU.mult, [ob, rdb], [btb])
        S.dma("pool", dst_dram, bt[0:64, 0:n], r=[btb], w=[dstb])

    def attn_finalize(rings, acc, n, extra_col, dst_dram, dstb, dbank, cp_eng="act"):
        osb, ob = attn_finalize_a(rings, acc, n, cp_eng)
        attn_finalize_b(rings, osb, ob, n, extra_col, dst_dram, dstb, dbank)

    def attn_rings(es):
        return {
            "osb": Ring(nc, es, "osb", [128, 512], F32, 3),
            "rd": Ring(nc, es, "rd", [64, 512], F32, 2),
            "bt": Ring(nc, es, "bt", [64, 512], BF16, 2),
        }

    def phase_attn0():
        with ExitStack() as es:
            rings = attn_rings(es)
            pt_ring = Ring(nc, es, "pt", [128, 512], BF16, 4)
            sbanks = Rot([0, 1, 2, 3])
            abanks = Rot([4, 5])
            dbanks = Rot([6, 7])
            qt_ring = Ring(nc, es, "qt", [128, 512], BF16, 3)
            pend = []
            for hh in range(8):
                g = hh // 4
                for j, (t0, n, isc) in enumerate(tiles):
                    qt, qtb = qt_ring.next()
                    if isc:
                        S.dma("sp", qt[64:128, 0:n], QS[hh, 64:128, t0:t0 + n], r=[QSB[hh][j]], w=[qtb])
                    else:
                        S.dma("sp", qt[:, 0:n], QS[hh, :, t0:t0 + n], r=[QSB[hh][j]], w=[qtb])
                    steps = []
                    for cc in range(CTX // 128):
                        steps.append((Kctx[64:128, g, cc * 128:(cc + 1) * 128], qt[64:128, 0:n],
                                      [KCB, qtb], NBL + cc, 0, n, []))
                    if not isc:
                        i4 = t0 // 128
                        for jb in range(i4 - 1, i4 + 5):
                            if jb < 0 or jb >= NBL:
                                continue
                            qb0, qb1 = max(jb - 1, i4), min(jb + 1, i4 + 3)
                            c0, c1 = (qb0 - i4) * 128, (qb1 - i4 + 1) * 128
                            masks = []
                            for qb in range(qb0, qb1 + 1):
                                if qb == jb - 1:
                                    masks.append(((qb - qb0) * 128, m1B))
                                elif qb == jb + 1:
                                    masks.append(((qb - qb0) * 128, m2B))
                            steps.append((Klat[0:64, g, jb * 128:(jb + 1) * 128], qt[0:64, c0:c1],
                                          [KLB[jb // 4], qtb], jb, c0, c1, masks))
                    acc = abanks.next()
                    for si, (lhsT, rhs, rdb_, vblk, c0, c1, masks) in enumerate(steps):
                        m = c1 - c0
                        sbk = sbanks.next()
                        MM(PS[:, sbk, 0:m], lhsT, rhs, True, True, rdb_, [PB[sbk]])
                        pt, ptb = pt_ring.next()
                        ACT(pt[:, 0:m], PS[:, sbk, 0:m], AF.Exp, [PB[sbk]], [ptb], scale=0.125)
                        for (mo, mk) in masks:
                            TTo("dve", pt[:, mo:mo + 128], pt[:, mo:mo + 128], mk[:], ALU.mult, [ptb, CB], [ptb])
                        while len(pend) >= 2:
                            pend.pop(0)()

                        def later(acc=acc, c0=c0, c1=c1, vblk=vblk, g=g, pt=pt, ptb=ptb, m=m, si=si, ns=len(steps), n=n, hh=hh, t0=t0, j=j):
                            MM(PS[0:65, acc, c0:c1], Vt[:, vblk, g, 0:65], pt[:, 0:m], si == 0, si == ns - 1,
                               [ptb, VB[min(vblk // 4, NT)]], [PB[acc]])
                            if si == ns - 1:
                                osb, ob = attn_finalize_a(rings, acc, n, "act")
                                pend.append(lambda: attn_finalize_b(rings, osb, ob, n, esink[0:64, hh:hh + 1], B0[hh * 64:(hh + 1) * 64, t0:t0 + n], B0B[j], dbanks.next()))
                        pend.append(later)
            while pend:
                pend.pop(0)()
            S.barrier()

    def phase_wout(l, Wsrc, Asrc, ASB, Bsrc, BSB, tl):
        with ExitStack() as es:
            Wa = sb(es, "Wa", [128, 4, D], BF16)
            Wb = sb(es, "Wb", [64, 8, D], BF16)
            WB = Buf()
            cast_load(Wa[:], Wsrc[0:512, :].rearrange("(c p) n -> p c n", p=128), WB)
            cast_load(Wb[:], Wsrc[512:1024, :].rearrange("(h d) n -> d h n", d=64), WB)
            rings = norm_rings(es, with_h=False)
            x_ring = Ring(nc, es, "xt", [128, 8, 512], F32, 2)
            a_ring = Ring(nc, es, "at", [128, 4, 512], BF16, 2)
            b_ring = Ring(nc, es, "bt2", [64, 8, 512], BF16, 2)
            y_ring = Ring(nc, es, "ysb", [128, 8, 512], F32, 2)
            Av = Asrc.rearrange("(c p) t -> p c t", p=128)
            Bv = Bsrc.rearrange("(h d) t -> d h t", d=64)
            banks = Rot([0, 1, 2, 3])
            loads = {}

            def issue_load(ji):
                j = tl[ji]
                t0, n, isc = tiles[j]
                xt, xb = x_ring.next()
                S.dma("sp", xt[:, :, 0:n], XTv[:, :, t0:t0 + n], r=[XTB[j]], w=[xb])
                at, ab = a_ring.next()
                S.dma("sp", at[:, :, 0:n], Av[:, :, t0:t0 + n], r=[ASB[j]], w=[ab])
                bt, bb = b_ring.next()
                S.dma("sp", bt[:, :, 0:n], Bv[:, :, t0:t0 + n], r=[BSB[j]], w=[bb])
                loads[ji] = (xt, xb, at, ab, bt, bb)

            issue_load(0)
            for ji, j in enumerate(tl):
                t0, n, isc = tiles[j]
                if ji + 1 < len(tl):
                    issue_load(ji + 1)
                xt, xb, at, ab, bt, bb = loads.pop(ji)
                s = 1 if isc else 0
                ysb, yb = y_ring.next()
                for fc in range(8):
                    bk = banks.next()
                    for c in range(4):
                        MM(PS[:, bk, 0:n], Wa[:, c, fc * 128:(fc + 1) * 128], at[:, c, 0:n], c == 0, False, [WB, ab], [PB[bk]])
                    for hh in range(8):
                        MM(PS[:, bk, 0:n], Wb[0:64, hh, fc * 128:(fc + 1) * 128], bt[0:64, hh, 0:n], False, hh == 7, [WB, bb], [PB[bk]])
                    CP("act", ysb[:, fc, 0:n], PS[:, bk, 0:n], [PB[bk]], [yb])
                postnorm_residual(rings, ysb, yb, xt, xb, n, G1[:, l, s, :], 7)
                S.dma("pool", XTv[:, :, t0:t0 + n], xt[:, :, 0:n], r=[xb], w=[XTB[j]])
            S.barrier()

    def phase_ffna(l, tl):
        with ExitStack() as es:
            Wu = sb(es, "Wu", [128, 8, 2 * FFN], BF16)
            WB = [Buf() for _ in range(8)]
            wsrc = W["ffn_w_up"][l].rearrange("(kc p) n -> p kc n", p=128)
            for blk in range(8):
                c0 = blk * 704
                cast_load(Wu[:, :, c0:c0 + 704], wsrc[:, :, c0:c0 + 704], WB[blk])
            rings = norm_rings(es)
            x_ring = Ring(nc, es, "xt", [128, 8, 512], F32, 2)
            st_ring = Ring(nc, es, "gst", [128, 4, 512], BF16, 3)
            banks = Rot([0, 1, 2, 3, 4, 5])
            GSv = GS.rearrange("(c p) t -> p c t", p=128)
            USv = US.rearrange("(c p) t -> p c t", p=128)
            loads = {}

            def issue_load(ji):
                j = tl[ji]
                t0, n, isc = tiles[j]
                xt, xb = x_ring.next()
                S.dma("sp", xt[:, :, 0:n], XTv[:, :, t0:t0 + n], r=[XTB[j]], w=[xb])
                loads[ji] = (xt, xb)

            TL = tl
            ACOL, SH0, SPLIT = A2, 24, 5

            def body(ji, hcur_):
                j = tl[ji]
                t0, n, isc = tiles[j]
                xt, xb = loads[ji]
                h, hb = hcur_
                for part, (dstv, dstB) in enumerate(((GSv, GSB), (USv, USB))):
                    k = 0
                    while k < NJ:
                        m = min(4, NJ - k)
                        st, stb = st_ring.next()
                        for q in range(m):
                            fc = part * NJ + k + q
                            bk = banks.next()
                            wb = WB[(fc * 128) // 704]
                            wb2 = WB[(fc * 128 + 127) // 704]
                            for kc in range(8):
                                MM(PS[:, bk, 0:n], Wu[:, kc, fc * 128:(fc + 1) * 128], h[:, kc, 0:n], kc == 0, kc == 7, [hb, wb, wb2], [PB[bk]])
                            CP("act" if (q % 2) else "dve", st[:, q, 0:n], PS[:, bk, 0:n], [PB[bk]], [stb])
                        S.dma("pool", dstv[:, k:k + m, t0:t0 + n], st[:, 0:m, 0:n], r=[stb], w=[dstB[j]])
                        yield
                        k += m
            def do_prenorm(jj):
                t0_, n_, isc_ = tiles[TL[jj]]
                s_ = 1 if isc_ else 0
                return prenorm(rings, loads[jj][0], loads[jj][1], n_, ACOL[:, l, s_, :], MODT[:, l, s_, SH0:SH0 + 8], 7)

            issue_load(0)
            if len(TL) > 1:
                issue_load(1)
            hcur = do_prenorm(0)
            for ji in range(len(TL)):
                gen = body(ji, hcur)
                k = 0
                done = ji + 1 >= len(TL)
                for _ in gen:
                    k += 1
                    if k == SPLIT and not done:
                        hcur = do_prenorm(ji + 1)
                        if ji + 2 < len(TL):
                            issue_load(ji + 2)
                        done = True
                if not done:
                    hcur = do_prenorm(ji + 1)
                    if ji + 2 < len(TL):
                        issue_load(ji + 2)
                loads.pop(ji)
            S.barrier()

    def phase_ffnb(l, tl):
        with ExitStack() as es:
            Wd = sb(es, "Wd", [128, NJ, D], BF16)
            WB = [Buf() for _ in range(2)]
            wsrc = W["ffn_w_down"][l].rearrange("(j p) n -> p j n", p=128)
            cast_load(Wd[:, 0:11, :], wsrc[:, 0:11, :], WB[0])
            cast_load(Wd[:, 11:22, :], wsrc[:, 11:22, :], WB[1])
            Dg = sb(es, "DgF", [128, NJ, 3, 128], BF16)
            DgB = Buf()
            fcw = cols["fcw%d" % l]
            for jj in range(NJ):
                for k in range(3):
                    TS("dve", Dg[:, jj, k, :], identB[:], fcw[:, k * NJ + jj:k * NJ + jj + 1], None, ALU.mult, None, [CB], [DgB])
            rings = norm_rings(es, with_h=False, nsq=1)
            x_ring = Ring(nc, es, "xt", [128, 8, 512], F32, 1)
            gH = [sb(es, "gtH%d" % i, [128, 11, 514], BF16) for i in range(2)]
            uH = [sb(es, "utH%d" % i, [128, 11, 512], BF16) for i in range(2)]
            gHB = [Buf(), Buf()]
            uHB = [Buf(), Buf()]
            ga_ring = Ring(nc, es, "ga", [128, 512], BF16, 3)
            act_ring = Ring(nc, es, "actt", [128, NJ, 512], BF16, 1)
            y_ring = Ring(nc, es, "ysb", [128, 8, 512], F32, 2)
            GSv = GS.rearrange("(c p) t -> p c t", p=128)
            USv = US.rearrange("(c p) t -> p c t", p=128)
            cbanks = Rot([0, 1, 2, 3])
            dbanks = Rot([4, 5, 6])
            loads = {}

            def load_gu(ji):
                j = tl[ji]
                t0, n, isc = tiles[j]
                seg0, seg1 = (T, TT) if isc else (0, T)
                lo, hi = max(t0 - 1, seg0), min(t0 + n + 1, seg1)
                rd = [GSB[j]]
                if j > 0 and not isc:
                    rd.append(GSB[j - 1])
                if j + 1 < NT:
                    rd.append(GSB[j + 1])
                for half in range(2):
                    gt, gb = gH[half], gHB[half]
                    if lo > t0 - 1:
                        MSET("pool", gt[:, :, 0:1], 0.0, [gb])
                    if hi < t0 + n + 1:
                        MSET("pool", gt[:, :, n + 1:n + 2], 0.0, [gb])
                    S.dma("sp", gt[:, :, lo - (t0 - 1):hi - (t0 - 1)], GSv[:, half * 11:(half + 1) * 11, lo:hi], r=rd, w=[gb])
                    S.dma("sp", uH[half][:, :, 0:n], USv[:, half * 11:(half + 1) * 11, t0:t0 + n], r=[USB[j]], w=[uHB[half]])

            def load_x(ji):
                j = tl[ji]
                t0, n, isc = tiles[j]
                xt, xb = x_ring.next()
                S.dma("sp", xt[:, :, 0:n], XTv[:, :, t0:t0 + n], r=[XTB[j]], w=[xb])
                loads[ji] = (xt, xb)

            load_gu(0)
            load_x(0)
            for ji, j in enumerate(tl):
                t0, n, isc = tiles[j]
                xt, xb = loads.pop(ji)
                s = 1 if isc else 0
                actt, actb = act_ring.next()
                for jj in range(NJ):
                    bk = cbanks.next()
                    gt, gb, ut, ub = gH[jj // 11], gHB[jj // 11], uH[jj // 11], uHB[jj // 11]
                    for k in range(3):
                        MM(PS[:, bk, 0:n], Dg[:, jj, k, :], gt[:, jj % 11, k:k + n], k == 0, k == 2, [gb, DgB], [PB[bk]])
                    ga, gab = ga_ring.next()
                    ACT(ga[:, 0:n], PS[:, bk, 0:n], AF.Gelu_apprx_tanh, [PB[bk], CB], [gab], bias=cols["fcb"][:, l * NJ + jj:l * NJ + jj + 1])
                    TTo("dve" if (jj % 2) else "pool", actt[:, jj, 0:n], ga[:, 0:n], ut[:, jj % 11, 0:n], ALU.mult, [gab, ub], [actb])
                if ji + 1 < len(tl):
                    load_gu(ji + 1)
                ysb, yb = y_ring.next()
                for fc in range(8):
                    bk = dbanks.next()
                    for jj in range(NJ):
                        MM(PS[:, bk, 0:n], Wd[:, jj, fc * 128:(fc + 1) * 128], actt[:, jj, 0:n], jj == 0, jj == NJ - 1, [actb] + WB, [PB[bk]])
                    CP("act", ysb[:, fc, 0:n], PS[:, bk, 0:n], [PB[bk]], [yb])
                postnorm_residual(rings, ysb, yb, xt, xb, n, G2[:, l, s, :], 7)
                S.dma("pool", XTv[:, :, t0:t0 + n], xt[:, :, 0:n], r=[xb], w=[XTB[j]])
                if ji + 1 < len(tl):
                    load_x(ji + 1)
            S.barrier()

    def phase_tout():
        with ExitStack() as es:
            x_ring = Ring(nc, es, "xt", [128, 8, 512], F32, 2)
            o_ring = Ring(nc, es, "ot", [128, D], F32, 3)
            banks = Rot([(0, 1), (2, 3), (4, 5), (6, 7)])
            for j, (t0, n, isc) in enumerate(lat_tiles):
                xt, xb = x_ring.next()
                S.dma("sp", xt[:, :, 0:n], XTv[:, :, t0:t0 + n], r=[XTB[j]], w=[xb])
                for b in range(n // 128):
                    b0, b1 = banks.next()
                    for fc in range(8):
                        bk = b0 if fc < 4 else b1
                        TR(PS[:, bk, (fc % 4) * 128:(fc % 4 + 1) * 128], xt[:, fc, b * 128:(b + 1) * 128], identF[:], [xb, CB], [PB[bk]])
                    ot, ob = o_ring.next()
                    CP("act", ot[:, 0:512], PS[:, b0, :], [PB[b0]], [ob])
                    CP("dve", ot[:, 512:1024], PS[:, b1, :], [PB[b1]], [ob])
                    S.dma("pool", y_out[t0 + b * 128:t0 + (b + 1) * 128, :], ot[:], r=[ob], w=[Buf()])
            S.barrier()

    L1 = ExitStack()
    L1T = {}

    def alloc_l1():
        L1T["CQN"] = sb(L1, "CQN", [128, 3, T], BF16)
        L1T["CKVN"] = sb(L1, "CKVN", [128, 2, TT], BF16)
        L1T["KRb"] = sb(L1, "KRb", [64, TT], BF16)

    CQB = [Buf() for _ in tiles]
    CKB = [Buf() for _ in tiles]
    KRB = [Buf() for _ in tiles]

    def phase_p1_l1():
        l = 1
        CQN, CKVN, KRb = L1T["CQN"], L1T["CKVN"], L1T["KRb"]
        with ExitStack() as es:
            Wt = sb(es, "Wt1", [128, 8, 1728], BF16)
            WB = [Buf() for _ in range(3)]
            wsrc = W["cd_w_in"][0].rearrange("(kc p) n -> p kc n", p=128)
            cast_load(Wt[:, :, 0:1024], wsrc[:, :, 0:1024], WB[0])
            cast_load(Wt[:, :, 1024:1664], wsrc[:, :, 1024:1664], WB[1])
            cast_load(Wt[:, :, 1664:1696], wsrc[:, :, 1664:1696], WB[2])
            cast_load(Wt[:, :, 1696:1728], wsrc[:, :, 1664:1696], WB[2])
            MSET("pool", KRb[32:64, 0:T], 0.0, KRB[:NT])
            MSET("pool", KRb[0:32, T:TT], 0.0, [KRB[NT]])
            rings = norm_rings(es)
            x_ring = Ring(nc, es, "xt", [128, 8, 512], F32, 2)
            xst_ring = Ring(nc, es, "xst", [128, 4, 512], F32, 1)
            gst_ring = Ring(nc, es, "gst1", [128, 4, 512], F32, 1)
            cqs_ring = Ring(nc, es, "cqs", [128, 3, 512], F32, 1)
            cs_ring = Ring(nc, es, "csm", [32, 2, 512], F32, 3)
            t1_ring = Ring(nc, es, "t1m", [32, 512], F32, 2)
            t2_ring = Ring(nc, es, "t2m", [32, 512], F32, 2)
            krs_ring = Ring(nc, es, "krs", [64, 512], BF16, 2)
            banks = Rot([0, 1, 2, 3, 4])
            XBv = XBS.rearrange("(c p) t -> p c t", p=128)
            GTv = GTS.rearrange("(c p) t -> p c t", p=128)
            loads = {}

            def issue_load(j):
                t0, n, isc = tiles[j]
                xt, xb = x_ring.next()
                S.dma("sp", xt[:, :, 0:n], XTv[:, :, t0:t0 + n], r=[XTB[j]], w=[xb])
                cs, csb = cs_ring.next()
                if not isc:
                    S.dma("sp", cs[:, 0, :], cmv[:, t0:t0 + n], w=[csb])
                    S.dma("sp", cs[:, 1, :], smv[:, t0:t0 + n], w=[csb])
                loads[j] = (xt, xb, cs, csb)

            TL = list(range(len(tiles)))
            ACOL, SH0, SPLIT = A1, 0, 4

            def body(j, hcur_):
                t0, n, isc = tiles[j]
                xt, xb, cs, csb = loads[j]
                h, hb = hcur_

                def proj(col0, bank, M=128):
                    wdep = [WB[0]] if col0 < 1024 else ([WB[1]] if col0 < 1664 else [WB[2]])
                    for kc in range(8):
                        MM(PS[0:M, bank, 0:n], Wt[:, kc, col0:col0 + M], h[:, kc, 0:n], kc == 0, kc == 7, [hb] + wdep, [PB[bank]])

                xst, xstb = xst_ring.next()
                for c in range(4):
                    bk = banks.next()
                    proj(128 * c, bk)
                    CP("act" if c % 2 else "dve", xst[:, c, 0:n], PS[:, bk, 0:n], [PB[bk]], [xstb])
                    yield
                S.dma("pool", XBv[:, :, t0:t0 + n], xst[:, :, 0:n], r=[xstb], w=[XBSB[j]])
                if not isc:
                    gst, gstb = gst_ring.next()
                    for c in range(4):
                        bk = banks.next()
                        proj(512 + 128 * c, bk)
                        ACT(gst[:, c, 0:n], PS[:, bk, 0:n], AF.Gelu_apprx_tanh, [PB[bk]], [gstb])
                        yield
                    S.dma("pool", GTv[:, :, t0:t0 + n], gst[:, :, 0:n], r=[gstb], w=[GTSB[j]])

                def lowrank_norm(col0, nch, dim, gcol, dst, dstb):
                    cqs, cqsb = cqs_ring.next()
                    for c in range(nch):
                        bk = banks.next()
                        proj(col0 + 128 * c, bk)
                        CP("act" if c % 2 else "dve", cqs[:, c, 0:n], PS[:, bk, 0:n], [PB[bk]], [cqsb])
                    sq, sqb = rings["sq"].next()
                    TTo("pool", sq[:, 0:nch, 0:n], cqs[:, 0:nch, 0:n], cqs[:, 0:nch, 0:n], ALU.mult, [cqsb], [sqb])
                    rs, rsb = stat_rstd(rings, sq, sqb, nch, n, dim, 7)
                    TTo("dve", cqs[:, 0:nch, 0:n], cqs[:, 0:nch, 0:n], rs[:, 0:n].unsqueeze(1).to_broadcast([128, nch, n]), ALU.mult, [cqsb, rsb], [cqsb])
                    for c in range(nch):
                        ACT(dst[:, c, t0:t0 + n], cqs[:, c, 0:n], AF.Identity, [cqsb, CB], [dstb], scale=gcol[:, c:c + 1])

                if not isc:
                    lowrank_norm(1024, 3, 384, cols["qn"], CQN, CQB[j])
                lowrank_norm(1408, 2, 256, cols["kvn"], CKVN, CKB[j])
                bk = banks.next()
                proj(1664, bk, M=64)
                if isc:
                    CP("act", KRb[32:64, t0:t0 + n], PS[32:64, bk, 0:n], [PB[bk]], [KRB[j]])
                else:
                    krs, krsb = krs_ring.next()
                    CP("act", krs[32:64, 0:n], PS[32:64, bk, 0:n], [PB[bk]], [krsb])
                    rbk = 5
                    MM(PS[0:32, rbk, 0:n], r32B[32:64, :], krs[32:64, 0:n], True, True, [krsb, CB], [PB[rbk]])
                    t1, t1b = t1_ring.next()
                    t2, t2b = t2_ring.next()
                    TTo("dve", t1[:, 0:n], PS[0:32, bk, 0:n], cs[:, 0, 0:n], ALU.mult, [PB[bk], csb], [t1b])
                    TTo("dve", t2[:, 0:n], PS[0:32, rbk, 0:n], cs[:, 1, 0:n], ALU.mult, [PB[rbk], csb], [t2b])
                    TTo("pool", KRb[0:32, t0:t0 + n], t1[:, 0:n], t2[:, 0:n], ALU.add, [t1b, t2b], [KRB[j]])
            def do_prenorm(jj):
                t0_, n_, isc_ = tiles[TL[jj]]
                s_ = 1 if isc_ else 0
                return prenorm(rings, loads[jj][0], loads[jj][1], n_, ACOL[:, l, s_, :], MODT[:, l, s_, SH0:SH0 + 8], 7)

            issue_load(0)
            if len(TL) > 1:
                issue_load(1)
            hcur = do_prenorm(0)
            for ji in range(len(TL)):
                gen = body(ji, hcur)
                k = 0
                done = ji + 1 >= len(TL)
                for _ in gen:
                    k += 1
                    if k == SPLIT and not done:
                        hcur = do_prenorm(ji + 1)
                        if ji + 2 < len(TL):
                            issue_load(ji + 2)
                        done = True
                if not done:
                    hcur = do_prenorm(ji + 1)
                    if ji + 2 < len(TL):
                        issue_load(ji + 2)
                loads.pop(ji)
            S.barrier()

    def phase_lru():
        with ExitStack() as es:
            GW = sb(es, "GW", [128, 2, 2, 4, 128], BF16)
            GWB = Buf()
            MSET("pool", GW[:], 0.0, [GWB])
            for d in range(2):
                for gate in range(2):
                    for nb in range(8):
                        p0 = (nb % 2) * 64
                        cast_load(GW[p0:p0 + 64, d, gate, nb // 2, p0:p0 + 64], W["lru_gate_w"][0, d, gate, nb], GWB)
            ytmp = sb(es, "ytmp", [128, 8], F32)
            yb_ = Buf()
            ACT(ytmp[:], cols["lam"][:], AF.Exp, [CB], [yb_], scale=-1.0)
            ACT(ytmp[:], ytmp[:], AF.Ln, [yb_, CB], [yb_], bias=onesF[:, 0:1])
            TS("dve", cch[:, 0, :], ytmp[:], -8.0, None, ALU.mult, None, [yb_], [CB])
            TS("dve", cch[:, 1, :], ytmp[:], -16.0, None, ALU.mult, None, [yb_], [CB])
            xb_ring = Ring(nc, es, "xbt", [128, 4, 515], F32, 2)
            xc_ring = Ring(nc, es, "xc", [128, 4, 512], F32, 2)
            xcb_ring = Ring(nc, es, "xcb", [128, 4, 512], BF16, 2)
            rg_ring = Ring(nc, es, "rg", [128, 4, 512], F32, 2)
            ig_ring = Ring(nc, es, "ig", [128, 4, 512], F32, 2)
            av_ring = Ring(nc, es, "av", [128, 4, 512], F32, 2)
            e2_ring = Ring(nc, es, "e2", [128, 4, 512], F32, 2)
            hv_rings = [Ring(nc, es, "hv%d" % d_, [128, 4, 512], F32, 2) for d_ in range(2)]
            XBv = XBS.rearrange("(c p) t -> p c t", p=128)
            GTv = GTS.rearrange("(c p) t -> p c t", p=128)
            HFv = HFS.rearrange("(c p) t -> p c t", p=128)
            C1v = C1.rearrange("(c p) t -> p c t", p=128)
            banks = Rot([0, 1, 2, 3, 4, 5])
            HBv = HBS.rearrange("(c p) t -> p c t", p=128)
            orders = [[NT] + list(range(NT)), [NT] + list(range(NT - 1, -1, -1))]
            prevs = [None, None]

            def stageA(d, j):
                t0, n, isc = tiles[j]
                seg0, seg1 = (T, TT) if isc else (0, T)
                xbt, xbb = xb_ring.next()
                rd = [XBSB[j]]
                if d == 0:
                    lo, hi = max(t0 - 3, seg0), t0 + n
                    if lo > t0 - 3:
                        MSET("pool", xbt[:, :, 0:3], 0.0, [xbb])
                    elif j > 0:
                        rd.append(XBSB[j - 1])
                    S.dma("sp", xbt[:, :, lo - (t0 - 3):n + 3], XBv[:, :, lo:hi], r=rd, w=[xbb])
                else:
                    lo, hi = t0, min(t0 + n + 3, seg1)
                    if hi < t0 + n + 3:
                        MSET("pool", xbt[:, :, n:n + 3], 0.0, [xbb])
                    elif j + 1 < NT:
                        rd.append(XBSB[j + 1])
                    S.dma("sp", xbt[:, :, 0:hi - lo], XBv[:, :, lo:hi], r=rd, w=[xbb])
                xc, xcb_ = xc_ring.next()
                for c in range(4):
                    wc = lambda k: cols["lcw"][:, d * 16 + k * 4 + c:d * 16 + k * 4 + c + 1]
                    TS("dve", xc[:, c, 0:n], xbt[:, c, 0:n], wc(0), cols["lcb"][:, d * 4 + c:d * 4 + c + 1], ALU.mult, ALU.add, [xbb, CB], [xcb_])
                    for k in range(1, 4):
                        STT(xc[:, c, 0:n], xbt[:, c, k:k + n], wc(k), xc[:, c, 0:n], ALU.mult, ALU.add, [xbb, xcb_, CB], [xcb_])
                xcb, xcbb = xcb_ring.next()
                CP("act", xcb[:, :, 0:n], xc[:, :, 0:n], [xcb_], [xcbb])
                rg, rgb = rg_ring.next()
                ig, igb = ig_ring.next()
                av, avb = av_ring.next()
                e2, e2b = e2_ring.next()
                for c in range(4):
                    b0 = banks.next()
                    MM(PS[:, b0, 0:n], GW[:, d, 0, c, :], xcb[:, c, 0:n], True, True, [xcbb, GWB], [PB[b0]])
                    ACT(rg[:, c, 0:n], PS[:, b0, 0:n], AF.Sigmoid, [PB[b0], CB], [rgb], bias=cols["lgb"][:, d * 8 + c:d * 8 + c + 1])
                    b1 = banks.next()
                    MM(PS[:, b1, 0:n], GW[:, d, 1, c, :], xcb[:, c, 0:n], True, True, [xcbb, GWB], [PB[b1]])
                    ACT(ig[:, c, 0:n], PS[:, b1, 0:n], AF.Sigmoid, [PB[b1], CB], [igb], bias=cols["lgb"][:, d * 8 + 4 + c:d * 8 + 4 + c + 1])
                for c in range(4):
                    ACT(av[:, c, 0:n], rg[:, c, 0:n], AF.Exp, [rgb, CB], [avb], scale=cch[:, 0, d * 4 + c:d * 4 + c + 1])
                    ACT(e2[:, c, 0:n], rg[:, c, 0:n], AF.Exp, [rgb, CB], [e2b], scale=cch[:, 1, d * 4 + c:d * 4 + c + 1])
                ACT(e2[:, :, 0:n], e2[:, :, 0:n], AF.Sqrt, [e2b, CB], [e2b], bias=onesF[:, 0:1], scale=-1.0)
                return (d, j, xc, xcb_, ig, igb, av, avb, e2, e2b)

            def stageB(ctx_):
                d, j, xc, xcb_, ig, igb, av, avb, e2, e2b = ctx_
                t0, n, isc = tiles[j]
                prev = prevs[d]
                TTo("dve", e2[:, :, 0:n], e2[:, :, 0:n], ig[:, :, 0:n], ALU.mult, [e2b, igb], [e2b])
                TTo("dve", e2[:, :, 0:n], e2[:, :, 0:n], xc[:, :, 0:n], ALU.mult, [e2b, xcb_], [e2b])
                hv, hvb = hv_rings[d].next()
                for c in range(4):
                    if prev is None:
                        init, rdp = 0.0, []
                    else:
                        ph, phb, pn = prev
                        init = ph[:, c, pn - 1:pn] if d == 0 else ph[:, c, 0:1]
                        rdp = [phb]
                    if d == 0:
                        o_, a_, b_ = hv[:, c, 0:n], av[:, c, 0:n], e2[:, c, 0:n]
                    else:
                        o_, a_, b_ = hv[:, c, 0:n][:, ::-1], av[:, c, 0:n][:, ::-1], e2[:, c, 0:n][:, ::-1]
                    S.op("dve", lambda e, o_=o_, a_=a_, b_=b_, init=init: e.tensor_tensor_scan(o_, a_, b_, init, ALU.mult, ALU.add),
                         [avb, e2b] + rdp, [hvb])
                prevs[d] = (hv, hvb, n)
                if not isc:
                    if d == 0:
                        S.dma("pool", HFv[:, :, t0:t0 + n], hv[:, :, 0:n], r=[hvb], w=[HFSB[j]])
                    else:
                        S.dma("pool", HBv[:, :, t0:t0 + n], hv[:, :, 0:n], r=[hvb], w=[HBSB[j]])

            items = [(d, orders[d][step]) for step in range(NT + 1) for d in range(2)]
            pend_ctx = None
            for (d, j) in items:
                ctx_ = stageA(d, j)
                if pend_ctx is not None:
                    stageB(pend_ctx)
                pend_ctx = ctx_
            stageB(pend_ctx)
            S.barrier()

    def phase_lru_combine():
        with ExitStack() as es:
            hf_ring = Ring(nc, es, "hf", [128, 4, 512], F32, 2)
            hb_ring = Ring(nc, es, "hb", [128, 4, 512], F32, 2)
            gg_ring = Ring(nc, es, "gg", [128, 4, 512], F32, 2)
            cl_ring = Ring(nc, es, "cl", [128, 4, 512], BF16, 2)
            GTv = GTS.rearrange("(c p) t -> p c t", p=128)
            HFv = HFS.rearrange("(c p) t -> p c t", p=128)
            HBv = HBS.rearrange("(c p) t -> p c t", p=128)
            C1v = C1.rearrange("(c p) t -> p c t", p=128)
            for j, (t0, n, isc) in enumerate(lat_tiles):
                hf, hfb = hf_ring.next()
                S.dma("sp", hf[:, :, 0:n], HFv[:, :, t0:t0 + n], r=[HFSB[j]], w=[hfb])
                hb, hbb = hb_ring.next()
                S.dma("sp", hb[:, :, 0:n], HBv[:, :, t0:t0 + n], r=[HBSB[j]], w=[hbb])
                gg, ggb = gg_ring.next()
                S.dma("sp", gg[:, :, 0:n], GTv[:, :, t0:t0 + n], r=[GTSB[j]], w=[ggb])
                cl, clb = cl_ring.next()
                TTo("dve", hf[:, :, 0:n], hf[:, :, 0:n], hb[:, :, 0:n], ALU.add, [hfb, hbb], [hfb])
                TTo("pool", cl[:, :, 0:n], hf[:, :, 0:n], gg[:, :, 0:n], ALU.mult, [hfb, ggb], [clb])
                S.dma("pool", C1v[:, :, t0:t0 + n], cl[:, :, 0:n], r=[clb], w=[C1B[j]])
            S.barrier()

    def phase_mla():
        CQN, CKVN, KRb = L1T["CQN"], L1T["CKVN"], L1T["KRb"]
        with ExitStack() as es:
            Wq = sb(es, "Wq", [128, 3, 8, 128], BF16)
            Wk = sb(es, "Wk", [128, 2, 8, 64], BF16)
            Wv = sb(es, "Wv", [128, 2, 8, 64], BF16)
            WB = Buf()
            qsrc = W["mla_w_uq"][0].rearrange("(kc p) (h e) -> p kc h e", p=128, e=96)
            ksrc = W["mla_w_ukv"][0].rearrange("(kc p) (h e) -> p kc h e", p=128, e=128)
            for kc in range(3):
                cast_load(Wq[:, kc, :, 64:128], qsrc[:, kc, :, 0:64], WB)
                cast_load(Wq[:, kc, :, 0:32], qsrc[:, kc, :, 64:96], WB)
                cast_load(Wq[:, kc, :, 32:64], qsrc[:, kc, :, 64:96], WB)
            for kc in range(2):
                cast_load(Wk[:, kc, :, :], ksrc[:, kc, :, 0:64], WB)
                cast_load(Wv[:, kc, :, :], ksrc[:, kc, :, 64:128], WB)
            Va = sb(es, "Va", [128, NB, 8, 66], BF16)
            VaB = Buf()
            MSET("pool", Va[:, :, :, 64:66], 1.0, [VaB])
            rings = attn_rings(es)
            k_ring = Ring(nc, es, "Kh", [128, TT], BF16, 2)
            q_ring = Ring(nc, es, "Qh", [128, T], BF16, 2)
            pt_ring = Ring(nc, es, "ptm", [128, 2, 512], BF16, 4)
            cs_ring = Ring(nc, es, "csq", [32, 2, 512], F32, 2)
            t1_ring = Ring(nc, es, "t1q", [32, 512], F32, 2)
            t2_ring = Ring(nc, es, "t2q", [32, 512], F32, 2)
            mbanks = Rot([6, 7])
            sbanks = Rot([0, 2])
            abanks = Rot([4, 5])
            for blk in range(NB):
                bk = mbanks.next()
                for kc in range(2):
                    MM(PS[:, bk, 0:512], CKVN[:, kc, blk * 128:(blk + 1) * 128], Wv[:, kc, :, :].rearrange("p h d -> p (h d)"),
                       kc == 0, kc == 1, [CKB[min(blk // 4, NT)], WB], [PB[bk]])
                CP("act" if blk % 2 else "dve", Va[:, blk, :, 0:64], PS[:, bk, 0:512].rearrange("p (h d) -> p h d", d=64), [PB[bk]], [VaB])
            sc = float(96 ** -0.5)
            pend = []
            qst_ = [None]
            hbufs = {}

            def get_bufs(h_):
                if h_ not in hbufs:
                    Kh_, KhB_ = k_ring.next()
                    Qh_, QhB_ = q_ring.next()
                    CP("pool", Kh_[0:64, :], KRb[0:64, :], KRB, [KhB_])
                    hbufs[h_] = (Kh_, KhB_, Qh_, QhB_)
                return hbufs[h_]

            def prod_k(h_, j):
                Kh_, KhB_, Qh_, QhB_ = get_bufs(h_)
                t0, n, isc = tiles[j]
                bk = mbanks.next()
                for kc in range(2):
                    MM(PS[64:128, bk, 0:n], Wk[:, kc, h_, :], CKVN[:, kc, t0:t0 + n], kc == 0, kc == 1, [CKB[j], WB], [PB[bk]])
                CP("dve", Kh_[64:128, t0:t0 + n], PS[64:128, bk, 0:n], [PB[bk]], [KhB_])

            def prod_q_a(h_, j):
                Kh_, KhB_, Qh_, QhB_ = get_bufs(h_)
                t0, n, isc = tiles[j]
                bk = mbanks.next()
                for kc in range(3):
                    MM(PS[:, bk, 0:n], Wq[:, kc, h_, :], CQN[:, kc, t0:t0 + n], kc == 0, kc == 2, [CQB[j], WB], [PB[bk]])
                CP("dve", Qh_[32:64, t0:t0 + n], PS[32:64, bk, 0:n], [PB[bk]], [QhB_])
                CP("dve", Qh_[64:128, t0:t0 + n], PS[64:128, bk, 0:n], [PB[bk]], [QhB_])
                t1, t1b = t1_ring.next()
                cs, csb = cs_ring.next()
                S.dma("sp", cs[:, 0, :], cmv[:, t0:t0 + n], w=[csb])
                S.dma("sp", cs[:, 1, :], smv[:, t0:t0 + n], w=[csb])
                TTo("dve", t1[:, 0:n], PS[0:32, bk, 0:n], cs[:, 0, 0:n], ALU.mult, [PB[bk], csb], [t1b])
                return (t1, t1b, cs, csb)

            def prod_q_b(h_, j, st_):
                Kh_, KhB_, Qh_, QhB_ = get_bufs(h_)
                t0, n, isc = tiles[j]
                t1, t1b, cs, csb = st_
                rbk = mbanks.next()
                MM(PS[0:32, rbk, 0:n], r32B[32:64, :], Qh_[32:64, t0:t0 + n], True, True, [QhB_, CB], [PB[rbk]])
                t2, t2b = t2_ring.next()
                TTo("dve", t2[:, 0:n], PS[0:32, rbk, 0:n], cs[:, 1, 0:n], ALU.mult, [PB[rbk], csb], [t2b])
                TTo("pool", Qh_[0:32, t0:t0 + n], t1[:, 0:n], t2[:, 0:n], ALU.add, [t1b, t2b], [QhB_])

            def prod_q(h_, j):
                prod_q_b(h_, j, prod_q_a(h_, j))

            for j in range(len(tiles)):
                prod_k(0, j)
            for j in range(NT):
                prod_q(0, j)
            for hh in range(8):
                Kh, KhB, Qh, QhB = get_bufs(hh)
                for j, (t0, n, isc) in enumerate(lat_tiles):
                    acc = abanks.next()
                    ngrp = (NB + 1) // 2
                    for gi in range(ngrp):
                        kbs = [kb for kb in (2 * gi, 2 * gi + 1) if kb < NB]
                        sb0 = sbanks.next()
                        for qi, kb in enumerate(kbs):
                            MM(PS[:, sb0 + qi, 0:n], Kh[:, kb * 128:(kb + 1) * 128], Qh[:, t0:t0 + n], True, True, [KhB, QhB], [PB[sb0 + qi]])
                        pt, ptb = pt_ring.next()
                        m = len(kbs)
                        ACT(pt[:, 0:m, 0:n], PS[:, sb0:sb0 + m, 0:n], AF.Exp, [PB[sb0 + q_] for q_ in range(m)], [ptb], scale=sc)
                        while len(pend) >= 2:
                            pend.pop(0)()

                        def later(kbs=kbs, acc=acc, n=n, pt=pt, ptb=ptb, gi=gi, hh=hh, t0=t0, j=j, last=(gi == ngrp - 1)):
                            for qi, kb in enumerate(kbs):
                                MM(PS[0:65, acc, 0:n], Va[:, kb, hh, 0:65], pt[:, qi, 0:n], gi == 0 and qi == 0, kb == NB - 1, [ptb, VaB], [PB[acc]])
                            if last:
                                osb, ob = attn_finalize_a(rings, acc, n, "dve")
                                pend.append(lambda: attn_finalize_b(rings, osb, ob, n, None, D1[hh * 64:(hh + 1) * 64, t0:t0 + n], D1B[j], mbanks.next()))
                        pend.append(later)
                        if hh + 1 < 8:
                            if gi == ngrp // 5:
                                prod_k(hh + 1, j)
                            if gi == (2 * ngrp) // 5:
                                qst_[0] = prod_q_a(hh + 1, j)
                            if gi == (4 * ngrp) // 5:
                                prod_q_b(hh + 1, j, qst_[0])
                            if gi == ngrp - 1 and j == NT - 1:
                                prod_k(hh + 1, NT)
            while pend:
                pend.pop(0)()
            S.barrier()

    def layer1():
        alloc_l1()
        phase_p1_l1()
        if stop == "p1_l1":
            return
        phase_lru()
        phase_lru_combine()
        if stop == "lru":
            return
        phase_mla()
        if stop == "mla":
            return
        L1.close()
        phase_wout(1, W["cd_w_out"][0], C1, C1B, D1, D1B, lat_t)
        if stop == "wout1":
            return
        phase_ffna(1, lat_t)
        phase_ffnb(1, lat_t)

    all_t = list(range(len(tiles)))
    lat_t = list(range(NT))

    def dump_small(parts):
        off = 0
        for t, w in parts:
            S.dma("sp", DBGV[:, off:off + w], t, r=[CB], w=[Buf()])
            off += w
        S.barrier()

    def run():
        phase_mod(0)
        if stop == "mod0":
            dump_small([(MODT[:, 0, :, :].rearrange("p s m -> p (s m)"), 96), (A1[:, 0].rearrange("p s m -> p (s m)"), 16),
                        (G1[:, 0].rearrange("p s m -> p (s m)"), 16), (A2[:, 0].rearrange("p s m -> p (s m)"), 16),
                        (G2[:, 0].rearrange("p s m -> p (s m)"), 16)])
            return
        phase_tin()
        if stop == "tin":
            return
        phase_p1_l0()
        if stop == "p1_l0":
            return
        phase_conva()
        if stop == "conva":
            return
        phase_attn0()
        if stop == "attn0":
            return
        L0.close()
        phase_wout(0, W["ab_w_out"][0], A0, A0B, B0, B0B, all_t)
        if stop == "wout0":
            return
        phase_ffna(0, all_t)
        if stop == "ffna0":
            return
        phase_ffnb(0, all_t)
        if stop == "ffnb0":
            return
        layer1()
        phase_tout()

    run()
    L1.close()
    L0.close()
    ges.close()
    build.stats = (S.nops, S.nwaits)
    return nc


def make_in_maps(inputs, T):
    consts = make_consts(T)
    f = lambda a: np.ascontiguousarray(np.asarray(a, dtype=np.float32))
    shared = {k: f(inputs[k]) for k in WEIGHT_SHAPES}
    shared.update(consts)
    shared["c_ctx"] = f(inputs["c_ctx"]).reshape(8, 128)
    x, c, ctx = f(inputs["x"]), f(inputs["c"]), f(inputs["ctx"])
    maps = []
    for b in range(x.shape[0]):
        m = dict(shared)
        m["x"] = np.ascontiguousarray(x[b])
        m["ctx"] = np.ascontiguousarray(ctx[b])
        m["c"] = np.ascontiguousarray(c[b]).reshape(8, 128)
        maps.append(m)
    return maps


def kernel(**inputs):
    T = int(np.asarray(inputs["x"]).shape[1])
    nc = build(T)
    in_maps = make_in_maps(inputs, T)
    res = run_bass_kernel_spmd(nc, in_maps, core_ids=list(range(len(in_maps))))
    return np.stack([np.asarray(r["y"], dtype=np.float32) for r in res.results], axis=0)
```

```python
import numpy as np
import ml_dtypes
from contextlib import ExitStack
import concourse.bass as bass
import concourse.mybir as mybir
from concourse.bass_utils import run_bass_kernel_spmd

F32 = mybir.dt.float32
BF16 = mybir.dt.bfloat16
AF = mybir.ActivationFunctionType
ALU = mybir.AluOpType

D = 1024
CTX = 256
EPS = 1e-6
FFN = 2816
NJ = FFN // 128


class Buf:
    __slots__ = ("w", "r")

    def __init__(self):
        self.w = None
        self.r = {}


class Op:
    __slots__ = ("eng", "chan", "seq", "fn", "waits", "signal", "clock", "val", "isdma")


class Sched:
    COMPUTE = ("pe", "act", "dve", "pool")

    def __init__(self, nc, es):
        self.nc = nc
        self.eobj = dict(pe=nc.tensor, act=nc.scalar, dve=nc.vector, pool=nc.gpsimd, sp=nc.sync)
        self.sem = {}
        for e in self.COMPUTE:
            self.sem[e] = es.enter_context(nc.semaphore("sem_" + e))
        self.nslot = {"sp": 12, "pool": 8}
        for q, n in self.nslot.items():
            for k in range(n):
                self.sem[(q, k)] = es.enter_context(nc.semaphore("dq_%s%d" % (q, k)))
        self.clock = {e: {} for e in self.eobj}
        self.seq = {e: 0 for e in self.COMPUTE}
        self.sigcount = {e: 0 for e in self.COMPUTE}
        self.dcount = {q: 0 for q in self.nslot}
        self.slot_last = {}
        self.last = {}
        self.pending = []
        self.bar = None
        self.nops = 0
        self.nwaits = 0
        self.dummy = es.enter_context(nc.sbuf_tensor("sched_dummy", [128, 8], F32))

    def _add(self, eng, chan, seq, fn, r, w, isdma, extra=()):
        op = Op()
        op.eng, op.chan, op.seq, op.fn, op.isdma = eng, chan, seq, fn, isdma
        op.signal = isdma
        op.val = 16 * (seq + 1) if isdma else None
        deps = {}

        def need(d):
            if d is None:
                return
            cur = deps.get(d.chan)
            if cur is None or cur.seq < d.seq:
                deps[d.chan] = d

        r = list(r)
        if self.bar is not None:
            r.append(self.bar)
        for b in r:
            need(b.w)
        for b in w:
            need(b.w)
            for d in b.r.values():
                need(d)
        for d in extra:
            need(d)
        clk = self.clock[eng]
        waits = []
        for d in deps.values():
            if d.chan == "pe" and eng == "pe":
                continue
            if clk.get(d.chan, -1) >= d.seq:
                continue
            waits.append(d)
            d.signal = True
            for k, v in d.clock.items():
                if clk.get(k, -1) < v:
                    clk[k] = v
            if clk.get(d.chan, -1) < d.seq:
                clk[d.chan] = d.seq
        op.waits = waits
        op.clock = dict(clk)
        for b in r:
            cur = b.r.get(chan)
            if cur is None or cur.seq < seq:
                b.r[chan] = op
        for b in w:
            b.w = op
            b.r = {}
        self.last[chan] = op
        self.pending.append(op)
        self.nops += 1
        self.nwaits += len(waits)
        return op

    def op(self, eng, fn, r=(), w=()):
        s = self.seq[eng]
        self.seq[eng] = s + 1
        return self._add(eng, eng, s, fn, r, w, False)

    def dma(self, q, out, in_, r=(), w=(), **kw):
        i = self.dcount[q]
        self.dcount[q] = i + 1
        n = self.nslot[q]
        slot, gen = i % n, i // n
        chan = (q, slot)
        prev = self.slot_last.get(chan)
        extra = [prev] if prev is not None else []
        op = self._add(q, chan, gen, lambda e: e.dma_start(out=out, in_=in_, **kw), r, w, True, extra)
        self.slot_last[chan] = op
        return op

    def barrier(self):
        b = Buf()
        extra = list(self.last.values())
        dummy = self.dummy
        s = self.seq["pool"]
        self.seq["pool"] = s + 1
        m = self._add("pool", "pool", s, lambda e: e.memset(dummy[:], 0.0), [], [b], False, extra)
        m.signal = True
        self.bar = b
        self.flush()

    def flush(self):
        for op in self.pending:
            e = self.eobj[op.eng]
            for d in op.waits:
                e.wait_ge(self.sem[d.chan], d.val)
            if op.signal and not op.isdma:
                self.sigcount[op.chan] += 1
                op.val = self.sigcount[op.chan]
            ins = op.fn(e)
            if op.signal:
                ins.then_inc(self.sem[op.chan], 16 if op.isdma else 1)
            op.fn = None
        self.pending = []


class Ring:
    uid = 0

    def __init__(self, nc, es, name, shape, dtype, n):
        Ring.uid += 1
        self.t = [es.enter_context(nc.sbuf_tensor("%s_%d_%d" % (name, Ring.uid, i), shape, dtype)) for i in range(n)]
        self.b = [Buf() for _ in range(n)]
        self.i = 0

    def next(self):
        k = self.i % len(self.t)
        self.i += 1
        return self.t[k], self.b[k]


class Rot:
    def __init__(self, items):
        self.items = list(items)
        self.i = 0

    def next(self):
        v = self.items[self.i % len(self.items)]
        self.i += 1
        return v


def _rope_tables(T, dim):
    rows = T // 64
    row = np.repeat(np.arange(rows), 64).astype(np.float32)
    col = np.tile(np.arange(64), rows).astype(np.float32)
    nf = dim // 4
    inv = (np.float32(10000.0) ** (-np.arange(nf, dtype=np.float32) / np.float32(nf))).astype(np.float32)
    ar = row[:, None] * inv
    ac = col[:, None] * inv
    ang = np.concatenate([ar, ar, ac, ac], axis=-1)
    return np.ascontiguousarray(np.cos(ang).T.astype(np.float32)), np.ascontiguousarray(np.sin(ang).T.astype(np.float32))


def _rot_T(dim):
    q = dim // 4
    R = np.zeros((dim, dim), np.float32)
    for i in range(q):
        R[i, q + i] = -1.0
        R[q + i, i] = 1.0
        R[2 * q + i, 3 * q + i] = -1.0
        R[3 * q + i, 2 * q + i] = 1.0
    return np.ascontiguousarray(R.T)


def make_consts(T):
    cw, sw = _rope_tables(T, 64)
    cm, sm = _rope_tables(T, 32)
    ident = np.eye(128, dtype=np.float32)
    sel = np.zeros((128, 64), np.float32)
    sel[64, :] = 1.0
    b = np.arange(128)[:, None]
    a = np.arange(128)[None, :]
    m1 = (b <= a).astype(np.float32)
    m2 = (a <= b).astype(np.float32)
    r64 = _rot_T(64)
    r32 = _rot_T(32)
    return {
        "k_ident": ident, "k_sel": sel, "k_m1": m1, "k_m2": m2,
        "k_r64": np.concatenate([r64, r64], 0), "k_r32": np.concatenate([r32, r32], 0),
        "k_cw": cw, "k_sw": sw, "k_cm": cm, "k_sm": sm,
    }


WEIGHT_SHAPES = {
    "w_mod": [2, 1024, 6144], "b_mod": [2, 6144], "norm_g": [2, 4, 1024], "ffn_w_up": [2, 1024, 5632],
    "ffn_conv_w": [2, 3, 2816], "ffn_conv_b": [2, 2816], "ffn_w_down": [2, 2816, 1024],
    "ab_w_in": [1, 1024, 1792], "a_conv_w": [1, 31, 512], "a_conv_b": [1, 512], "a_ln_g": [1, 512],
    "a_ln_b": [1, 512], "b_sink": [1, 8], "ab_w_out": [1, 1024, 1024], "cd_w_in": [1, 1024, 1696],
    "lru_conv_w": [1, 2, 4, 512], "lru_conv_b": [1, 2, 512], "lru_gate_w": [1, 2, 2, 8, 64, 64],
    "lru_gate_b": [1, 2, 2, 512], "lru_lambda": [1, 2, 512], "mla_q_norm": [1, 384],
    "mla_w_uq": [1, 384, 768], "mla_kv_norm": [1, 256], "mla_w_ukv": [1, 256, 1024],
    "cd_w_out": [1, 1024, 1024],
}


def build(T=4096, dbg=(), stop=None):
    nc = bass.Bass("TRN2", target_bir_lowering=False)
    TT = T + CTX
    NT = T // 512
    NBL = T // 128
    NB = NBL + CTX // 128
    tiles = [(i * 512, 512, False) for i in range(NT)] + [(T, CTX, True)]
    lat_tiles = tiles[:NT]

    def din(name, shape):
        return nc.dram_tensor(name, list(shape), F32, kind="ExternalInput").ap()

    x_in = din("x", [T, D])
    ctx_in = din("ctx", [CTX, D])
    c_in = din("c", [8, 128])
    cctx_in = din("c_ctx", [8, 128])
    W = {k: din(k, s) for k, s in WEIGHT_SHAPES.items()}
    KC = {k: din(k, v.shape) for k, v in make_consts(T).items()}
    y_out = nc.dram_tensor("y", [T, D], F32, kind="ExternalOutput").ap()

    def scratch(name, shape, dt):
        kind = "ExternalOutput" if name in dbg else "Internal"
        return nc.dram_tensor(name, list(shape), dt, kind=kind).ap()

    XT = scratch("XT", [D, TT], F32)
    U0 = scratch("U0", [512, TT], BF16)
    A0 = scratch("A0", [512, TT], BF16)
    B0 = scratch("B0", [512, TT], BF16)
    GS = scratch("GS", [FFN, TT], BF16)
    US = scratch("US", [FFN, TT], BF16)
    XBS = scratch("XBS", [512, TT], F32)
    GTS = scratch("GTS", [512, T], F32)
    HFS = scratch("HFS", [512, T], F32)
    HBS = scratch("HBS", [512, T], F32)
    C1 = scratch("C1", [512, T], BF16)
    D1 = scratch("D1", [512, T], BF16)
    DBGV = scratch("DBGV", [128, 512], F32)
    QS = scratch("QS", [8, 128, TT], BF16)
    XTv = XT.rearrange("(c p) t -> p c t", p=128)
    XTB = [Buf() for _ in tiles]
    U0B = [Buf() for _ in tiles]
    A0B = [Buf() for _ in tiles]
    B0B = [Buf() for _ in tiles]
    GSB = [Buf() for _ in tiles]
    USB = [Buf() for _ in tiles]
    XBSB = [Buf() for _ in tiles]
    GTSB = [Buf() for _ in tiles]
    HFSB = [Buf() for _ in tiles]
    HBSB = [Buf() for _ in tiles]
    C1B = [Buf() for _ in tiles]
    D1B = [Buf() for _ in tiles]

    ges = ExitStack()
    S = Sched(nc, ges)
    PS = ges.enter_context(nc.psum_tensor("PS", [128, 8, 512], F32))
    PB = [Buf() for _ in range(8)]

    def sb(es, name, shape, dt):
        Ring.uid += 1
        return es.enter_context(nc.sbuf_tensor("%s_%d" % (name, Ring.uid), list(shape), dt))

    def MM(out, lhsT, rhs, st, sp, r, w):
        S.op("pe", lambda e: e.matmul(out, lhsT, rhs, start=st, stop=sp), r, w)

    def TR(out, in_, ident, r, w):
        S.op("pe", lambda e: e.transpose(out, in_, ident), r, w)

    def ACT(out, in_, func, r, w, bias=None, scale=None):
        kw = {}
        if bias is not None:
            kw["bias"] = bias
        if scale is not None:
            kw["scale"] = scale
        S.op("act", lambda e: e.activation(out, in_, func, **kw), r, w)

    def CP(eng, out, in_, r, w):
        if eng == "act":
            S.op("act", lambda e: e.copy(out, in_), r, w)
        else:
            S.op(eng, lambda e: e.tensor_copy(out, in_), r, w)

    def TTo(eng, out, a, b, op, r, w):
        S.op(eng, lambda e: e.tensor_tensor(out, a, b, op), r, w)

    def TS(eng, out, a, s1, s2, op0, op1, r, w):
        if s2 is None:
            S.op(eng, lambda e: e.tensor_scalar(out, a, s1, None, op0), r, w)
        else:
            S.op(eng, lambda e: e.tensor_scalar(out, a, s1, s2, op0, op1), r, w)

    def STT(out, in0, scalar, in1, op0, op1, r, w):
        S.op("dve", lambda e: e.scalar_tensor_tensor(out, in0, scalar, in1, op0, op1), r, w)

    def RCP(out, in_, r, w):
        S.op("dve", lambda e: e.reciprocal(out, in_), r, w)

    def MSET(eng, ap, val, w):
        S.op(eng, lambda e: e.memset(ap, val), [], w)

    identF = sb(ges, "identF", [128, 128], F32)
    identB = sb(ges, "identB", [128, 128], BF16)
    onesB = sb(ges, "onesB", [128, 128], BF16)
    onesF = sb(ges, "onesF", [128, 128], F32)
    selF = sb(ges, "selF", [128, 64], F32)
    m1B = sb(ges, "m1B", [128, 128], BF16)
    m2B = sb(ges, "m2B", [128, 128], BF16)
    r64B = sb(ges, "r64B", [128, 64], BF16)
    r32B = sb(ges, "r32B", [64, 32], BF16)
    CB = Buf()
    S.dma("sp", identF[:], KC["k_ident"], w=[CB])
    S.dma("sp", selF[:], KC["k_sel"], w=[CB])
    S.dma("pool", m1B[:], KC["k_m1"], w=[CB])
    S.dma("pool", m2B[:], KC["k_m2"], w=[CB])
    S.dma("pool", r64B[:], KC["k_r64"], w=[CB])
    S.dma("pool", r32B[:], KC["k_r32"], w=[CB])
    CP("dve", identB[:], identF[:], [CB], [CB])
    MSET("dve", onesB[:], 1.0, [CB])
    MSET("dve", onesF[:], 1.0, [CB])

    cols = {}
    colspec = {
        "g": (W["norm_g"], 64), "bm": (W["b_mod"], 96), "fcb": (W["ffn_conv_b"], 44),
        "fcw0": (W["ffn_conv_w"][0], 66), "fcw1": (W["ffn_conv_w"][1], 66),
        "acb": (W["a_conv_b"], 4), "alg": (W["a_ln_g"], 4), "alb": (W["a_ln_b"], 4),
        "acw": (W["a_conv_w"], 124), "lcw": (W["lru_conv_w"], 32), "lcb": (W["lru_conv_b"], 8),
        "lgb": (W["lru_gate_b"], 16), "lam": (W["lru_lambda"], 8), "qn": (W["mla_q_norm"], 3),
        "kvn": (W["mla_kv_norm"], 2), "c": (c_in, 8), "cc": (cctx_in, 8),
    }
    for name, (src, n) in colspec.items():
        cols[name] = sb(ges, "col_" + name, [128, n], F32)
    esink = sb(ges, "esink", [64, 8], F32)
    scT = sb(ges, "scT", [128, 8, 2], F32)
    MODT = sb(ges, "MODT", [128, 2, 2, 48], F32)
    A1 = sb(ges, "A1", [128, 2, 2, 8], F32)
    G1 = sb(ges, "G1", [128, 2, 2, 8], F32)
    A2 = sb(ges, "A2", [128, 2, 2, 8], F32)
    G2 = sb(ges, "G2", [128, 2, 2, 8], F32)
    epsT = sb(ges, "epsT", [128, 1], F32)
    cch = sb(ges, "cch", [128, 2, 8], F32)
    pre = ExitStack()
    rows_ring = Ring(nc, pre, "rows", [128, 128], F32, 2)
    for i, (name, (src, n)) in enumerate(colspec.items()):
        dst = cols[name]
        nd = len(src.shape)
        if nd == 1:
            s2 = src.rearrange("(r p) -> r p", p=128)
        elif nd == 2 and src.shape[1] == 128:
            s2 = src
        else:
            names = " ".join("a%d" % k for k in range(nd - 1))
            s2 = src.rearrange("%s (r p) -> (%s r) p" % (names, names), p=128)
        rt, rb = rows_ring.next()
        S.dma("sp", rt[0:n, :], s2, w=[rb])
        bank = 6 + (i % 2)
        TR(PS[:, bank, 0:n], rt[0:n, :], identF[0:n, 0:n], [rb, CB], [PB[bank]])
        CP("dve", dst[:], PS[:, bank, 0:n], [PB[bank]], [CB])
    sk = sb(pre, "sk", [1, 8], F32)
    skb = Buf()
    S.dma("sp", sk[:], W["b_sink"], w=[skb])
    MM(PS[0:64, 5, 0:8], onesF[0:1, 0:64], sk[0:1, :], True, True, [skb, CB], [PB[5]])
    ACT(esink[:], PS[0:64, 5, 0:8], AF.Exp, [PB[5]], [CB])
    ACT(scT[:, :, 0], cols["c"][:], AF.Silu, [CB], [CB])
    ACT(scT[:, :, 1], cols["cc"][:], AF.Silu, [CB], [CB])
    S.barrier()
    pre.close()

    gc = cols["g"]

    def mod_load(l, jb, wring):
        wsrc = W["w_mod"][l].rearrange("(kc p) n -> p kc n", p=128)
        wt, wb = wring.next()
        S.dma("sp", wt[:], wsrc[:, :, jb * 768:(jb + 1) * 768], w=[wb])
        return wt, wb

    def mod_mm(l, jb, wt, wb):
        for jj in range(6):
            j = jb * 6 + jj
            for kc in range(8):
                MM(PS[:, 6, 2 * j:2 * j + 2], wt[:, kc, jj * 128:(jj + 1) * 128], scT[:, kc, :],
                   kc == 0, kc == 7, [wb, CB], [PB[6]])

    def mod_block(l, jb, wring):
        wt, wb = mod_load(l, jb, wring)
        mod_mm(l, jb, wt, wb)

    def mod_finish(l):
        pv = PS[:, 6, 0:96].rearrange("p (j s) -> p j s", s=2)
        for s in range(2):
            TTo("dve", MODT[:, l, s, :], pv[:, :, s], cols["bm"][:, l * 48:(l + 1) * 48], ALU.add, [PB[6], CB], [CB])
            STT(A1[:, l, s, :], MODT[:, l, s, 8:16], 1.0, gc[:, l * 32:l * 32 + 8], ALU.add, ALU.mult, [CB], [CB])
            TTo("dve", G1[:, l, s, :], MODT[:, l, s, 16:24], gc[:, l * 32 + 8:l * 32 + 16], ALU.mult, [CB], [CB])
            STT(A2[:, l, s, :], MODT[:, l, s, 32:40], 1.0, gc[:, l * 32 + 16:l * 32 + 24], ALU.add, ALU.mult, [CB], [CB])
            TTo("dve", G2[:, l, s, :], MODT[:, l, s, 40:48], gc[:, l * 32 + 24:l * 32 + 32], ALU.mult, [CB], [CB])

    def phase_mod(l):
        with ExitStack() as es:
            wring = Ring(nc, es, "wmod", [128, 8, 768], F32, 2)
            for jb in range(8):
                mod_block(l, jb, wring)
            mod_finish(l)
            S.barrier()

    def phase_tin():
        with ExitStack() as es:
            xin_ring = Ring(nc, es, "xin", [128, D], F32, 3)
            xt_ring = Ring(nc, es, "xtt", [128, 8, 512], F32, 2)
            for j, (t0, n, isc) in enumerate(tiles):
                src = ctx_in if isc else x_in
                s0 = 0 if isc else t0
                for b in range(n // 128):
                    xin, xb_ = xin_ring.next()
                    S.dma("sp", xin[:], src[s0 + b * 128:s0 + (b + 1) * 128, :], w=[xb_])
                    for fc in range(8):
                        TR(PS[:, fc, b * 128:(b + 1) * 128], xin[:, fc * 128:(fc + 1) * 128], identF[:], [xb_, CB], [PB[fc]])
                xt, xtb = xt_ring.next()
                for fc in range(8):
                    CP("act" if fc % 2 else "dve", xt[:, fc, 0:n], PS[:, fc, 0:n], [PB[fc]], [xtb])
                S.dma("pool", XTv[:, :, t0:t0 + n], xt[:, :, 0:n], r=[xtb], w=[XTB[j]])
            S.barrier()

    def stat_rstd(es_rings, src, srcb, nch, n, dim, bank):
        for c in range(nch):
            MM(PS[:, bank, 0:n], onesB[:], src[:, c, 0:n], c == 0, c == nch - 1, [srcb, CB], [PB[bank]])
        rs, rsb = es_rings["rs"].next()
        ACT(rs[:, 0:n], PS[:, bank, 0:n], AF.Sqrt, [PB[bank]], [rsb], bias=epsT[:, 0:1], scale=1.0 / dim)
        RCP(rs[:, 0:n], rs[:, 0:n], [rsb], [rsb])
        return rs, rsb

    MSET("dve", epsT[:], EPS, [CB])

    def prenorm_gen(rings, xt, xb, n, Acol, SHcol, bank):
        sq, sqb = rings["sq"].next()
        ACT(sq[:, :, 0:n], xt[:, :, 0:n], AF.Square, [xb], [sqb])
        yield
        for c in range(8):
            MM(PS[:, bank, 0:n], onesB[:], sq[:, c, 0:n], c == 0, c == 7, [sqb, CB], [PB[bank]])
        yield
        rs, rsb = rings["rs"].next()
        ACT(rs[:, 0:n], PS[:, bank, 0:n], AF.Sqrt, [PB[bank]], [rsb], bias=epsT[:, 0:1], scale=1.0 / D)
        RCP(rs[:, 0:n], rs[:, 0:n], [rsb], [rsb])
        yield
        TTo("dve", xt[:, :, 0:n], xt[:, :, 0:n], rs[:, 0:n].unsqueeze(1).to_broadcast([128, 8, n]), ALU.mult, [xb, rsb], [xb])
        yield
        h, hb = rings["h"].next()
        for c in range(8):
            ACT(h[:, c, 0:n], xt[:, c, 0:n], AF.Identity, [xb, CB], [hb], bias=SHcol[:, c:c + 1], scale=Acol[:, c:c + 1])
        return h, hb

    def prenorm(rings, xt, xb, n, Acol, SHcol, bank):
        g_ = prenorm_gen(rings, xt, xb, n, Acol, SHcol, bank)
        while True:
            try:
                next(g_)
            except StopIteration as e_:
                return e_.value

    def postnorm_residual(rings, ysb, yb, xt, xb, n, Gcol, bank):
        sq, sqb = rings["sq"].next()
        ACT(sq[:, :, 0:n], ysb[:, :, 0:n], AF.Square, [yb], [sqb])
        rs, rsb = stat_rstd(rings, sq, sqb, 8, n, D, bank)
        TTo("dve", ysb[:, :, 0:n], ysb[:, :, 0:n], rs[:, 0:n].unsqueeze(1).to_broadcast([128, 8, n]), ALU.mult, [yb, rsb], [yb])
        for c in range(8):
            STT(xt[:, c, 0:n], ysb[:, c, 0:n], Gcol[:, c:c + 1], xt[:, c, 0:n], ALU.mult, ALU.add, [yb, xb, CB], [xb])

    def norm_rings(es, with_h=True, nsq=2):
        rings = {
            "sq": Ring(nc, es, "sq", [128, 8, 512], BF16, nsq),
            "rs": Ring(nc, es, "rs", [128, 512], F32, 2),
        }
        if with_h:
            rings["h"] = Ring(nc, es, "h", [128, 8, 512], BF16, 2)
        return rings

    def cast_load(dst, src, wb):
        S.dma("pool", dst, src, w=[wb])

    L0 = ExitStack()
    Klat = sb(L0, "Klat", [64, 2, T], BF16)
    Kctx = sb(L0, "Kctx", [128, 2, CTX], BF16)
    Vt = sb(L0, "Vt", [128, NB, 2, 66], BF16)
    QSB = [[Buf() for _ in tiles] for _ in range(8)]
    KLB = [Buf() for _ in tiles]
    KCB = Buf()
    VB = [Buf() for _ in tiles]
    cwv, swv = KC["k_cw"], KC["k_sw"]
    cmv, smv = KC["k_cm"], KC["k_sm"]

    def phase_p1_l0():
        l = 0
        with ExitStack() as es:
            NCOL = 1024 + 1024 + 256 + 128
            Wt = sb(es, "Wt0", [128, 8, NCOL], BF16)
            WB = [Buf() for _ in range(5)]
            wsrc = W["ab_w_in"][0].rearrange("(kc p) n -> p kc n", p=128)
            cast_load(Wt[:, :, 0:1024], wsrc[:, :, 0:1024], WB[0])
            qd = Wt[:, :, 1024:2048].rearrange("p k (h two d) -> p k h two d", two=2, d=64)
            qs = wsrc[:, :, 1024:1536].rearrange("p k (h d) -> p k h d", d=64)
            for dup in range(2):
                for kc in range(8):
                    cast_load(qd[:, kc, :, dup, :], qs[:, kc, :, :], WB[1 + dup])
            kd = Wt[:, :, 2048:2304].rearrange("p k (h two d) -> p k h two d", two=2, d=64)
            ks = wsrc[:, :, 1536:1664].rearrange("p k (h d) -> p k h d", d=64)
            for dup in range(2):
                for kc in range(8):
                    cast_load(kd[:, kc, :, dup, :], ks[:, kc, :, :], WB[3])
            cast_load(Wt[:, :, 2304:2432], wsrc[:, :, 1664:1792], WB[4])
            MSET("pool", Vt[:, :, :, 64:66], 1.0, VB)
            rings = norm_rings(es)
            x_ring = Ring(nc, es, "xt", [128, 8, 512], F32, 2)
            sg_ring = Ring(nc, es, "sg", [128, 512], F32, 2)
            ust_ring = Ring(nc, es, "ust", [128, 4, 512], BF16, 2)
            cs_ring = Ring(nc, es, "cs", [64, 2, 512], F32, 3)
            t1_ring = Ring(nc, es, "t1", [64, 512], F32, 2)
            t2_ring = Ring(nc, es, "t2", [64, 512], F32, 2)
            kraw_ring = Ring(nc, es, "kraw", [128, 512], BF16, 2)
            qst_ring = Ring(nc, es, "qst", [128, 512], BF16, 3)
            banks = Rot([0, 1, 2, 3, 4])
            rbanks = Rot([5, 6])
            loads = {}

            def issue_load(j):
                t0, n, isc = tiles[j]
                xt, xb = x_ring.next()
                S.dma("sp", xt[:, :, 0:n], XTv[:, :, t0:t0 + n], r=[XTB[j]], w=[xb])
                cs, csb = cs_ring.next()
                if not isc:
                    S.dma("sp", cs[:, 0, :], cwv[:, t0:t0 + n], w=[csb])
                    S.dma("sp", cs[:, 1, :], swv[:, t0:t0 + n], w=[csb])
                loads[j] = (xt, xb, cs, csb)

            TL = list(range(len(tiles)))
            ACOL, SH0, SPLIT = A1, 0, 3

            def body(j, hcur_):
                t0, n, isc = tiles[j]
                xt, xb, cs, csb = loads[j]
                h, hb = hcur_

                def proj(col0, bank, M=128):
                    wdep = [WB[0]] if col0 < 1024 else ([WB[1], WB[2]] if col0 < 2048 else [WB[3]])
                    for kc in range(8):
                        MM(PS[0:M, bank, 0:n], Wt[:, kc, col0:col0 + M], h[:, kc, 0:n], kc == 0, kc == 7, [hb] + wdep, [PB[bank]])

                ust, ustb = ust_ring.next()
                for i in range(4):
                    bg = banks.next()
                    proj(512 + 128 * i, bg)
                    sg, sgb = sg_ring.next()
                    ACT(sg[:, 0:n], PS[:, bg, 0:n], AF.Sigmoid, [PB[bg]], [sgb])
                    bv = banks.next()
                    proj(128 * i, bv)
                    TTo("dve", ust[:, i, 0:n], PS[:, bv, 0:n], sg[:, 0:n], ALU.mult, [PB[bv], sgb], [ustb])
                    yield
                S.dma("pool", U0.rearrange("(c p) t -> p c t", p=128)[:, :, t0:t0 + n], ust[:, :, 0:n], r=[ustb], w=[U0B[j]])

                def rope(bank, rawsrc, rawb, dst, dstb):
                    rbk = rbanks.next()
                    MM(PS[0:64, rbk, 0:n], r64B[64:128, :], rawsrc, True, True, [rawb, CB], [PB[rbk]])
                    t1, t1b = t1_ring.next()
                    t2, t2b = t2_ring.next()
                    TTo("dve", t1[:, 0:n], PS[0:64, bank, 0:n], cs[:, 0, 0:n], ALU.mult, [PB[bank], csb], [t1b])
                    TTo("dve", t2[:, 0:n], PS[0:64, rbk, 0:n], cs[:, 1, 0:n], ALU.mult, [PB[rbk], csb], [t2b])
                    TTo("pool", dst, t1[:, 0:n], t2[:, 0:n], ALU.add, [t1b, t2b], [dstb])

                rpend = []

                def run_rpend():
                    while rpend:
                        rpend.pop(0)()

                for hh in range(8):
                    bq = banks.next()
                    proj(1024 + 128 * hh, bq)
                    qst, qstb = qst_ring.next()
                    CP("act", qst[64:128, 0:n], PS[64:128, bq, 0:n], [PB[bq]], [qstb])
                    run_rpend()
                    if not isc:
                        def rq(bq=bq, qst=qst, qstb=qstb, hh=hh):
                            rope(bq, qst[64:128, 0:n], qstb, qst[0:64, 0:n], qstb)
                            S.dma("pool", QS[hh, :, t0:t0 + n], qst[:, 0:n], r=[qstb], w=[QSB[hh][j]])
                        rpend.append(rq)
                    else:
                        S.dma("pool", QS[hh, 64:128, t0:t0 + n], qst[64:128, 0:n], r=[qstb], w=[QSB[hh][j]])
                    yield
                for g in range(2):
                    bk = banks.next()
                    proj(2048 + 128 * g, bk)
                    if isc:
                        CP("act", Kctx[64:128, g, :], PS[64:128, bk, 0:n], [PB[bk]], [KCB])
                        run_rpend()
                    else:
                        kr, krb = kraw_ring.next()
                        CP("act", kr[64:128, 0:n], PS[64:128, bk, 0:n], [PB[bk]], [krb])
                        run_rpend()

                        def rk(bk=bk, kr=kr, krb=krb, g=g):
                            rope(bk, kr[64:128, 0:n], krb, Klat[0:64, g, t0:t0 + n], KLB[j])
                        rpend.append(rk)
                run_rpend()
                for b in range(n // 128):
                    bv = banks.next()
                    for kc in range(8):
                        MM(PS[:, bv, 0:128], h[:, kc, b * 128:(b + 1) * 128], Wt[:, kc, 2304:2432], kc == 0, kc == 7, [hb, WB[4]], [PB[bv]])
                    blk = (t0 // 128) + b
                    CP("act" if b % 2 else "dve", Vt[:, blk, :, 0:64], PS[:, bv, 0:128].rearrange("p (g d) -> p g d", g=2), [PB[bv]], [VB[j]])
            def pn_gen(jj):
                t0_, n_, isc_ = tiles[TL[jj]]
                s_ = 1 if isc_ else 0
                return prenorm_gen(rings, loads[jj][0], loads[jj][1], n_, ACOL[:, l, s_, :], MODT[:, l, s_, SH0:SH0 + 8], 7)

            def drain(g_):
                while True:
                    try:
                        next(g_)
                    except StopIteration as e_:
                        return e_.value

            issue_load(0)
            if len(TL) > 1:
                issue_load(1)
            hcur = drain(pn_gen(0))
            for ji in range(len(TL)):
                gen = body(ji, hcur)
                k = 0
                pn = None
                hnext = None
                for _ in gen:
                    k += 1
                    if k == SPLIT and ji + 1 < len(TL):
                        pn = pn_gen(ji + 1)
                    if pn is not None:
                        try:
                            next(pn)
                        except StopIteration as e_:
                            hnext = e_.value
                            pn = None
                            if ji + 2 < len(TL):
                                issue_load(ji + 2)
                if ji + 1 < len(TL) and hnext is None:
                    if pn is None:
                        pn = pn_gen(ji + 1)
                    hnext = drain(pn)
                    if ji + 2 < len(TL):
                        issue_load(ji + 2)
                hcur = hnext
                loads.pop(ji)
            S.barrier()

    def phase_conva():
        with ExitStack() as es:
            Dg = sb(es, "DgA", [128, 4, 31, 128], BF16)
            DgB = Buf()
            for c in range(4):
                for k in range(31):
                    col = cols["acw"][:, k * 4 + c:k * 4 + c + 1]
                    TS("dve", Dg[:, c, k, :], identB[:], col, None, ALU.mult, None, [CB], [DgB])
            up_ring = Ring(nc, es, "up", [128, 4, 512 + 30], BF16, 2)
            ucv_ring = Ring(nc, es, "ucv", [128, 4, 512], F32, 2)
            usq_ring = Ring(nc, es, "usq", [128, 4, 512], F32, 2)
            st_ring = Ring(nc, es, "lnst", [128, 3, 512], F32, 2)
            tt_ring = Ring(nc, es, "lntt", [128, 512], F32, 2)
            ao_ring = Ring(nc, es, "ao", [128, 4, 512], BF16, 2)
            U0v = U0.rearrange("(c p) t -> p c t", p=128)
            A0v = A0.rearrange("(c p) t -> p c t", p=128)
            banks = Rot([0, 1, 2, 3])
            wring = Ring(nc, es, "wmod", [128, 8, 768], F32, 2)
            mod_jb = [0]
            mod_q = []

            def mod_step():
                k = mod_jb[0]
                if k > 8:
                    return
                if k < 8:
                    mod_q.append((k,) + mod_load(1, k, wring))
                if k >= 1:
                    kk, wt_, wb_ = mod_q.pop(0)
                    mod_mm(1, kk, wt_, wb_)
                mod_jb[0] += 1

            for j, (t0, n, isc) in enumerate(tiles):
                mod_step()
                seg0, seg1 = (T, TT) if isc else (0, T)
                lo, hi = max(t0 - 15, seg0), min(t0 + n + 15, seg1)
                up, upb = up_ring.next()
                rd = [U0B[j]]
                if j > 0 and not isc:
                    rd.append(U0B[j - 1])
                if j + 1 < NT:
                    rd.append(U0B[j + 1])
                if lo > t0 - 15:
                    MSET("pool", up[:, :, 0:15], 0.0, [upb])
                if hi < t0 + n + 15:
                    MSET("pool", up[:, :, n + 15:n + 30], 0.0, [upb])
                S.dma("sp", up[:, :, lo - (t0 - 15):hi - (t0 - 15)], U0v[:, :, lo:hi], r=rd, w=[upb])
                ucv, ucvb = ucv_ring.next()
                usq, usqb = usq_ring.next()
                for c in range(4):
                    bk = banks.next()
                    for k in range(31):
                        MM(PS[:, bk, 0:n], Dg[:, c, k, :], up[:, c, k:k + n], k == 0, k == 30, [upb, DgB], [PB[bk]])
                    ACT(ucv[:, c, 0:n], PS[:, bk, 0:n], AF.Identity, [PB[bk], CB], [ucvb], bias=cols["acb"][:, c:c + 1])
                    ACT(usq[:, c, 0:n], PS[:, bk, 0:n], AF.Square, [PB[bk], CB], [usqb], bias=cols["acb"][:, c:c + 1])
                for c in range(4):
                    MM(PS[:, 4, 0:n], onesF[:], ucv[:, c, 0:n], c == 0, c == 3, [ucvb, CB], [PB[4]])
                for c in range(4):
                    MM(PS[:, 5, 0:n], onesF[:], usq[:, c, 0:n], c == 0, c == 3, [usqb, CB], [PB[5]])
                st, stb = st_ring.next()
                TS("dve", st[:, 0, 0:n], PS[:, 4, 0:n], 1.0 / 512, None, ALU.mult, None, [PB[4]], [stb])
                TTo("dve", st[:, 1, 0:n], st[:, 0, 0:n], st[:, 0, 0:n], ALU.mult, [stb], [stb])
                STT(st[:, 2, 0:n], PS[:, 5, 0:n], 1.0 / 512, st[:, 1, 0:n], ALU.mult, ALU.subtract, [PB[5], stb], [stb])
                ACT(st[:, 2, 0:n], st[:, 2, 0:n], AF.Sqrt, [stb, CB], [stb], bias=epsT[:, 0:1])
                RCP(st[:, 2, 0:n], st[:, 2, 0:n], [stb], [stb])
                ao, aob = ao_ring.next()
                for c in range(4):
                    tt, ttb = tt_ring.next()
                    TTo("dve", tt[:, 0:n], ucv[:, c, 0:n], st[:, 0, 0:n], ALU.subtract, [ucvb, stb], [ttb])
                    TTo("dve", tt[:, 0:n], tt[:, 0:n], st[:, 2, 0:n], ALU.mult, [ttb, stb], [ttb])
                    ACT(ao[:, c, 0:n], tt[:, 0:n], AF.Silu, [ttb, CB], [aob], bias=cols["alb"][:, c:c + 1], scale=cols["alg"][:, c:c + 1])
                S.dma("pool", A0v[:, :, t0:t0 + n], ao[:, :, 0:n], r=[aob], w=[A0B[j]])
            while mod_jb[0] <= 8:
                mod_step()
            mod_finish(1)
            S.barrier()

    def attn_finalize_a(rings, acc, n, cp_eng="act"):
        osb, ob = rings["osb"].next()
        CP(cp_eng, osb[0:65, 0:n], PS[0:65, acc, 0:n], [PB[acc]], [ob])
        return osb, ob

    def attn_finalize_b(rings, osb, ob, n, extra_col, dst_dram, dstb, dbank):
        hl, hlb = rings["hl"].next()
        CP("dve", hl[64:65, 0, 0:n], osb[64:65, 0:n], [ob], [hlb])
        TTo("dve", hl[64:65, 1, 0:n], osb[64:65, 0:n], hl[64:65, 0, 0:n], ALU.subtract, [ob, hlb], [hlb])
        MM(PS[0:64, dbank, 0:n], onesB[64:65, 0:64], hl[64:65, 0, 0:n], True, False, [hlb, CB], [PB[dbank]])
        MM(PS[0:64, dbank, 0:n], onesB[64:65, 0:64], hl[64:65, 1, 0:n], False, True, [hlb, CB], [PB[dbank]])
        rd, rdb = rings["rd"].next()
        if extra_col is not None:
            TS("dve", rd[0:64, 0:n], PS[0:64, dbank, 0:n], extra_col, None, ALU.add, None, [PB[dbank], CB], [rdb])
            RCP(rd[0:64, 0:n], rd[0:64, 0:n], [rdb], [rdb])
        else:
            RCP(rd[0:64, 0:n], PS[0:64, dbank, 0:n], [PB[dbank]], [rdb])
        bt, btb = rings["bt"].next()
        TTo("dve", bt[0:64, 0:n], osb[0:64, 0:n], rd[0:64, 0:n], ALU.mult, [ob, rdb], [btb])
        S.dma("pool", dst_dram, bt[0:64, 0:n], r=[btb], w=[dstb])

    def attn_finalize(rings, acc, n, extra_col, dst_dram, dstb, dbank, cp_eng="act"):
        osb, ob = attn_finalize_a(rings, acc, n, cp_eng)
        attn_finalize_b(rings, osb, ob, n, extra_col, dst_dram, dstb, dbank)

    def attn_rings(es):
        return {
            "osb": Ring(nc, es, "osb", [128, 512], F32, 3),
            "rd": Ring(nc, es, "rd", [64, 512], F32, 2),
            "hl": Ring(nc, es, "hl", [128, 2, 512], BF16, 2),
            "bt": Ring(nc, es, "bt", [64, 512], BF16, 2),
        }

    def phase_attn0():
        with ExitStack() as es:
            rings = attn_rings(es)
            pt_ring = Ring(nc, es, "pt", [128, 512], BF16, 4)
            sbanks = Rot([0, 1, 2, 3])
            abanks = Rot([4, 5])
            dbanks = Rot([6, 7])
            qt_ring = Ring(nc, es, "qt", [128, 512], BF16, 3)
            pend = []
            for hh in range(8):
                g = hh // 4
                for j, (t0, n, isc) in enumerate(tiles):
                    qt, qtb = qt_ring.next()
                    if isc:
                        S.dma("sp", qt[64:128, 0:n], QS[hh, 64:128, t0:t0 + n], r=[QSB[hh][j]], w=[qtb])
                    else:
                        S.dma("sp", qt[:, 0:n], QS[hh, :, t0:t0 + n], r=[QSB[hh][j]], w=[qtb])
                    steps = []
                    for cc in range(CTX // 128):
                        steps.append((Kctx[64:128, g, cc * 128:(cc + 1) * 128], qt[64:128, 0:n],
                                      [KCB, qtb], NBL + cc, 0, n, []))
                    if not isc:
                        i4 = t0 // 128
                        for jb in range(i4 - 1, i4 + 5):
                            if jb < 0 or jb >= NBL:
                                continue
                            qb0, qb1 = max(jb - 1, i4), min(jb + 1, i4 + 3)
                            c0, c1 = (qb0 - i4) * 128, (qb1 - i4 + 1) * 128
                            masks = []
                            for qb in range(qb0, qb1 + 1):
                                if qb == jb - 1:
                                    masks.append(((qb - qb0) * 128, m1B))
                                elif qb == jb + 1:
                                    masks.append(((qb - qb0) * 128, m2B))
                            steps.append((Klat[0:64, g, jb * 128:(jb + 1) * 128], qt[0:64, c0:c1],
                                          [KLB[jb // 4], qtb], jb, c0, c1, masks))
                    acc = abanks.next()
                    for si, (lhsT, rhs, rdb_, vblk, c0, c1, masks) in enumerate(steps):
                        m = c1 - c0
                        sbk = sbanks.next()
                        MM(PS[:, sbk, 0:m], lhsT, rhs, True, True, rdb_, [PB[sbk]])
                        pt, ptb = pt_ring.next()
                        ACT(pt[:, 0:m], PS[:, sbk, 0:m], AF.Exp, [PB[sbk]], [ptb], scale=0.125)
                        for (mo, mk) in masks:
                            TTo("pool", pt[:, mo:mo + 128], pt[:, mo:mo + 128], mk[:], ALU.mult, [ptb, CB], [ptb])
                        while len(pend) >= 2:
                            pend.pop(0)()

                        def later(acc=acc, c0=c0, c1=c1, vblk=vblk, g=g, pt=pt, ptb=ptb, m=m, si=si, ns=len(steps), n=n, hh=hh, t0=t0, j=j):
                            MM(PS[0:65, acc, c0:c1], Vt[:, vblk, g, 0:65], pt[:, 0:m], si == 0, si == ns - 1,
                               [ptb, VB[min(vblk // 4, NT)]], [PB[acc]])
                            if si == ns - 1:
                                osb, ob = attn_finalize_a(rings, acc, n, "act")
                                pend.append(lambda: attn_finalize_b(rings, osb, ob, n, esink[0:64, hh:hh + 1], B0[hh * 64:(hh + 1) * 64, t0:t0 + n], B0B[j], dbanks.next()))
                        pend.append(later)
            while pend:
                pend.pop(0)()
            S.barrier()

    def phase_wout(l, Wsrc, Asrc, ASB, Bsrc, BSB, tl):
        with ExitStack() as es:
            Wa = sb(es, "Wa", [128, 4, D], BF16)
            Wb = sb(es, "Wb", [64, 8, D], BF16)
            WB = Buf()
            cast_load(Wa[:], Wsrc[0:512, :].rearrange("(c p) n -> p c n", p=128), WB)
            cast_load(Wb[:], Wsrc[512:1024, :].rearrange("(h d) n -> d h n", d=64), WB)
            rings = norm_rings(es, with_h=False)
            x_ring = Ring(nc, es, "xt", [128, 8, 512], F32, 2)
            a_ring = Ring(nc, es, "at", [128, 4, 512], BF16, 2)
            b_ring = Ring(nc, es, "bt2", [64, 8, 512], BF16, 2)
            y_ring = Ring(nc, es, "ysb", [128, 8, 512], F32, 2)
            Av = Asrc.rearrange("(c p) t -> p c t", p=128)
            Bv = Bsrc.rearrange("(h d) t -> d h t", d=64)
            banks = Rot([0, 1, 2, 3])
            loads = {}

            def issue_load(ji):
                j = tl[ji]
                t0, n, isc = tiles[j]
                xt, xb = x_ring.next()
                S.dma("sp", xt[:, :, 0:n], XTv[:, :, t0:t0 + n], r=[XTB[j]], w=[xb])
                at, ab = a_ring.next()
                S.dma("sp", at[:, :, 0:n], Av[:, :, t0:t0 + n], r=[ASB[j]], w=[ab])
                bt, bb = b_ring.next()
                S.dma("sp", bt[:, :, 0:n], Bv[:, :, t0:t0 + n], r=[BSB[j]], w=[bb])
                loads[ji] = (xt, xb, at, ab, bt, bb)

            issue_load(0)
            for ji, j in enumerate(tl):
                t0, n, isc = tiles[j]
                if ji + 1 < len(tl):
                    issue_load(ji + 1)
                xt, xb, at, ab, bt, bb = loads.pop(ji)
                s = 1 if isc else 0
                ysb, yb = y_ring.next()
                for fc in range(8):
                    bk = banks.next()
                    for c in range(4):
                        MM(PS[:, bk, 0:n], Wa[:, c, fc * 128:(fc + 1) * 128], at[:, c, 0:n], c == 0, False, [WB, ab], [PB[bk]])
                    for hh in range(8):
                        MM(PS[:, bk, 0:n], Wb[0:64, hh, fc * 128:(fc + 1) * 128], bt[0:64, hh, 0:n], False, hh == 7, [WB, bb], [PB[bk]])
                    CP("act", ysb[:, fc, 0:n], PS[:, bk, 0:n], [PB[bk]], [yb])
                postnorm_residual(rings, ysb, yb, xt, xb, n, G1[:, l, s, :], 7)
                S.dma("pool", XTv[:, :, t0:t0 + n], xt[:, :, 0:n], r=[xb], w=[XTB[j]])
            S.barrier()

    def phase_ffna(l, tl):
        with ExitStack() as es:
            Wu = sb(es, "Wu", [128, 8, 2 * FFN], BF16)
            WB = [Buf() for _ in range(8)]
            wsrc = W["ffn_w_up"][l].rearrange("(kc p) n -> p kc n", p=128)
            for blk in range(8):
                c0 = blk * 704
                cast_load(Wu[:, :, c0:c0 + 704], wsrc[:, :, c0:c0 + 704], WB[blk])
            rings = norm_rings(es)
            x_ring = Ring(nc, es, "xt", [128, 8, 512], F32, 2)
            st_ring = Ring(nc, es, "gst", [128, 4, 512], BF16, 3)
            banks = Rot([0, 1, 2, 3, 4, 5])
            GSv = GS.rearrange("(c p) t -> p c t", p=128)
            USv = US.rearrange("(c p) t -> p c t", p=128)
            loads = {}

            def issue_load(ji):
                j = tl[ji]
                t0, n, isc = tiles[j]
                xt, xb = x_ring.next()
                S.dma("sp", xt[:, :, 0:n], XTv[:, :, t0:t0 + n], r=[XTB[j]], w=[xb])
                loads[ji] = (xt, xb)

            TL = tl
            ACOL, SH0, SPLIT = A2, 24, 4

            def body(ji, hcur_):
                j = tl[ji]
                t0, n, isc = tiles[j]
                xt, xb = loads[ji]
                h, hb = hcur_
                for part, (dstv, dstB) in enumerate(((GSv, GSB), (USv, USB))):
                    k = 0
                    while k < NJ:
                        m = min(4, NJ - k)
                        st, stb = st_ring.next()
                        for q in range(m):
                            fc = part * NJ + k + q
                            bk = banks.next()
                            wb = WB[(fc * 128) // 704]
                            wb2 = WB[(fc * 128 + 127) // 704]
                            for kc in range(8):
                                MM(PS[:, bk, 0:n], Wu[:, kc, fc * 128:(fc + 1) * 128], h[:, kc, 0:n], kc == 0, kc == 7, [hb, wb, wb2], [PB[bk]])
                            CP("act" if (q % 2) else "dve", st[:, q, 0:n], PS[:, bk, 0:n], [PB[bk]], [stb])
                        S.dma("pool", dstv[:, k:k + m, t0:t0 + n], st[:, 0:m, 0:n], r=[stb], w=[dstB[j]])
                        yield
                        k += m
            def pn_gen(jj):
                t0_, n_, isc_ = tiles[TL[jj]]
                s_ = 1 if isc_ else 0
                return prenorm_gen(rings, loads[jj][0], loads[jj][1], n_, ACOL[:, l, s_, :], MODT[:, l, s_, SH0:SH0 + 8], 7)

            def drain(g_):
                while True:
                    try:
                        next(g_)
                    except StopIteration as e_:
                        return e_.value

            issue_load(0)
            if len(TL) > 1:
                issue_load(1)
            hcur = drain(pn_gen(0))
            for ji in range(len(TL)):
                gen = body(ji, hcur)
                k = 0
                pn = None
                hnext = None
                for _ in gen:
                    k += 1
                    if k == SPLIT and ji + 1 < len(TL):
                        pn = pn_gen(ji + 1)
                    if pn is not None:
                        try:
                            next(pn)
                        except StopIteration as e_:
                            hnext = e_.value
                            pn = None
                            if ji + 2 < len(TL):
                                issue_load(ji + 2)
                if ji + 1 < len(TL) and hnext is None:
                    if pn is None:
                        pn = pn_gen(ji + 1)
                    hnext = drain(pn)
                    if ji + 2 < len(TL):
                        issue_load(ji + 2)
                hcur = hnext
                loads.pop(ji)
            S.barrier()

    def phase_ffnb(l, tl):
        with ExitStack() as es:
            Wd = sb(es, "Wd", [128, NJ, D], BF16)
            WB = [Buf() for _ in range(2)]
            wsrc = W["ffn_w_down"][l].rearrange("(j p) n -> p j n", p=128)
            cast_load(Wd[:, 0:11, :], wsrc[:, 0:11, :], WB[0])
            cast_load(Wd[:, 11:22, :], wsrc[:, 11:22, :], WB[1])
            Dg = sb(es, "DgF", [128, NJ, 3, 128], BF16)
            DgB = Buf()
            fcw = cols["fcw%d" % l]
            for jj in range(NJ):
                for k in range(3):
                    TS("dve", Dg[:, jj, k, :], identB[:], fcw[:, k * NJ + jj:k * NJ + jj + 1], None, ALU.mult, None, [CB], [DgB])
            rings = norm_rings(es, with_h=False, nsq=1)
            x_ring = Ring(nc, es, "xt", [128, 8, 512], F32, 1)
            gH = [sb(es, "gtH%d" % i, [128, 11, 514], BF16) for i in range(2)]
            uH = [sb(es, "utH%d" % i, [128, 11, 512], BF16) for i in range(2)]
            gHB = [Buf(), Buf()]
            uHB = [Buf(), Buf()]
            ga_ring = Ring(nc, es, "ga", [128, 512], BF16, 3)
            act_ring = Ring(nc, es, "actt", [128, NJ, 512], BF16, 1)
            y_ring = Ring(nc, es, "ysb", [128, 8, 512], F32, 2)
            GSv = GS.rearrange("(c p) t -> p c t", p=128)
            USv = US.rearrange("(c p) t -> p c t", p=128)
            cbanks = Rot([0, 1, 2, 3])
            dbanks = Rot([4, 5, 6])
            loads = {}

            def load_gu(ji):
                j = tl[ji]
                t0, n, isc = tiles[j]
                seg0, seg1 = (T, TT) if isc else (0, T)
                lo, hi = max(t0 - 1, seg0), min(t0 + n + 1, seg1)
                rd = [GSB[j]]
                if j > 0 and not isc:
                    rd.append(GSB[j - 1])
                if j + 1 < NT:
                    rd.append(GSB[j + 1])
                for half in range(2):
                    gt, gb = gH[half], gHB[half]
                    if lo > t0 - 1:
                        MSET("pool", gt[:, :, 0:1], 0.0, [gb])
                    if hi < t0 + n + 1:
                        MSET("pool", gt[:, :, n + 1:n + 2], 0.0, [gb])
                    S.dma("sp", gt[:, :, lo - (t0 - 1):hi - (t0 - 1)], GSv[:, half * 11:(half + 1) * 11, lo:hi], r=rd, w=[gb])
                    S.dma("sp", uH[half][:, :, 0:n], USv[:, half * 11:(half + 1) * 11, t0:t0 + n], r=[USB[j]], w=[uHB[half]])

            def load_x(ji):
                j = tl[ji]
                t0, n, isc = tiles[j]
                xt, xb = x_ring.next()
                S.dma("sp", xt[:, :, 0:n], XTv[:, :, t0:t0 + n], r=[XTB[j]], w=[xb])
                loads[ji] = (xt, xb)

            acts = {}

            def conv(ji):
                j = tl[ji]
                t0, n, isc = tiles[j]
                actt, actb = act_ring.next()
                for jj in range(NJ):
                    bk = cbanks.next()
                    gt, gb, ut, ub = gH[jj // 11], gHB[jj // 11], uH[jj // 11], uHB[jj // 11]
                    for k in range(3):
                        MM(PS[:, bk, 0:n], Dg[:, jj, k, :], gt[:, jj % 11, k:k + n], k == 0, k == 2, [gb, DgB], [PB[bk]])
                    ga, gab = ga_ring.next()
                    ACT(ga[:, 0:n], PS[:, bk, 0:n], AF.Gelu_apprx_tanh, [PB[bk], CB], [gab], bias=cols["fcb"][:, l * NJ + jj:l * NJ + jj + 1])
                    TTo("dve" if (jj % 2) else "pool", actt[:, jj, 0:n], ga[:, 0:n], ut[:, jj % 11, 0:n], ALU.mult, [gab, ub], [actb])
                acts[ji] = (actt, actb)
                if ji + 1 < len(tl):
                    load_gu(ji + 1)

            load_gu(0)
            load_x(0)
            conv(0)
            for ji, j in enumerate(tl):
                t0, n, isc = tiles[j]
                xt, xb = loads.pop(ji)
                s = 1 if isc else 0
                actt, actb = acts.pop(ji)
                ysb, yb = y_ring.next()
                for fc in range(8):
                    bk = dbanks.next()
                    for jj in range(NJ):
                        MM(PS[:, bk, 0:n], Wd[:, jj, fc * 128:(fc + 1) * 128], actt[:, jj, 0:n], jj == 0, jj == NJ - 1, [actb] + WB, [PB[bk]])
                    CP("act", ysb[:, fc, 0:n], PS[:, bk, 0:n], [PB[bk]], [yb])
                if ji + 1 < len(tl):
                    conv(ji + 1)
                postnorm_residual(rings, ysb, yb, xt, xb, n, G2[:, l, s, :], 7)
                S.dma("pool", XTv[:, :, t0:t0 + n], xt[:, :, 0:n], r=[xb], w=[XTB[j]])
                if ji + 1 < len(tl):
                    load_x(ji + 1)
            S.barrier()

    def phase_tout():
        with ExitStack() as es:
            x_ring = Ring(nc, es, "xt", [128, 8, 512], F32, 2)
            o_ring = Ring(nc, es, "ot", [128, D], F32, 3)
            banks = Rot([(0, 1), (2, 3), (4, 5), (6, 7)])
            for j, (t0, n, isc) in enumerate(lat_tiles):
                xt, xb = x_ring.next()
                S.dma("sp", xt[:, :, 0:n], XTv[:, :, t0:t0 + n], r=[XTB[j]], w=[xb])
                for b in range(n // 128):
                    b0, b1 = banks.next()
                    for fc in range(8):
                        bk = b0 if fc < 4 else b1
                        TR(PS[:, bk, (fc % 4) * 128:(fc % 4 + 1) * 128], xt[:, fc, b * 128:(b + 1) * 128], identF[:], [xb, CB], [PB[bk]])
                    ot, ob = o_ring.next()
                    CP("act", ot[:, 0:512], PS[:, b0, :], [PB[b0]], [ob])
                    CP("dve", ot[:, 512:1024], PS[:, b1, :], [PB[b1]], [ob])
                    S.dma("pool", y_out[t0 + b * 128:t0 + (b + 1) * 128, :], ot[:], r=[ob], w=[Buf()])
            S.barrier()

    L1 = ExitStack()
    L1T = {}

    def alloc_l1():
        L1T["CQN"] = sb(L1, "CQN", [128, 3, T], BF16)
        L1T["CKVN"] = sb(L1, "CKVN", [128, 2, TT], BF16)
        L1T["KRb"] = sb(L1, "KRb", [64, TT], BF16)

    CQB = [Buf() for _ in tiles]
    CKB = [Buf() for _ in tiles]
    KRB = [Buf() for _ in tiles]

    def phase_p1_l1():
        l = 1
        CQN, CKVN, KRb = L1T["CQN"], L1T["CKVN"], L1T["KRb"]
        with ExitStack() as es:
            Wt = sb(es, "Wt1", [128, 8, 1728], BF16)
            WB = [Buf() for _ in range(3)]
            wsrc = W["cd_w_in"][0].rearrange("(kc p) n -> p kc n", p=128)
            cast_load(Wt[:, :, 0:1024], wsrc[:, :, 0:1024], WB[0])
            cast_load(Wt[:, :, 1024:1664], wsrc[:, :, 1024:1664], WB[1])
            cast_load(Wt[:, :, 1664:1696], wsrc[:, :, 1664:1696], WB[2])
            cast_load(Wt[:, :, 1696:1728], wsrc[:, :, 1664:1696], WB[2])
            MSET("pool", KRb[32:64, 0:T], 0.0, KRB[:NT])
            MSET("pool", KRb[0:32, T:TT], 0.0, [KRB[NT]])
            rings = norm_rings(es)
            x_ring = Ring(nc, es, "xt", [128, 8, 512], F32, 2)
            xst_ring = Ring(nc, es, "xst", [128, 4, 512], F32, 1)
            gst_ring = Ring(nc, es, "gst1", [128, 4, 512], F32, 1)
            cqs_ring = Ring(nc, es, "cqs", [128, 3, 512], F32, 1)
            cs_ring = Ring(nc, es, "csm", [32, 2, 512], F32, 3)
            t1_ring = Ring(nc, es, "t1m", [32, 512], F32, 2)
            t2_ring = Ring(nc, es, "t2m", [32, 512], F32, 2)
            krs_ring = Ring(nc, es, "krs", [64, 512], BF16, 2)
            banks = Rot([0, 1, 2, 3, 4])
            XBv = XBS.rearrange("(c p) t -> p c t", p=128)
            GTv = GTS.rearrange("(c p) t -> p c t", p=128)
            loads = {}

            def issue_load(j):
                t0, n, isc = tiles[j]
                xt, xb = x_ring.next()
                S.dma("sp", xt[:, :, 0:n], XTv[:, :, t0:t0 + n], r=[XTB[j]], w=[xb])
                cs, csb = cs_ring.next()
                if not isc:
                    S.dma("sp", cs[:, 0, :], cmv[:, t0:t0 + n], w=[csb])
                    S.dma("sp", cs[:, 1, :], smv[:, t0:t0 + n], w=[csb])
                loads[j] = (xt, xb, cs, csb)

            TL = list(range(len(tiles)))
            ACOL, SH0, SPLIT = A1, 0, 2

            def body(j, hcur_):
                t0, n, isc = tiles[j]
                xt, xb, cs, csb = loads[j]
                h, hb = hcur_

                def proj(col0, bank, M=128):
                    wdep = [WB[0]] if col0 < 1024 else ([WB[1]] if col0 < 1664 else [WB[2]])
                    for kc in range(8):
                        MM(PS[0:M, bank, 0:n], Wt[:, kc, col0:col0 + M], h[:, kc, 0:n], kc == 0, kc == 7, [hb] + wdep, [PB[bank]])

                xst, xstb = xst_ring.next()
                for c in range(4):
                    bk = banks.next()
                    proj(128 * c, bk)
                    CP("act" if c % 2 else "dve", xst[:, c, 0:n], PS[:, bk, 0:n], [PB[bk]], [xstb])
                    yield
                S.dma("pool", XBv[:, :, t0:t0 + n], xst[:, :, 0:n], r=[xstb], w=[XBSB[j]])
                if not isc:
                    gst, gstb = gst_ring.next()
                    for c in range(4):
                        bk = banks.next()
                        proj(512 + 128 * c, bk)
                        ACT(gst[:, c, 0:n], PS[:, bk, 0:n], AF.Gelu_apprx_tanh, [PB[bk]], [gstb])
                        yield
                    S.dma("pool", GTv[:, :, t0:t0 + n], gst[:, :, 0:n], r=[gstb], w=[GTSB[j]])

                def lowrank_norm(col0, nch, dim, gcol, dst, dstb):
                    cqs, cqsb = cqs_ring.next()
                    for c in range(nch):
                        bk = banks.next()
                        proj(col0 + 128 * c, bk)
                        CP("act" if c % 2 else "dve", cqs[:, c, 0:n], PS[:, bk, 0:n], [PB[bk]], [cqsb])
                    sq, sqb = rings["sq"].next()
                    TTo("pool", sq[:, 0:nch, 0:n], cqs[:, 0:nch, 0:n], cqs[:, 0:nch, 0:n], ALU.mult, [cqsb], [sqb])
                    rs, rsb = stat_rstd(rings, sq, sqb, nch, n, dim, 7)
                    TTo("dve", cqs[:, 0:nch, 0:n], cqs[:, 0:nch, 0:n], rs[:, 0:n].unsqueeze(1).to_broadcast([128, nch, n]), ALU.mult, [cqsb, rsb], [cqsb])
                    for c in range(nch):
                        ACT(dst[:, c, t0:t0 + n], cqs[:, c, 0:n], AF.Identity, [cqsb, CB], [dstb], scale=gcol[:, c:c + 1])

                if not isc:
                    lowrank_norm(1024, 3, 384, cols["qn"], CQN, CQB[j])
                lowrank_norm(1408, 2, 256, cols["kvn"], CKVN, CKB[j])
                bk = banks.next()
                proj(1664, bk, M=64)
                if isc:
                    CP("act", KRb[32:64, t0:t0 + n], PS[32:64, bk, 0:n], [PB[bk]], [KRB[j]])
                else:
                    krs, krsb = krs_ring.next()
                    CP("act", krs[32:64, 0:n], PS[32:64, bk, 0:n], [PB[bk]], [krsb])
                    rbk = 5
                    MM(PS[0:32, rbk, 0:n], r32B[32:64, :], krs[32:64, 0:n], True, True, [krsb, CB], [PB[rbk]])
                    t1, t1b = t1_ring.next()
                    t2, t2b = t2_ring.next()
                    TTo("dve", t1[:, 0:n], PS[0:32, bk, 0:n], cs[:, 0, 0:n], ALU.mult, [PB[bk], csb], [t1b])
                    TTo("dve", t2[:, 0:n], PS[0:32, rbk, 0:n], cs[:, 1, 0:n], ALU.mult, [PB[rbk], csb], [t2b])
                    TTo("pool", KRb[0:32, t0:t0 + n], t1[:, 0:n], t2[:, 0:n], ALU.add, [t1b, t2b], [KRB[j]])
            def pn_gen(jj):
                t0_, n_, isc_ = tiles[TL[jj]]
                s_ = 1 if isc_ else 0
                return prenorm_gen(rings, loads[jj][0], loads[jj][1], n_, ACOL[:, l, s_, :], MODT[:, l, s_, SH0:SH0 + 8], 7)

            def drain(g_):
                while True:
                    try:
                        next(g_)
                    except StopIteration as e_:
                        return e_.value

            issue_load(0)
            if len(TL) > 1:
                issue_load(1)
            hcur = drain(pn_gen(0))
            for ji in range(len(TL)):
                gen = body(ji, hcur)
                k = 0
                pn = None
                hnext = None
                for _ in gen:
                    k += 1
                    if k == SPLIT and ji + 1 < len(TL):
                        pn = pn_gen(ji + 1)
                    if pn is not None:
                        try:
                            next(pn)
                        except StopIteration as e_:
                            hnext = e_.value
                            pn = None
                            if ji + 2 < len(TL):
                                issue_load(ji + 2)
                if ji + 1 < len(TL) and hnext is None:
                    if pn is None:
                        pn = pn_gen(ji + 1)
                    hnext = drain(pn)
                    if ji + 2 < len(TL):
                        issue_load(ji + 2)
                hcur = hnext
                loads.pop(ji)
            S.barrier()

    def phase_lru():
        with ExitStack() as es:
            GW = sb(es, "GW", [128, 2, 2, 4, 128], BF16)
            GWB = Buf()
            MSET("pool", GW[:], 0.0, [GWB])
            for d in range(2):
                for gate in range(2):
                    for nb in range(8):
                        p0 = (nb % 2) * 64
                        cast_load(GW[p0:p0 + 64, d, gate, nb // 2, p0:p0 + 64], W["lru_gate_w"][0, d, gate, nb], GWB)
            ytmp = sb(es, "ytmp", [128, 8], F32)
            yb_ = Buf()
            ACT(ytmp[:], cols["lam"][:], AF.Exp, [CB], [yb_], scale=-1.0)
            ACT(ytmp[:], ytmp[:], AF.Ln, [yb_, CB], [yb_], bias=onesF[:, 0:1])
            TS("dve", cch[:, 0, :], ytmp[:], -8.0, None, ALU.mult, None, [yb_], [CB])
            TS("dve", cch[:, 1, :], ytmp[:], -16.0, None, ALU.mult, None, [yb_], [CB])
            xb_ring = Ring(nc, es, "xbt", [128, 4, 515], F32, 2)
            xc_ring = Ring(nc, es, "xc", [128, 4, 512], F32, 2)
            xcb_ring = Ring(nc, es, "xcb", [128, 4, 512], BF16, 2)
            rg_ring = Ring(nc, es, "rg", [128, 4, 512], F32, 2)
            ig_ring = Ring(nc, es, "ig", [128, 4, 512], F32, 2)
            av_ring = Ring(nc, es, "av", [128, 4, 512], F32, 2)
            e2_ring = Ring(nc, es, "e2", [128, 4, 512], F32, 2)
            hv_rings = [Ring(nc, es, "hv%d" % d_, [128, 4, 512], F32, 2) for d_ in range(2)]
            XBv = XBS.rearrange("(c p) t -> p c t", p=128)
            GTv = GTS.rearrange("(c p) t -> p c t", p=128)
            HFv = HFS.rearrange("(c p) t -> p c t", p=128)
            C1v = C1.rearrange("(c p) t -> p c t", p=128)
            banks = Rot([0, 1, 2, 3, 4, 5])
            HBv = HBS.rearrange("(c p) t -> p c t", p=128)
            orders = [[NT] + list(range(NT)), [NT] + list(range(NT - 1, -1, -1))]
            prevs = [None, None]

            def stageA(d, j):
                t0, n, isc = tiles[j]
                seg0, seg1 = (T, TT) if isc else (0, T)
                xbt, xbb = xb_ring.next()
                rd = [XBSB[j]]
                if d == 0:
                    lo, hi = max(t0 - 3, seg0), t0 + n
                    if lo > t0 - 3:
                        MSET("pool", xbt[:, :, 0:3], 0.0, [xbb])
                    elif j > 0:
                        rd.append(XBSB[j - 1])
                    S.dma("sp", xbt[:, :, lo - (t0 - 3):n + 3], XBv[:, :, lo:hi], r=rd, w=[xbb])
                else:
                    lo, hi = t0, min(t0 + n + 3, seg1)
                    if hi < t0 + n + 3:
                        MSET("pool", xbt[:, :, n:n + 3], 0.0, [xbb])
                    elif j + 1 < NT:
                        rd.append(XBSB[j + 1])
                    S.dma("sp", xbt[:, :, 0:hi - lo], XBv[:, :, lo:hi], r=rd, w=[xbb])
                xc, xcb_ = xc_ring.next()
                for c in range(4):
                    wc = lambda k: cols["lcw"][:, d * 16 + k * 4 + c:d * 16 + k * 4 + c + 1]
                    TS("dve", xc[:, c, 0:n], xbt[:, c, 0:n], wc(0), cols["lcb"][:, d * 4 + c:d * 4 + c + 1], ALU.mult, ALU.add, [xbb, CB], [xcb_])
                    for k in range(1, 4):
                        STT(xc[:, c, 0:n], xbt[:, c, k:k + n], wc(k), xc[:, c, 0:n], ALU.mult, ALU.add, [xbb, xcb_, CB], [xcb_])
                xcb, xcbb = xcb_ring.next()
                CP("act", xcb[:, :, 0:n], xc[:, :, 0:n], [xcb_], [xcbb])
                rg, rgb = rg_ring.next()
                ig, igb = ig_ring.next()
                av, avb = av_ring.next()
                e2, e2b = e2_ring.next()
                for c in range(4):
                    b0 = banks.next()
                    MM(PS[:, b0, 0:n], GW[:, d, 0, c, :], xcb[:, c, 0:n], True, True, [xcbb, GWB], [PB[b0]])
                    ACT(rg[:, c, 0:n], PS[:, b0, 0:n], AF.Sigmoid, [PB[b0], CB], [rgb], bias=cols["lgb"][:, d * 8 + c:d * 8 + c + 1])
                    b1 = banks.next()
                    MM(PS[:, b1, 0:n], GW[:, d, 1, c, :], xcb[:, c, 0:n], True, True, [xcbb, GWB], [PB[b1]])
                    ACT(ig[:, c, 0:n], PS[:, b1, 0:n], AF.Sigmoid, [PB[b1], CB], [igb], bias=cols["lgb"][:, d * 8 + 4 + c:d * 8 + 4 + c + 1])
                for c in range(4):
                    ACT(av[:, c, 0:n], rg[:, c, 0:n], AF.Exp, [rgb, CB], [avb], scale=cch[:, 0, d * 4 + c:d * 4 + c + 1])
                    ACT(e2[:, c, 0:n], rg[:, c, 0:n], AF.Exp, [rgb, CB], [e2b], scale=cch[:, 1, d * 4 + c:d * 4 + c + 1])
                ACT(e2[:, :, 0:n], e2[:, :, 0:n], AF.Sqrt, [e2b, CB], [e2b], bias=onesF[:, 0:1], scale=-1.0)
                return (d, j, xc, xcb_, ig, igb, av, avb, e2, e2b)

            def stageB(ctx_):
                d, j, xc, xcb_, ig, igb, av, avb, e2, e2b = ctx_
                t0, n, isc = tiles[j]
                prev = prevs[d]
                TTo("dve", e2[:, :, 0:n], e2[:, :, 0:n], ig[:, :, 0:n], ALU.mult, [e2b, igb], [e2b])
                TTo("dve", e2[:, :, 0:n], e2[:, :, 0:n], xc[:, :, 0:n], ALU.mult, [e2b, xcb_], [e2b])
                hv, hvb = hv_rings[d].next()
                for c in range(4):
                    if prev is None:
                        init, rdp = 0.0, []
                    else:
                        ph, phb, pn = prev
                        init = ph[:, c, pn - 1:pn] if d == 0 else ph[:, c, 0:1]
                        rdp = [phb]
                    if d == 0:
                        o_, a_, b_ = hv[:, c, 0:n], av[:, c, 0:n], e2[:, c, 0:n]
                    else:
                        o_, a_, b_ = hv[:, c, 0:n][:, ::-1], av[:, c, 0:n][:, ::-1], e2[:, c, 0:n][:, ::-1]
                    S.op("dve", lambda e, o_=o_, a_=a_, b_=b_, init=init: e.tensor_tensor_scan(o_, a_, b_, init, ALU.mult, ALU.add),
                         [avb, e2b] + rdp, [hvb])
                prevs[d] = (hv, hvb, n)
                if not isc:
                    if d == 0:
                        S.dma("pool", HFv[:, :, t0:t0 + n], hv[:, :, 0:n], r=[hvb], w=[HFSB[j]])
                    else:
                        S.dma("pool", HBv[:, :, t0:t0 + n], hv[:, :, 0:n], r=[hvb], w=[HBSB[j]])

            items = [(d, orders[d][step]) for step in range(NT + 1) for d in range(2)]
            pend_ctx = None
            for (d, j) in items:
                ctx_ = stageA(d, j)
                if pend_ctx is not None:
                    stageB(pend_ctx)
                pend_ctx = ctx_
            stageB(pend_ctx)
            S.barrier()

    def phase_lru_combine():
        with ExitStack() as es:
            hf_ring = Ring(nc, es, "hf", [128, 4, 512], F32, 2)
            hb_ring = Ring(nc, es, "hb", [128, 4, 512], F32, 2)
            gg_ring = Ring(nc, es, "gg", [128, 4, 512], F32, 2)
            cl_ring = Ring(nc, es, "cl", [128, 4, 512], BF16, 2)
            GTv = GTS.rearrange("(c p) t -> p c t", p=128)
            HFv = HFS.rearrange("(c p) t -> p c t", p=128)
            HBv = HBS.rearrange("(c p) t -> p c t", p=128)
            C1v = C1.rearrange("(c p) t -> p c t", p=128)
            for j, (t0, n, isc) in enumerate(lat_tiles):
                hf, hfb = hf_ring.next()
                S.dma("sp", hf[:, :, 0:n], HFv[:, :, t0:t0 + n], r=[HFSB[j]], w=[hfb])
                hb, hbb = hb_ring.next()
                S.dma("sp", hb[:, :, 0:n], HBv[:, :, t0:t0 + n], r=[HBSB[j]], w=[hbb])
                gg, ggb = gg_ring.next()
                S.dma("sp", gg[:, :, 0:n], GTv[:, :, t0:t0 + n], r=[GTSB[j]], w=[ggb])
                cl, clb = cl_ring.next()
                TTo("dve", hf[:, :, 0:n], hf[:, :, 0:n], hb[:, :, 0:n], ALU.add, [hfb, hbb], [hfb])
                TTo("pool", cl[:, :, 0:n], hf[:, :, 0:n], gg[:, :, 0:n], ALU.mult, [hfb, ggb], [clb])
                S.dma("pool", C1v[:, :, t0:t0 + n], cl[:, :, 0:n], r=[clb], w=[C1B[j]])
            S.barrier()

    def phase_mla():
        CQN, CKVN, KRb = L1T["CQN"], L1T["CKVN"], L1T["KRb"]
        with ExitStack() as es:
            Wq = sb(es, "Wq", [128, 3, 8, 128], BF16)
            Wk = sb(es, "Wk", [128, 2, 8, 64], BF16)
            Wv = sb(es, "Wv", [128, 2, 8, 64], BF16)
            WB = Buf()
            qsrc = W["mla_w_uq"][0].rearrange("(kc p) (h e) -> p kc h e", p=128, e=96)
            ksrc = W["mla_w_ukv"][0].rearrange("(kc p) (h e) -> p kc h e", p=128, e=128)
            for kc in range(3):
                cast_load(Wq[:, kc, :, 64:128], qsrc[:, kc, :, 0:64], WB)
                cast_load(Wq[:, kc, :, 0:32], qsrc[:, kc, :, 64:96], WB)
                cast_load(Wq[:, kc, :, 32:64], qsrc[:, kc, :, 64:96], WB)
            for kc in range(2):
                cast_load(Wk[:, kc, :, :], ksrc[:, kc, :, 0:64], WB)
                cast_load(Wv[:, kc, :, :], ksrc[:, kc, :, 64:128], WB)
            Va = sb(es, "Va", [128, NB, 8, 66], BF16)
            VaB = Buf()
            MSET("pool", Va[:, :, :, 64:66], 1.0, [VaB])
            rings = attn_rings(es)
            k_ring = Ring(nc, es, "Kh", [128, TT], BF16, 2)
            q_ring = Ring(nc, es, "Qh", [128, T], BF16, 2)
            pt_ring = Ring(nc, es, "ptm", [128, 2, 512], BF16, 4)
            cs_ring = Ring(nc, es, "csq", [32, 2, 512], F32, 2)
            t1_ring = Ring(nc, es, "t1q", [32, 512], F32, 2)
            t2_ring = Ring(nc, es, "t2q", [32, 512], F32, 2)
            mbanks = Rot([6, 7])
            sbanks = Rot([0, 2])
            abanks = Rot([4, 5])
            for blk in range(NB):
                bk = mbanks.next()
                for kc in range(2):
                    MM(PS[:, bk, 0:512], CKVN[:, kc, blk * 128:(blk + 1) * 128], Wv[:, kc, :, :].rearrange("p h d -> p (h d)"),
                       kc == 0, kc == 1, [CKB[min(blk // 4, NT)], WB], [PB[bk]])
                CP("act" if blk % 2 else "dve", Va[:, blk, :, 0:64], PS[:, bk, 0:512].rearrange("p (h d) -> p h d", d=64), [PB[bk]], [VaB])
            sc = float(96 ** -0.5)
            pend = []
            qst_ = [None]
            hbufs = {}

            def get_bufs(h_):
                if h_ not in hbufs:
                    Kh_, KhB_ = k_ring.next()
                    Qh_, QhB_ = q_ring.next()
                    CP("pool", Kh_[0:64, :], KRb[0:64, :], KRB, [KhB_])
                    hbufs[h_] = (Kh_, KhB_, Qh_, QhB_)
                return hbufs[h_]

            def prod_k(h_, j):
                Kh_, KhB_, Qh_, QhB_ = get_bufs(h_)
                t0, n, isc = tiles[j]
                bk = mbanks.next()
                for kc in range(2):
                    MM(PS[64:128, bk, 0:n], Wk[:, kc, h_, :], CKVN[:, kc, t0:t0 + n], kc == 0, kc == 1, [CKB[j], WB], [PB[bk]])
                CP("dve", Kh_[64:128, t0:t0 + n], PS[64:128, bk, 0:n], [PB[bk]], [KhB_])

            def prod_q_a(h_, j):
                Kh_, KhB_, Qh_, QhB_ = get_bufs(h_)
                t0, n, isc = tiles[j]
                bk = mbanks.next()
                for kc in range(3):
                    MM(PS[:, bk, 0:n], Wq[:, kc, h_, :], CQN[:, kc, t0:t0 + n], kc == 0, kc == 2, [CQB[j], WB], [PB[bk]])
                CP("dve", Qh_[32:64, t0:t0 + n], PS[32:64, bk, 0:n], [PB[bk]], [QhB_])
                CP("dve", Qh_[64:128, t0:t0 + n], PS[64:128, bk, 0:n], [PB[bk]], [QhB_])
                t1, t1b = t1_ring.next()
                cs, csb = cs_ring.next()
                S.dma("sp", cs[:, 0, :], cmv[:, t0:t0 + n], w=[csb])
                S.dma("sp", cs[:, 1, :], smv[:, t0:t0 + n], w=[csb])
                TTo("dve", t1[:, 0:n], PS[0:32, bk, 0:n], cs[:, 0, 0:n], ALU.mult, [PB[bk], csb], [t1b])
                return (t1, t1b, cs, csb)

            def prod_q_b(h_, j, st_):
                Kh_, KhB_, Qh_, QhB_ = get_bufs(h_)
                t0, n, isc = tiles[j]
                t1, t1b, cs, csb = st_
                rbk = mbanks.next()
                MM(PS[0:32, rbk, 0:n], r32B[32:64, :], Qh_[32:64, t0:t0 + n], True, True, [QhB_, CB], [PB[rbk]])
                t2, t2b = t2_ring.next()
                TTo("dve", t2[:, 0:n], PS[0:32, rbk, 0:n], cs[:, 1, 0:n], ALU.mult, [PB[rbk], csb], [t2b])
                TTo("pool", Qh_[0:32, t0:t0 + n], t1[:, 0:n], t2[:, 0:n], ALU.add, [t1b, t2b], [QhB_])

            def prod_q(h_, j):
                prod_q_b(h_, j, prod_q_a(h_, j))

            for j in range(len(tiles)):
                prod_k(0, j)
            for j in range(NT):
                prod_q(0, j)
            for hh in range(8):
                Kh, KhB, Qh, QhB = get_bufs(hh)
                for j, (t0, n, isc) in enumerate(lat_tiles):
                    acc = abanks.next()
                    ngrp = (NB + 1) // 2
                    for gi in range(ngrp):
                        kbs = [kb for kb in (2 * gi, 2 * gi + 1) if kb < NB]
                        sb0 = sbanks.next()
                        for qi, kb in enumerate(kbs):
                            MM(PS[:, sb0 + qi, 0:n], Kh[:, kb * 128:(kb + 1) * 128], Qh[:, t0:t0 + n], True, True, [KhB, QhB], [PB[sb0 + qi]])
                        pt, ptb = pt_ring.next()
                        m = len(kbs)
                        ACT(pt[:, 0:m, 0:n], PS[:, sb0:sb0 + m, 0:n], AF.Exp, [PB[sb0 + q_] for q_ in range(m)], [ptb], scale=sc)
                        while len(pend) >= 2:
                            pend.pop(0)()

                        def later(kbs=kbs, acc=acc, n=n, pt=pt, ptb=ptb, gi=gi, hh=hh, t0=t0, j=j, last=(gi == ngrp - 1)):
                            for qi, kb in enumerate(kbs):
                                MM(PS[0:65, acc, 0:n], Va[:, kb, hh, 0:65], pt[:, qi, 0:n], gi == 0 and qi == 0, kb == NB - 1, [ptb, VaB], [PB[acc]])
                            if last:
                                osb, ob = attn_finalize_a(rings, acc, n, "dve")
                                pend.append(lambda: attn_finalize_b(rings, osb, ob, n, None, D1[hh * 64:(hh + 1) * 64, t0:t0 + n], D1B[j], mbanks.next()))
                        pend.append(later)
                        if hh + 1 < 8:
                            if gi == ngrp // 5:
                                prod_k(hh + 1, j)
                            if gi == (2 * ngrp) // 5:
                                qst_[0] = prod_q_a(hh + 1, j)
                            if gi == (4 * ngrp) // 5:
                                prod_q_b(hh + 1, j, qst_[0])
                            if gi == ngrp - 1 and j == NT - 1:
                                prod_k(hh + 1, NT)
            while pend:
                pend.pop(0)()
            S.barrier()

    def layer1():
        alloc_l1()
        phase_p1_l1()
        if stop == "p1_l1":
            return
        phase_lru()
        phase_lru_combine()
        if stop == "lru":
            return
        phase_mla()
        if stop == "mla":
            return
        L1.close()
        phase_wout(1, W["cd_w_out"][0], C1, C1B, D1, D1B, lat_t)
        if stop == "wout1":
            return
        phase_ffna(1, lat_t)
        phase_ffnb(1, lat_t)

    all_t = list(range(len(tiles)))
    lat_t = list(range(NT))

    def dump_small(parts):
        off = 0
        for t, w in parts:
            S.dma("sp", DBGV[:, off:off + w], t, r=[CB], w=[Buf()])
            off += w
        S.barrier()

    def run():
        phase_mod(0)
        if stop == "mod0":
            dump_small([(MODT[:, 0, :, :].rearrange("p s m -> p (s m)"), 96), (A1[:, 0].rearrange("p s m -> p (s m)"), 16),
                        (G1[:, 0].rearrange("p s m -> p (s m)"), 16), (A2[:, 0].rearrange("p s m -> p (s m)"), 16),
                        (G2[:, 0].rearrange("p s m -> p (s m)"), 16)])
            return
        phase_tin()
        if stop == "tin":
            return
        phase_p1_l0()
        if stop == "p1_l0":
            return
        phase_conva()
        if stop == "conva":
            return
        phase_attn0()
        if stop == "attn0":
            return
        L0.close()
        phase_wout(0, W["ab_w_out"][0], A0, A0B, B0, B0B, all_t)
        if stop == "wout0":
            return
        phase_ffna(0, all_t)
        if stop == "ffna0":
            return
        phase_ffnb(0, all_t)
        if stop == "ffnb0":
            return
        layer1()
        phase_tout()

    run()
    L1.close()
    L0.close()
    ges.close()
    build.stats = (S.nops, S.nwaits)
    return nc


def make_in_maps(inputs, T):
    consts = make_consts(T)
    f = lambda a: np.ascontiguousarray(np.asarray(a, dtype=np.float32))
    shared = {k: f(inputs[k]) for k in WEIGHT_SHAPES}
    shared.update(consts)
    shared["c_ctx"] = f(inputs["c_ctx"]).reshape(8, 128)
    x, c, ctx = f(inputs["x"]), f(inputs["c"]), f(inputs["ctx"])
    maps = []
    for b in range(x.shape[0]):
        m = dict(shared)
        m["x"] = np.ascontiguousarray(x[b])
        m["ctx"] = np.ascontiguousarray(ctx[b])
        m["c"] = np.ascontiguousarray(c[b]).reshape(8, 128)
        maps.append(m)
    return maps


def kernel(**inputs):
    T = int(np.asarray(inputs["x"]).shape[1])
    nc = build(T)
    in_maps = make_in_maps(inputs, T)
    res = run_bass_kernel_spmd(nc, in_maps, core_ids=list(range(len(in_maps))))
    return np.stack([np.asarray(r["y"], dtype=np.float32) for r in res.results], axis=0)
```

```python
import numpy as np
import ml_dtypes
from contextlib import ExitStack
import concourse.bass as bass
import concourse.mybir as mybir
from concourse.bass_utils import run_bass_kernel_spmd

F32 = mybir.dt.float32
BF16 = mybir.dt.bfloat16
AF = mybir.ActivationFunctionType
ALU = mybir.AluOpType

D = 1024
CTX = 256
EPS = 1e-6
FFN = 2816
NJ = FFN // 128


class Buf:
    __slots__ = ("w", "r")

    def __init__(self):
        self.w = None
        self.r = {}


class Op:
    __slots__ = ("eng", "chan", "seq", "fn", "waits", "signal", "clock", "val", "isdma")


class Sched:
    COMPUTE = ("pe", "act", "dve", "pool")

    def __init__(self, nc, es):
        self.nc = nc
        self.eobj = dict(pe=nc.tensor, act=nc.scalar, dve=nc.vector, pool=nc.gpsimd, sp=nc.sync)
        self.sem = {}
        for e in self.COMPUTE:
            self.sem[e] = es.enter_context(nc.semaphore("sem_" + e))
        self.nslot = {"sp": 12, "pool": 8}
        for q, n in self.nslot.items():
            for k in range(n):
                self.sem[(q, k)] = es.enter_context(nc.semaphore("dq_%s%d" % (q, k)))
        self.clock = {e: {} for e in self.eobj}
        self.seq = {e: 0 for e in self.COMPUTE}
        self.sigcount = {e: 0 for e in self.COMPUTE}
        self.dcount = {q: 0 for q in self.nslot}
        self.slot_last = {}
        self.last = {}
        self.pending = []
        self.bar = None
        self.nops = 0
        self.nwaits = 0
        self.dummy = es.enter_context(nc.sbuf_tensor("sched_dummy", [128, 8], F32))

    def _add(self, eng, chan, seq, fn, r, w, isdma, extra=()):
        op = Op()
        op.eng, op.chan, op.seq, op.fn, op.isdma = eng, chan, seq, fn, isdma
        op.signal = isdma
        op.val = 16 * (seq + 1) if isdma else None
        deps = {}

        def need(d):
            if d is None:
                return
            cur = deps.get(d.chan)
            if cur is None or cur.seq < d.seq:
                deps[d.chan] = d

        r = list(r)
        if self.bar is not None:
            r.append(self.bar)
        for b in r:
            need(b.w)
        for b in w:
            need(b.w)
            for d in b.r.values():
                need(d)
        for d in extra:
            need(d)
        clk = self.clock[eng]
        waits = []
        for d in deps.values():
            if d.chan == "pe" and eng == "pe":
                continue
            if clk.get(d.chan, -1) >= d.seq:
                continue
            waits.append(d)
            d.signal = True
            for k, v in d.clock.items():
                if clk.get(k, -1) < v:
                    clk[k] = v
            if clk.get(d.chan, -1) < d.seq:
                clk[d.chan] = d.seq
        op.waits = waits
        op.clock = dict(clk)
        for b in r:
            cur = b.r.get(chan)
            if cur is None or cur.seq < seq:
                b.r[chan] = op
        for b in w:
            b.w = op
            b.r = {}
        self.last[chan] = op
        self.pending.append(op)
        self.nops += 1
        self.nwaits += len(waits)
        return op

    def op(self, eng, fn, r=(), w=()):
        s = self.seq[eng]
        self.seq[eng] = s + 1
        return self._add(eng, eng, s, fn, r, w, False)

    def dma(self, q, out, in_, r=(), w=(), **kw):
        i = self.dcount[q]
        self.dcount[q] = i + 1
        n = self.nslot[q]
        slot, gen = i % n, i // n
        chan = (q, slot)
        prev = self.slot_last.get(chan)
        extra = [prev] if prev is not None else []
        op = self._add(q, chan, gen, lambda e: e.dma_start(out=out, in_=in_, **kw), r, w, True, extra)
        self.slot_last[chan] = op
        return op

    def barrier(self):
        b = Buf()
        extra = list(self.last.values())
        dummy = self.dummy
        s = self.seq["pool"]
        self.seq["pool"] = s + 1
        m = self._add("pool", "pool", s, lambda e: e.memset(dummy[:], 0.0), [], [b], False, extra)
        m.signal = True
        self.bar = b
        self.flush()

    def flush(self):
        for op in self.pending:
            e = self.eobj[op.eng]
            for d in op.waits:
                e.wait_ge(self.sem[d.chan], d.val)
            if op.signal and not op.isdma:
                self.sigcount[op.chan] += 1
                op.val = self.sigcount[op.chan]
            ins = op.fn(e)
            if op.signal:
                ins.then_inc(self.sem[op.chan], 16 if op.isdma else 1)
            op.fn = None
        self.pending = []


class Ring:
    uid = 0

    def __init__(self, nc, es, name, shape, dtype, n):
        Ring.uid += 1
        self.t = [es.enter_context(nc.sbuf_tensor("%s_%d_%d" % (name, Ring.uid, i), shape, dtype)) for i in range(n)]
        self.b = [Buf() for _ in range(n)]
        self.i = 0

    def next(self):
        k = self.i % len(self.t)
        self.i += 1
        return self.t[k], self.b[k]


class Rot:
    def __init__(self, items):
        self.items = list(items)
        self.i = 0

    def next(self):
        v = self.items[self.i % len(self.items)]
        self.i += 1
        return v


def _rope_tables(T, dim):
    rows = T // 64
    row = np.repeat(np.arange(rows), 64).astype(np.float32)
    col = np.tile(np.arange(64), rows).astype(np.float32)
    nf = dim // 4
    inv = (np.float32(10000.0) ** (-np.arange(nf, dtype=np.float32) / np.float32(nf))).astype(np.float32)
    ar = row[:, None] * inv
    ac = col[:, None] * inv
    ang = np.concatenate([ar, ar, ac, ac], axis=-1)
    return np.ascontiguousarray(np.cos(ang).T.astype(np.float32)), np.ascontiguousarray(np.sin(ang).T.astype(np.float32))


def _rot_T(dim):
    q = dim // 4
    R = np.zeros((dim, dim), np.float32)
    for i in range(q):
        R[i, q + i] = -1.0
        R[q + i, i] = 1.0
        R[2 * q + i, 3 * q + i] = -1.0
        R[3 * q + i, 2 * q + i] = 1.0
    return np.ascontiguousarray(R.T)


def make_consts(T):
    cw, sw = _rope_tables(T, 64)
    cm, sm = _rope_tables(T, 32)
    ident = np.eye(128, dtype=np.float32)
    sel = np.zeros((128, 64), np.float32)
    sel[64, :] = 1.0
    b = np.arange(128)[:, None]
    a = np.arange(128)[None, :]
    m1 = (b <= a).astype(np.float32)
    m2 = (a <= b).astype(np.float32)
    r64 = _rot_T(64)
    r32 = _rot_T(32)
    return {
        "k_ident": ident, "k_sel": sel, "k_m1": m1, "k_m2": m2,
        "k_r64": np.concatenate([r64, r64], 0), "k_r32": np.concatenate([r32, r32], 0),
        "k_cw": cw, "k_sw": sw, "k_cm": cm, "k_sm": sm,
    }


WEIGHT_SHAPES = {
    "w_mod": [2, 1024, 6144], "b_mod": [2, 6144], "norm_g": [2, 4, 1024], "ffn_w_up": [2, 1024, 5632],
    "ffn_conv_w": [2, 3, 2816], "ffn_conv_b": [2, 2816], "ffn_w_down": [2, 2816, 1024],
    "ab_w_in": [1, 1024, 1792], "a_conv_w": [1, 31, 512], "a_conv_b": [1, 512], "a_ln_g": [1, 512],
    "a_ln_b": [1, 512], "b_sink": [1, 8], "ab_w_out": [1, 1024, 1024], "cd_w_in": [1, 1024, 1696],
    "lru_conv_w": [1, 2, 4, 512], "lru_conv_b": [1, 2, 512], "lru_gate_w": [1, 2, 2, 8, 64, 64],
    "lru_gate_b": [1, 2, 2, 512], "lru_lambda": [1, 2, 512], "mla_q_norm": [1, 384],
    "mla_w_uq": [1, 384, 768], "mla_kv_norm": [1, 256], "mla_w_ukv": [1, 256, 1024],
    "cd_w_out": [1, 1024, 1024],
}


def build(T=4096, dbg=(), stop=None):
    nc = bass.Bass("TRN2", target_bir_lowering=False)
    TT = T + CTX
    NT = T // 512
    NBL = T // 128
    NB = NBL + CTX // 128
    tiles = [(i * 512, 512, False) for i in range(NT)] + [(T, CTX, True)]
    lat_tiles = tiles[:NT]

    def din(name, shape):
        return nc.dram_tensor(name, list(shape), F32, kind="ExternalInput").ap()

    x_in = din("x", [T, D])
    ctx_in = din("ctx", [CTX, D])
    c_in = din("c", [8, 128])
    cctx_in = din("c_ctx", [8, 128])
    W = {k: din(k, s) for k, s in WEIGHT_SHAPES.items()}
    KC = {k: din(k, v.shape) for k, v in make_consts(T).items()}
    y_out = nc.dram_tensor("y", [T, D], F32, kind="ExternalOutput").ap()

    def scratch(name, shape, dt):
        kind = "ExternalOutput" if name in dbg else "Internal"
        return nc.dram_tensor(name, list(shape), dt, kind=kind).ap()

    XT = scratch("XT", [D, TT], F32)
    U0 = scratch("U0", [512, TT], BF16)
    A0 = scratch("A0", [512, TT], BF16)
    B0 = scratch("B0", [512, TT], BF16)
    GS = scratch("GS", [FFN, TT], BF16)
    US = scratch("US", [FFN, TT], BF16)
    XBS = scratch("XBS", [512, TT], F32)
    GTS = scratch("GTS", [512, T], F32)
    HFS = scratch("HFS", [512, T], F32)
    HBS = scratch("HBS", [512, T], F32)
    C1 = scratch("C1", [512, T], BF16)
    D1 = scratch("D1", [512, T], BF16)
    DBGV = scratch("DBGV", [128, 512], F32)
    QS = scratch("QS", [8, 128, TT], BF16)
    XTv = XT.rearrange("(c p) t -> p c t", p=128)
    XTB = [Buf() for _ in tiles]
    U0B = [Buf() for _ in tiles]
    A0B = [Buf() for _ in tiles]
    B0B = [Buf() for _ in tiles]
    GSB = [Buf() for _ in tiles]
    USB = [Buf() for _ in tiles]
    XBSB = [Buf() for _ in tiles]
    GTSB = [Buf() for _ in tiles]
    HFSB = [Buf() for _ in tiles]
    HBSB = [Buf() for _ in tiles]
    C1B = [Buf() for _ in tiles]
    D1B = [Buf() for _ in tiles]

    ges = ExitStack()
    S = Sched(nc, ges)
    PS = ges.enter_context(nc.psum_tensor("PS", [128, 8, 512], F32))
    PB = [Buf() for _ in range(8)]

    def sb(es, name, shape, dt):
        Ring.uid += 1
        return es.enter_context(nc.sbuf_tensor("%s_%d" % (name, Ring.uid), list(shape), dt))

    def MM(out, lhsT, rhs, st, sp, r, w):
        S.op("pe", lambda e: e.matmul(out, lhsT, rhs, start=st, stop=sp), r, w)

    def TR(out, in_, ident, r, w):
        S.op("pe", lambda e: e.transpose(out, in_, ident), r, w)

    def ACT(out, in_, func, r, w, bias=None, scale=None):
        kw = {}
        if bias is not None:
            kw["bias"] = bias
        if scale is not None:
            kw["scale"] = scale
        S.op("act", lambda e: e.activation(out, in_, func, **kw), r, w)

    def CP(eng, out, in_, r, w):
        if eng == "act":
            S.op("act", lambda e: e.copy(out, in_), r, w)
        else:
            S.op(eng, lambda e: e.tensor_copy(out, in_), r, w)

    def TTo(eng, out, a, b, op, r, w):
        S.op(eng, lambda e: e.tensor_tensor(out, a, b, op), r, w)

    def TS(eng, out, a, s1, s2, op0, op1, r, w):
        if s2 is None:
            S.op(eng, lambda e: e.tensor_scalar(out, a, s1, None, op0), r, w)
        else:
            S.op(eng, lambda e: e.tensor_scalar(out, a, s1, s2, op0, op1), r, w)

    def STT(out, in0, scalar, in1, op0, op1, r, w):
        S.op("dve", lambda e: e.scalar_tensor_tensor(out, in0, scalar, in1, op0, op1), r, w)

    def RCP(out, in_, r, w):
        S.op("dve", lambda e: e.reciprocal(out, in_), r, w)

    def MSET(eng, ap, val, w):
        S.op(eng, lambda e: e.memset(ap, val), [], w)

    identF = sb(ges, "identF", [128, 128], F32)
    identB = sb(ges, "identB", [128, 128], BF16)
    onesB = sb(ges, "onesB", [128, 128], BF16)
    onesF = sb(ges, "onesF", [128, 128], F32)
    selF = sb(ges, "selF", [128, 64], F32)
    m1B = sb(ges, "m1B", [128, 128], BF16)
    m2B = sb(ges, "m2B", [128, 128], BF16)
    r64B = sb(ges, "r64B", [128, 64], BF16)
    r32B = sb(ges, "r32B", [64, 32], BF16)
    CB = Buf()
    S.dma("sp", identF[:], KC["k_ident"], w=[CB])
    S.dma("sp", selF[:], KC["k_sel"], w=[CB])
    S.dma("pool", m1B[:], KC["k_m1"], w=[CB])
    S.dma("pool", m2B[:], KC["k_m2"], w=[CB])
    S.dma("pool", r64B[:], KC["k_r64"], w=[CB])
    S.dma("pool", r32B[:], KC["k_r32"], w=[CB])
    CP("dve", identB[:], identF[:], [CB], [CB])
    MSET("dve", onesB[:], 1.0, [CB])
    MSET("dve", onesF[:], 1.0, [CB])

    cols = {}
    colspec = {
        "g": (W["norm_g"], 64), "bm": (W["b_mod"], 96), "fcb": (W["ffn_conv_b"], 44),
        "fcw0": (W["ffn_conv_w"][0], 66), "fcw1": (W["ffn_conv_w"][1], 66),
        "acb": (W["a_conv_b"], 4), "alg": (W["a_ln_g"], 4), "alb": (W["a_ln_b"], 4),
        "acw": (W["a_conv_w"], 124), "lcw": (W["lru_conv_w"], 32), "lcb": (W["lru_conv_b"], 8),
        "lgb": (W["lru_gate_b"], 16), "lam": (W["lru_lambda"], 8), "qn": (W["mla_q_norm"], 3),
        "kvn": (W["mla_kv_norm"], 2), "c": (c_in, 8), "cc": (cctx_in, 8),
    }
    for name, (src, n) in colspec.items():
        cols[name] = sb(ges, "col_" + name, [128, n], F32)
    esink = sb(ges, "esink", [64, 8], F32)
    scT = sb(ges, "scT", [128, 8, 2], F32)
    MODT = sb(ges, "MODT", [128, 2, 2, 48], F32)
    A1 = sb(ges, "A1", [128, 2, 2, 8], F32)
    G1 = sb(ges, "G1", [128, 2, 2, 8], F32)
    A2 = sb(ges, "A2", [128, 2, 2, 8], F32)
    G2 = sb(ges, "G2", [128, 2, 2, 8], F32)
    epsT = sb(ges, "epsT", [128, 1], F32)
    cch = sb(ges, "cch", [128, 2, 8], F32)
    pre = ExitStack()
    rows_ring = Ring(nc, pre, "rows", [128, 128], F32, 2)
    for i, (name, (src, n)) in enumerate(colspec.items()):
        dst = cols[name]
        nd = len(src.shape)
        if nd == 1:
            s2 = src.rearrange("(r p) -> r p", p=128)
        elif nd == 2 and src.shape[1] == 128:
            s2 = src
        else:
            names = " ".join("a%d" % k for k in range(nd - 1))
            s2 = src.rearrange("%s (r p) -> (%s r) p" % (names, names), p=128)
        rt, rb = rows_ring.next()
        S.dma("sp", rt[0:n, :], s2, w=[rb])
        bank = 6 + (i % 2)
        TR(PS[:, bank, 0:n], rt[0:n, :], identF[0:n, 0:n], [rb, CB], [PB[bank]])
        CP("dve", dst[:], PS[:, bank, 0:n], [PB[bank]], [CB])
    sk = sb(pre, "sk", [1, 8], F32)
    skb = Buf()
    S.dma("sp", sk[:], W["b_sink"], w=[skb])
    MM(PS[0:64, 5, 0:8], onesF[0:1, 0:64], sk[0:1, :], True, True, [skb, CB], [PB[5]])
    ACT(esink[:], PS[0:64, 5, 0:8], AF.Exp, [PB[5]], [CB])
    ACT(scT[:, :, 0], cols["c"][:], AF.Silu, [CB], [CB])
    ACT(scT[:, :, 1], cols["cc"][:], AF.Silu, [CB], [CB])
    S.barrier()
    pre.close()

    gc = cols["g"]

    def mod_load(l, jb, wring):
        wsrc = W["w_mod"][l].rearrange("(kc p) n -> p kc n", p=128)
        wt, wb = wring.next()
        S.dma("sp", wt[:], wsrc[:, :, jb * 768:(jb + 1) * 768], w=[wb])
        return wt, wb

    def mod_mm(l, jb, wt, wb):
        for jj in range(6):
            j = jb * 6 + jj
            for kc in range(8):
                MM(PS[:, 6, 2 * j:2 * j + 2], wt[:, kc, jj * 128:(jj + 1) * 128], scT[:, kc, :],
                   kc == 0, kc == 7, [wb, CB], [PB[6]])

    def mod_block(l, jb, wring):
        wt, wb = mod_load(l, jb, wring)
        mod_mm(l, jb, wt, wb)

    def mod_finish(l):
        pv = PS[:, 6, 0:96].rearrange("p (j s) -> p j s", s=2)
        for s in range(2):
            TTo("dve", MODT[:, l, s, :], pv[:, :, s], cols["bm"][:, l * 48:(l + 1) * 48], ALU.add, [PB[6], CB], [CB])
            STT(A1[:, l, s, :], MODT[:, l, s, 8:16], 1.0, gc[:, l * 32:l * 32 + 8], ALU.add, ALU.mult, [CB], [CB])
            TTo("dve", G1[:, l, s, :], MODT[:, l, s, 16:24], gc[:, l * 32 + 8:l * 32 + 16], ALU.mult, [CB], [CB])
            STT(A2[:, l, s, :], MODT[:, l, s, 32:40], 1.0, gc[:, l * 32 + 16:l * 32 + 24], ALU.add, ALU.mult, [CB], [CB])
            TTo("dve", G2[:, l, s, :], MODT[:, l, s, 40:48], gc[:, l * 32 + 24:l * 32 + 32], ALU.mult, [CB], [CB])

    def phase_mod(l):
        with ExitStack() as es:
            wring = Ring(nc, es, "wmod", [128, 8, 768], F32, 2)
            for jb in range(8):
                mod_block(l, jb, wring)
            mod_finish(l)
            S.barrier()

    def phase_tin():
        with ExitStack() as es:
            xin_ring = Ring(nc, es, "xin", [128, D], F32, 3)
            xt_ring = Ring(nc, es, "xtt", [128, 8, 512], F32, 2)
            for j, (t0, n, isc) in enumerate(tiles):
                src = ctx_in if isc else x_in
                s0 = 0 if isc else t0
                for b in range(n // 128):
                    xin, xb_ = xin_ring.next()
                    S.dma("sp", xin[:], src[s0 + b * 128:s0 + (b + 1) * 128, :], w=[xb_])
                    for fc in range(8):
                        TR(PS[:, fc, b * 128:(b + 1) * 128], xin[:, fc * 128:(fc + 1) * 128], identF[:], [xb_, CB], [PB[fc]])
                xt, xtb = xt_ring.next()
                for fc in range(8):
                    CP("act" if fc % 2 else "dve", xt[:, fc, 0:n], PS[:, fc, 0:n], [PB[fc]], [xtb])
                S.dma("pool", XTv[:, :, t0:t0 + n], xt[:, :, 0:n], r=[xtb], w=[XTB[j]])
            S.barrier()

    def stat_rstd(es_rings, src, srcb, nch, n, dim, bank):
        for c in range(nch):
            MM(PS[:, bank, 0:n], onesB[:], src[:, c, 0:n], c == 0, c == nch - 1, [srcb, CB], [PB[bank]])
        rs, rsb = es_rings["rs"].next()
        ACT(rs[:, 0:n], PS[:, bank, 0:n], AF.Sqrt, [PB[bank]], [rsb], bias=epsT[:, 0:1], scale=1.0 / dim)
        RCP(rs[:, 0:n], rs[:, 0:n], [rsb], [rsb])
        return rs, rsb

    MSET("dve", epsT[:], EPS, [CB])

    def prenorm_gen(rings, xt, xb, n, Acol, SHcol, bank):
        sq, sqb = rings["sq"].next()
        ACT(sq[:, :, 0:n], xt[:, :, 0:n], AF.Square, [xb], [sqb])
        yield
        for c in range(8):
            MM(PS[:, bank, 0:n], onesB[:], sq[:, c, 0:n], c == 0, c == 7, [sqb, CB], [PB[bank]])
        yield
        rs, rsb = rings["rs"].next()
        ACT(rs[:, 0:n], PS[:, bank, 0:n], AF.Sqrt, [PB[bank]], [rsb], bias=epsT[:, 0:1], scale=1.0 / D)
        RCP(rs[:, 0:n], rs[:, 0:n], [rsb], [rsb])
        yield
        TTo("dve", xt[:, :, 0:n], xt[:, :, 0:n], rs[:, 0:n].unsqueeze(1).to_broadcast([128, 8, n]), ALU.mult, [xb, rsb], [xb])
        yield
        h, hb = rings["h"].next()
        for c in range(8):
            ACT(h[:, c, 0:n], xt[:, c, 0:n], AF.Identity, [xb, CB], [hb], bias=SHcol[:, c:c + 1], scale=Acol[:, c:c + 1])
        return h, hb

    def prenorm(rings, xt, xb, n, Acol, SHcol, bank):
        g_ = prenorm_gen(rings, xt, xb, n, Acol, SHcol, bank)
        while True:
            try:
                next(g_)
            except StopIteration as e_:
                return e_.value

    def postnorm_residual(rings, ysb, yb, xt, xb, n, Gcol, bank):
        sq, sqb = rings["sq"].next()
        ACT(sq[:, :, 0:n], ysb[:, :, 0:n], AF.Square, [yb], [sqb])
        rs, rsb = stat_rstd(rings, sq, sqb, 8, n, D, bank)
        TTo("dve", ysb[:, :, 0:n], ysb[:, :, 0:n], rs[:, 0:n].unsqueeze(1).to_broadcast([128, 8, n]), ALU.mult, [yb, rsb], [yb])
        for c in range(8):
            STT(xt[:, c, 0:n], ysb[:, c, 0:n], Gcol[:, c:c + 1], xt[:, c, 0:n], ALU.mult, ALU.add, [yb, xb, CB], [xb])

    def norm_rings(es, with_h=True, nsq=2):
        rings = {
            "sq": Ring(nc, es, "sq", [128, 8, 512], BF16, nsq),
            "rs": Ring(nc, es, "rs", [128, 512], F32, 2),
        }
        if with_h:
            rings["h"] = Ring(nc, es, "h", [128, 8, 512], BF16, 2)
        return rings

    def cast_load(dst, src, wb):
        S.dma("pool", dst, src, w=[wb])

    L0 = ExitStack()
    Klat = sb(L0, "Klat", [64, 2, T], BF16)
    Kctx = sb(L0, "Kctx", [128, 2, CTX], BF16)
    Vt = sb(L0, "Vt", [128, NB, 2, 66], BF16)
    QSB = [[Buf() for _ in tiles] for _ in range(8)]
    KLB = [Buf() for _ in tiles]
    KCB = Buf()
    VB = [Buf() for _ in tiles]
    cwv, swv = KC["k_cw"], KC["k_sw"]
    cmv, smv = KC["k_cm"], KC["k_sm"]

    def phase_p1_l0():
        l = 0
        with ExitStack() as es:
            NCOL = 1024 + 1024 + 256 + 128
            Wt = sb(es, "Wt0", [128, 8, NCOL], BF16)
            WB = [Buf() for _ in range(5)]
            wsrc = W["ab_w_in"][0].rearrange("(kc p) n -> p kc n", p=128)
            cast_load(Wt[:, :, 0:1024], wsrc[:, :, 0:1024], WB[0])
            qd = Wt[:, :, 1024:2048].rearrange("p k (h two d) -> p k h two d", two=2, d=64)
            qs = wsrc[:, :, 1024:1536].rearrange("p k (h d) -> p k h d", d=64)
            for dup in range(2):
                for kc in range(8):
                    cast_load(qd[:, kc, :, dup, :], qs[:, kc, :, :], WB[1 + dup])
            kd = Wt[:, :, 2048:2304].rearrange("p k (h two d) -> p k h two d", two=2, d=64)
            ks = wsrc[:, :, 1536:1664].rearrange("p k (h d) -> p k h d", d=64)
            for dup in range(2):
                for kc in range(8):
                    cast_load(kd[:, kc, :, dup, :], ks[:, kc, :, :], WB[3])
            cast_load(Wt[:, :, 2304:2432], wsrc[:, :, 1664:1792], WB[4])
            MSET("pool", Vt[:, :, :, 64:66], 1.0, VB)
            rings = norm_rings(es)
            x_ring = Ring(nc, es, "xt", [128, 8, 512], F32, 2)
            sg_ring = Ring(nc, es, "sg", [128, 512], F32, 2)
            ust_ring = Ring(nc, es, "ust", [128, 4, 512], BF16, 2)
            cs_ring = Ring(nc, es, "cs", [64, 2, 512], F32, 3)
            t1_ring = Ring(nc, es, "t1", [64, 512], F32, 2)
            t2_ring = Ring(nc, es, "t2", [64, 512], F32, 2)
            kraw_ring = Ring(nc, es, "kraw", [128, 512], BF16, 2)
            qst_ring = Ring(nc, es, "qst", [128, 512], BF16, 3)
            banks = Rot([0, 1, 2, 3, 4])
            rbanks = Rot([5, 6])
            loads = {}

            def issue_load(j):
                t0, n, isc = tiles[j]
                xt, xb = x_ring.next()
                S.dma("sp", xt[:, :, 0:n], XTv[:, :, t0:t0 + n], r=[XTB[j]], w=[xb])
                cs, csb = cs_ring.next()
                if not isc:
                    S.dma("sp", cs[:, 0, :], cwv[:, t0:t0 + n], w=[csb])
                    S.dma("sp", cs[:, 1, :], swv[:, t0:t0 + n], w=[csb])
                loads[j] = (xt, xb, cs, csb)

            TL = list(range(len(tiles)))
            ACOL, SH0, SPLIT = A1, 0, 3

            def body(j, hcur_):
                t0, n, isc = tiles[j]
                xt, xb, cs, csb = loads[j]
                h, hb = hcur_

                def proj(col0, bank, M=128):
                    wdep = [WB[0]] if col0 < 1024 else ([WB[1], WB[2]] if col0 < 2048 else [WB[3]])
                    for kc in range(8):
                        MM(PS[0:M, bank, 0:n], Wt[:, kc, col0:col0 + M], h[:, kc, 0:n], kc == 0, kc == 7, [hb] + wdep, [PB[bank]])

                ust, ustb = ust_ring.next()
                for i in range(4):
                    bg = banks.next()
                    proj(512 + 128 * i, bg)
                    sg, sgb = sg_ring.next()
                    ACT(sg[:, 0:n], PS[:, bg, 0:n], AF.Sigmoid, [PB[bg]], [sgb])
                    bv = banks.next()
                    proj(128 * i, bv)
                    TTo("dve", ust[:, i, 0:n], PS[:, bv, 0:n], sg[:, 0:n], ALU.mult, [PB[bv], sgb], [ustb])
                    yield
                S.dma("pool", U0.rearrange("(c p) t -> p c t", p=128)[:, :, t0:t0 + n], ust[:, :, 0:n], r=[ustb], w=[U0B[j]])

                def rope(bank, rawsrc, rawb, dst, dstb):
                    rbk = rbanks.next()
                    MM(PS[0:64, rbk, 0:n], r64B[64:128, :], rawsrc, True, True, [rawb, CB], [PB[rbk]])
                    t1, t1b = t1_ring.next()
                    t2, t2b = t2_ring.next()
                    TTo("dve", t1[:, 0:n], PS[0:64, bank, 0:n], cs[:, 0, 0:n], ALU.mult, [PB[bank], csb], [t1b])
                    TTo("dve", t2[:, 0:n], PS[0:64, rbk, 0:n], cs[:, 1, 0:n], ALU.mult, [PB[rbk], csb], [t2b])
                    TTo("pool", dst, t1[:, 0:n], t2[:, 0:n], ALU.add, [t1b, t2b], [dstb])

                rpend = []

                def run_rpend():
                    while rpend:
                        rpend.pop(0)()

                for hh in range(8):
                    bq = banks.next()
                    proj(1024 + 128 * hh, bq)
                    qst, qstb = qst_ring.next()
                    CP("act", qst[64:128, 0:n], PS[64:128, bq, 0:n], [PB[bq]], [qstb])
                    run_rpend()
                    if not isc:
                        def rq(bq=bq, qst=qst, qstb=qstb, hh=hh):
                            rope(bq, qst[64:128, 0:n], qstb, qst[0:64, 0:n], qstb)
                            S.dma("pool", QS[hh, :, t0:t0 + n], qst[:, 0:n], r=[qstb], w=[QSB[hh][j]])
                        rpend.append(rq)
                    else:
                        S.dma("pool", QS[hh, 64:128, t0:t0 + n], qst[64:128, 0:n], r=[qstb], w=[QSB[hh][j]])
                    yield
                for g in range(2):
                    bk = banks.next()
                    proj(2048 + 128 * g, bk)
                    if isc:
                        CP("act", Kctx[64:128, g, :], PS[64:128, bk, 0:n], [PB[bk]], [KCB])
                        run_rpend()
                    else:
                        kr, krb = kraw_ring.next()
                        CP("act", kr[64:128, 0:n], PS[64:128, bk, 0:n], [PB[bk]], [krb])
                        run_rpend()

                        def rk(bk=bk, kr=kr, krb=krb, g=g):
                            rope(bk, kr[64:128, 0:n], krb, Klat[0:64, g, t0:t0 + n], KLB[j])
                        rpend.append(rk)
                run_rpend()
                for b in range(n // 128):
                    bv = banks.next()
                    for kc in range(8):
                        MM(PS[:, bv, 0:128], h[:, kc, b * 128:(b + 1) * 128], Wt[:, kc, 2304:2432], kc == 0, kc == 7, [hb, WB[4]], [PB[bv]])
                    blk = (t0 // 128) + b
                    CP("act" if b % 2 else "dve", Vt[:, blk, :, 0:64], PS[:, bv, 0:128].rearrange("p (g d) -> p g d", g=2), [PB[bv]], [VB[j]])
            def pn_gen(jj):
                t0_, n_, isc_ = tiles[TL[jj]]
                s_ = 1 if isc_ else 0
                return prenorm_gen(rings, loads[jj][0], loads[jj][1], n_, ACOL[:, l, s_, :], MODT[:, l, s_, SH0:SH0 + 8], 7)

            def drain(g_):
                while True:
                    try:
                        next(g_)
                    except StopIteration as e_:
                        return e_.value

            issue_load(0)
            if len(TL) > 1:
                issue_load(1)
            hcur = drain(pn_gen(0))
            for ji in range(len(TL)):
                gen = body(ji, hcur)
                k = 0
                pn = None
                hnext = None
                for _ in gen:
                    k += 1
                    if k == SPLIT and ji + 1 < len(TL):
                        pn = pn_gen(ji + 1)
                    if pn is not None:
                        try:
                            next(pn)
                        except StopIteration as e_:
                            hnext = e_.value
                            pn = None
                            if ji + 2 < len(TL):
                                issue_load(ji + 2)
                if ji + 1 < len(TL) and hnext is None:
                    if pn is None:
                        pn = pn_gen(ji + 1)
                    hnext = drain(pn)
                    if ji + 2 < len(TL):
                        issue_load(ji + 2)
                hcur = hnext
                loads.pop(ji)
            S.barrier()

    def phase_conva():
        with ExitStack() as es:
            Dg = sb(es, "DgA", [128, 4, 31, 128], BF16)
            DgB = Buf()
            for c in range(4):
                for k in range(31):
                    col = cols["acw"][:, k * 4 + c:k * 4 + c + 1]
                    TS("dve", Dg[:, c, k, :], identB[:], col, None, ALU.mult, None, [CB], [DgB])
            up_ring = Ring(nc, es, "up", [128, 4, 512 + 30], BF16, 2)
            ucv_ring = Ring(nc, es, "ucv", [128, 4, 512], F32, 2)
            usq_ring = Ring(nc, es, "usq", [128, 4, 512], F32, 2)
            st_ring = Ring(nc, es, "lnst", [128, 3, 512], F32, 2)
            tt_ring = Ring(nc, es, "lntt", [128, 512], F32, 2)
            ao_ring = Ring(nc, es, "ao", [128, 4, 512], BF16, 2)
            U0v = U0.rearrange("(c p) t -> p c t", p=128)
            A0v = A0.rearrange("(c p) t -> p c t", p=128)
            banks = Rot([0, 1, 2, 3])
            wring = Ring(nc, es, "wmod", [128, 8, 768], F32, 2)
            mod_jb = [0]
            mod_q = []

            def mod_step():
                k = mod_jb[0]
                if k > 8:
                    return
                if k < 8:
                    mod_q.append((k,) + mod_load(1, k, wring))
                if k >= 1:
                    kk, wt_, wb_ = mod_q.pop(0)
                    mod_mm(1, kk, wt_, wb_)
                mod_jb[0] += 1

            for j, (t0, n, isc) in enumerate(tiles):
                mod_step()
                seg0, seg1 = (T, TT) if isc else (0, T)
                lo, hi = max(t0 - 15, seg0), min(t0 + n + 15, seg1)
                up, upb = up_ring.next()
                rd = [U0B[j]]
                if j > 0 and not isc:
                    rd.append(U0B[j - 1])
                if j + 1 < NT:
                    rd.append(U0B[j + 1])
                if lo > t0 - 15:
                    MSET("pool", up[:, :, 0:15], 0.0, [upb])
                if hi < t0 + n + 15:
                    MSET("pool", up[:, :, n + 15:n + 30], 0.0, [upb])
                S.dma("sp", up[:, :, lo - (t0 - 15):hi - (t0 - 15)], U0v[:, :, lo:hi], r=rd, w=[upb])
                ucv, ucvb = ucv_ring.next()
                usq, usqb = usq_ring.next()
                for c in range(4):
                    bk = banks.next()
                    for k in range(31):
                        MM(PS[:, bk, 0:n], Dg[:, c, k, :], up[:, c, k:k + n], k == 0, k == 30, [upb, DgB], [PB[bk]])
                    ACT(ucv[:, c, 0:n], PS[:, bk, 0:n], AF.Identity, [PB[bk], CB], [ucvb], bias=cols["acb"][:, c:c + 1])
                    ACT(usq[:, c, 0:n], PS[:, bk, 0:n], AF.Square, [PB[bk], CB], [usqb], bias=cols["acb"][:, c:c + 1])
                for c in range(4):
                    MM(PS[:, 4, 0:n], onesF[:], ucv[:, c, 0:n], c == 0, c == 3, [ucvb, CB], [PB[4]])
                for c in range(4):
                    MM(PS[:, 5, 0:n], onesF[:], usq[:, c, 0:n], c == 0, c == 3, [usqb, CB], [PB[5]])
                st, stb = st_ring.next()
                TS("dve", st[:, 0, 0:n], PS[:, 4, 0:n], 1.0 / 512, None, ALU.mult, None, [PB[4]], [stb])
                TTo("dve", st[:, 1, 0:n], st[:, 0, 0:n], st[:, 0, 0:n], ALU.mult, [stb], [stb])
                STT(st[:, 2, 0:n], PS[:, 5, 0:n], 1.0 / 512, st[:, 1, 0:n], ALU.mult, ALU.subtract, [PB[5], stb], [stb])
                ACT(st[:, 2, 0:n], st[:, 2, 0:n], AF.Sqrt, [stb, CB], [stb], bias=epsT[:, 0:1])
                RCP(st[:, 2, 0:n], st[:, 2, 0:n], [stb], [stb])
                ao, aob = ao_ring.next()
                for c in range(4):
                    tt, ttb = tt_ring.next()
                    TTo("dve", tt[:, 0:n], ucv[:, c, 0:n], st[:, 0, 0:n], ALU.subtract, [ucvb, stb], [ttb])
                    TTo("dve", tt[:, 0:n], tt[:, 0:n], st[:, 2, 0:n], ALU.mult, [ttb, stb], [ttb])
                    ACT(ao[:, c, 0:n], tt[:, 0:n], AF.Silu, [ttb, CB], [aob], bias=cols["alb"][:, c:c + 1], scale=cols["alg"][:, c:c + 1])
                S.dma("pool", A0v[:, :, t0:t0 + n], ao[:, :, 0:n], r=[aob], w=[A0B[j]])
            while mod_jb[0] <= 8:
                mod_step()
            mod_finish(1)
            S.barrier()

    def attn_finalize_a(rings, acc, n, cp_eng="act"):
        osb, ob = rings["osb"].next()
        CP(cp_eng, osb[0:65, 0:n], PS[0:65, acc, 0:n], [PB[acc]], [ob])
        return osb, ob

    def attn_finalize_b(rings, osb, ob, n, extra_col, dst_dram, dstb, dbank, act_recip=False):
        hl, hlb = rings["hl"].next()
        CP("dve", hl[64:65, 0, 0:n], osb[64:65, 0:n], [ob], [hlb])
        TTo("dve", hl[64:65, 1, 0:n], osb[64:65, 0:n], hl[64:65, 0, 0:n], ALU.subtract, [ob, hlb], [hlb])
        MM(PS[0:64, dbank, 0:n], onesB[64:65, 0:64], hl[64:65, 0, 0:n], True, False, [hlb, CB], [PB[dbank]])
        MM(PS[0:64, dbank, 0:n], onesB[64:65, 0:64], hl[64:65, 1, 0:n], False, True, [hlb, CB], [PB[dbank]])
        rd, rdb = rings["rd"].next()
        if act_recip:
            ACT(rd[0:64, 0:n], PS[0:64, dbank, 0:n], AF.Ln, [PB[dbank], CB], [rdb], bias=extra_col)
            ACT(rd[0:64, 0:n], rd[0:64, 0:n], AF.Exp, [rdb], [rdb], scale=-1.0)
        elif extra_col is not None:
            TS("dve", rd[0:64, 0:n], PS[0:64, dbank, 0:n], extra_col, None, ALU.add, None, [PB[dbank], CB], [rdb])
            RCP(rd[0:64, 0:n], rd[0:64, 0:n], [rdb], [rdb])
        else:
            RCP(rd[0:64, 0:n], PS[0:64, dbank, 0:n], [PB[dbank]], [rdb])
        bt, btb = rings["bt"].next()
        TTo("dve", bt[0:64, 0:n], osb[0:64, 0:n], rd[0:64, 0:n], ALU.mult, [ob, rdb], [btb])
        S.dma("pool", dst_dram, bt[0:64, 0:n], r=[btb], w=[dstb])

    def attn_finalize(rings, acc, n, extra_col, dst_dram, dstb, dbank, cp_eng="act"):
        osb, ob = attn_finalize_a(rings, acc, n, cp_eng)
        attn_finalize_b(rings, osb, ob, n, extra_col, dst_dram, dstb, dbank)

    def attn_rings(es):
        return {
            "osb": Ring(nc, es, "osb", [128, 512], F32, 3),
            "rd": Ring(nc, es, "rd", [64, 512], F32, 2),
            "hl": Ring(nc, es, "hl", [128, 2, 512], BF16, 2),
            "bt": Ring(nc, es, "bt", [64, 512], BF16, 2),
        }

    def phase_attn0():
        with ExitStack() as es:
            rings = attn_rings(es)
            pt_ring = Ring(nc, es, "pt", [128, 512], BF16, 4)
            sbanks = Rot([0, 1, 2, 3])
            abanks = Rot([4, 5])
            dbanks = Rot([6, 7])
            qt_ring = Ring(nc, es, "qt", [128, 512], BF16, 3)
            pend = []
            for hh in range(8):
                g = hh // 4
                for j, (t0, n, isc) in enumerate(tiles):
                    qt, qtb = qt_ring.next()
                    if isc:
                        S.dma("sp", qt[64:128, 0:n], QS[hh, 64:128, t0:t0 + n], r=[QSB[hh][j]], w=[qtb])
                    else:
                        S.dma("sp", qt[:, 0:n], QS[hh, :, t0:t0 + n], r=[QSB[hh][j]], w=[qtb])
                    steps = []
                    for cc in range(CTX // 128):
                        steps.append((Kctx[64:128, g, cc * 128:(cc + 1) * 128], qt[64:128, 0:n],
                                      [KCB, qtb], NBL + cc, 0, n, []))
                    if not isc:
                        i4 = t0 // 128
                        for jb in range(i4 - 1, i4 + 5):
                            if jb < 0 or jb >= NBL:
                                continue
                            qb0, qb1 = max(jb - 1, i4), min(jb + 1, i4 + 3)
                            c0, c1 = (qb0 - i4) * 128, (qb1 - i4 + 1) * 128
                            masks = []
                            for qb in range(qb0, qb1 + 1):
                                if qb == jb - 1:
                                    masks.append(((qb - qb0) * 128, m1B))
                                elif qb == jb + 1:
                                    masks.append(((qb - qb0) * 128, m2B))
                            steps.append((Klat[0:64, g, jb * 128:(jb + 1) * 128], qt[0:64, c0:c1],
                                          [KLB[jb // 4], qtb], jb, c0, c1, masks))
                    acc = abanks.next()
                    for si, (lhsT, rhs, rdb_, vblk, c0, c1, masks) in enumerate(steps):
                        m = c1 - c0
                        sbk = sbanks.next()
                        MM(PS[:, sbk, 0:m], lhsT, rhs, True, True, rdb_, [PB[sbk]])
                        pt, ptb = pt_ring.next()
                        ACT(pt[:, 0:m], PS[:, sbk, 0:m], AF.Exp, [PB[sbk]], [ptb], scale=0.125)
                        for (mo, mk) in masks:
                            TTo("dve", pt[:, mo:mo + 128], pt[:, mo:mo + 128], mk[:], ALU.mult, [ptb, CB], [ptb])
                        while len(pend) >= 2:
                            pend.pop(0)()

                        def later(acc=acc, c0=c0, c1=c1, vblk=vblk, g=g, pt=pt, ptb=ptb, m=m, si=si, ns=len(steps), n=n, hh=hh, t0=t0, j=j):
                            MM(PS[0:65, acc, c0:c1], Vt[:, vblk, g, 0:65], pt[:, 0:m], si == 0, si == ns - 1,
                               [ptb, VB[min(vblk // 4, NT)]], [PB[acc]])
                            if si == ns - 1:
                                osb, ob = attn_finalize_a(rings, acc, n, "act")
                                pend.append(lambda: attn_finalize_b(rings, osb, ob, n, esink[0:64, hh:hh + 1], B0[hh * 64:(hh + 1) * 64, t0:t0 + n], B0B[j], dbanks.next(), act_recip=True))
                        pend.append(later)
            while pend:
                pend.pop(0)()
            S.barrier()

    def phase_wout(l, Wsrc, Asrc, ASB, Bsrc, BSB, tl):
        with ExitStack() as es:
            Wa = sb(es, "Wa", [128, 4, D], BF16)
            Wb = sb(es, "Wb", [64, 8, D], BF16)
            WB = Buf()
            cast_load(Wa[:], Wsrc[0:512, :].rearrange("(c p) n -> p c n", p=128), WB)
            cast_load(Wb[:], Wsrc[512:1024, :].rearrange("(h d) n -> d h n", d=64), WB)
            rings = norm_rings(es, with_h=False)
            x_ring = Ring(nc, es, "xt", [128, 8, 512], F32, 2)
            a_ring = Ring(nc, es, "at", [128, 4, 512], BF16, 2)
            b_ring = Ring(nc, es, "bt2", [64, 8, 512], BF16, 2)
            y_ring = Ring(nc, es, "ysb", [128, 8, 512], F32, 2)
            Av = Asrc.rearrange("(c p) t -> p c t", p=128)
            Bv = Bsrc.rearrange("(h d) t -> d h t", d=64)
            banks = Rot([0, 1, 2, 3])
            loads = {}

            def issue_load(ji):
                j = tl[ji]
                t0, n, isc = tiles[j]
                xt, xb = x_ring.next()
                S.dma("sp", xt[:, :, 0:n], XTv[:, :, t0:t0 + n], r=[XTB[j]], w=[xb])
                at, ab = a_ring.next()
                S.dma("sp", at[:, :, 0:n], Av[:, :, t0:t0 + n], r=[ASB[j]], w=[ab])
                bt, bb = b_ring.next()
                S.dma("sp", bt[:, :, 0:n], Bv[:, :, t0:t0 + n], r=[BSB[j]], w=[bb])
                loads[ji] = (xt, xb, at, ab, bt, bb)

            issue_load(0)
            for ji, j in enumerate(tl):
                t0, n, isc = tiles[j]
                if ji + 1 < len(tl):
                    issue_load(ji + 1)
                xt, xb, at, ab, bt, bb = loads.pop(ji)
                s = 1 if isc else 0
                ysb, yb = y_ring.next()
                for fc in range(8):
                    bk = banks.next()
                    for c in range(4):
                        MM(PS[:, bk, 0:n], Wa[:, c, fc * 128:(fc + 1) * 128], at[:, c, 0:n], c == 0, False, [WB, ab], [PB[bk]])
                    for hh in range(8):
                        MM(PS[:, bk, 0:n], Wb[0:64, hh, fc * 128:(fc + 1) * 128], bt[0:64, hh, 0:n], False, hh == 7, [WB, bb], [PB[bk]])
                    CP("act", ysb[:, fc, 0:n], PS[:, bk, 0:n], [PB[bk]], [yb])
                postnorm_residual(rings, ysb, yb, xt, xb, n, G1[:, l, s, :], 7)
                S.dma("pool", XTv[:, :, t0:t0 + n], xt[:, :, 0:n], r=[xb], w=[XTB[j]])
            S.barrier()

    def phase_ffna(l, tl):
        with ExitStack() as es:
            Wu = sb(es, "Wu", [128, 8, 2 * FFN], BF16)
            WB = [Buf() for _ in range(8)]
            wsrc = W["ffn_w_up"][l].rearrange("(kc p) n -> p kc n", p=128)
            for blk in range(8):
                c0 = blk * 704
                cast_load(Wu[:, :, c0:c0 + 704], wsrc[:, :, c0:c0 + 704], WB[blk])
            rings = norm_rings(es)
            x_ring = Ring(nc, es, "xt", [128, 8, 512], F32, 2)
            st_ring = Ring(nc, es, "gst", [128, 4, 512], BF16, 3)
            banks = Rot([0, 1, 2, 3, 4, 5])
            GSv = GS.rearrange("(c p) t -> p c t", p=128)
            USv = US.rearrange("(c p) t -> p c t", p=128)
            loads = {}

            def issue_load(ji):
                j = tl[ji]
                t0, n, isc = tiles[j]
                xt, xb = x_ring.next()
                S.dma("sp", xt[:, :, 0:n], XTv[:, :, t0:t0 + n], r=[XTB[j]], w=[xb])
                loads[ji] = (xt, xb)

            TL = tl
            ACOL, SH0, SPLIT = A2, 24, 4

            def body(ji, hcur_):
                j = tl[ji]
                t0, n, isc = tiles[j]
                xt, xb = loads[ji]
                h, hb = hcur_
                for part, (dstv, dstB) in enumerate(((GSv, GSB), (USv, USB))):
                    k = 0
                    while k < NJ:
                        m = min(4, NJ - k)
                        st, stb = st_ring.next()
                        for q in range(m):
                            fc = part * NJ + k + q
                            bk = banks.next()
                            wb = WB[(fc * 128) // 704]
                            wb2 = WB[(fc * 128 + 127) // 704]
                            for kc in range(8):
                                MM(PS[:, bk, 0:n], Wu[:, kc, fc * 128:(fc + 1) * 128], h[:, kc, 0:n], kc == 0, kc == 7, [hb, wb, wb2], [PB[bk]])
                            CP("act" if (q % 2) else "dve", st[:, q, 0:n], PS[:, bk, 0:n], [PB[bk]], [stb])
                        S.dma("pool", dstv[:, k:k + m, t0:t0 + n], st[:, 0:m, 0:n], r=[stb], w=[dstB[j]])
                        yield
                        k += m
            def pn_gen(jj):
                t0_, n_, isc_ = tiles[TL[jj]]
                s_ = 1 if isc_ else 0
                return prenorm_gen(rings, loads[jj][0], loads[jj][1], n_, ACOL[:, l, s_, :], MODT[:, l, s_, SH0:SH0 + 8], 7)

            def drain(g_):
                while True:
                    try:
                        next(g_)
                    except StopIteration as e_:
                        return e_.value

            issue_load(0)
            if len(TL) > 1:
                issue_load(1)
            hcur = drain(pn_gen(0))
            for ji in range(len(TL)):
                gen = body(ji, hcur)
                k = 0
                pn = None
                hnext = None
                for _ in gen:
                    k += 1
                    if k == SPLIT and ji + 1 < len(TL):
                        pn = pn_gen(ji + 1)
                    if pn is not None:
                        try:
                            next(pn)
                        except StopIteration as e_:
                            hnext = e_.value
                            pn = None
                            if ji + 2 < len(TL):
                                issue_load(ji + 2)
                if ji + 1 < len(TL) and hnext is None:
                    if pn is None:
                        pn = pn_gen(ji + 1)
                    hnext = drain(pn)
                    if ji + 2 < len(TL):
                        issue_load(ji + 2)
                hcur = hnext
                loads.pop(ji)
            S.barrier()

    def phase_ffnb(l, tl):
        with ExitStack() as es:
            Wd = sb(es, "Wd", [128, NJ, D], BF16)
            WB = [Buf() for _ in range(2)]
            wsrc = W["ffn_w_down"][l].rearrange("(j p) n -> p j n", p=128)
            cast_load(Wd[:, 0:11, :], wsrc[:, 0:11, :], WB[0])
            cast_load(Wd[:, 11:22, :], wsrc[:, 11:22, :], WB[1])
            Dg = sb(es, "DgF", [128, NJ, 3, 128], BF16)
            DgB = Buf()
            fcw = cols["fcw%d" % l]
            for jj in range(NJ):
                for k in range(3):
                    TS("dve", Dg[:, jj, k, :], identB[:], fcw[:, k * NJ + jj:k * NJ + jj + 1], None, ALU.mult, None, [CB], [DgB])
            rings = norm_rings(es, with_h=False, nsq=1)
            x_ring = Ring(nc, es, "xt", [128, 8, 512], F32, 1)
            gH = [sb(es, "gtH%d" % i, [128, 11, 514], BF16) for i in range(2)]
            uH = [sb(es, "utH%d" % i, [128, 11, 512], BF16) for i in range(2)]
            gHB = [Buf(), Buf()]
            uHB = [Buf(), Buf()]
            ga_ring = Ring(nc, es, "ga", [128, 512], BF16, 3)
            act_ring = Ring(nc, es, "actt", [128, NJ, 512], BF16, 1)
            y_ring = Ring(nc, es, "ysb", [128, 8, 512], F32, 2)
            GSv = GS.rearrange("(c p) t -> p c t", p=128)
            USv = US.rearrange("(c p) t -> p c t", p=128)
            cbanks = Rot([0, 1, 2, 3])
            dbanks = Rot([4, 5, 6])
            loads = {}

            def load_gu(ji):
                j = tl[ji]
                t0, n, isc = tiles[j]
                seg0, seg1 = (T, TT) if isc else (0, T)
                lo, hi = max(t0 - 1, seg0), min(t0 + n + 1, seg1)
                rd = [GSB[j]]
                if j > 0 and not isc:
                    rd.append(GSB[j - 1])
                if j + 1 < NT:
                    rd.append(GSB[j + 1])
                for half in range(2):
                    gt, gb = gH[half], gHB[half]
                    if lo > t0 - 1:
                        MSET("pool", gt[:, :, 0:1], 0.0, [gb])
                    if hi < t0 + n + 1:
                        MSET("pool", gt[:, :, n + 1:n + 2], 0.0, [gb])
                    S.dma("sp", gt[:, :, lo - (t0 - 1):hi - (t0 - 1)], GSv[:, half * 11:(half + 1) * 11, lo:hi], r=rd, w=[gb])
                    S.dma("sp", uH[half][:, :, 0:n], USv[:, half * 11:(half + 1) * 11, t0:t0 + n], r=[USB[j]], w=[uHB[half]])

            def load_x(ji):
                j = tl[ji]
                t0, n, isc = tiles[j]
                xt, xb = x_ring.next()
                S.dma("sp", xt[:, :, 0:n], XTv[:, :, t0:t0 + n], r=[XTB[j]], w=[xb])
                loads[ji] = (xt, xb)

            acts = {}

            def conv(ji):
                j = tl[ji]
                t0, n, isc = tiles[j]
                actt, actb = act_ring.next()
                for jj in range(NJ):
                    bk = cbanks.next()
                    gt, gb, ut, ub = gH[jj // 11], gHB[jj // 11], uH[jj // 11], uHB[jj // 11]
                    for k in range(3):
                        MM(PS[:, bk, 0:n], Dg[:, jj, k, :], gt[:, jj % 11, k:k + n], k == 0, k == 2, [gb, DgB], [PB[bk]])
                    ga, gab = ga_ring.next()
                    ACT(ga[:, 0:n], PS[:, bk, 0:n], AF.Gelu_apprx_tanh, [PB[bk], CB], [gab], bias=cols["fcb"][:, l * NJ + jj:l * NJ + jj + 1])
                    TTo("dve" if (jj % 2) else "pool", actt[:, jj, 0:n], ga[:, 0:n], ut[:, jj % 11, 0:n], ALU.mult, [gab, ub], [actb])
                acts[ji] = (actt, actb)
                if ji + 1 < len(tl):
                    load_gu(ji + 1)

            load_gu(0)
            load_x(0)
            conv(0)
            for ji, j in enumerate(tl):
                t0, n, isc = tiles[j]
                xt, xb = loads.pop(ji)
                s = 1 if isc else 0
                actt, actb = acts.pop(ji)
                ysb, yb = y_ring.next()
                for fc in range(8):
                    bk = dbanks.next()
                    for jj in range(NJ):
                        MM(PS[:, bk, 0:n], Wd[:, jj, fc * 128:(fc + 1) * 128], actt[:, jj, 0:n], jj == 0, jj == NJ - 1, [actb] + WB, [PB[bk]])
                    CP("act", ysb[:, fc, 0:n], PS[:, bk, 0:n], [PB[bk]], [yb])
                if ji + 1 < len(tl):
                    conv(ji + 1)
                postnorm_residual(rings, ysb, yb, xt, xb, n, G2[:, l, s, :], 7)
                S.dma("pool", XTv[:, :, t0:t0 + n], xt[:, :, 0:n], r=[xb], w=[XTB[j]])
                if ji + 1 < len(tl):
                    load_x(ji + 1)
            S.barrier()

    def phase_tout():
        with ExitStack() as es:
            x_ring = Ring(nc, es, "xt", [128, 8, 512], F32, 2)
            o_ring = Ring(nc, es, "ot", [128, D], F32, 3)
            banks = Rot([(0, 1), (2, 3), (4, 5), (6, 7)])
            for j, (t0, n, isc) in enumerate(lat_tiles):
                xt, xb = x_ring.next()
                S.dma("sp", xt[:, :, 0:n], XTv[:, :, t0:t0 + n], r=[XTB[j]], w=[xb])
                for b in range(n // 128):
                    b0, b1 = banks.next()
                    for fc in range(8):
                        bk = b0 if fc < 4 else b1
                        TR(PS[:, bk, (fc % 4) * 128:(fc % 4 + 1) * 128], xt[:, fc, b * 128:(b + 1) * 128], identF[:], [xb, CB], [PB[bk]])
                    ot, ob = o_ring.next()
                    CP("act", ot[:, 0:512], PS[:, b0, :], [PB[b0]], [ob])
                    CP("dve", ot[:, 512:1024], PS[:, b1, :], [PB[b1]], [ob])
                    S.dma("pool", y_out[t0 + b * 128:t0 + (b + 1) * 128, :], ot[:], r=[ob], w=[Buf()])
            S.barrier()

    L1 = ExitStack()
    L1T = {}

    def alloc_l1():
        L1T["CQN"] = sb(L1, "CQN", [128, 3, T], BF16)
        L1T["CKVN"] = sb(L1, "CKVN", [128, 2, TT], BF16)
        L1T["KRb"] = sb(L1, "KRb", [64, TT], BF16)

    CQB = [Buf() for _ in tiles]
    CKB = [Buf() for _ in tiles]
    KRB = [Buf() for _ in tiles]

    def phase_p1_l1():
        l = 1
        CQN, CKVN, KRb = L1T["CQN"], L1T["CKVN"], L1T["KRb"]
        with ExitStack() as es:
            Wt = sb(es, "Wt1", [128, 8, 1728], BF16)
            WB = [Buf() for _ in range(3)]
            wsrc = W["cd_w_in"][0].rearrange("(kc p) n -> p kc n", p=128)
            cast_load(Wt[:, :, 0:1024], wsrc[:, :, 0:1024], WB[0])
            cast_load(Wt[:, :, 1024:1664], wsrc[:, :, 1024:1664], WB[1])
            cast_load(Wt[:, :, 1664:1696], wsrc[:, :, 1664:1696], WB[2])
            cast_load(Wt[:, :, 1696:1728], wsrc[:, :, 1664:1696], WB[2])
            MSET("pool", KRb[32:64, 0:T], 0.0, KRB[:NT])
            MSET("pool", KRb[0:32, T:TT], 0.0, [KRB[NT]])
            rings = norm_rings(es)
            x_ring = Ring(nc, es, "xt", [128, 8, 512], F32, 2)
            xst_ring = Ring(nc, es, "xst", [128, 4, 512], F32, 1)
            gst_ring = Ring(nc, es, "gst1", [128, 4, 512], F32, 1)
            cqs_ring = Ring(nc, es, "cqs", [128, 3, 512], F32, 1)
            cs_ring = Ring(nc, es, "csm", [32, 2, 512], F32, 3)
            t1_ring = Ring(nc, es, "t1m", [32, 512], F32, 2)
            t2_ring = Ring(nc, es, "t2m", [32, 512], F32, 2)
            krs_ring = Ring(nc, es, "krs", [64, 512], BF16, 2)
            banks = Rot([0, 1, 2, 3, 4])
            XBv = XBS.rearrange("(c p) t -> p c t", p=128)
            GTv = GTS.rearrange("(c p) t -> p c t", p=128)
            loads = {}

            def issue_load(j):
                t0, n, isc = tiles[j]
                xt, xb = x_ring.next()
                S.dma("sp", xt[:, :, 0:n], XTv[:, :, t0:t0 + n], r=[XTB[j]], w=[xb])
                cs, csb = cs_ring.next()
                if not isc:
                    S.dma("sp", cs[:, 0, :], cmv[:, t0:t0 + n], w=[csb])
                    S.dma("sp", cs[:, 1, :], smv[:, t0:t0 + n], w=[csb])
                loads[j] = (xt, xb, cs, csb)

            TL = list(range(len(tiles)))
            ACOL, SH0, SPLIT = A1, 0, 2

            def body(j, hcur_):
                t0, n, isc = tiles[j]
                xt, xb, cs, csb = loads[j]
                h, hb = hcur_

                def proj(col0, bank, M=128):
                    wdep = [WB[0]] if col0 < 1024 else ([WB[1]] if col0 < 1664 else [WB[2]])
                    for kc in range(8):
                        MM(PS[0:M, bank, 0:n], Wt[:, kc, col0:col0 + M], h[:, kc, 0:n], kc == 0, kc == 7, [hb] + wdep, [PB[bank]])

                xst, xstb = xst_ring.next()
                for c in range(4):
                    bk = banks.next()
                    proj(128 * c, bk)
                    CP("act" if c % 2 else "dve", xst[:, c, 0:n], PS[:, bk, 0:n], [PB[bk]], [xstb])
                    yield
                S.dma("pool", XBv[:, :, t0:t0 + n], xst[:, :, 0:n], r=[xstb], w=[XBSB[j]])
                if not isc:
                    gst, gstb = gst_ring.next()
                    for c in range(4):
                        bk = banks.next()
                        proj(512 + 128 * c, bk)
                        ACT(gst[:, c, 0:n], PS[:, bk, 0:n], AF.Gelu_apprx_tanh, [PB[bk]], [gstb])
                        yield
                    S.dma("pool", GTv[:, :, t0:t0 + n], gst[:, :, 0:n], r=[gstb], w=[GTSB[j]])

                def lowrank_norm(col0, nch, dim, gcol, dst, dstb):
                    cqs, cqsb = cqs_ring.next()
                    for c in range(nch):
                        bk = banks.next()
                        proj(col0 + 128 * c, bk)
                        CP("act" if c % 2 else "dve", cqs[:, c, 0:n], PS[:, bk, 0:n], [PB[bk]], [cqsb])
                    sq, sqb = rings["sq"].next()
                    TTo("pool", sq[:, 0:nch, 0:n], cqs[:, 0:nch, 0:n], cqs[:, 0:nch, 0:n], ALU.mult, [cqsb], [sqb])
                    rs, rsb = stat_rstd(rings, sq, sqb, nch, n, dim, 7)
                    TTo("dve", cqs[:, 0:nch, 0:n], cqs[:, 0:nch, 0:n], rs[:, 0:n].unsqueeze(1).to_broadcast([128, nch, n]), ALU.mult, [cqsb, rsb], [cqsb])
                    for c in range(nch):
                        ACT(dst[:, c, t0:t0 + n], cqs[:, c, 0:n], AF.Identity, [cqsb, CB], [dstb], scale=gcol[:, c:c + 1])

                if not isc:
                    lowrank_norm(1024, 3, 384, cols["qn"], CQN, CQB[j])
                lowrank_norm(1408, 2, 256, cols["kvn"], CKVN, CKB[j])
                bk = banks.next()
                proj(1664, bk, M=64)
                if isc:
                    CP("act", KRb[32:64, t0:t0 + n], PS[32:64, bk, 0:n], [PB[bk]], [KRB[j]])
                else:
                    krs, krsb = krs_ring.next()
                    CP("act", krs[32:64, 0:n], PS[32:64, bk, 0:n], [PB[bk]], [krsb])
                    rbk = 5
                    MM(PS[0:32, rbk, 0:n], r32B[32:64, :], krs[32:64, 0:n], True, True, [krsb, CB], [PB[rbk]])
                    t1, t1b = t1_ring.next()
                    t2, t2b = t2_ring.next()
                    TTo("dve", t1[:, 0:n], PS[0:32, bk, 0:n], cs[:, 0, 0:n], ALU.mult, [PB[bk], csb], [t1b])
                    TTo("dve", t2[:, 0:n], PS[0:32, rbk, 0:n], cs[:, 1, 0:n], ALU.mult, [PB[rbk], csb], [t2b])
                    TTo("pool", KRb[0:32, t0:t0 + n], t1[:, 0:n], t2[:, 0:n], ALU.add, [t1b, t2b], [KRB[j]])
            def pn_gen(jj):
                t0_, n_, isc_ = tiles[TL[jj]]
                s_ = 1 if isc_ else 0
                return prenorm_gen(rings, loads[jj][0], loads[jj][1], n_, ACOL[:, l, s_, :], MODT[:, l, s_, SH0:SH0 + 8], 7)

            def drain(g_):
                while True:
                    try:
                        next(g_)
                    except StopIteration as e_:
                        return e_.value

            issue_load(0)
            if len(TL) > 1:
                issue_load(1)
            hcur = drain(pn_gen(0))
            for ji in range(len(TL)):
                gen = body(ji, hcur)
                k = 0
                pn = None
                hnext = None
                for _ in gen:
                    k += 1
                    if k == SPLIT and ji + 1 < len(TL):
                        pn = pn_gen(ji + 1)
                    if pn is not None:
                        try:
                            next(pn)
                        except StopIteration as e_:
                            hnext = e_.value
                            pn = None
                            if ji + 2 < len(TL):
                                issue_load(ji + 2)
                if ji + 1 < len(TL) and hnext is None:
                    if pn is None:
                        pn = pn_gen(ji + 1)
                    hnext = drain(pn)
                    if ji + 2 < len(TL):
                        issue_load(ji + 2)
                hcur = hnext
                loads.pop(ji)
            S.barrier()

    def phase_lru():
        with ExitStack() as es:
            GW = sb(es, "GW", [128, 2, 2, 4, 128], BF16)
            GWB = Buf()
            MSET("pool", GW[:], 0.0, [GWB])
            for d in range(2):
                for gate in range(2):
                    for nb in range(8):
                        p0 = (nb % 2) * 64
                        cast_load(GW[p0:p0 + 64, d, gate, nb // 2, p0:p0 + 64], W["lru_gate_w"][0, d, gate, nb], GWB)
            ytmp = sb(es, "ytmp", [128, 8], F32)
            yb_ = Buf()
            ACT(ytmp[:], cols["lam"][:], AF.Exp, [CB], [yb_], scale=-1.0)
            ACT(ytmp[:], ytmp[:], AF.Ln, [yb_, CB], [yb_], bias=onesF[:, 0:1])
            TS("dve", cch[:, 0, :], ytmp[:], -8.0, None, ALU.mult, None, [yb_], [CB])
            TS("dve", cch[:, 1, :], ytmp[:], -16.0, None, ALU.mult, None, [yb_], [CB])
            xb_ring = Ring(nc, es, "xbt", [128, 4, 515], F32, 2)
            xc_ring = Ring(nc, es, "xc", [128, 4, 512], F32, 2)
            xcb_ring = Ring(nc, es, "xcb", [128, 4, 512], BF16, 2)
            rg_ring = Ring(nc, es, "rg", [128, 4, 512], F32, 2)
            ig_ring = Ring(nc, es, "ig", [128, 4, 512], F32, 2)
            av_ring = Ring(nc, es, "av", [128, 4, 512], F32, 2)
            e2_ring = Ring(nc, es, "e2", [128, 4, 512], F32, 2)
            hv_rings = [Ring(nc, es, "hv%d" % d_, [128, 4, 512], F32, 2) for d_ in range(2)]
            XBv = XBS.rearrange("(c p) t -> p c t", p=128)
            GTv = GTS.rearrange("(c p) t -> p c t", p=128)
            HFv = HFS.rearrange("(c p) t -> p c t", p=128)
            C1v = C1.rearrange("(c p) t -> p c t", p=128)
            banks = Rot([0, 1, 2, 3, 4, 5])
            HBv = HBS.rearrange("(c p) t -> p c t", p=128)
            orders = [[NT] + list(range(NT)), [NT] + list(range(NT - 1, -1, -1))]
            prevs = [None, None]

            def stageA(d, j):
                t0, n, isc = tiles[j]
                seg0, seg1 = (T, TT) if isc else (0, T)
                xbt, xbb = xb_ring.next()
                rd = [XBSB[j]]
                if d == 0:
                    lo, hi = max(t0 - 3, seg0), t0 + n
                    if lo > t0 - 3:
                        MSET("pool", xbt[:, :, 0:3], 0.0, [xbb])
                    elif j > 0:
                        rd.append(XBSB[j - 1])
                    S.dma("sp", xbt[:, :, lo - (t0 - 3):n + 3], XBv[:, :, lo:hi], r=rd, w=[xbb])
                else:
                    lo, hi = t0, min(t0 + n + 3, seg1)
                    if hi < t0 + n + 3:
                        MSET("pool", xbt[:, :, n:n + 3], 0.0, [xbb])
                    elif j + 1 < NT:
                        rd.append(XBSB[j + 1])
                    S.dma("sp", xbt[:, :, 0:hi - lo], XBv[:, :, lo:hi], r=rd, w=[xbb])
                xc, xcb_ = xc_ring.next()
                for c in range(4):
                    wc = lambda k: cols["lcw"][:, d * 16 + k * 4 + c:d * 16 + k * 4 + c + 1]
                    TS("dve", xc[:, c, 0:n], xbt[:, c, 0:n], wc(0), cols["lcb"][:, d * 4 + c:d * 4 + c + 1], ALU.mult, ALU.add, [xbb, CB], [xcb_])
                    for k in range(1, 4):
                        STT(xc[:, c, 0:n], xbt[:, c, k:k + n], wc(k), xc[:, c, 0:n], ALU.mult, ALU.add, [xbb, xcb_, CB], [xcb_])
                xcb, xcbb = xcb_ring.next()
                CP("act", xcb[:, :, 0:n], xc[:, :, 0:n], [xcb_], [xcbb])
                rg, rgb = rg_ring.next()
                ig, igb = ig_ring.next()
                av, avb = av_ring.next()
                e2, e2b = e2_ring.next()
                for c in range(4):
                    b0 = banks.next()
                    MM(PS[:, b0, 0:n], GW[:, d, 0, c, :], xcb[:, c, 0:n], True, True, [xcbb, GWB], [PB[b0]])
                    ACT(rg[:, c, 0:n], PS[:, b0, 0:n], AF.Sigmoid, [PB[b0], CB], [rgb], bias=cols["lgb"][:, d * 8 + c:d * 8 + c + 1])
                    b1 = banks.next()
                    MM(PS[:, b1, 0:n], GW[:, d, 1, c, :], xcb[:, c, 0:n], True, True, [xcbb, GWB], [PB[b1]])
                    ACT(ig[:, c, 0:n], PS[:, b1, 0:n], AF.Sigmoid, [PB[b1], CB], [igb], bias=cols["lgb"][:, d * 8 + 4 + c:d * 8 + 4 + c + 1])
                for c in range(4):
                    ACT(av[:, c, 0:n], rg[:, c, 0:n], AF.Exp, [rgb, CB], [avb], scale=cch[:, 0, d * 4 + c:d * 4 + c + 1])
                    ACT(e2[:, c, 0:n], rg[:, c, 0:n], AF.Exp, [rgb, CB], [e2b], scale=cch[:, 1, d * 4 + c:d * 4 + c + 1])
                ACT(e2[:, :, 0:n], e2[:, :, 0:n], AF.Sqrt, [e2b, CB], [e2b], bias=onesF[:, 0:1], scale=-1.0)
                return (d, j, xc, xcb_, ig, igb, av, avb, e2, e2b)

            def stageB(ctx_):
                d, j, xc, xcb_, ig, igb, av, avb, e2, e2b = ctx_
                t0, n, isc = tiles[j]
                prev = prevs[d]
                TTo("dve", e2[:, :, 0:n], e2[:, :, 0:n], ig[:, :, 0:n], ALU.mult, [e2b, igb], [e2b])
                TTo("dve", e2[:, :, 0:n], e2[:, :, 0:n], xc[:, :, 0:n], ALU.mult, [e2b, xcb_], [e2b])
                hv, hvb = hv_rings[d].next()
                for c in range(4):
                    if prev is None:
                        init, rdp = 0.0, []
                    else:
                        ph, phb, pn = prev
                        init = ph[:, c, pn - 1:pn] if d == 0 else ph[:, c, 0:1]
                        rdp = [phb]
                    if d == 0:
                        o_, a_, b_ = hv[:, c, 0:n], av[:, c, 0:n], e2[:, c, 0:n]
                    else:
                        o_, a_, b_ = hv[:, c, 0:n][:, ::-1], av[:, c, 0:n][:, ::-1], e2[:, c, 0:n][:, ::-1]
                    S.op("dve", lambda e, o_=o_, a_=a_, b_=b_, init=init: e.tensor_tensor_scan(o_, a_, b_, init, ALU.mult, ALU.add),
                         [avb, e2b] + rdp, [hvb])
                prevs[d] = (hv, hvb, n)
                if not isc:
                    if d == 0:
                        S.dma("pool", HFv[:, :, t0:t0 + n], hv[:, :, 0:n], r=[hvb], w=[HFSB[j]])
                    else:
                        S.dma("pool", HBv[:, :, t0:t0 + n], hv[:, :, 0:n], r=[hvb], w=[HBSB[j]])

            items = [(d, orders[d][step]) for step in range(NT + 1) for d in range(2)]
            pend_ctx = None
            for (d, j) in items:
                ctx_ = stageA(d, j)
                if pend_ctx is not None:
                    stageB(pend_ctx)
                pend_ctx = ctx_
            stageB(pend_ctx)
            S.barrier()

    def phase_lru_combine():
        with ExitStack() as es:
            hf_ring = Ring(nc, es, "hf", [128, 4, 512], F32, 2)
            hb_ring = Ring(nc, es, "hb", [128, 4, 512], F32, 2)
            gg_ring = Ring(nc, es, "gg", [128, 4, 512], F32, 2)
            cl_ring = Ring(nc, es, "cl", [128, 4, 512], BF16, 2)
            GTv = GTS.rearrange("(c p) t -> p c t", p=128)
            HFv = HFS.rearrange("(c p) t -> p c t", p=128)
            HBv = HBS.rearrange("(c p) t -> p c t", p=128)
            C1v = C1.rearrange("(c p) t -> p c t", p=128)
            for j, (t0, n, isc) in enumerate(lat_tiles):
                hf, hfb = hf_ring.next()
                S.dma("sp", hf[:, :, 0:n], HFv[:, :, t0:t0 + n], r=[HFSB[j]], w=[hfb])
                hb, hbb = hb_ring.next()
                S.dma("sp", hb[:, :, 0:n], HBv[:, :, t0:t0 + n], r=[HBSB[j]], w=[hbb])
                gg, ggb = gg_ring.next()
                S.dma("sp", gg[:, :, 0:n], GTv[:, :, t0:t0 + n], r=[GTSB[j]], w=[ggb])
                cl, clb = cl_ring.next()
                TTo("dve", hf[:, :, 0:n], hf[:, :, 0:n], hb[:, :, 0:n], ALU.add, [hfb, hbb], [hfb])
                TTo("pool", cl[:, :, 0:n], hf[:, :, 0:n], gg[:, :, 0:n], ALU.mult, [hfb, ggb], [clb])
                S.dma("pool", C1v[:, :, t0:t0 + n], cl[:, :, 0:n], r=[clb], w=[C1B[j]])
            S.barrier()

    def phase_mla():
        CQN, CKVN, KRb = L1T["CQN"], L1T["CKVN"], L1T["KRb"]
        with ExitStack() as es:
            Wq = sb(es, "Wq", [128, 3, 8, 128], BF16)
            Wk = sb(es, "Wk", [128, 2, 8, 64], BF16)
            Wv = sb(es, "Wv", [128, 2, 8, 64], BF16)
            WB = Buf()
            qsrc = W["mla_w_uq"][0].rearrange("(kc p) (h e) -> p kc h e", p=128, e=96)
            ksrc = W["mla_w_ukv"][0].rearrange("(kc p) (h e) -> p kc h e", p=128, e=128)
            for kc in range(3):
                cast_load(Wq[:, kc, :, 64:128], qsrc[:, kc, :, 0:64], WB)
                cast_load(Wq[:, kc, :, 0:32], qsrc[:, kc, :, 64:96], WB)
                cast_load(Wq[:, kc, :, 32:64], qsrc[:, kc, :, 64:96], WB)
            for kc in range(2):
                cast_load(Wk[:, kc, :, :], ksrc[:, kc, :, 0:64], WB)
                cast_load(Wv[:, kc, :, :], ksrc[:, kc, :, 64:128], WB)
            Va = sb(es, "Va", [128, NB, 8, 66], BF16)
            VaB = Buf()
            MSET("pool", Va[:, :, :, 64:66], 1.0, [VaB])
            rings = attn_rings(es)
            k_ring = Ring(nc, es, "Kh", [128, TT], BF16, 2)
            q_ring = Ring(nc, es, "Qh", [128, T], BF16, 2)
            pt_ring = Ring(nc, es, "ptm", [128, 2, 512], BF16, 4)
            cs_ring = Ring(nc, es, "csq", [32, 2, 512], F32, 2)
            t1_ring = Ring(nc, es, "t1q", [32, 512], F32, 2)
            t2_ring = Ring(nc, es, "t2q", [32, 512], F32, 2)
            mbanks = Rot([6, 7])
            sbanks = Rot([0, 2])
            abanks = Rot([4, 5])
            for blk in range(NB):
                bk = mbanks.next()
                for kc in range(2):
                    MM(PS[:, bk, 0:512], CKVN[:, kc, blk * 128:(blk + 1) * 128], Wv[:, kc, :, :].rearrange("p h d -> p (h d)"),
                       kc == 0, kc == 1, [CKB[min(blk // 4, NT)], WB], [PB[bk]])
                CP("act" if blk % 2 else "dve", Va[:, blk, :, 0:64], PS[:, bk, 0:512].rearrange("p (h d) -> p h d", d=64), [PB[bk]], [VaB])
            sc = float(96 ** -0.5)
            pend = []
            qst_ = [None]
            hbufs = {}

            def get_bufs(h_):
                if h_ not in hbufs:
                    Kh_, KhB_ = k_ring.next()
                    Qh_, QhB_ = q_ring.next()
                    CP("pool", Kh_[0:64, :], KRb[0:64, :], KRB, [KhB_])
                    hbufs[h_] = (Kh_, KhB_, Qh_, QhB_)
                return hbufs[h_]

            def prod_k(h_, j):
                Kh_, KhB_, Qh_, QhB_ = get_bufs(h_)
                t0, n, isc = tiles[j]
                bk = mbanks.next()
                for kc in range(2):
                    MM(PS[64:128, bk, 0:n], Wk[:, kc, h_, :], CKVN[:, kc, t0:t0 + n], kc == 0, kc == 1, [CKB[j], WB], [PB[bk]])
                CP("dve", Kh_[64:128, t0:t0 + n], PS[64:128, bk, 0:n], [PB[bk]], [KhB_])

            def prod_q_a(h_, j):
                Kh_, KhB_, Qh_, QhB_ = get_bufs(h_)
                t0, n, isc = tiles[j]
                bk = mbanks.next()
                for kc in range(3):
                    MM(PS[:, bk, 0:n], Wq[:, kc, h_, :], CQN[:, kc, t0:t0 + n], kc == 0, kc == 2, [CQB[j], WB], [PB[bk]])
                CP("dve", Qh_[32:64, t0:t0 + n], PS[32:64, bk, 0:n], [PB[bk]], [QhB_])
                CP("dve", Qh_[64:128, t0:t0 + n], PS[64:128, bk, 0:n], [PB[bk]], [QhB_])
                t1, t1b = t1_ring.next()
                cs, csb = cs_ring.next()
                S.dma("sp", cs[:, 0, :], cmv[:, t0:t0 + n], w=[csb])
                S.dma("sp", cs[:, 1, :], smv[:, t0:t0 + n], w=[csb])
                TTo("dve", t1[:, 0:n], PS[0:32, bk, 0:n], cs[:, 0, 0:n], ALU.mult, [PB[bk], csb], [t1b])
                return (t1, t1b, cs, csb)

            def prod_q_b(h_, j, st_):
                Kh_, KhB_, Qh_, QhB_ = get_bufs(h_)
                t0, n, isc = tiles[j]
                t1, t1b, cs, csb = st_
                rbk = mbanks.next()
                MM(PS[0:32, rbk, 0:n], r32B[32:64, :], Qh_[32:64, t0:t0 + n], True, True, [QhB_, CB], [PB[rbk]])
                t2, t2b = t2_ring.next()
                TTo("dve", t2[:, 0:n], PS[0:32, rbk, 0:n], cs[:, 1, 0:n], ALU.mult, [PB[rbk], csb], [t2b])
                TTo("pool", Qh_[0:32, t0:t0 + n], t1[:, 0:n], t2[:, 0:n], ALU.add, [t1b, t2b], [QhB_])

            def prod_q(h_, j):
                prod_q_b(h_, j, prod_q_a(h_, j))

            for j in range(len(tiles)):
                prod_k(0, j)
            for j in range(NT):
                prod_q(0, j)
            for hh in range(8):
                Kh, KhB, Qh, QhB = get_bufs(hh)
                for j, (t0, n, isc) in enumerate(lat_tiles):
                    acc = abanks.next()
                    ngrp = (NB + 1) // 2
                    for gi in range(ngrp):
                        kbs = [kb for kb in (2 * gi, 2 * gi + 1) if kb < NB]
                        sb0 = sbanks.next()
                        for qi, kb in enumerate(kbs):
                            MM(PS[:, sb0 + qi, 0:n], Kh[:, kb * 128:(kb + 1) * 128], Qh[:, t0:t0 + n], True, True, [KhB, QhB], [PB[sb0 + qi]])
                        pt, ptb = pt_ring.next()
                        m = len(kbs)
                        ACT(pt[:, 0:m, 0:n], PS[:, sb0:sb0 + m, 0:n], AF.Exp, [PB[sb0 + q_] for q_ in range(m)], [ptb], scale=sc)
                        while len(pend) >= 2:
                            pend.pop(0)()

                        def later(kbs=kbs, acc=acc, n=n, pt=pt, ptb=ptb, gi=gi, hh=hh, t0=t0, j=j, last=(gi == ngrp - 1)):
                            for qi, kb in enumerate(kbs):
                                MM(PS[0:65, acc, 0:n], Va[:, kb, hh, 0:65], pt[:, qi, 0:n], gi == 0 and qi == 0, kb == NB - 1, [ptb, VaB], [PB[acc]])
                            if last:
                                osb, ob = attn_finalize_a(rings, acc, n, "dve")
                                pend.append(lambda: attn_finalize_b(rings, osb, ob, n, None, D1[hh * 64:(hh + 1) * 64, t0:t0 + n], D1B[j], mbanks.next()))
                        pend.append(later)
                        if hh + 1 < 8:
                            if gi == ngrp // 5:
                                prod_k(hh + 1, j)
                            if gi == (2 * ngrp) // 5:
                                qst_[0] = prod_q_a(hh + 1, j)
                            if gi == (4 * ngrp) // 5:
                                prod_q_b(hh + 1, j, qst_[0])
                            if gi == ngrp - 1 and j == NT - 1:
                                prod_k(hh + 1, NT)
            while pend:
                pend.pop(0)()
            S.barrier()

    def layer1():
        alloc_l1()
        phase_p1_l1()
        if stop == "p1_l1":
            return
        phase_lru()
        phase_lru_combine()
        if stop == "lru":
            return
        phase_mla()
        if stop == "mla":
            return
        L1.close()
        phase_wout(1, W["cd_w_out"][0], C1, C1B, D1, D1B, lat_t)
        if stop == "wout1":
            return
        phase_ffna(1, lat_t)
        phase_ffnb(1, lat_t)

    all_t = list(range(len(tiles)))
    lat_t = list(range(NT))

    def dump_small(parts):
        off = 0
        for t, w in parts:
            S.dma("sp", DBGV[:, off:off + w], t, r=[CB], w=[Buf()])
            off += w
        S.barrier()

    def run():
        phase_mod(0)
        if stop == "mod0":
            dump_small([(MODT[:, 0, :, :].rearrange("p s m -> p (s m)"), 96), (A1[:, 0].rearrange("p s m -> p (s m)"), 16),
                        (G1[:, 0].rearrange("p s m -> p (s m)"), 16), (A2[:, 0].rearrange("p s m -> p (s m)"), 16),
                        (G2[:, 0].rearrange("p s m -> p (s m)"), 16)])
            return
        phase_tin()
        if stop == "tin":
            return
        phase_p1_l0()
        if stop == "p1_l0":
            return
        phase_conva()
        if stop == "conva":
            return
        phase_attn0()
        if stop == "attn0":
            return
        L0.close()
        phase_wout(0, W["ab_w_out"][0], A0, A0B, B0, B0B, all_t)
        if stop == "wout0":
            return
        phase_ffna(0, all_t)
        if stop == "ffna0":
            return
        phase_ffnb(0, all_t)
        if stop == "ffnb0":
            return
        layer1()
        phase_tout()

    run()
    L1.close()
    L0.close()
    ges.close()
    build.stats = (S.nops, S.nwaits)
    return nc


def make_in_maps(inputs, T):
    consts = make_consts(T)
    f = lambda a: np.ascontiguousarray(np.asarray(a, dtype=np.float32))
    shared = {k: f(inputs[k]) for k in WEIGHT_SHAPES}
    shared.update(consts)
    shared["c_ctx"] = f(inputs["c_ctx"]).reshape(8, 128)
    x, c, ctx = f(inputs["x"]), f(inputs["c"]), f(inputs["ctx"])
    maps = []
    for b in range(x.shape[0]):
        m = dict(shared)
        m["x"] = np.ascontiguousarray(x[b])
        m["ctx"] = np.ascontiguousarray(ctx[b])
        m["c"] = np.ascontiguousarray(c[b]).reshape(8, 128)
        maps.append(m)
    return maps


def kernel(**inputs):
    T = int(np.asarray(inputs["x"]).shape[1])
    nc = build(T)
    in_maps = make_in_maps(inputs, T)
    res = run_bass_kernel_spmd(nc, in_maps, core_ids=list(range(len(in_maps))))
    return np.stack([np.asarray(r["y"], dtype=np.float32) for r in res.results], axis=0)
```

```python
import numpy as np
import ml_dtypes
from contextlib import ExitStack
import concourse.bass as bass
import concourse.mybir as mybir
from concourse.bass_utils import run_bass_kernel_spmd

F32 = mybir.dt.float32
BF16 = mybir.dt.bfloat16
AF = mybir.ActivationFunctionType
ALU = mybir.AluOpType

D = 1024
CTX = 256
EPS = 1e-6
FFN = 2816
NJ = FFN // 128


class Buf:
    __slots__ = ("w", "r")

    def __init__(self):
        self.w = None
        self.r = {}


class Op:
    __slots__ = ("eng", "chan", "seq", "fn", "waits", "signal", "clock", "val", "isdma")


class Sched:
    COMPUTE = ("pe", "act", "dve", "pool")

    def __init__(self, nc, es):
        self.nc = nc
        self.eobj = dict(pe=nc.tensor, act=nc.scalar, dve=nc.vector, pool=nc.gpsimd, sp=nc.sync)
        self.sem = {}
        for e in self.COMPUTE:
            self.sem[e] = es.enter_context(nc.semaphore("sem_" + e))
        self.nslot = {"sp": 12, "pool": 8}
        for q, n in self.nslot.items():
            for k in range(n):
                self.sem[(q, k)] = es.enter_context(nc.semaphore("dq_%s%d" % (q, k)))
        self.clock = {e: {} for e in self.eobj}
        self.seq = {e: 0 for e in self.COMPUTE}
        self.sigcount = {e: 0 for e in self.COMPUTE}
        self.dcount = {q: 0 for q in self.nslot}
        self.slot_last = {}
        self.last = {}
        self.pending = []
        self.bar = None
        self.nops = 0
        self.nwaits = 0
        self.dummy = es.enter_context(nc.sbuf_tensor("sched_dummy", [128, 8], F32))

    def _add(self, eng, chan, seq, fn, r, w, isdma, extra=()):
        op = Op()
        op.eng, op.chan, op.seq, op.fn, op.isdma = eng, chan, seq, fn, isdma
        op.signal = isdma
        op.val = 16 * (seq + 1) if isdma else None
        deps = {}

        def need(d):
            if d is None:
                return
            cur = deps.get(d.chan)
            if cur is None or cur.seq < d.seq:
                deps[d.chan] = d

        r = list(r)
        if self.bar is not None:
            r.append(self.bar)
        for b in r:
            need(b.w)
        for b in w:
            need(b.w)
            for d in b.r.values():
                need(d)
        for d in extra:
            need(d)
        clk = self.clock[eng]
        waits = []
        for d in deps.values():
            if d.chan == "pe" and eng == "pe":
                continue
            if clk.get(d.chan, -1) >= d.seq:
                continue
            waits.append(d)
            d.signal = True
            for k, v in d.clock.items():
                if clk.get(k, -1) < v:
                    clk[k] = v
            if clk.get(d.chan, -1) < d.seq:
                clk[d.chan] = d.seq
        op.waits = waits
        op.clock = dict(clk)
        for b in r:
            cur = b.r.get(chan)
            if cur is None or cur.seq < seq:
                b.r[chan] = op
        for b in w:
            b.w = op
            b.r = {}
        self.last[chan] = op
        self.pending.append(op)
        self.nops += 1
        self.nwaits += len(waits)
        return op

    def op(self, eng, fn, r=(), w=()):
        s = self.seq[eng]
        self.seq[eng] = s + 1
        return self._add(eng, eng, s, fn, r, w, False)

    def dma(self, q, out, in_, r=(), w=(), **kw):
        i = self.dcount[q]
        self.dcount[q] = i + 1
        n = self.nslot[q]
        slot, gen = i % n, i // n
        chan = (q, slot)
        prev = self.slot_last.get(chan)
        extra = [prev] if prev is not None else []
        op = self._add(q, chan, gen, lambda e: e.dma_start(out=out, in_=in_, **kw), r, w, True, extra)
        self.slot_last[chan] = op
        return op

    def barrier(self):
        b = Buf()
        extra = list(self.last.values())
        dummy = self.dummy
        s = self.seq["pool"]
        self.seq["pool"] = s + 1
        m = self._add("pool", "pool", s, lambda e: e.memset(dummy[:], 0.0), [], [b], False, extra)
        m.signal = True
        self.bar = b
        self.flush()

    def flush(self):
        for op in self.pending:
            e = self.eobj[op.eng]
            for d in op.waits:
                e.wait_ge(self.sem[d.chan], d.val)
            if op.signal and not op.isdma:
                self.sigcount[op.chan] += 1
                op.val = self.sigcount[op.chan]
            ins = op.fn(e)
            if op.signal:
                ins.then_inc(self.sem[op.chan], 16 if op.isdma else 1)
            op.fn = None
        self.pending = []


class Ring:
    uid = 0

    def __init__(self, nc, es, name, shape, dtype, n):
        Ring.uid += 1
        self.t = [es.enter_context(nc.sbuf_tensor("%s_%d_%d" % (name, Ring.uid, i), shape, dtype)) for i in range(n)]
        self.b = [Buf() for _ in range(n)]
        self.i = 0

    def next(self):
        k = self.i % len(self.t)
        self.i += 1
        return self.t[k], self.b[k]


class Rot:
    def __init__(self, items):
        self.items = list(items)
        self.i = 0

    def next(self):
        v = self.items[self.i % len(self.items)]
        self.i += 1
        return v


def _rope_tables(T, dim):
    rows = T // 64
    row = np.repeat(np.arange(rows), 64).astype(np.float32)
    col = np.tile(np.arange(64), rows).astype(np.float32)
    nf = dim // 4
    inv = (np.float32(10000.0) ** (-np.arange(nf, dtype=np.float32) / np.float32(nf))).astype(np.float32)
    ar = row[:, None] * inv
    ac = col[:, None] * inv
    ang = np.concatenate([ar, ar, ac, ac], axis=-1)
    return np.ascontiguousarray(np.cos(ang).T.astype(np.float32)), np.ascontiguousarray(np.sin(ang).T.astype(np.float32))


def _rot_T(dim):
    q = dim // 4
    R = np.zeros((dim, dim), np.float32)
    for i in range(q):
        R[i, q + i] = -1.0
        R[q + i, i] = 1.0
        R[2 * q + i, 3 * q + i] = -1.0
        R[3 * q + i, 2 * q + i] = 1.0
    return np.ascontiguousarray(R.T)


def make_consts(T):
    cw, sw = _rope_tables(T, 64)
    cm, sm = _rope_tables(T, 32)
    ident = np.eye(128, dtype=np.float32)
    sel = np.zeros((128, 64), np.float32)
    sel[64, :] = 1.0
    b = np.arange(128)[:, None]
    a = np.arange(128)[None, :]
    m1 = (b <= a).astype(np.float32)
    m2 = (a <= b).astype(np.float32)
    r64 = _rot_T(64)
    r32 = _rot_T(32)
    return {
        "k_ident": ident, "k_sel": sel, "k_m1": m1, "k_m2": m2,
        "k_r64": np.concatenate([r64, r64], 0), "k_r32": np.concatenate([r32, r32], 0),
        "k_cw": cw, "k_sw": sw, "k_cm": cm, "k_sm": sm,
    }


WEIGHT_SHAPES = {
    "w_mod": [2, 1024, 6144], "b_mod": [2, 6144], "norm_g": [2, 4, 1024], "ffn_w_up": [2, 1024, 5632],
    "ffn_conv_w": [2, 3, 2816], "ffn_conv_b": [2, 2816], "ffn_w_down": [2, 2816, 1024],
    "ab_w_in": [1, 1024, 1792], "a_conv_w": [1, 31, 512], "a_conv_b": [1, 512], "a_ln_g": [1, 512],
    "a_ln_b": [1, 512], "b_sink": [1, 8], "ab_w_out": [1, 1024, 1024], "cd_w_in": [1, 1024, 1696],
    "lru_conv_w": [1, 2, 4, 512], "lru_conv_b": [1, 2, 512], "lru_gate_w": [1, 2, 2, 8, 64, 64],
    "lru_gate_b": [1, 2, 2, 512], "lru_lambda": [1, 2, 512], "mla_q_norm": [1, 384],
    "mla_w_uq": [1, 384, 768], "mla_kv_norm": [1, 256], "mla_w_ukv": [1, 256, 1024],
    "cd_w_out": [1, 1024, 1024],
}


def build(T=4096, dbg=(), stop=None):
    nc = bass.Bass("TRN2", target_bir_lowering=False)
    TT = T + CTX
    NT = T // 512
    NBL = T // 128
    NB = NBL + CTX // 128
    tiles = [(i * 512, 512, False) for i in range(NT)] + [(T, CTX, True)]
    lat_tiles = tiles[:NT]

    def din(name, shape):
        return nc.dram_tensor(name, list(shape), F32, kind="ExternalInput").ap()

    x_in = din("x", [T, D])
    ctx_in = din("ctx", [CTX, D])
    c_in = din("c", [8, 128])
    cctx_in = din("c_ctx", [8, 128])
    W = {k: din(k, s) for k, s in WEIGHT_SHAPES.items()}
    KC = {k: din(k, v.shape) for k, v in make_consts(T).items()}
    y_out = nc.dram_tensor("y", [T, D], F32, kind="ExternalOutput").ap()

    def scratch(name, shape, dt):
        kind = "ExternalOutput" if name in dbg else "Internal"
        return nc.dram_tensor(name, list(shape), dt, kind=kind).ap()

    XT = scratch("XT", [D, TT], F32)
    U0 = scratch("U0", [512, TT], BF16)
    A0 = scratch("A0", [512, TT], BF16)
    B0 = scratch("B0", [512, TT], BF16)
    GS = scratch("GS", [FFN, TT], BF16)
    US = scratch("US", [FFN, TT], BF16)
    XBS = scratch("XBS", [512, TT], F32)
    GTS = scratch("GTS", [512, T], F32)
    HFS = scratch("HFS", [512, T], F32)
    HBS = scratch("HBS", [512, T], F32)
    C1 = scratch("C1", [512, T], BF16)
    D1 = scratch("D1", [512, T], BF16)
    DBGV = scratch("DBGV", [128, 512], F32)
    QS = scratch("QS", [8, 128, TT], BF16)
    XTv = XT.rearrange("(c p) t -> p c t", p=128)
    XTB = [Buf() for _ in tiles]
    U0B = [Buf() for _ in tiles]
    A0B = [Buf() for _ in tiles]
    B0B = [Buf() for _ in tiles]
    GSB = [Buf() for _ in tiles]
    USB = [Buf() for _ in tiles]
    XBSB = [Buf() for _ in tiles]
    GTSB = [Buf() for _ in tiles]
    HFSB = [Buf() for _ in tiles]
    HBSB = [Buf() for _ in tiles]
    C1B = [Buf() for _ in tiles]
    D1B = [Buf() for _ in tiles]

    ges = ExitStack()
    S = Sched(nc, ges)
    PS = ges.enter_context(nc.psum_tensor("PS", [128, 8, 512], F32))
    PB = [Buf() for _ in range(8)]

    def sb(es, name, shape, dt):
        Ring.uid += 1
        return es.enter_context(nc.sbuf_tensor("%s_%d" % (name, Ring.uid), list(shape), dt))

    def MM(out, lhsT, rhs, st, sp, r, w):
        S.op("pe", lambda e: e.matmul(out, lhsT, rhs, start=st, stop=sp), r, w)

    def TR(out, in_, ident, r, w):
        S.op("pe", lambda e: e.transpose(out, in_, ident), r, w)

    def ACT(out, in_, func, r, w, bias=None, scale=None):
        kw = {}
        if bias is not None:
            kw["bias"] = bias
        if scale is not None:
            kw["scale"] = scale
        S.op("act", lambda e: e.activation(out, in_, func, **kw), r, w)

    def CP(eng, out, in_, r, w):
        if eng == "act":
            S.op("act", lambda e: e.copy(out, in_), r, w)
        else:
            S.op(eng, lambda e: e.tensor_copy(out, in_), r, w)

    def TTo(eng, out, a, b, op, r, w):
        S.op(eng, lambda e: e.tensor_tensor(out, a, b, op), r, w)

    def TS(eng, out, a, s1, s2, op0, op1, r, w):
        if s2 is None:
            S.op(eng, lambda e: e.tensor_scalar(out, a, s1, None, op0), r, w)
        else:
            S.op(eng, lambda e: e.tensor_scalar(out, a, s1, s2, op0, op1), r, w)

    def STT(out, in0, scalar, in1, op0, op1, r, w):
        S.op("dve", lambda e: e.scalar_tensor_tensor(out, in0, scalar, in1, op0, op1), r, w)

    def RCP(out, in_, r, w):
        S.op("dve", lambda e: e.reciprocal(out, in_), r, w)

    def MSET(eng, ap, val, w):
        S.op(eng, lambda e: e.memset(ap, val), [], w)

    identF = sb(ges, "identF", [128, 128], F32)
    identB = sb(ges, "identB", [128, 128], BF16)
    onesB = sb(ges, "onesB", [128, 128], BF16)
    onesF = sb(ges, "onesF", [128, 128], F32)
    selF = sb(ges, "selF", [128, 64], F32)
    m1B = sb(ges, "m1B", [128, 128], BF16)
    m2B = sb(ges, "m2B", [128, 128], BF16)
    r64B = sb(ges, "r64B", [128, 64], BF16)
    r32B = sb(ges, "r32B", [64, 32], BF16)
    CB = Buf()
    S.dma("sp", identF[:], KC["k_ident"], w=[CB])
    S.dma("sp", selF[:], KC["k_sel"], w=[CB])
    S.dma("pool", m1B[:], KC["k_m1"], w=[CB])
    S.dma("pool", m2B[:], KC["k_m2"], w=[CB])
    S.dma("pool", r64B[:], KC["k_r64"], w=[CB])
    S.dma("pool", r32B[:], KC["k_r32"], w=[CB])
    CP("dve", identB[:], identF[:], [CB], [CB])
    MSET("dve", onesB[:], 1.0, [CB])
    MSET("dve", onesF[:], 1.0, [CB])

    cols = {}
    colspec = {
        "g": (W["norm_g"], 64), "bm": (W["b_mod"], 96), "fcb": (W["ffn_conv_b"], 44),
        "fcw0": (W["ffn_conv_w"][0], 66), "fcw1": (W["ffn_conv_w"][1], 66),
        "acb": (W["a_conv_b"], 4), "alg": (W["a_ln_g"], 4), "alb": (W["a_ln_b"], 4),
        "acw": (W["a_conv_w"], 124), "lcw": (W["lru_conv_w"], 32), "lcb": (W["lru_conv_b"], 8),
        "lgb": (W["lru_gate_b"], 16), "lam": (W["lru_lambda"], 8), "qn": (W["mla_q_norm"], 3),
        "kvn": (W["mla_kv_norm"], 2), "c": (c_in, 8), "cc": (cctx_in, 8),
    }
    for name, (src, n) in colspec.items():
        cols[name] = sb(ges, "col_" + name, [128, n], F32)
    esink = sb(ges, "esink", [64, 8], F32)
    scT = sb(ges, "scT", [128, 8, 2], F32)
    MODT = sb(ges, "MODT", [128, 2, 2, 48], F32)
    A1 = sb(ges, "A1", [128, 2, 2, 8], F32)
    G1 = sb(ges, "G1", [128, 2, 2, 8], F32)
    A2 = sb(ges, "A2", [128, 2, 2, 8], F32)
    G2 = sb(ges, "G2", [128, 2, 2, 8], F32)
    epsT = sb(ges, "epsT", [128, 1], F32)
    cch = sb(ges, "cch", [128, 2, 8], F32)
    pre = ExitStack()
    rows_ring = Ring(nc, pre, "rows", [128, 128], F32, 2)
    for i, (name, (src, n)) in enumerate(colspec.items()):
        dst = cols[name]
        nd = len(src.shape)
        if nd == 1:
            s2 = src.rearrange("(r p) -> r p", p=128)
        elif nd == 2 and src.shape[1] == 128:
            s2 = src
        else:
            names = " ".join("a%d" % k for k in range(nd - 1))
            s2 = src.rearrange("%s (r p) -> (%s r) p" % (names, names), p=128)
        rt, rb = rows_ring.next()
        S.dma("sp", rt[0:n, :], s2, w=[rb])
        bank = 6 + (i % 2)
        TR(PS[:, bank, 0:n], rt[0:n, :], identF[0:n, 0:n], [rb, CB], [PB[bank]])
        CP("dve", dst[:], PS[:, bank, 0:n], [PB[bank]], [CB])
    sk = sb(pre, "sk", [1, 8], F32)
    skb = Buf()
    S.dma("sp", sk[:], W["b_sink"], w=[skb])
    MM(PS[0:64, 5, 0:8], onesF[0:1, 0:64], sk[0:1, :], True, True, [skb, CB], [PB[5]])
    ACT(esink[:], PS[0:64, 5, 0:8], AF.Exp, [PB[5]], [CB])
    ACT(scT[:, :, 0], cols["c"][:], AF.Silu, [CB], [CB])
    ACT(scT[:, :, 1], cols["cc"][:], AF.Silu, [CB], [CB])
    S.barrier()
    pre.close()

    gc = cols["g"]

    def mod_load(l, jb, wring):
        wsrc = W["w_mod"][l].rearrange("(kc p) n -> p kc n", p=128)
        wt, wb = wring.next()
        S.dma("sp", wt[:], wsrc[:, :, jb * 768:(jb + 1) * 768], w=[wb])
        return wt, wb

    def mod_mm(l, jb, wt, wb):
        for jj in range(6):
            j = jb * 6 + jj
            for kc in range(8):
                MM(PS[:, 6, 2 * j:2 * j + 2], wt[:, kc, jj * 128:(jj + 1) * 128], scT[:, kc, :],
                   kc == 0, kc == 7, [wb, CB], [PB[6]])

    def mod_block(l, jb, wring):
        wt, wb = mod_load(l, jb, wring)
        mod_mm(l, jb, wt, wb)

    def mod_finish(l):
        pv = PS[:, 6, 0:96].rearrange("p (j s) -> p j s", s=2)
        for s in range(2):
            TTo("dve", MODT[:, l, s, :], pv[:, :, s], cols["bm"][:, l * 48:(l + 1) * 48], ALU.add, [PB[6], CB], [CB])
            STT(A1[:, l, s, :], MODT[:, l, s, 8:16], 1.0, gc[:, l * 32:l * 32 + 8], ALU.add, ALU.mult, [CB], [CB])
            TTo("dve", G1[:, l, s, :], MODT[:, l, s, 16:24], gc[:, l * 32 + 8:l * 32 + 16], ALU.mult, [CB], [CB])
            STT(A2[:, l, s, :], MODT[:, l, s, 32:40], 1.0, gc[:, l * 32 + 16:l * 32 + 24], ALU.add, ALU.mult, [CB], [CB])
            TTo("dve", G2[:, l, s, :], MODT[:, l, s, 40:48], gc[:, l * 32 + 24:l * 32 + 32], ALU.mult, [CB], [CB])

    def phase_mod(l):
        with ExitStack() as es:
            wring = Ring(nc, es, "wmod", [128, 8, 768], F32, 2)
            for jb in range(8):
                mod_block(l, jb, wring)
            mod_finish(l)
            S.barrier()

    def phase_tin():
        with ExitStack() as es:
            xin_ring = Ring(nc, es, "xin", [128, D], F32, 3)
            xt_ring = Ring(nc, es, "xtt", [128, 8, 512], F32, 2)
            for j, (t0, n, isc) in enumerate(tiles):
                src = ctx_in if isc else x_in
                s0 = 0 if isc else t0
                for b in range(n // 128):
                    xin, xb_ = xin_ring.next()
                    S.dma("sp", xin[:], src[s0 + b * 128:s0 + (b + 1) * 128, :], w=[xb_])
                    for fc in range(8):
                        TR(PS[:, fc, b * 128:(b + 1) * 128], xin[:, fc * 128:(fc + 1) * 128], identF[:], [xb_, CB], [PB[fc]])
                xt, xtb = xt_ring.next()
                for fc in range(8):
                    CP("act" if fc % 2 else "dve", xt[:, fc, 0:n], PS[:, fc, 0:n], [PB[fc]], [xtb])
                S.dma("pool", XTv[:, :, t0:t0 + n], xt[:, :, 0:n], r=[xtb], w=[XTB[j]])
            S.barrier()

    def stat_rstd(es_rings, src, srcb, nch, n, dim, bank):
        for c in range(nch):
            MM(PS[:, bank, 0:n], onesB[:], src[:, c, 0:n], c == 0, c == nch - 1, [srcb, CB], [PB[bank]])
        rs, rsb = es_rings["rs"].next()
        ACT(rs[:, 0:n], PS[:, bank, 0:n], AF.Sqrt, [PB[bank]], [rsb], bias=epsT[:, 0:1], scale=1.0 / dim)
        RCP(rs[:, 0:n], rs[:, 0:n], [rsb], [rsb])
        return rs, rsb

    MSET("dve", epsT[:], EPS, [CB])

    def prenorm_gen(rings, xt, xb, n, Acol, SHcol, bank):
        sq, sqb = rings["sq"].next()
        ACT(sq[:, :, 0:n], xt[:, :, 0:n], AF.Square, [xb], [sqb])
        yield
        for c in range(8):
            MM(PS[:, bank, 0:n], onesB[:], sq[:, c, 0:n], c == 0, c == 7, [sqb, CB], [PB[bank]])
        yield
        rs, rsb = rings["rs"].next()
        ACT(rs[:, 0:n], PS[:, bank, 0:n], AF.Sqrt, [PB[bank]], [rsb], bias=epsT[:, 0:1], scale=1.0 / D)
        RCP(rs[:, 0:n], rs[:, 0:n], [rsb], [rsb])
        yield
        TTo("dve", xt[:, :, 0:n], xt[:, :, 0:n], rs[:, 0:n].unsqueeze(1).to_broadcast([128, 8, n]), ALU.mult, [xb, rsb], [xb])
        yield
        h, hb = rings["h"].next()
        for c in range(8):
            ACT(h[:, c, 0:n], xt[:, c, 0:n], AF.Identity, [xb, CB], [hb], bias=SHcol[:, c:c + 1], scale=Acol[:, c:c + 1])
        return h, hb

    def prenorm(rings, xt, xb, n, Acol, SHcol, bank):
        g_ = prenorm_gen(rings, xt, xb, n, Acol, SHcol, bank)
        while True:
            try:
                next(g_)
            except StopIteration as e_:
                return e_.value

    def postnorm_residual(rings, ysb, yb, xt, xb, n, Gcol, bank, sqpre=None):
        if sqpre is None:
            sq, sqb = rings["sq"].next()
            ACT(sq[:, :, 0:n], ysb[:, :, 0:n], AF.Square, [yb], [sqb])
        else:
            sq, sqb = sqpre
        rs, rsb = stat_rstd(rings, sq, sqb, 8, n, D, bank)
        TTo("dve", ysb[:, :, 0:n], ysb[:, :, 0:n], rs[:, 0:n].unsqueeze(1).to_broadcast([128, 8, n]), ALU.mult, [yb, rsb], [yb])
        for c in range(8):
            STT(xt[:, c, 0:n], ysb[:, c, 0:n], Gcol[:, c:c + 1], xt[:, c, 0:n], ALU.mult, ALU.add, [yb, xb, CB], [xb])

    def norm_rings(es, with_h=True, nsq=2):
        rings = {
            "sq": Ring(nc, es, "sq", [128, 8, 512], BF16, nsq),
            "rs": Ring(nc, es, "rs", [128, 512], F32, 2),
        }
        if with_h:
            rings["h"] = Ring(nc, es, "h", [128, 8, 512], BF16, 2)
        return rings

    def cast_load(dst, src, wb):
        S.dma("pool", dst, src, w=[wb])

    L0 = ExitStack()
    Klat = sb(L0, "Klat", [64, 2, T], BF16)
    Kctx = sb(L0, "Kctx", [128, 2, CTX], BF16)
    Vt = sb(L0, "Vt", [128, NB, 2, 66], BF16)
    QSB = [[Buf() for _ in tiles] for _ in range(8)]
    KLB = [Buf() for _ in tiles]
    KCB = Buf()
    VB = [Buf() for _ in tiles]
    cwv, swv = KC["k_cw"], KC["k_sw"]
    cmv, smv = KC["k_cm"], KC["k_sm"]

    def phase_p1_l0():
        l = 0
        with ExitStack() as es:
            NCOL = 1024 + 1024 + 256 + 128
            Wt = sb(es, "Wt0", [128, 8, NCOL], BF16)
            WB = [Buf() for _ in range(5)]
            wsrc = W["ab_w_in"][0].rearrange("(kc p) n -> p kc n", p=128)
            cast_load(Wt[:, :, 0:1024], wsrc[:, :, 0:1024], WB[0])
            qd = Wt[:, :, 1024:2048].rearrange("p k (h two d) -> p k h two d", two=2, d=64)
            qs = wsrc[:, :, 1024:1536].rearrange("p k (h d) -> p k h d", d=64)
            for dup in range(2):
                for kc in range(8):
                    cast_load(qd[:, kc, :, dup, :], qs[:, kc, :, :], WB[1 + dup])
            kd = Wt[:, :, 2048:2304].rearrange("p k (h two d) -> p k h two d", two=2, d=64)
            ks = wsrc[:, :, 1536:1664].rearrange("p k (h d) -> p k h d", d=64)
            for dup in range(2):
                for kc in range(8):
                    cast_load(kd[:, kc, :, dup, :], ks[:, kc, :, :], WB[3])
            cast_load(Wt[:, :, 2304:2432], wsrc[:, :, 1664:1792], WB[4])
            MSET("pool", Vt[:, :, :, 64:66], 1.0, VB)
            rings = norm_rings(es)
            x_ring = Ring(nc, es, "xt", [128, 8, 512], F32, 2)
            sg_ring = Ring(nc, es, "sg", [128, 512], F32, 2)
            ust_ring = Ring(nc, es, "ust", [128, 4, 512], BF16, 2)
            cs_ring = Ring(nc, es, "cs", [64, 2, 512], F32, 3)
            t1_ring = Ring(nc, es, "t1", [64, 512], F32, 2)
            t2_ring = Ring(nc, es, "t2", [64, 512], F32, 2)
            kraw_ring = Ring(nc, es, "kraw", [128, 512], BF16, 2)
            qst_ring = Ring(nc, es, "qst", [128, 512], BF16, 3)
            banks = Rot([0, 1, 2, 3, 4])
            rbanks = Rot([5, 6])
            loads = {}

            def issue_load(j):
                t0, n, isc = tiles[j]
                xt, xb = x_ring.next()
                S.dma("sp", xt[:, :, 0:n], XTv[:, :, t0:t0 + n], r=[XTB[j]], w=[xb])
                cs, csb = cs_ring.next()
                if not isc:
                    S.dma("sp", cs[:, 0, :], cwv[:, t0:t0 + n], w=[csb])
                    S.dma("sp", cs[:, 1, :], swv[:, t0:t0 + n], w=[csb])
                loads[j] = (xt, xb, cs, csb)

            TL = list(range(len(tiles)))
            ACOL, SH0, SPLIT = A1, 0, 3

            def body(j, hcur_):
                t0, n, isc = tiles[j]
                xt, xb, cs, csb = loads[j]
                h, hb = hcur_

                def proj(col0, bank, M=128):
                    wdep = [WB[0]] if col0 < 1024 else ([WB[1], WB[2]] if col0 < 2048 else [WB[3]])
                    for kc in range(8):
                        MM(PS[0:M, bank, 0:n], Wt[:, kc, col0:col0 + M], h[:, kc, 0:n], kc == 0, kc == 7, [hb] + wdep, [PB[bank]])

                ust, ustb = ust_ring.next()
                for i in range(4):
                    bg = banks.next()
                    proj(512 + 128 * i, bg)
                    sg, sgb = sg_ring.next()
                    ACT(sg[:, 0:n], PS[:, bg, 0:n], AF.Sigmoid, [PB[bg]], [sgb])
                    bv = banks.next()
                    proj(128 * i, bv)
                    TTo("dve", ust[:, i, 0:n], PS[:, bv, 0:n], sg[:, 0:n], ALU.mult, [PB[bv], sgb], [ustb])
                    yield
                S.dma("pool", U0.rearrange("(c p) t -> p c t", p=128)[:, :, t0:t0 + n], ust[:, :, 0:n], r=[ustb], w=[U0B[j]])

                def rope(bank, rawsrc, rawb, dst, dstb):
                    rbk = rbanks.next()
                    MM(PS[0:64, rbk, 0:n], r64B[64:128, :], rawsrc, True, True, [rawb, CB], [PB[rbk]])
                    t1, t1b = t1_ring.next()
                    t2, t2b = t2_ring.next()
                    TTo("dve", t1[:, 0:n], PS[0:64, bank, 0:n], cs[:, 0, 0:n], ALU.mult, [PB[bank], csb], [t1b])
                    TTo("dve", t2[:, 0:n], PS[0:64, rbk, 0:n], cs[:, 1, 0:n], ALU.mult, [PB[rbk], csb], [t2b])
                    TTo("pool", dst, t1[:, 0:n], t2[:, 0:n], ALU.add, [t1b, t2b], [dstb])

                rpend = []

                def run_rpend():
                    while rpend:
                        rpend.pop(0)()

                for hh in range(8):
                    bq = banks.next()
                    proj(1024 + 128 * hh, bq)
                    qst, qstb = qst_ring.next()
                    CP("act", qst[64:128, 0:n], PS[64:128, bq, 0:n], [PB[bq]], [qstb])
                    run_rpend()
                    if not isc:
                        def rq(bq=bq, qst=qst, qstb=qstb, hh=hh):
                            rope(bq, qst[64:128, 0:n], qstb, qst[0:64, 0:n], qstb)
                            S.dma("pool", QS[hh, :, t0:t0 + n], qst[:, 0:n], r=[qstb], w=[QSB[hh][j]])
                        rpend.append(rq)
                    else:
                        S.dma("pool", QS[hh, 64:128, t0:t0 + n], qst[64:128, 0:n], r=[qstb], w=[QSB[hh][j]])
                    yield
                for g in range(2):
                    bk = banks.next()
                    proj(2048 + 128 * g, bk)
                    if isc:
                        CP("act", Kctx[64:128, g, :], PS[64:128, bk, 0:n], [PB[bk]], [KCB])
                        run_rpend()
                    else:
                        kr, krb = kraw_ring.next()
                        CP("act", kr[64:128, 0:n], PS[64:128, bk, 0:n], [PB[bk]], [krb])
                        run_rpend()

                        def rk(bk=bk, kr=kr, krb=krb, g=g):
                            rope(bk, kr[64:128, 0:n], krb, Klat[0:64, g, t0:t0 + n], KLB[j])
                        rpend.append(rk)
                run_rpend()
                for b in range(n // 128):
                    bv = banks.next()
                    for kc in range(8):
                        MM(PS[:, bv, 0:128], h[:, kc, b * 128:(b + 1) * 128], Wt[:, kc, 2304:2432], kc == 0, kc == 7, [hb, WB[4]], [PB[bv]])
                    blk = (t0 // 128) + b
                    CP("act" if b % 2 else "dve", Vt[:, blk, :, 0:64], PS[:, bv, 0:128].rearrange("p (g d) -> p g d", g=2), [PB[bv]], [VB[j]])
            def pn_gen(jj):
                t0_, n_, isc_ = tiles[TL[jj]]
                s_ = 1 if isc_ else 0
                return prenorm_gen(rings, loads[jj][0], loads[jj][1], n_, ACOL[:, l, s_, :], MODT[:, l, s_, SH0:SH0 + 8], 7)

            def drain(g_):
                while True:
                    try:
                        next(g_)
                    except StopIteration as e_:
                        return e_.value

            issue_load(0)
            if len(TL) > 1:
                issue_load(1)
            hcur = drain(pn_gen(0))
            for ji in range(len(TL)):
                gen = body(ji, hcur)
                k = 0
                pn = None
                hnext = None
                for _ in gen:
                    k += 1
                    if k == SPLIT and ji + 1 < len(TL):
                        pn = pn_gen(ji + 1)
                    if pn is not None:
                        try:
                            next(pn)
                        except StopIteration as e_:
                            hnext = e_.value
                            pn = None
                            if ji + 2 < len(TL):
                                issue_load(ji + 2)
                if ji + 1 < len(TL) and hnext is None:
                    if pn is None:
                        pn = pn_gen(ji + 1)
                    hnext = drain(pn)
                    if ji + 2 < len(TL):
                        issue_load(ji + 2)
                hcur = hnext
                loads.pop(ji)
            S.barrier()

    def phase_conva():
        with ExitStack() as es:
            Dg = sb(es, "DgA", [128, 4, 31, 128], BF16)
            DgB = Buf()
            for c in range(4):
                for k in range(31):
                    col = cols["acw"][:, k * 4 + c:k * 4 + c + 1]
                    TS("dve", Dg[:, c, k, :], identB[:], col, None, ALU.mult, None, [CB], [DgB])
            up_ring = Ring(nc, es, "up", [128, 4, 512 + 30], BF16, 2)
            ucv_ring = Ring(nc, es, "ucv", [128, 4, 512], F32, 2)
            usq_ring = Ring(nc, es, "usq", [128, 4, 512], F32, 2)
            st_ring = Ring(nc, es, "lnst", [128, 3, 512], F32, 2)
            tt_ring = Ring(nc, es, "lntt", [128, 512], F32, 2)
            ao_ring = Ring(nc, es, "ao", [128, 4, 512], BF16, 2)
            U0v = U0.rearrange("(c p) t -> p c t", p=128)
            A0v = A0.rearrange("(c p) t -> p c t", p=128)
            banks = Rot([0, 1, 2, 3])
            wring = Ring(nc, es, "wmod", [128, 8, 768], F32, 2)
            mod_jb = [0]
            mod_q = []

            def mod_step():
                k = mod_jb[0]
                if k > 8:
                    return
                if k < 8:
                    mod_q.append((k,) + mod_load(1, k, wring))
                if k >= 1:
                    kk, wt_, wb_ = mod_q.pop(0)
                    mod_mm(1, kk, wt_, wb_)
                mod_jb[0] += 1

            for j, (t0, n, isc) in enumerate(tiles):
                mod_step()
                seg0, seg1 = (T, TT) if isc else (0, T)
                lo, hi = max(t0 - 15, seg0), min(t0 + n + 15, seg1)
                up, upb = up_ring.next()
                rd = [U0B[j]]
                if j > 0 and not isc:
                    rd.append(U0B[j - 1])
                if j + 1 < NT:
                    rd.append(U0B[j + 1])
                if lo > t0 - 15:
                    MSET("pool", up[:, :, 0:15], 0.0, [upb])
                if hi < t0 + n + 15:
                    MSET("pool", up[:, :, n + 15:n + 30], 0.0, [upb])
                S.dma("sp", up[:, :, lo - (t0 - 15):hi - (t0 - 15)], U0v[:, :, lo:hi], r=rd, w=[upb])
                ucv, ucvb = ucv_ring.next()
                usq, usqb = usq_ring.next()
                for c in range(4):
                    bk = banks.next()
                    for k in range(31):
                        MM(PS[:, bk, 0:n], Dg[:, c, k, :], up[:, c, k:k + n], k == 0, k == 30, [upb, DgB], [PB[bk]])
                    ACT(ucv[:, c, 0:n], PS[:, bk, 0:n], AF.Identity, [PB[bk], CB], [ucvb], bias=cols["acb"][:, c:c + 1])
                    ACT(usq[:, c, 0:n], PS[:, bk, 0:n], AF.Square, [PB[bk], CB], [usqb], bias=cols["acb"][:, c:c + 1])
                for c in range(4):
                    MM(PS[:, 4, 0:n], onesF[:], ucv[:, c, 0:n], c == 0, c == 3, [ucvb, CB], [PB[4]])
                for c in range(4):
                    MM(PS[:, 5, 0:n], onesF[:], usq[:, c, 0:n], c == 0, c == 3, [usqb, CB], [PB[5]])
                st, stb = st_ring.next()
                TS("dve", st[:, 0, 0:n], PS[:, 4, 0:n], 1.0 / 512, None, ALU.mult, None, [PB[4]], [stb])
                TTo("dve", st[:, 1, 0:n], st[:, 0, 0:n], st[:, 0, 0:n], ALU.mult, [stb], [stb])
                STT(st[:, 2, 0:n], PS[:, 5, 0:n], 1.0 / 512, st[:, 1, 0:n], ALU.mult, ALU.subtract, [PB[5], stb], [stb])
                ACT(st[:, 2, 0:n], st[:, 2, 0:n], AF.Sqrt, [stb, CB], [stb], bias=epsT[:, 0:1])
                RCP(st[:, 2, 0:n], st[:, 2, 0:n], [stb], [stb])
                ao, aob = ao_ring.next()
                for c in range(4):
                    tt, ttb = tt_ring.next()
                    TTo("dve", tt[:, 0:n], ucv[:, c, 0:n], st[:, 0, 0:n], ALU.subtract, [ucvb, stb], [ttb])
                    TTo("dve", tt[:, 0:n], tt[:, 0:n], st[:, 2, 0:n], ALU.mult, [ttb, stb], [ttb])
                    ACT(ao[:, c, 0:n], tt[:, 0:n], AF.Silu, [ttb, CB], [aob], bias=cols["alb"][:, c:c + 1], scale=cols["alg"][:, c:c + 1])
                S.dma("pool", A0v[:, :, t0:t0 + n], ao[:, :, 0:n], r=[aob], w=[A0B[j]])
            while mod_jb[0] <= 8:
                mod_step()
            mod_finish(1)
            S.barrier()

    def attn_finalize_a(rings, acc, n, cp_eng="act"):
        osb, ob = rings["osb"].next()
        CP(cp_eng, osb[0:65, 0:n], PS[0:65, acc, 0:n], [PB[acc]], [ob])
        return osb, ob

    def attn_finalize_b(rings, osb, ob, n, extra_col, dst_dram, dstb, dbank, act_recip=False):
        hl, hlb = rings["hl"].next()
        CP("dve", hl[64:65, 0, 0:n], osb[64:65, 0:n], [ob], [hlb])
        TTo("dve", hl[64:65, 1, 0:n], osb[64:65, 0:n], hl[64:65, 0, 0:n], ALU.subtract, [ob, hlb], [hlb])
        MM(PS[0:64, dbank, 0:n], onesB[64:65, 0:64], hl[64:65, 0, 0:n], True, False, [hlb, CB], [PB[dbank]])
        MM(PS[0:64, dbank, 0:n], onesB[64:65, 0:64], hl[64:65, 1, 0:n], False, True, [hlb, CB], [PB[dbank]])
        rd, rdb = rings["rd"].next()
        if act_recip:
            ACT(rd[0:64, 0:n], PS[0:64, dbank, 0:n], AF.Ln, [PB[dbank], CB], [rdb], bias=extra_col)
            ACT(rd[0:64, 0:n], rd[0:64, 0:n], AF.Exp, [rdb], [rdb], scale=-1.0)
        elif extra_col is not None:
            TS("dve", rd[0:64, 0:n], PS[0:64, dbank, 0:n], extra_col, None, ALU.add, None, [PB[dbank], CB], [rdb])
            RCP(rd[0:64, 0:n], rd[0:64, 0:n], [rdb], [rdb])
        else:
            RCP(rd[0:64, 0:n], PS[0:64, dbank, 0:n], [PB[dbank]], [rdb])
        bt, btb = rings["bt"].next()
        TTo("dve", bt[0:64, 0:n], osb[0:64, 0:n], rd[0:64, 0:n], ALU.mult, [ob, rdb], [btb])
        S.dma("pool", dst_dram, bt[0:64, 0:n], r=[btb], w=[dstb])

    def attn_finalize(rings, acc, n, extra_col, dst_dram, dstb, dbank, cp_eng="act"):
        osb, ob = attn_finalize_a(rings, acc, n, cp_eng)
        attn_finalize_b(rings, osb, ob, n, extra_col, dst_dram, dstb, dbank)

    def attn_rings(es):
        return {
            "osb": Ring(nc, es, "osb", [128, 512], F32, 3),
            "rd": Ring(nc, es, "rd", [64, 512], F32, 2),
            "hl": Ring(nc, es, "hl", [128, 2, 512], BF16, 2),
            "bt": Ring(nc, es, "bt", [64, 512], BF16, 2),
        }

    def phase_attn0():
        with ExitStack() as es:
            rings = attn_rings(es)
            pt_ring = Ring(nc, es, "pt", [128, 512], BF16, 4)
            sbanks = Rot([0, 1, 2, 3])
            abanks = Rot([4, 5])
            dbanks = Rot([6, 7])
            qt_ring = Ring(nc, es, "qt", [128, 512], BF16, 3)
            pend = []
            for hh in range(8):
                g = hh // 4
                for j, (t0, n, isc) in enumerate(tiles):
                    qt, qtb = qt_ring.next()
                    if isc:
                        S.dma("sp", qt[64:128, 0:n], QS[hh, 64:128, t0:t0 + n], r=[QSB[hh][j]], w=[qtb])
                    else:
                        S.dma("sp", qt[:, 0:n], QS[hh, :, t0:t0 + n], r=[QSB[hh][j]], w=[qtb])
                    steps = []
                    for cc in range(CTX // 128):
                        steps.append((Kctx[64:128, g, cc * 128:(cc + 1) * 128], qt[64:128, 0:n],
                                      [KCB, qtb], NBL + cc, 0, n, []))
                    if not isc:
                        i4 = t0 // 128
                        for jb in range(i4 - 1, i4 + 5):
                            if jb < 0 or jb >= NBL:
                                continue
                            qb0, qb1 = max(jb - 1, i4), min(jb + 1, i4 + 3)
                            c0, c1 = (qb0 - i4) * 128, (qb1 - i4 + 1) * 128
                            masks = []
                            for qb in range(qb0, qb1 + 1):
                                if qb == jb - 1:
                                    masks.append(((qb - qb0) * 128, m1B))
                                elif qb == jb + 1:
                                    masks.append(((qb - qb0) * 128, m2B))
                            steps.append((Klat[0:64, g, jb * 128:(jb + 1) * 128], qt[0:64, c0:c1],
                                          [KLB[jb // 4], qtb], jb, c0, c1, masks))
                    acc = abanks.next()
                    for si, (lhsT, rhs, rdb_, vblk, c0, c1, masks) in enumerate(steps):
                        m = c1 - c0
                        sbk = sbanks.next()
                        MM(PS[:, sbk, 0:m], lhsT, rhs, True, True, rdb_, [PB[sbk]])
                        pt, ptb = pt_ring.next()
                        ACT(pt[:, 0:m], PS[:, sbk, 0:m], AF.Exp, [PB[sbk]], [ptb], scale=0.125)
                        for (mo, mk) in masks:
                            TTo("dve", pt[:, mo:mo + 128], pt[:, mo:mo + 128], mk[:], ALU.mult, [ptb, CB], [ptb])
                        while len(pend) >= 2:
                            pend.pop(0)()

                        def later(acc=acc, c0=c0, c1=c1, vblk=vblk, g=g, pt=pt, ptb=ptb, m=m, si=si, ns=len(steps), n=n, hh=hh, t0=t0, j=j):
                            MM(PS[0:65, acc, c0:c1], Vt[:, vblk, g, 0:65], pt[:, 0:m], si == 0, si == ns - 1,
                               [ptb, VB[min(vblk // 4, NT)]], [PB[acc]])
                            if si == ns - 1:
                                osb, ob = attn_finalize_a(rings, acc, n, "act")
                                pend.append(lambda: attn_finalize_b(rings, osb, ob, n, esink[0:64, hh:hh + 1], B0[hh * 64:(hh + 1) * 64, t0:t0 + n], B0B[j], dbanks.next(), act_recip=True))
                        pend.append(later)
            while pend:
                pend.pop(0)()
            S.barrier()

    def phase_wout(l, Wsrc, Asrc, ASB, Bsrc, BSB, tl):
        with ExitStack() as es:
            Wa = sb(es, "Wa", [128, 4, D], BF16)
            Wb = sb(es, "Wb", [64, 8, D], BF16)
            WB = Buf()
            cast_load(Wa[:], Wsrc[0:512, :].rearrange("(c p) n -> p c n", p=128), WB)
            cast_load(Wb[:], Wsrc[512:1024, :].rearrange("(h d) n -> d h n", d=64), WB)
            rings = norm_rings(es, with_h=False)
            x_ring = Ring(nc, es, "xt", [128, 8, 512], F32, 2)
            a_ring = Ring(nc, es, "at", [128, 4, 512], BF16, 2)
            b_ring = Ring(nc, es, "bt2", [64, 8, 512], BF16, 2)
            y_ring = Ring(nc, es, "ysb", [128, 8, 512], F32, 2)
            Av = Asrc.rearrange("(c p) t -> p c t", p=128)
            Bv = Bsrc.rearrange("(h d) t -> d h t", d=64)
            banks = Rot([0, 1, 2, 3])
            loads = {}

            def issue_load(ji):
                j = tl[ji]
                t0, n, isc = tiles[j]
                xt, xb = x_ring.next()
                S.dma("sp", xt[:, :, 0:n], XTv[:, :, t0:t0 + n], r=[XTB[j]], w=[xb])
                at, ab = a_ring.next()
                S.dma("sp", at[:, :, 0:n], Av[:, :, t0:t0 + n], r=[ASB[j]], w=[ab])
                bt, bb = b_ring.next()
                S.dma("sp", bt[:, :, 0:n], Bv[:, :, t0:t0 + n], r=[BSB[j]], w=[bb])
                loads[ji] = (xt, xb, at, ab, bt, bb)

            issue_load(0)
            for ji, j in enumerate(tl):
                t0, n, isc = tiles[j]
                if ji + 1 < len(tl):
                    issue_load(ji + 1)
                xt, xb, at, ab, bt, bb = loads.pop(ji)
                s = 1 if isc else 0
                ysb, yb = y_ring.next()
                sq, sqb = rings["sq"].next()
                for fc in range(8):
                    bk = banks.next()
                    for c in range(4):
                        MM(PS[:, bk, 0:n], Wa[:, c, fc * 128:(fc + 1) * 128], at[:, c, 0:n], c == 0, False, [WB, ab], [PB[bk]])
                    for hh in range(8):
                        MM(PS[:, bk, 0:n], Wb[0:64, hh, fc * 128:(fc + 1) * 128], bt[0:64, hh, 0:n], False, hh == 7, [WB, bb], [PB[bk]])
                    CP("act", ysb[:, fc, 0:n], PS[:, bk, 0:n], [PB[bk]], [yb])
                    TTo("dve", sq[:, fc, 0:n], ysb[:, fc, 0:n], ysb[:, fc, 0:n], ALU.mult, [yb], [sqb])
                postnorm_residual(rings, ysb, yb, xt, xb, n, G1[:, l, s, :], 7, sqpre=(sq, sqb))
                S.dma("pool", XTv[:, :, t0:t0 + n], xt[:, :, 0:n], r=[xb], w=[XTB[j]])
            S.barrier()

    def phase_ffna(l, tl):
        with ExitStack() as es:
            Wu = sb(es, "Wu", [128, 8, 2 * FFN], BF16)
            WB = [Buf() for _ in range(8)]
            wsrc = W["ffn_w_up"][l].rearrange("(kc p) n -> p kc n", p=128)
            for blk in range(8):
                c0 = blk * 704
                cast_load(Wu[:, :, c0:c0 + 704], wsrc[:, :, c0:c0 + 704], WB[blk])
            rings = norm_rings(es)
            x_ring = Ring(nc, es, "xt", [128, 8, 512], F32, 2)
            st_ring = Ring(nc, es, "gst", [128, 4, 512], BF16, 3)
            banks = Rot([0, 1, 2, 3, 4, 5])
            GSv = GS.rearrange("(c p) t -> p c t", p=128)
            USv = US.rearrange("(c p) t -> p c t", p=128)
            loads = {}

            def issue_load(ji):
                j = tl[ji]
                t0, n, isc = tiles[j]
                xt, xb = x_ring.next()
                S.dma("sp", xt[:, :, 0:n], XTv[:, :, t0:t0 + n], r=[XTB[j]], w=[xb])
                loads[ji] = (xt, xb)

            TL = tl
            ACOL, SH0, SPLIT = A2, 24, 4

            def body(ji, hcur_):
                j = tl[ji]
                t0, n, isc = tiles[j]
                xt, xb = loads[ji]
                h, hb = hcur_
                for part, (dstv, dstB) in enumerate(((GSv, GSB), (USv, USB))):
                    k = 0
                    while k < NJ:
                        m = min(4, NJ - k)
                        st, stb = st_ring.next()
                        for q in range(m):
                            fc = part * NJ + k + q
                            bk = banks.next()
                            wb = WB[(fc * 128) // 704]
                            wb2 = WB[(fc * 128 + 127) // 704]
                            for kc in range(8):
                                MM(PS[:, bk, 0:n], Wu[:, kc, fc * 128:(fc + 1) * 128], h[:, kc, 0:n], kc == 0, kc == 7, [hb, wb, wb2], [PB[bk]])
                            CP("act" if (q % 2) else "dve", st[:, q, 0:n], PS[:, bk, 0:n], [PB[bk]], [stb])
                        S.dma("pool", dstv[:, k:k + m, t0:t0 + n], st[:, 0:m, 0:n], r=[stb], w=[dstB[j]])
                        yield
                        k += m
            def pn_gen(jj):
                t0_, n_, isc_ = tiles[TL[jj]]
                s_ = 1 if isc_ else 0
                return prenorm_gen(rings, loads[jj][0], loads[jj][1], n_, ACOL[:, l, s_, :], MODT[:, l, s_, SH0:SH0 + 8], 7)

            def drain(g_):
                while True:
                    try:
                        next(g_)
                    except StopIteration as e_:
                        return e_.value

            issue_load(0)
            if len(TL) > 1:
                issue_load(1)
            hcur = drain(pn_gen(0))
            for ji in range(len(TL)):
                gen = body(ji, hcur)
                k = 0
                pn = None
                hnext = None
                for _ in gen:
                    k += 1
                    if k == SPLIT and ji + 1 < len(TL):
                        pn = pn_gen(ji + 1)
                    if pn is not None:
                        try:
                            next(pn)
                        except StopIteration as e_:
                            hnext = e_.value
                            pn = None
                            if ji + 2 < len(TL):
                                issue_load(ji + 2)
                if ji + 1 < len(TL) and hnext is None:
                    if pn is None:
                        pn = pn_gen(ji + 1)
                    hnext = drain(pn)
                    if ji + 2 < len(TL):
                        issue_load(ji + 2)
                hcur = hnext
                loads.pop(ji)
            S.barrier()

    def phase_ffnb(l, tl):
        with ExitStack() as es:
            Wd = sb(es, "Wd", [128, NJ, D], BF16)
            WB = [Buf() for _ in range(2)]
            wsrc = W["ffn_w_down"][l].rearrange("(j p) n -> p j n", p=128)
            cast_load(Wd[:, 0:11, :], wsrc[:, 0:11, :], WB[0])
            cast_load(Wd[:, 11:22, :], wsrc[:, 11:22, :], WB[1])
            Dg = sb(es, "DgF", [128, NJ, 3, 128], BF16)
            DgB = Buf()
            fcw = cols["fcw%d" % l]
            for jj in range(NJ):
                for k in range(3):
                    TS("dve", Dg[:, jj, k, :], identB[:], fcw[:, k * NJ + jj:k * NJ + jj + 1], None, ALU.mult, None, [CB], [DgB])
            rings = norm_rings(es, with_h=False, nsq=1)
            x_ring = Ring(nc, es, "xt", [128, 8, 512], F32, 1)
            gH = [sb(es, "gtH%d" % i, [128, 11, 514], BF16) for i in range(2)]
            uH = [sb(es, "utH%d" % i, [128, 11, 512], BF16) for i in range(2)]
            gHB = [Buf(), Buf()]
            uHB = [Buf(), Buf()]
            ga_ring = Ring(nc, es, "ga", [128, 512], BF16, 3)
            act_ring = Ring(nc, es, "actt", [128, NJ, 512], BF16, 1)
            y_ring = Ring(nc, es, "ysb", [128, 8, 512], F32, 2)
            GSv = GS.rearrange("(c p) t -> p c t", p=128)
            USv = US.rearrange("(c p) t -> p c t", p=128)
            cbanks = Rot([0, 1, 2, 3])
            dbanks = Rot([4, 5, 6])
            loads = {}

            def load_gu(ji):
                j = tl[ji]
                t0, n, isc = tiles[j]
                seg0, seg1 = (T, TT) if isc else (0, T)
                lo, hi = max(t0 - 1, seg0), min(t0 + n + 1, seg1)
                rd = [GSB[j]]
                if j > 0 and not isc:
                    rd.append(GSB[j - 1])
                if j + 1 < NT:
                    rd.append(GSB[j + 1])
                for half in range(2):
                    gt, gb = gH[half], gHB[half]
                    if lo > t0 - 1:
                        MSET("pool", gt[:, :, 0:1], 0.0, [gb])
                    if hi < t0 + n + 1:
                        MSET("pool", gt[:, :, n + 1:n + 2], 0.0, [gb])
                    S.dma("sp", gt[:, :, lo - (t0 - 1):hi - (t0 - 1)], GSv[:, half * 11:(half + 1) * 11, lo:hi], r=rd, w=[gb])
                    S.dma("sp", uH[half][:, :, 0:n], USv[:, half * 11:(half + 1) * 11, t0:t0 + n], r=[USB[j]], w=[uHB[half]])

            def load_x(ji):
                j = tl[ji]
                t0, n, isc = tiles[j]
                xt, xb = x_ring.next()
                S.dma("sp", xt[:, :, 0:n], XTv[:, :, t0:t0 + n], r=[XTB[j]], w=[xb])
                loads[ji] = (xt, xb)

            acts = {}

            def conv(ji):
                j = tl[ji]
                t0, n, isc = tiles[j]
                actt, actb = act_ring.next()
                for jj in range(NJ):
                    bk = cbanks.next()
                    gt, gb, ut, ub = gH[jj // 11], gHB[jj // 11], uH[jj // 11], uHB[jj // 11]
                    for k in range(3):
                        MM(PS[:, bk, 0:n], Dg[:, jj, k, :], gt[:, jj % 11, k:k + n], k == 0, k == 2, [gb, DgB], [PB[bk]])
                    ga, gab = ga_ring.next()
                    ACT(ga[:, 0:n], PS[:, bk, 0:n], AF.Gelu_apprx_tanh, [PB[bk], CB], [gab], bias=cols["fcb"][:, l * NJ + jj:l * NJ + jj + 1])
                    TTo("dve" if (jj % 2) else "pool", actt[:, jj, 0:n], ga[:, 0:n], ut[:, jj % 11, 0:n], ALU.mult, [gab, ub], [actb])
                acts[ji] = (actt, actb)
                if ji + 1 < len(tl):
                    load_gu(ji + 1)

            load_gu(0)
            load_x(0)
            conv(0)
            for ji, j in enumerate(tl):
                t0, n, isc = tiles[j]
                xt, xb = loads.pop(ji)
                s = 1 if isc else 0
                actt, actb = acts.pop(ji)
                ysb, yb = y_ring.next()
                for fc in range(8):
                    bk = dbanks.next()
                    for jj in range(NJ):
                        MM(PS[:, bk, 0:n], Wd[:, jj, fc * 128:(fc + 1) * 128], actt[:, jj, 0:n], jj == 0, jj == NJ - 1, [actb] + WB, [PB[bk]])
                    CP("act", ysb[:, fc, 0:n], PS[:, bk, 0:n], [PB[bk]], [yb])
                if ji + 1 < len(tl):
                    conv(ji + 1)
                postnorm_residual(rings, ysb, yb, xt, xb, n, G2[:, l, s, :], 7)
                S.dma("pool", XTv[:, :, t0:t0 + n], xt[:, :, 0:n], r=[xb], w=[XTB[j]])
                if ji + 1 < len(tl):
                    load_x(ji + 1)
            S.barrier()

    def phase_tout():
        with ExitStack() as es:
            x_ring = Ring(nc, es, "xt", [128, 8, 512], F32, 2)
            o_ring = Ring(nc, es, "ot", [128, D], F32, 3)
            banks = Rot([(0, 1), (2, 3), (4, 5), (6, 7)])
            for j, (t0, n, isc) in enumerate(lat_tiles):
                xt, xb = x_ring.next()
                S.dma("sp", xt[:, :, 0:n], XTv[:, :, t0:t0 + n], r=[XTB[j]], w=[xb])
                for b in range(n // 128):
                    b0, b1 = banks.next()
                    for fc in range(8):
                        bk = b0 if fc < 4 else b1
                        TR(PS[:, bk, (fc % 4) * 128:(fc % 4 + 1) * 128], xt[:, fc, b * 128:(b + 1) * 128], identF[:], [xb, CB], [PB[bk]])
                    ot, ob = o_ring.next()
                    CP("act", ot[:, 0:512], PS[:, b0, :], [PB[b0]], [ob])
                    CP("dve", ot[:, 512:1024], PS[:, b1, :], [PB[b1]], [ob])
                    S.dma("pool", y_out[t0 + b * 128:t0 + (b + 1) * 128, :], ot[:], r=[ob], w=[Buf()])
            S.barrier()

    L1 = ExitStack()
    L1T = {}

    def alloc_l1():
        L1T["CQN"] = sb(L1, "CQN", [128, 3, T], BF16)
        L1T["CKVN"] = sb(L1, "CKVN", [128, 2, TT], BF16)
        L1T["KRb"] = sb(L1, "KRb", [64, TT], BF16)

    CQB = [Buf() for _ in tiles]
    CKB = [Buf() for _ in tiles]
    KRB = [Buf() for _ in tiles]

    def phase_p1_l1():
        l = 1
        CQN, CKVN, KRb = L1T["CQN"], L1T["CKVN"], L1T["KRb"]
        with ExitStack() as es:
            Wt = sb(es, "Wt1", [128, 8, 1728], BF16)
            WB = [Buf() for _ in range(3)]
            wsrc = W["cd_w_in"][0].rearrange("(kc p) n -> p kc n", p=128)
            cast_load(Wt[:, :, 0:1024], wsrc[:, :, 0:1024], WB[0])
            cast_load(Wt[:, :, 1024:1664], wsrc[:, :, 1024:1664], WB[1])
            cast_load(Wt[:, :, 1664:1696], wsrc[:, :, 1664:1696], WB[2])
            cast_load(Wt[:, :, 1696:1728], wsrc[:, :, 1664:1696], WB[2])
            MSET("pool", KRb[32:64, 0:T], 0.0, KRB[:NT])
            MSET("pool", KRb[0:32, T:TT], 0.0, [KRB[NT]])
            rings = norm_rings(es)
            x_ring = Ring(nc, es, "xt", [128, 8, 512], F32, 2)
            xst_ring = Ring(nc, es, "xst", [128, 4, 512], F32, 1)
            gst_ring = Ring(nc, es, "gst1", [128, 4, 512], F32, 1)
            cqs_ring = Ring(nc, es, "cqs", [128, 3, 512], F32, 1)
            cs_ring = Ring(nc, es, "csm", [32, 2, 512], F32, 3)
            t1_ring = Ring(nc, es, "t1m", [32, 512], F32, 2)
            t2_ring = Ring(nc, es, "t2m", [32, 512], F32, 2)
            krs_ring = Ring(nc, es, "krs", [64, 512], BF16, 2)
            banks = Rot([0, 1, 2, 3, 4])
            XBv = XBS.rearrange("(c p) t -> p c t", p=128)
            GTv = GTS.rearrange("(c p) t -> p c t", p=128)
            loads = {}

            def issue_load(j):
                t0, n, isc = tiles[j]
                xt, xb = x_ring.next()
                S.dma("sp", xt[:, :, 0:n], XTv[:, :, t0:t0 + n], r=[XTB[j]], w=[xb])
                cs, csb = cs_ring.next()
                if not isc:
                    S.dma("sp", cs[:, 0, :], cmv[:, t0:t0 + n], w=[csb])
                    S.dma("sp", cs[:, 1, :], smv[:, t0:t0 + n], w=[csb])
                loads[j] = (xt, xb, cs, csb)

            TL = list(range(len(tiles)))
            ACOL, SH0, SPLIT = A1, 0, 2

            def body(j, hcur_):
                t0, n, isc = tiles[j]
                xt, xb, cs, csb = loads[j]
                h, hb = hcur_

                def proj(col0, bank, M=128):
                    wdep = [WB[0]] if col0 < 1024 else ([WB[1]] if col0 < 1664 else [WB[2]])
                    for kc in range(8):
                        MM(PS[0:M, bank, 0:n], Wt[:, kc, col0:col0 + M], h[:, kc, 0:n], kc == 0, kc == 7, [hb] + wdep, [PB[bank]])

                xst, xstb = xst_ring.next()
                for c in range(4):
                    bk = banks.next()
                    proj(128 * c, bk)
                    CP("act" if c % 2 else "dve", xst[:, c, 0:n], PS[:, bk, 0:n], [PB[bk]], [xstb])
                    yield
                S.dma("pool", XBv[:, :, t0:t0 + n], xst[:, :, 0:n], r=[xstb], w=[XBSB[j]])
                if not isc:
                    gst, gstb = gst_ring.next()
                    for c in range(4):
                        bk = banks.next()
                        proj(512 + 128 * c, bk)
                        ACT(gst[:, c, 0:n], PS[:, bk, 0:n], AF.Gelu_apprx_tanh, [PB[bk]], [gstb])
                        yield
                    S.dma("pool", GTv[:, :, t0:t0 + n], gst[:, :, 0:n], r=[gstb], w=[GTSB[j]])

                def lowrank_norm(col0, nch, dim, gcol, dst, dstb):
                    cqs, cqsb = cqs_ring.next()
                    for c in range(nch):
                        bk = banks.next()
                        proj(col0 + 128 * c, bk)
                        CP("act" if c % 2 else "dve", cqs[:, c, 0:n], PS[:, bk, 0:n], [PB[bk]], [cqsb])
                    sq, sqb = rings["sq"].next()
                    ACT(sq[:, 0:nch, 0:n], cqs[:, 0:nch, 0:n], AF.Square, [cqsb], [sqb])
                    rs, rsb = stat_rstd(rings, sq, sqb, nch, n, dim, 7)
                    TTo("dve", cqs[:, 0:nch, 0:n], cqs[:, 0:nch, 0:n], rs[:, 0:n].unsqueeze(1).to_broadcast([128, nch, n]), ALU.mult, [cqsb, rsb], [cqsb])
                    for c in range(nch):
                        ACT(dst[:, c, t0:t0 + n], cqs[:, c, 0:n], AF.Identity, [cqsb, CB], [dstb], scale=gcol[:, c:c + 1])

                if not isc:
                    lowrank_norm(1024, 3, 384, cols["qn"], CQN, CQB[j])
                lowrank_norm(1408, 2, 256, cols["kvn"], CKVN, CKB[j])
                bk = banks.next()
                proj(1664, bk, M=64)
                if isc:
                    CP("act", KRb[32:64, t0:t0 + n], PS[32:64, bk, 0:n], [PB[bk]], [KRB[j]])
                else:
                    krs, krsb = krs_ring.next()
                    CP("act", krs[32:64, 0:n], PS[32:64, bk, 0:n], [PB[bk]], [krsb])
                    rbk = 5
                    MM(PS[0:32, rbk, 0:n], r32B[32:64, :], krs[32:64, 0:n], True, True, [krsb, CB], [PB[rbk]])
                    t1, t1b = t1_ring.next()
                    t2, t2b = t2_ring.next()
                    TTo("dve", t1[:, 0:n], PS[0:32, bk, 0:n], cs[:, 0, 0:n], ALU.mult, [PB[bk], csb], [t1b])
                    TTo("dve", t2[:, 0:n], PS[0:32, rbk, 0:n], cs[:, 1, 0:n], ALU.mult, [PB[rbk], csb], [t2b])
                    TTo("pool", KRb[0:32, t0:t0 + n], t1[:, 0:n], t2[:, 0:n], ALU.add, [t1b, t2b], [KRB[j]])
            def pn_gen(jj):
                t0_, n_, isc_ = tiles[TL[jj]]
                s_ = 1 if isc_ else 0
                return prenorm_gen(rings, loads[jj][0], loads[jj][1], n_, ACOL[:, l, s_, :], MODT[:, l, s_, SH0:SH0 + 8], 7)

            def drain(g_):
                while True:
                    try:
                        next(g_)
                    except StopIteration as e_:
                        return e_.value

            issue_load(0)
            if len(TL) > 1:
                issue_load(1)
            hcur = drain(pn_gen(0))
            for ji in range(len(TL)):
                gen = body(ji, hcur)
                k = 0
                pn = None
                hnext = None
                for _ in gen:
                    k += 1
                    if k == SPLIT and ji + 1 < len(TL):
                        pn = pn_gen(ji + 1)
                    if pn is not None:
                        try:
                            next(pn)
                        except StopIteration as e_:
                            hnext = e_.value
                            pn = None
                            if ji + 2 < len(TL):
                                issue_load(ji + 2)
                if ji + 1 < len(TL) and hnext is None:
                    if pn is None:
                        pn = pn_gen(ji + 1)
                    hnext = drain(pn)
                    if ji + 2 < len(TL):
                        issue_load(ji + 2)
                hcur = hnext
                loads.pop(ji)
            S.barrier()

    def phase_lru():
        with ExitStack() as es:
            GW = sb(es, "GW", [128, 2, 2, 4, 128], BF16)
            GWB = Buf()
            MSET("pool", GW[:], 0.0, [GWB])
            for d in range(2):
                for gate in range(2):
                    for nb in range(8):
                        p0 = (nb % 2) * 64
                        cast_load(GW[p0:p0 + 64, d, gate, nb // 2, p0:p0 + 64], W["lru_gate_w"][0, d, gate, nb], GWB)
            ytmp = sb(es, "ytmp", [128, 8], F32)
            yb_ = Buf()
            ACT(ytmp[:], cols["lam"][:], AF.Exp, [CB], [yb_], scale=-1.0)
            ACT(ytmp[:], ytmp[:], AF.Ln, [yb_, CB], [yb_], bias=onesF[:, 0:1])
            TS("dve", cch[:, 0, :], ytmp[:], -8.0, None, ALU.mult, None, [yb_], [CB])
            TS("dve", cch[:, 1, :], ytmp[:], -16.0, None, ALU.mult, None, [yb_], [CB])
            xb_ring = Ring(nc, es, "xbt", [128, 4, 515], F32, 2)
            xc_ring = Ring(nc, es, "xc", [128, 4, 512], F32, 2)
            xcb_ring = Ring(nc, es, "xcb", [128, 4, 512], BF16, 2)
            rg_ring = Ring(nc, es, "rg", [128, 4, 512], F32, 2)
            ig_ring = Ring(nc, es, "ig", [128, 4, 512], F32, 2)
            av_ring = Ring(nc, es, "av", [128, 4, 512], F32, 2)
            e2_ring = Ring(nc, es, "e2", [128, 4, 512], F32, 2)
            hv_rings = [Ring(nc, es, "hv%d" % d_, [128, 4, 512], F32, 2) for d_ in range(2)]
            XBv = XBS.rearrange("(c p) t -> p c t", p=128)
            GTv = GTS.rearrange("(c p) t -> p c t", p=128)
            HFv = HFS.rearrange("(c p) t -> p c t", p=128)
            C1v = C1.rearrange("(c p) t -> p c t", p=128)
            banks = Rot([0, 1, 2, 3, 4, 5])
            HBv = HBS.rearrange("(c p) t -> p c t", p=128)
            orders = [[NT] + list(range(NT)), [NT] + list(range(NT - 1, -1, -1))]
            prevs = [None, None]

            def stageA(d, j):
                t0, n, isc = tiles[j]
                seg0, seg1 = (T, TT) if isc else (0, T)
                xbt, xbb = xb_ring.next()
                rd = [XBSB[j]]
                if d == 0:
                    lo, hi = max(t0 - 3, seg0), t0 + n
                    if lo > t0 - 3:
                        MSET("pool", xbt[:, :, 0:3], 0.0, [xbb])
                    elif j > 0:
                        rd.append(XBSB[j - 1])
                    S.dma("sp", xbt[:, :, lo - (t0 - 3):n + 3], XBv[:, :, lo:hi], r=rd, w=[xbb])
                else:
                    lo, hi = t0, min(t0 + n + 3, seg1)
                    if hi < t0 + n + 3:
                        MSET("pool", xbt[:, :, n:n + 3], 0.0, [xbb])
                    elif j + 1 < NT:
                        rd.append(XBSB[j + 1])
                    S.dma("sp", xbt[:, :, 0:hi - lo], XBv[:, :, lo:hi], r=rd, w=[xbb])
                xc, xcb_ = xc_ring.next()
                for c in range(4):
                    wc = lambda k: cols["lcw"][:, d * 16 + k * 4 + c:d * 16 + k * 4 + c + 1]
                    TS("dve", xc[:, c, 0:n], xbt[:, c, 0:n], wc(0), cols["lcb"][:, d * 4 + c:d * 4 + c + 1], ALU.mult, ALU.add, [xbb, CB], [xcb_])
                    for k in range(1, 4):
                        STT(xc[:, c, 0:n], xbt[:, c, k:k + n], wc(k), xc[:, c, 0:n], ALU.mult, ALU.add, [xbb, xcb_, CB], [xcb_])
                xcb, xcbb = xcb_ring.next()
                CP("act", xcb[:, :, 0:n], xc[:, :, 0:n], [xcb_], [xcbb])
                rg, rgb = rg_ring.next()
                ig, igb = ig_ring.next()
                av, avb = av_ring.next()
                e2, e2b = e2_ring.next()
                for c in range(4):
                    b0 = banks.next()
                    MM(PS[:, b0, 0:n], GW[:, d, 0, c, :], xcb[:, c, 0:n], True, True, [xcbb, GWB], [PB[b0]])
                    ACT(rg[:, c, 0:n], PS[:, b0, 0:n], AF.Sigmoid, [PB[b0], CB], [rgb], bias=cols["lgb"][:, d * 8 + c:d * 8 + c + 1])
                    b1 = banks.next()
                    MM(PS[:, b1, 0:n], GW[:, d, 1, c, :], xcb[:, c, 0:n], True, True, [xcbb, GWB], [PB[b1]])
                    ACT(ig[:, c, 0:n], PS[:, b1, 0:n], AF.Sigmoid, [PB[b1], CB], [igb], bias=cols["lgb"][:, d * 8 + 4 + c:d * 8 + 4 + c + 1])
                for c in range(4):
                    ACT(av[:, c, 0:n], rg[:, c, 0:n], AF.Exp, [rgb, CB], [avb], scale=cch[:, 0, d * 4 + c:d * 4 + c + 1])
                    ACT(e2[:, c, 0:n], rg[:, c, 0:n], AF.Exp, [rgb, CB], [e2b], scale=cch[:, 1, d * 4 + c:d * 4 + c + 1])
                ACT(e2[:, :, 0:n], e2[:, :, 0:n], AF.Sqrt, [e2b, CB], [e2b], bias=onesF[:, 0:1], scale=-1.0)
                return (d, j, xc, xcb_, ig, igb, av, avb, e2, e2b)

            def stageB(ctx_):
                d, j, xc, xcb_, ig, igb, av, avb, e2, e2b = ctx_
                t0, n, isc = tiles[j]
                prev = prevs[d]
                TTo("dve", e2[:, :, 0:n], e2[:, :, 0:n], ig[:, :, 0:n], ALU.mult, [e2b, igb], [e2b])
                TTo("dve", e2[:, :, 0:n], e2[:, :, 0:n], xc[:, :, 0:n], ALU.mult, [e2b, xcb_], [e2b])
                hv, hvb = hv_rings[d].next()
                for c in range(4):
                    if prev is None:
                        init, rdp = 0.0, []
                    else:
                        ph, phb, pn = prev
                        init = ph[:, c, pn - 1:pn] if d == 0 else ph[:, c, 0:1]
                        rdp = [phb]
                    if d == 0:
                        o_, a_, b_ = hv[:, c, 0:n], av[:, c, 0:n], e2[:, c, 0:n]
                    else:
                        o_, a_, b_ = hv[:, c, 0:n][:, ::-1], av[:, c, 0:n][:, ::-1], e2[:, c, 0:n][:, ::-1]
                    S.op("dve", lambda e, o_=o_, a_=a_, b_=b_, init=init: e.tensor_tensor_scan(o_, a_, b_, init, ALU.mult, ALU.add),
                         [avb, e2b] + rdp, [hvb])
                prevs[d] = (hv, hvb, n)
                if not isc:
                    if d == 0:
                        S.dma("pool", HFv[:, :, t0:t0 + n], hv[:, :, 0:n], r=[hvb], w=[HFSB[j]])
                    else:
                        S.dma("pool", HBv[:, :, t0:t0 + n], hv[:, :, 0:n], r=[hvb], w=[HBSB[j]])

            items = [(d, orders[d][step]) for step in range(NT + 1) for d in range(2)]
            pend_ctx = None
            for (d, j) in items:
                ctx_ = stageA(d, j)
                if pend_ctx is not None:
                    stageB(pend_ctx)
                pend_ctx = ctx_
            stageB(pend_ctx)
            S.barrier()

    def phase_lru_combine():
        with ExitStack() as es:
            hf_ring = Ring(nc, es, "hf", [128, 4, 512], F32, 2)
            hb_ring = Ring(nc, es, "hb", [128, 4, 512], F32, 2)
            gg_ring = Ring(nc, es, "gg", [128, 4, 512], F32, 2)
            cl_ring = Ring(nc, es, "cl", [128, 4, 512], BF16, 2)
            GTv = GTS.rearrange("(c p) t -> p c t", p=128)
            HFv = HFS.rearrange("(c p) t -> p c t", p=128)
            HBv = HBS.rearrange("(c p) t -> p c t", p=128)
            C1v = C1.rearrange("(c p) t -> p c t", p=128)
            for j, (t0, n, isc) in enumerate(lat_tiles):
                hf, hfb = hf_ring.next()
                S.dma("sp", hf[:, :, 0:n], HFv[:, :, t0:t0 + n], r=[HFSB[j]], w=[hfb])
                hb, hbb = hb_ring.next()
                S.dma("sp", hb[:, :, 0:n], HBv[:, :, t0:t0 + n], r=[HBSB[j]], w=[hbb])
                gg, ggb = gg_ring.next()
                S.dma("sp", gg[:, :, 0:n], GTv[:, :, t0:t0 + n], r=[GTSB[j]], w=[ggb])
                cl, clb = cl_ring.next()
                TTo("dve", hf[:, :, 0:n], hf[:, :, 0:n], hb[:, :, 0:n], ALU.add, [hfb, hbb], [hfb])
                TTo("pool", cl[:, :, 0:n], hf[:, :, 0:n], gg[:, :, 0:n], ALU.mult, [hfb, ggb], [clb])
                S.dma("pool", C1v[:, :, t0:t0 + n], cl[:, :, 0:n], r=[clb], w=[C1B[j]])
            S.barrier()

    def phase_mla():
        CQN, CKVN, KRb = L1T["CQN"], L1T["CKVN"], L1T["KRb"]
        with ExitStack() as es:
            Wq = sb(es, "Wq", [128, 3, 8, 128], BF16)
            Wk = sb(es, "Wk", [128, 2, 8, 64], BF16)
            Wv = sb(es, "Wv", [128, 2, 8, 64], BF16)
            WB = Buf()
            qsrc = W["mla_w_uq"][0].rearrange("(kc p) (h e) -> p kc h e", p=128, e=96)
            ksrc = W["mla_w_ukv"][0].rearrange("(kc p) (h e) -> p kc h e", p=128, e=128)
            for kc in range(3):
                cast_load(Wq[:, kc, :, 64:128], qsrc[:, kc, :, 0:64], WB)
                cast_load(Wq[:, kc, :, 0:32], qsrc[:, kc, :, 64:96], WB)
                cast_load(Wq[:, kc, :, 32:64], qsrc[:, kc, :, 64:96], WB)
            for kc in range(2):
                cast_load(Wk[:, kc, :, :], ksrc[:, kc, :, 0:64], WB)
                cast_load(Wv[:, kc, :, :], ksrc[:, kc, :, 64:128], WB)
            Va = sb(es, "Va", [128, NB, 8, 66], BF16)
            VaB = Buf()
            MSET("pool", Va[:, :, :, 64:66], 1.0, [VaB])
            rings = attn_rings(es)
            k_ring = Ring(nc, es, "Kh", [128, TT], BF16, 2)
            q_ring = Ring(nc, es, "Qh", [128, T], BF16, 2)
            pt_ring = Ring(nc, es, "ptm", [128, 2, 512], BF16, 4)
            cs_ring = Ring(nc, es, "csq", [32, 2, 512], F32, 2)
            t1_ring = Ring(nc, es, "t1q", [32, 512], F32, 2)
            t2_ring = Ring(nc, es, "t2q", [32, 512], F32, 2)
            mbanks = Rot([6, 7])
            sbanks = Rot([0, 2])
            abanks = Rot([4, 5])
            for blk in range(NB):
                bk = mbanks.next()
                for kc in range(2):
                    MM(PS[:, bk, 0:512], CKVN[:, kc, blk * 128:(blk + 1) * 128], Wv[:, kc, :, :].rearrange("p h d -> p (h d)"),
                       kc == 0, kc == 1, [CKB[min(blk // 4, NT)], WB], [PB[bk]])
                CP("act" if blk % 2 else "dve", Va[:, blk, :, 0:64], PS[:, bk, 0:512].rearrange("p (h d) -> p h d", d=64), [PB[bk]], [VaB])
            sc = float(96 ** -0.5)
            pend = []
            qst_ = [None]
            hbufs = {}

            def get_bufs(h_):
                if h_ not in hbufs:
                    Kh_, KhB_ = k_ring.next()
                    Qh_, QhB_ = q_ring.next()
                    CP("pool", Kh_[0:64, :], KRb[0:64, :], KRB, [KhB_])
                    hbufs[h_] = (Kh_, KhB_, Qh_, QhB_)
                return hbufs[h_]

            def prod_k(h_, j):
                Kh_, KhB_, Qh_, QhB_ = get_bufs(h_)
                t0, n, isc = tiles[j]
                bk = mbanks.next()
                for kc in range(2):
                    MM(PS[64:128, bk, 0:n], Wk[:, kc, h_, :], CKVN[:, kc, t0:t0 + n], kc == 0, kc == 1, [CKB[j], WB], [PB[bk]])
                CP("dve", Kh_[64:128, t0:t0 + n], PS[64:128, bk, 0:n], [PB[bk]], [KhB_])

            def prod_q_a(h_, j):
                Kh_, KhB_, Qh_, QhB_ = get_bufs(h_)
                t0, n, isc = tiles[j]
                bk = mbanks.next()
                for kc in range(3):
                    MM(PS[:, bk, 0:n], Wq[:, kc, h_, :], CQN[:, kc, t0:t0 + n], kc == 0, kc == 2, [CQB[j], WB], [PB[bk]])
                CP("dve", Qh_[32:64, t0:t0 + n], PS[32:64, bk, 0:n], [PB[bk]], [QhB_])
                CP("dve", Qh_[64:128, t0:t0 + n], PS[64:128, bk, 0:n], [PB[bk]], [QhB_])
                t1, t1b = t1_ring.next()
                cs, csb = cs_ring.next()
                S.dma("sp", cs[:, 0, :], cmv[:, t0:t0 + n], w=[csb])
                S.dma("sp", cs[:, 1, :], smv[:, t0:t0 + n], w=[csb])
                TTo("dve", t1[:, 0:n], PS[0:32, bk, 0:n], cs[:, 0, 0:n], ALU.mult, [PB[bk], csb], [t1b])
                return (t1, t1b, cs, csb)

            def prod_q_b(h_, j, st_):
                Kh_, KhB_, Qh_, QhB_ = get_bufs(h_)
                t0, n, isc = tiles[j]
                t1, t1b, cs, csb = st_
                rbk = mbanks.next()
                MM(PS[0:32, rbk, 0:n], r32B[32:64, :], Qh_[32:64, t0:t0 + n], True, True, [QhB_, CB], [PB[rbk]])
                t2, t2b = t2_ring.next()
                TTo("dve", t2[:, 0:n], PS[0:32, rbk, 0:n], cs[:, 1, 0:n], ALU.mult, [PB[rbk], csb], [t2b])
                TTo("pool", Qh_[0:32, t0:t0 + n], t1[:, 0:n], t2[:, 0:n], ALU.add, [t1b, t2b], [QhB_])

            def prod_q(h_, j):
                prod_q_b(h_, j, prod_q_a(h_, j))

            for j in range(len(tiles)):
                prod_k(0, j)
            for j in range(NT):
                prod_q(0, j)
            for hh in range(8):
                Kh, KhB, Qh, QhB = get_bufs(hh)
                for j, (t0, n, isc) in enumerate(lat_tiles):
                    acc = abanks.next()
                    ngrp = (NB + 1) // 2
                    for gi in range(ngrp):
                        kbs = [kb for kb in (2 * gi, 2 * gi + 1) if kb < NB]
                        sb0 = sbanks.next()
                        for qi, kb in enumerate(kbs):
                            MM(PS[:, sb0 + qi, 0:n], Kh[:, kb * 128:(kb + 1) * 128], Qh[:, t0:t0 + n], True, True, [KhB, QhB], [PB[sb0 + qi]])
                        pt, ptb = pt_ring.next()
                        m = len(kbs)
                        ACT(pt[:, 0:m, 0:n], PS[:, sb0:sb0 + m, 0:n], AF.Exp, [PB[sb0 + q_] for q_ in range(m)], [ptb], scale=sc)
                        while len(pend) >= 2:
                            pend.pop(0)()

                        def later(kbs=kbs, acc=acc, n=n, pt=pt, ptb=ptb, gi=gi, hh=hh, t0=t0, j=j, last=(gi == ngrp - 1)):
                            for qi, kb in enumerate(kbs):
                                MM(PS[0:65, acc, 0:n], Va[:, kb, hh, 0:65], pt[:, qi, 0:n], gi == 0 and qi == 0, kb == NB - 1, [ptb, VaB], [PB[acc]])
                            if last:
                                osb, ob = attn_finalize_a(rings, acc, n, "dve")
                                pend.append(lambda: attn_finalize_b(rings, osb, ob, n, None, D1[hh * 64:(hh + 1) * 64, t0:t0 + n], D1B[j], mbanks.next()))
                        pend.append(later)
                        if hh + 1 < 8:
                            if gi == ngrp // 5:
                                prod_k(hh + 1, j)
                            if gi == (2 * ngrp) // 5:
                                qst_[0] = prod_q_a(hh + 1, j)
                            if gi == (4 * ngrp) // 5:
                                prod_q_b(hh + 1, j, qst_[0])
                            if gi == ngrp - 1 and j == NT - 1:
                                prod_k(hh + 1, NT)
            while pend:
                pend.pop(0)()
            S.barrier()

    def layer1():
        alloc_l1()
        phase_p1_l1()
        if stop == "p1_l1":
            return
        phase_lru()
        phase_lru_combine()
        if stop == "lru":
            return
        phase_mla()
        if stop == "mla":
            return
        L1.close()
        phase_wout(1, W["cd_w_out"][0], C1, C1B, D1, D1B, lat_t)
        if stop == "wout1":
            return
        phase_ffna(1, lat_t)
        phase_ffnb(1, lat_t)

    all_t = list(range(len(tiles)))
    lat_t = list(range(NT))

    def dump_small(parts):
        off = 0
        for t, w in parts:
            S.dma("sp", DBGV[:, off:off + w], t, r=[CB], w=[Buf()])
            off += w
        S.barrier()

    def run():
        phase_mod(0)
        if stop == "mod0":
            dump_small([(MODT[:, 0, :, :].rearrange("p s m -> p (s m)"), 96), (A1[:, 0].rearrange("p s m -> p (s m)"), 16),
                        (G1[:, 0].rearrange("p s m -> p (s m)"), 16), (A2[:, 0].rearrange("p s m -> p (s m)"), 16),
                        (G2[:, 0].rearrange("p s m -> p (s m)"), 16)])
            return
        phase_tin()
        if stop == "tin":
            return
        phase_p1_l0()
        if stop == "p1_l0":
            return
        phase_conva()
        if stop == "conva":
            return
        phase_attn0()
        if stop == "attn0":
            return
        L0.close()
        phase_wout(0, W["ab_w_out"][0], A0, A0B, B0, B0B, all_t)
        if stop == "wout0":
            return
        phase_ffna(0, all_t)
        if stop == "ffna0":
            return
        phase_ffnb(0, all_t)
        if stop == "ffnb0":
            return
        layer1()
        phase_tout()

    run()
    L1.close()
    L0.close()
    ges.close()
    build.stats = (S.nops, S.nwaits)
    return nc


def make_in_maps(inputs, T):
    consts = make_consts(T)
    f = lambda a: np.ascontiguousarray(np.asarray(a, dtype=np.float32))
    shared = {k: f(inputs[k]) for k in WEIGHT_SHAPES}
    shared.update(consts)
    shared["c_ctx"] = f(inputs["c_ctx"]).reshape(8, 128)
    x, c, ctx = f(inputs["x"]), f(inputs["c"]), f(inputs["ctx"])
    maps = []
    for b in range(x.shape[0]):
        m = dict(shared)
        m["x"] = np.ascontiguousarray(x[b])
        m["ctx"] = np.ascontiguousarray(ctx[b])
        m["c"] = np.ascontiguousarray(c[b]).reshape(8, 128)
        maps.append(m)
    return maps


def kernel(**inputs):
    T = int(np.asarray(inputs["x"]).shape[1])
    nc = build(T)
    in_maps = make_in_maps(inputs, T)
    res = run_bass_kernel_spmd(nc, in_maps, core_ids=list(range(len(in_maps))))
    return np.stack([np.asarray(r["y"], dtype=np.float32) for r in res.results], axis=0)
```

```python
import numpy as np
import ml_dtypes
from contextlib import ExitStack
import concourse.bass as bass
import concourse.mybir as mybir
from concourse.bass_utils import run_bass_kernel_spmd

F32 = mybir.dt.float32
BF16 = mybir.dt.bfloat16
AF = mybir.ActivationFunctionType
ALU = mybir.AluOpType

D = 1024
CTX = 256
EPS = 1e-6
FFN = 2816
NJ = FFN // 128


class Buf:
    __slots__ = ("w", "r")

    def __init__(self):
        self.w = None
        self.r = {}


class Op:
    __slots__ = ("eng", "chan", "seq", "fn", "waits", "signal", "clock", "val", "isdma")


class Sched:
    COMPUTE = ("pe", "act", "dve", "pool")

    def __init__(self, nc, es):
        self.nc = nc
        self.eobj = dict(pe=nc.tensor, act=nc.scalar, dve=nc.vector, pool=nc.gpsimd, sp=nc.sync)
        self.sem = {}
        for e in self.COMPUTE:
            self.sem[e] = es.enter_context(nc.semaphore("sem_" + e))
        self.nslot = {"sp": 12, "pool": 8}
        for q, n in self.nslot.items():
            for k in range(n):
                self.sem[(q, k)] = es.enter_context(nc.semaphore("dq_%s%d" % (q, k)))
        self.clock = {e: {} for e in self.eobj}
        self.seq = {e: 0 for e in self.COMPUTE}
        self.sigcount = {e: 0 for e in self.COMPUTE}
        self.dcount = {q: 0 for q in self.nslot}
        self.slot_last = {}
        self.last = {}
        self.pending = []
        self.bar = None
        self.nops = 0
        self.nwaits = 0
        self.dummy = es.enter_context(nc.sbuf_tensor("sched_dummy", [128, 8], F32))

    def _add(self, eng, chan, seq, fn, r, w, isdma, extra=()):
        op = Op()
        op.eng, op.chan, op.seq, op.fn, op.isdma = eng, chan, seq, fn, isdma
        op.signal = isdma
        op.val = 16 * (seq + 1) if isdma else None
        deps = {}

        def need(d):
            if d is None:
                return
            cur = deps.get(d.chan)
            if cur is None or cur.seq < d.seq:
                deps[d.chan] = d

        r = list(r)
        if self.bar is not None:
            r.append(self.bar)
        for b in r:
            need(b.w)
        for b in w:
            need(b.w)
            for d in b.r.values():
                need(d)
        for d in extra:
            need(d)
        clk = self.clock[eng]
        waits = []
        for d in deps.values():
            if d.chan == "pe" and eng == "pe":
                continue
            if clk.get(d.chan, -1) >= d.seq:
                continue
            waits.append(d)
            d.signal = True
            for k, v in d.clock.items():
                if clk.get(k, -1) < v:
                    clk[k] = v
            if clk.get(d.chan, -1) < d.seq:
                clk[d.chan] = d.seq
        op.waits = waits
        op.clock = dict(clk)
        for b in r:
            cur = b.r.get(chan)
            if cur is None or cur.seq < seq:
                b.r[chan] = op
        for b in w:
            b.w = op
            b.r = {}
        self.last[chan] = op
        self.pending.append(op)
        self.nops += 1
        self.nwaits += len(waits)
        return op

    def op(self, eng, fn, r=(), w=()):
        s = self.seq[eng]
        self.seq[eng] = s + 1
        return self._add(eng, eng, s, fn, r, w, False)

    def dma(self, q, out, in_, r=(), w=(), **kw):
        i = self.dcount[q]
        self.dcount[q] = i + 1
        n = self.nslot[q]
        slot, gen = i % n, i // n
        chan = (q, slot)
        prev = self.slot_last.get(chan)
        extra = [prev] if prev is not None else []
        op = self._add(q, chan, gen, lambda e: e.dma_start(out=out, in_=in_, **kw), r, w, True, extra)
        self.slot_last[chan] = op
        return op

    def barrier(self):
        b = Buf()
        extra = list(self.last.values())
        dummy = self.dummy
        s = self.seq["pool"]
        self.seq["pool"] = s + 1
        m = self._add("pool", "pool", s, lambda e: e.memset(dummy[:], 0.0), [], [b], False, extra)
        m.signal = True
        self.bar = b
        self.flush()

    def flush(self):
        for op in self.pending:
            e = self.eobj[op.eng]
            for d in op.waits:
                e.wait_ge(self.sem[d.chan], d.val)
            if op.signal and not op.isdma:
                self.sigcount[op.chan] += 1
                op.val = self.sigcount[op.chan]
            ins = op.fn(e)
            if op.signal:
                ins.then_inc(self.sem[op.chan], 16 if op.isdma else 1)
            op.fn = None
        self.pending = []


class Ring:
    uid = 0

    def __init__(self, nc, es, name, shape, dtype, n):
        Ring.uid += 1
        self.t = [es.enter_context(nc.sbuf_tensor("%s_%d_%d" % (name, Ring.uid, i), shape, dtype)) for i in range(n)]
        self.b = [Buf() for _ in range(n)]
        self.i = 0

    def next(self):
        k = self.i % len(self.t)
        self.i += 1
        return self.t[k], self.b[k]


class Rot:
    def __init__(self, items):
        self.items = list(items)
        self.i = 0

    def next(self):
        v = self.items[self.i % len(self.items)]
        self.i += 1
        return v


def _rope_tables(T, dim):
    rows = T // 64
    row = np.repeat(np.arange(rows), 64).astype(np.float32)
    col = np.tile(np.arange(64), rows).astype(np.float32)
    nf = dim // 4
    inv = (np.float32(10000.0) ** (-np.arange(nf, dtype=np.float32) / np.float32(nf))).astype(np.float32)
    ar = row[:, None] * inv
    ac = col[:, None] * inv
    ang = np.concatenate([ar, ar, ac, ac], axis=-1)
    return np.ascontiguousarray(np.cos(ang).T.astype(np.float32)), np.ascontiguousarray(np.sin(ang).T.astype(np.float32))


def _rot_T(dim):
    q = dim // 4
    R = np.zeros((dim, dim), np.float32)
    for i in range(q):
        R[i, q + i] = -1.0
        R[q + i, i] = 1.0
        R[2 * q + i, 3 * q + i] = -1.0
        R[3 * q + i, 2 * q + i] = 1.0
    return np.ascontiguousarray(R.T)


def make_consts(T):
    cw, sw = _rope_tables(T, 64)
    cm, sm = _rope_tables(T, 32)
    ident = np.eye(128, dtype=np.float32)
    sel = np.zeros((128, 64), np.float32)
    sel[64, :] = 1.0
    b = np.arange(128)[:, None]
    a = np.arange(128)[None, :]
    m1 = (b <= a).astype(np.float32)
    m2 = (a <= b).astype(np.float32)
    r64 = _rot_T(64)
    r32 = _rot_T(32)
    return {
        "k_ident": ident, "k_sel": sel, "k_m1": m1, "k_m2": m2,
        "k_r64": np.concatenate([r64, r64], 0), "k_r32": np.concatenate([r32, r32], 0),
        "k_cw": cw, "k_sw": sw, "k_cm": cm, "k_sm": sm,
    }


WEIGHT_SHAPES = {
    "w_mod": [2, 1024, 6144], "b_mod": [2, 6144], "norm_g": [2, 4, 1024], "ffn_w_up": [2, 1024, 5632],
    "ffn_conv_w": [2, 3, 2816], "ffn_conv_b": [2, 2816], "ffn_w_down": [2, 2816, 1024],
    "ab_w_in": [1, 1024, 1792], "a_conv_w": [1, 31, 512], "a_conv_b": [1, 512], "a_ln_g": [1, 512],
    "a_ln_b": [1, 512], "b_sink": [1, 8], "ab_w_out": [1, 1024, 1024], "cd_w_in": [1, 1024, 1696],
    "lru_conv_w": [1, 2, 4, 512], "lru_conv_b": [1, 2, 512], "lru_gate_w": [1, 2, 2, 8, 64, 64],
    "lru_gate_b": [1, 2, 2, 512], "lru_lambda": [1, 2, 512], "mla_q_norm": [1, 384],
    "mla_w_uq": [1, 384, 768], "mla_kv_norm": [1, 256], "mla_w_ukv": [1, 256, 1024],
    "cd_w_out": [1, 1024, 1024],
}


def build(T=4096, dbg=(), stop=None):
    nc = bass.Bass("TRN2", target_bir_lowering=False)
    TT = T + CTX
    NT = T // 512
    NBL = T // 128
    NB = NBL + CTX // 128
    tiles = [(i * 512, 512, False) for i in range(NT)] + [(T, CTX, True)]
    lat_tiles = tiles[:NT]

    def din(name, shape):
        return nc.dram_tensor(name, list(shape), F32, kind="ExternalInput").ap()

    x_in = din("x", [T, D])
    ctx_in = din("ctx", [CTX, D])
    c_in = din("c", [8, 128])
    cctx_in = din("c_ctx", [8, 128])
    W = {k: din(k, s) for k, s in WEIGHT_SHAPES.items()}
    KC = {k: din(k, v.shape) for k, v in make_consts(T).items()}
    y_out = nc.dram_tensor("y", [T, D], F32, kind="ExternalOutput").ap()

    def scratch(name, shape, dt):
        kind = "ExternalOutput" if name in dbg else "Internal"
        return nc.dram_tensor(name, list(shape), dt, kind=kind).ap()

    XT = scratch("XT", [D, TT], F32)
    U0 = scratch("U0", [512, TT], BF16)
    A0 = scratch("A0", [512, TT], BF16)
    B0 = scratch("B0", [512, TT], BF16)
    GS = scratch("GS", [FFN, TT], BF16)
    US = scratch("US", [FFN, TT], BF16)
    XBS = scratch("XBS", [512, TT], F32)
    GTS = scratch("GTS", [512, T], F32)
    HFS = scratch("HFS", [512, T], F32)
    HBS = scratch("HBS", [512, T], F32)
    C1 = scratch("C1", [512, T], BF16)
    D1 = scratch("D1", [512, T], BF16)
    DBGV = scratch("DBGV", [128, 512], F32)
    QS = scratch("QS", [8, 128, TT], BF16)
    XTv = XT.rearrange("(c p) t -> p c t", p=128)
    XTB = [Buf() for _ in tiles]
    U0B = [Buf() for _ in tiles]
    A0B = [Buf() for _ in tiles]
    B0B = [Buf() for _ in tiles]
    GSB = [Buf() for _ in tiles]
    USB = [Buf() for _ in tiles]
    XBSB = [Buf() for _ in tiles]
    GTSB = [Buf() for _ in tiles]
    HFSB = [Buf() for _ in tiles]
    HBSB = [Buf() for _ in tiles]
    C1B = [Buf() for _ in tiles]
    D1B = [Buf() for _ in tiles]

    ges = ExitStack()
    S = Sched(nc, ges)
    PS = ges.enter_context(nc.psum_tensor("PS", [128, 8, 512], F32))
    PB = [Buf() for _ in range(8)]

    def sb(es, name, shape, dt):
        Ring.uid += 1
        return es.enter_context(nc.sbuf_tensor("%s_%d" % (name, Ring.uid), list(shape), dt))

    def MM(out, lhsT, rhs, st, sp, r, w):
        S.op("pe", lambda e: e.matmul(out, lhsT, rhs, start=st, stop=sp), r, w)

    def TR(out, in_, ident, r, w):
        S.op("pe", lambda e: e.transpose(out, in_, ident), r, w)

    def ACT(out, in_, func, r, w, bias=None, scale=None):
        kw = {}
        if bias is not None:
            kw["bias"] = bias
        if scale is not None:
            kw["scale"] = scale
        S.op("act", lambda e: e.activation(out, in_, func, **kw), r, w)

    def CP(eng, out, in_, r, w):
        if eng == "act":
            S.op("act", lambda e: e.copy(out, in_), r, w)
        else:
            S.op(eng, lambda e: e.tensor_copy(out, in_), r, w)

    def TTo(eng, out, a, b, op, r, w):
        S.op(eng, lambda e: e.tensor_tensor(out, a, b, op), r, w)

    def TS(eng, out, a, s1, s2, op0, op1, r, w):
        if s2 is None:
            S.op(eng, lambda e: e.tensor_scalar(out, a, s1, None, op0), r, w)
        else:
            S.op(eng, lambda e: e.tensor_scalar(out, a, s1, s2, op0, op1), r, w)

    def STT(out, in0, scalar, in1, op0, op1, r, w):
        S.op("dve", lambda e: e.scalar_tensor_tensor(out, in0, scalar, in1, op0, op1), r, w)

    def RCP(out, in_, r, w):
        S.op("dve", lambda e: e.reciprocal(out, in_), r, w)

    def MSET(eng, ap, val, w):
        S.op(eng, lambda e: e.memset(ap, val), [], w)

    identF = sb(ges, "identF", [128, 128], F32)
    identB = sb(ges, "identB", [128, 128], BF16)
    onesB = sb(ges, "onesB", [128, 128], BF16)
    onesF = sb(ges, "onesF", [128, 128], F32)
    selF = sb(ges, "selF", [128, 64], F32)
    m1B = sb(ges, "m1B", [128, 128], BF16)
    m2B = sb(ges, "m2B", [128, 128], BF16)
    r64B = sb(ges, "r64B", [128, 64], BF16)
    r32B = sb(ges, "r32B", [64, 32], BF16)
    CB = Buf()
    S.dma("sp", identF[:], KC["k_ident"], w=[CB])
    S.dma("sp", selF[:], KC["k_sel"], w=[CB])
    S.dma("pool", m1B[:], KC["k_m1"], w=[CB])
    S.dma("pool", m2B[:], KC["k_m2"], w=[CB])
    S.dma("pool", r64B[:], KC["k_r64"], w=[CB])
    S.dma("pool", r32B[:], KC["k_r32"], w=[CB])
    CP("dve", identB[:], identF[:], [CB], [CB])
    MSET("dve", onesB[:], 1.0, [CB])
    MSET("dve", onesF[:], 1.0, [CB])

    cols = {}
    colspec = {
        "g": (W["norm_g"], 64), "bm": (W["b_mod"], 96), "fcb": (W["ffn_conv_b"], 44),
        "fcw0": (W["ffn_conv_w"][0], 66), "fcw1": (W["ffn_conv_w"][1], 66),
        "acb": (W["a_conv_b"], 4), "alg": (W["a_ln_g"], 4), "alb": (W["a_ln_b"], 4),
        "acw": (W["a_conv_w"], 124), "lcw": (W["lru_conv_w"], 32), "lcb": (W["lru_conv_b"], 8),
        "lgb": (W["lru_gate_b"], 16), "lam": (W["lru_lambda"], 8), "qn": (W["mla_q_norm"], 3),
        "kvn": (W["mla_kv_norm"], 2), "c": (c_in, 8), "cc": (cctx_in, 8),
    }
    for name, (src, n) in colspec.items():
        cols[name] = sb(ges, "col_" + name, [128, n], F32)
    esink = sb(ges, "esink", [64, 8], F32)
    scT = sb(ges, "scT", [128, 8, 2], F32)
    MODT = sb(ges, "MODT", [128, 2, 2, 48], F32)
    A1 = sb(ges, "A1", [128, 2, 2, 8], F32)
    G1 = sb(ges, "G1", [128, 2, 2, 8], F32)
    A2 = sb(ges, "A2", [128, 2, 2, 8], F32)
    G2 = sb(ges, "G2", [128, 2, 2, 8], F32)
    epsT = sb(ges, "epsT", [128, 1], F32)
    cch = sb(ges, "cch", [128, 2, 8], F32)
    pre = ExitStack()
    rows_ring = Ring(nc, pre, "rows", [128, 128], F32, 2)
    for i, (name, (src, n)) in enumerate(colspec.items()):
        dst = cols[name]
        nd = len(src.shape)
        if nd == 1:
            s2 = src.rearrange("(r p) -> r p", p=128)
        elif nd == 2 and src.shape[1] == 128:
            s2 = src
        else:
            names = " ".join("a%d" % k for k in range(nd - 1))
            s2 = src.rearrange("%s (r p) -> (%s r) p" % (names, names), p=128)
        rt, rb = rows_ring.next()
        S.dma("sp", rt[0:n, :], s2, w=[rb])
        bank = 6 + (i % 2)
        TR(PS[:, bank, 0:n], rt[0:n, :], identF[0:n, 0:n], [rb, CB], [PB[bank]])
        CP("dve", dst[:], PS[:, bank, 0:n], [PB[bank]], [CB])
    sk = sb(pre, "sk", [1, 8], F32)
    skb = Buf()
    S.dma("sp", sk[:], W["b_sink"], w=[skb])
    MM(PS[0:64, 5, 0:8], onesF[0:1, 0:64], sk[0:1, :], True, True, [skb, CB], [PB[5]])
    ACT(esink[:], PS[0:64, 5, 0:8], AF.Exp, [PB[5]], [CB])
    ACT(scT[:, :, 0], cols["c"][:], AF.Silu, [CB], [CB])
    ACT(scT[:, :, 1], cols["cc"][:], AF.Silu, [CB], [CB])
    S.barrier()
    pre.close()

    gc = cols["g"]

    def mod_load(l, jb, wring):
        wsrc = W["w_mod"][l].rearrange("(kc p) n -> p kc n", p=128)
        wt, wb = wring.next()
        S.dma("sp", wt[:], wsrc[:, :, jb * 768:(jb + 1) * 768], w=[wb])
        return wt, wb

    def mod_mm(l, jb, wt, wb):
        for jj in range(6):
            j = jb * 6 + jj
            for kc in range(8):
                MM(PS[:, 6, 2 * j:2 * j + 2], wt[:, kc, jj * 128:(jj + 1) * 128], scT[:, kc, :],
                   kc == 0, kc == 7, [wb, CB], [PB[6]])

    def mod_block(l, jb, wring):
        wt, wb = mod_load(l, jb, wring)
        mod_mm(l, jb, wt, wb)

    def mod_finish(l):
        pv = PS[:, 6, 0:96].rearrange("p (j s) -> p j s", s=2)
        for s in range(2):
            TTo("dve", MODT[:, l, s, :], pv[:, :, s], cols["bm"][:, l * 48:(l + 1) * 48], ALU.add, [PB[6], CB], [CB])
            STT(A1[:, l, s, :], MODT[:, l, s, 8:16], 1.0, gc[:, l * 32:l * 32 + 8], ALU.add, ALU.mult, [CB], [CB])
            TTo("dve", G1[:, l, s, :], MODT[:, l, s, 16:24], gc[:, l * 32 + 8:l * 32 + 16], ALU.mult, [CB], [CB])
            STT(A2[:, l, s, :], MODT[:, l, s, 32:40], 1.0, gc[:, l * 32 + 16:l * 32 + 24], ALU.add, ALU.mult, [CB], [CB])
            TTo("dve", G2[:, l, s, :], MODT[:, l, s, 40:48], gc[:, l * 32 + 24:l * 32 + 32], ALU.mult, [CB], [CB])

    def phase_mod(l):
        with ExitStack() as es:
            wring = Ring(nc, es, "wmod", [128, 8, 768], F32, 2)
            for jb in range(8):
                mod_block(l, jb, wring)
            mod_finish(l)
            S.barrier()

    def phase_tin(with_mod=False):
        with ExitStack() as es:
            xin_ring = Ring(nc, es, "xin", [128, D], F32, 5)
            xt_ring = Ring(nc, es, "xtt", [128, 8, 512], F32, 2)
            mod_jb = [0]
            mod_q = []
            if with_mod:
                wring = Ring(nc, es, "wmod", [128, 8, 768], F32, 2)

            def mod_step():
                k = mod_jb[0]
                if (not with_mod) or k > 8:
                    return
                if k < 8:
                    mod_q.append((k,) + mod_load(0, k, wring))
                if k >= 1:
                    kk, wt_, wb_ = mod_q.pop(0)
                    mod_mm(0, kk, wt_, wb_)
                mod_jb[0] += 1

            for j, (t0, n, isc) in enumerate(tiles):
                mod_step()
                src = ctx_in if isc else x_in
                s0 = 0 if isc else t0
                nb = n // 128
                xins = []
                for b in range(nb):
                    xin, xb_ = xin_ring.next()
                    S.dma("sp", xin[:], src[s0 + b * 128:s0 + (b + 1) * 128, :], w=[xb_])
                    xins.append((xin, xb_))
                xt, xtb = xt_ring.next()
                for half in range(2):
                    for b in range(nb):
                        xin, xb_ = xins[b]
                        for fc in range(half * 4, half * 4 + 4):
                            TR(PS[:, fc % 4, b * 128:(b + 1) * 128], xin[:, fc * 128:(fc + 1) * 128], identF[:], [xb_, CB], [PB[fc % 4]])
                    for fc in range(half * 4, half * 4 + 4):
                        CP("act" if fc % 2 else "dve", xt[:, fc, 0:n], PS[:, fc % 4, 0:n], [PB[fc % 4]], [xtb])
                S.dma("pool", XTv[:, :, t0:t0 + n], xt[:, :, 0:n], r=[xtb], w=[XTB[j]])
            if with_mod:
                while mod_jb[0] <= 8:
                    mod_step()
                mod_finish(0)
            S.barrier()

    def stat_rstd(es_rings, src, srcb, nch, n, dim, bank):
        for c in range(nch):
            MM(PS[:, bank, 0:n], onesB[:], src[:, c, 0:n], c == 0, c == nch - 1, [srcb, CB], [PB[bank]])
        rs, rsb = es_rings["rs"].next()
        ACT(rs[:, 0:n], PS[:, bank, 0:n], AF.Sqrt, [PB[bank]], [rsb], bias=epsT[:, 0:1], scale=1.0 / dim)
        RCP(rs[:, 0:n], rs[:, 0:n], [rsb], [rsb])
        return rs, rsb

    MSET("dve", epsT[:], EPS, [CB])

    def prenorm_gen(rings, xt, xb, n, Acol, SHcol, bank):
        sq, sqb = rings["sq"].next()
        ACT(sq[:, :, 0:n], xt[:, :, 0:n], AF.Square, [xb], [sqb])
        yield
        for c in range(8):
            MM(PS[:, bank, 0:n], onesB[:], sq[:, c, 0:n], c == 0, c == 7, [sqb, CB], [PB[bank]])
        yield
        rs, rsb = rings["rs"].next()
        ACT(rs[:, 0:n], PS[:, bank, 0:n], AF.Sqrt, [PB[bank]], [rsb], bias=epsT[:, 0:1], scale=1.0 / D)
        RCP(rs[:, 0:n], rs[:, 0:n], [rsb], [rsb])
        yield
        TTo("dve", xt[:, :, 0:n], xt[:, :, 0:n], rs[:, 0:n].unsqueeze(1).to_broadcast([128, 8, n]), ALU.mult, [xb, rsb], [xb])
        yield
        h, hb = rings["h"].next()
        for c in range(8):
            ACT(h[:, c, 0:n], xt[:, c, 0:n], AF.Identity, [xb, CB], [hb], bias=SHcol[:, c:c + 1], scale=Acol[:, c:c + 1])
        return h, hb

    def prenorm(rings, xt, xb, n, Acol, SHcol, bank):
        g_ = prenorm_gen(rings, xt, xb, n, Acol, SHcol, bank)
        while True:
            try:
                next(g_)
            except StopIteration as e_:
                return e_.value

    def postnorm_residual(rings, ysb, yb, xt, xb, n, Gcol, bank, sqpre=None):
        if sqpre is None:
            sq, sqb = rings["sq"].next()
            ACT(sq[:, :, 0:n], ysb[:, :, 0:n], AF.Square, [yb], [sqb])
        else:
            sq, sqb = sqpre
        rs, rsb = stat_rstd(rings, sq, sqb, 8, n, D, bank)
        TTo("dve", ysb[:, :, 0:n], ysb[:, :, 0:n], rs[:, 0:n].unsqueeze(1).to_broadcast([128, 8, n]), ALU.mult, [yb, rsb], [yb])
        for c in range(8):
            STT(xt[:, c, 0:n], ysb[:, c, 0:n], Gcol[:, c:c + 1], xt[:, c, 0:n], ALU.mult, ALU.add, [yb, xb, CB], [xb])

    def norm_rings(es, with_h=True, nsq=2):
        rings = {
            "sq": Ring(nc, es, "sq", [128, 8, 512], BF16, nsq),
            "rs": Ring(nc, es, "rs", [128, 512], F32, 2),
        }
        if with_h:
            rings["h"] = Ring(nc, es, "h", [128, 8, 512], BF16, 2)
        return rings

    def cast_load(dst, src, wb):
        S.dma("pool", dst, src, w=[wb])

    L0 = ExitStack()
    Klat = sb(L0, "Klat", [64, 2, T], BF16)
    Kctx = sb(L0, "Kctx", [128, 2, CTX], BF16)
    Vt = sb(L0, "Vt", [128, NB, 2, 66], BF16)
    QSB = [[Buf() for _ in tiles] for _ in range(8)]
    KLB = [Buf() for _ in tiles]
    KCB = Buf()
    VB = [Buf() for _ in tiles]
    cwv, swv = KC["k_cw"], KC["k_sw"]
    cmv, smv = KC["k_cm"], KC["k_sm"]

    def phase_p1_l0():
        l = 0
        with ExitStack() as es:
            NCOL = 1024 + 1024 + 256 + 128
            Wt = sb(es, "Wt0", [128, 8, NCOL], BF16)
            WB = [Buf() for _ in range(5)]
            wsrc = W["ab_w_in"][0].rearrange("(kc p) n -> p kc n", p=128)
            cast_load(Wt[:, :, 0:1024], wsrc[:, :, 0:1024], WB[0])
            qd = Wt[:, :, 1024:2048].rearrange("p k (h two d) -> p k h two d", two=2, d=64)
            qs = wsrc[:, :, 1024:1536].rearrange("p k (h d) -> p k h d", d=64)
            for dup in range(2):
                for kc in range(8):
                    cast_load(qd[:, kc, :, dup, :], qs[:, kc, :, :], WB[1 + dup])
            kd = Wt[:, :, 2048:2304].rearrange("p k (h two d) -> p k h two d", two=2, d=64)
            ks = wsrc[:, :, 1536:1664].rearrange("p k (h d) -> p k h d", d=64)
            for dup in range(2):
                for kc in range(8):
                    cast_load(kd[:, kc, :, dup, :], ks[:, kc, :, :], WB[3])
            cast_load(Wt[:, :, 2304:2432], wsrc[:, :, 1664:1792], WB[4])
            MSET("pool", Vt[:, :, :, 64:66], 1.0, VB)
            rings = norm_rings(es)
            x_ring = Ring(nc, es, "xt", [128, 8, 512], F32, 2)
            sg_ring = Ring(nc, es, "sg", [128, 512], F32, 2)
            ust_ring = Ring(nc, es, "ust", [128, 4, 512], BF16, 2)
            cs_ring = Ring(nc, es, "cs", [64, 2, 512], F32, 3)
            t1_ring = Ring(nc, es, "t1", [64, 512], F32, 2)
            t2_ring = Ring(nc, es, "t2", [64, 512], F32, 2)
            kraw_ring = Ring(nc, es, "kraw", [128, 512], BF16, 2)
            qst_ring = Ring(nc, es, "qst", [128, 512], BF16, 3)
            banks = Rot([0, 1, 2, 3, 4])
            rbanks = Rot([5, 6])
            loads = {}

            def issue_load(j):
                t0, n, isc = tiles[j]
                xt, xb = x_ring.next()
                S.dma("sp", xt[:, :, 0:n], XTv[:, :, t0:t0 + n], r=[XTB[j]], w=[xb])
                cs, csb = cs_ring.next()
                if not isc:
                    S.dma("sp", cs[:, 0, :], cwv[:, t0:t0 + n], w=[csb])
                    S.dma("sp", cs[:, 1, :], swv[:, t0:t0 + n], w=[csb])
                loads[j] = (xt, xb, cs, csb)

            TL = list(range(len(tiles)))
            ACOL, SH0, SPLIT = A1, 0, 3

            def body(j, hcur_):
                t0, n, isc = tiles[j]
                xt, xb, cs, csb = loads[j]
                h, hb = hcur_

                def proj(col0, bank, M=128):
                    wdep = [WB[0]] if col0 < 1024 else ([WB[1], WB[2]] if col0 < 2048 else [WB[3]])
                    for kc in range(8):
                        MM(PS[0:M, bank, 0:n], Wt[:, kc, col0:col0 + M], h[:, kc, 0:n], kc == 0, kc == 7, [hb] + wdep, [PB[bank]])

                ust, ustb = ust_ring.next()
                for i in range(4):
                    bg = banks.next()
                    proj(512 + 128 * i, bg)
                    sg, sgb = sg_ring.next()
                    ACT(sg[:, 0:n], PS[:, bg, 0:n], AF.Sigmoid, [PB[bg]], [sgb])
                    bv = banks.next()
                    proj(128 * i, bv)
                    TTo("dve", ust[:, i, 0:n], PS[:, bv, 0:n], sg[:, 0:n], ALU.mult, [PB[bv], sgb], [ustb])
                    yield
                S.dma("pool", U0.rearrange("(c p) t -> p c t", p=128)[:, :, t0:t0 + n], ust[:, :, 0:n], r=[ustb], w=[U0B[j]])

                def rope(bank, rawsrc, rawb, dst, dstb):
                    rbk = rbanks.next()
                    MM(PS[0:64, rbk, 0:n], r64B[64:128, :], rawsrc, True, True, [rawb, CB], [PB[rbk]])
                    t1, t1b = t1_ring.next()
                    t2, t2b = t2_ring.next()
                    TTo("dve", t1[:, 0:n], PS[0:64, bank, 0:n], cs[:, 0, 0:n], ALU.mult, [PB[bank], csb], [t1b])
                    TTo("dve", t2[:, 0:n], PS[0:64, rbk, 0:n], cs[:, 1, 0:n], ALU.mult, [PB[rbk], csb], [t2b])
                    TTo("pool", dst, t1[:, 0:n], t2[:, 0:n], ALU.add, [t1b, t2b], [dstb])

                rpend = []

                def run_rpend():
                    while rpend:
                        rpend.pop(0)()

                for hh in range(8):
                    bq = banks.next()
                    proj(1024 + 128 * hh, bq)
                    qst, qstb = qst_ring.next()
                    CP("act", qst[64:128, 0:n], PS[64:128, bq, 0:n], [PB[bq]], [qstb])
                    run_rpend()
                    if not isc:
                        def rq(bq=bq, qst=qst, qstb=qstb, hh=hh):
                            rope(bq, qst[64:128, 0:n], qstb, qst[0:64, 0:n], qstb)
                            S.dma("pool", QS[hh, :, t0:t0 + n], qst[:, 0:n], r=[qstb], w=[QSB[hh][j]])
                        rpend.append(rq)
                    else:
                        S.dma("pool", QS[hh, 64:128, t0:t0 + n], qst[64:128, 0:n], r=[qstb], w=[QSB[hh][j]])
                    yield
                for g in range(2):
                    bk = banks.next()
                    proj(2048 + 128 * g, bk)
                    if isc:
                        CP("act", Kctx[64:128, g, :], PS[64:128, bk, 0:n], [PB[bk]], [KCB])
                        run_rpend()
                    else:
                        kr, krb = kraw_ring.next()
                        CP("act", kr[64:128, 0:n], PS[64:128, bk, 0:n], [PB[bk]], [krb])
                        run_rpend()

                        def rk(bk=bk, kr=kr, krb=krb, g=g):
                            rope(bk, kr[64:128, 0:n], krb, Klat[0:64, g, t0:t0 + n], KLB[j])
                        rpend.append(rk)
                run_rpend()
                for b in range(n // 128):
                    bv = banks.next()
                    for kc in range(8):
                        MM(PS[:, bv, 0:128], h[:, kc, b * 128:(b + 1) * 128], Wt[:, kc, 2304:2432], kc == 0, kc == 7, [hb, WB[4]], [PB[bv]])
                    blk = (t0 // 128) + b
                    CP("act" if b % 2 else "dve", Vt[:, blk, :, 0:64], PS[:, bv, 0:128].rearrange("p (g d) -> p g d", g=2), [PB[bv]], [VB[j]])
            def pn_gen(jj):
                t0_, n_, isc_ = tiles[TL[jj]]
                s_ = 1 if isc_ else 0
                return prenorm_gen(rings, loads[jj][0], loads[jj][1], n_, ACOL[:, l, s_, :], MODT[:, l, s_, SH0:SH0 + 8], 7)

            def drain(g_):
                while True:
                    try:
                        next(g_)
                    except StopIteration as e_:
                        return e_.value

            issue_load(0)
            if len(TL) > 1:
                issue_load(1)
            hcur = drain(pn_gen(0))
            for ji in range(len(TL)):
                gen = body(ji, hcur)
                k = 0
                pn = None
                hnext = None
                for _ in gen:
                    k += 1
                    if k == SPLIT and ji + 1 < len(TL):
                        pn = pn_gen(ji + 1)
                    if pn is not None:
                        try:
                            next(pn)
                        except StopIteration as e_:
                            hnext = e_.value
                            pn = None
                            if ji + 2 < len(TL):
                                issue_load(ji + 2)
                if ji + 1 < len(TL) and hnext is None:
                    if pn is None:
                        pn = pn_gen(ji + 1)
                    hnext = drain(pn)
                    if ji + 2 < len(TL):
                        issue_load(ji + 2)
                hcur = hnext
                loads.pop(ji)
            S.barrier()

    def phase_conva():
        with ExitStack() as es:
            Dg = sb(es, "DgA", [128, 4, 31, 128], BF16)
            DgB = Buf()
            for c in range(4):
                for k in range(31):
                    col = cols["acw"][:, k * 4 + c:k * 4 + c + 1]
                    TS("dve", Dg[:, c, k, :], identB[:], col, None, ALU.mult, None, [CB], [DgB])
            up_ring = Ring(nc, es, "up", [128, 4, 512 + 30], BF16, 2)
            ucv_ring = Ring(nc, es, "ucv", [128, 4, 512], F32, 2)
            usq_ring = Ring(nc, es, "usq", [128, 4, 512], F32, 2)
            st_ring = Ring(nc, es, "lnst", [128, 3, 512], F32, 2)
            tt_ring = Ring(nc, es, "lntt", [128, 512], F32, 2)
            ao_ring = Ring(nc, es, "ao", [128, 4, 512], BF16, 2)
            U0v = U0.rearrange("(c p) t -> p c t", p=128)
            A0v = A0.rearrange("(c p) t -> p c t", p=128)
            banks = Rot([0, 1, 2, 3])
            wring = Ring(nc, es, "wmod", [128, 8, 768], F32, 2)
            mod_jb = [0]
            mod_q = []

            def mod_step():
                k = mod_jb[0]
                if k > 8:
                    return
                if k < 8:
                    mod_q.append((k,) + mod_load(1, k, wring))
                if k >= 1:
                    kk, wt_, wb_ = mod_q.pop(0)
                    mod_mm(1, kk, wt_, wb_)
                mod_jb[0] += 1

            for j, (t0, n, isc) in enumerate(tiles):
                mod_step()
                seg0, seg1 = (T, TT) if isc else (0, T)
                lo, hi = max(t0 - 15, seg0), min(t0 + n + 15, seg1)
                up, upb = up_ring.next()
                rd = [U0B[j]]
                if j > 0 and not isc:
                    rd.append(U0B[j - 1])
                if j + 1 < NT:
                    rd.append(U0B[j + 1])
                if lo > t0 - 15:
                    MSET("pool", up[:, :, 0:15], 0.0, [upb])
                if hi < t0 + n + 15:
                    MSET("pool", up[:, :, n + 15:n + 30], 0.0, [upb])
                S.dma("sp", up[:, :, lo - (t0 - 15):hi - (t0 - 15)], U0v[:, :, lo:hi], r=rd, w=[upb])
                ucv, ucvb = ucv_ring.next()
                usq, usqb = usq_ring.next()
                for c in range(4):
                    bk = banks.next()
                    for k in range(31):
                        MM(PS[:, bk, 0:n], Dg[:, c, k, :], up[:, c, k:k + n], k == 0, k == 30, [upb, DgB], [PB[bk]])
                    ACT(ucv[:, c, 0:n], PS[:, bk, 0:n], AF.Identity, [PB[bk], CB], [ucvb], bias=cols["acb"][:, c:c + 1])
                    ACT(usq[:, c, 0:n], PS[:, bk, 0:n], AF.Square, [PB[bk], CB], [usqb], bias=cols["acb"][:, c:c + 1])
                for c in range(4):
                    MM(PS[:, 4, 0:n], onesF[:], ucv[:, c, 0:n], c == 0, c == 3, [ucvb, CB], [PB[4]])
                for c in range(4):
                    MM(PS[:, 5, 0:n], onesF[:], usq[:, c, 0:n], c == 0, c == 3, [usqb, CB], [PB[5]])
                st, stb = st_ring.next()
                TS("dve", st[:, 0, 0:n], PS[:, 4, 0:n], 1.0 / 512, None, ALU.mult, None, [PB[4]], [stb])
                TTo("dve", st[:, 1, 0:n], st[:, 0, 0:n], st[:, 0, 0:n], ALU.mult, [stb], [stb])
                STT(st[:, 2, 0:n], PS[:, 5, 0:n], 1.0 / 512, st[:, 1, 0:n], ALU.mult, ALU.subtract, [PB[5], stb], [stb])
                ACT(st[:, 2, 0:n], st[:, 2, 0:n], AF.Sqrt, [stb, CB], [stb], bias=epsT[:, 0:1])
                RCP(st[:, 2, 0:n], st[:, 2, 0:n], [stb], [stb])
                ao, aob = ao_ring.next()
                for c in range(4):
                    tt, ttb = tt_ring.next()
                    TTo("dve", tt[:, 0:n], ucv[:, c, 0:n], st[:, 0, 0:n], ALU.subtract, [ucvb, stb], [ttb])
                    TTo("dve", tt[:, 0:n], tt[:, 0:n], st[:, 2, 0:n], ALU.mult, [ttb, stb], [ttb])
                    ACT(ao[:, c, 0:n], tt[:, 0:n], AF.Silu, [ttb, CB], [aob], bias=cols["alb"][:, c:c + 1], scale=cols["alg"][:, c:c + 1])
                S.dma("pool", A0v[:, :, t0:t0 + n], ao[:, :, 0:n], r=[aob], w=[A0B[j]])
            while mod_jb[0] <= 8:
                mod_step()
            mod_finish(1)
            S.barrier()

    def attn_finalize_a(rings, acc, n, cp_eng="act"):
        osb, ob = rings["osb"].next()
        CP(cp_eng, osb[0:65, 0:n], PS[0:65, acc, 0:n], [PB[acc]], [ob])
        return osb, ob

    def attn_finalize_b(rings, osb, ob, n, extra_col, dst_dram, dstb, dbank, act_recip=False):
        hl, hlb = rings["hl"].next()
        CP("dve", hl[64:65, 0, 0:n], osb[64:65, 0:n], [ob], [hlb])
        TTo("dve", hl[64:65, 1, 0:n], osb[64:65, 0:n], hl[64:65, 0, 0:n], ALU.subtract, [ob, hlb], [hlb])
        MM(PS[0:64, dbank, 0:n], onesB[64:65, 0:64], hl[64:65, 0, 0:n], True, False, [hlb, CB], [PB[dbank]])
        MM(PS[0:64, dbank, 0:n], onesB[64:65, 0:64], hl[64:65, 1, 0:n], False, True, [hlb, CB], [PB[dbank]])
        rd, rdb = rings["rd"].next()
        if act_recip:
            ACT(rd[0:64, 0:n], PS[0:64, dbank, 0:n], AF.Ln, [PB[dbank], CB], [rdb], bias=extra_col)
            ACT(rd[0:64, 0:n], rd[0:64, 0:n], AF.Exp, [rdb], [rdb], scale=-1.0)
        elif extra_col is not None:
            TS("dve", rd[0:64, 0:n], PS[0:64, dbank, 0:n], extra_col, None, ALU.add, None, [PB[dbank], CB], [rdb])
            RCP(rd[0:64, 0:n], rd[0:64, 0:n], [rdb], [rdb])
        else:
            RCP(rd[0:64, 0:n], PS[0:64, dbank, 0:n], [PB[dbank]], [rdb])
        bt, btb = rings["bt"].next()
        TTo("dve", bt[0:64, 0:n], osb[0:64, 0:n], rd[0:64, 0:n], ALU.mult, [ob, rdb], [btb])
        S.dma("pool", dst_dram, bt[0:64, 0:n], r=[btb], w=[dstb])

    def attn_finalize(rings, acc, n, extra_col, dst_dram, dstb, dbank, cp_eng="act"):
        osb, ob = attn_finalize_a(rings, acc, n, cp_eng)
        attn_finalize_b(rings, osb, ob, n, extra_col, dst_dram, dstb, dbank)

    def attn_rings(es):
        return {
            "osb": Ring(nc, es, "osb", [128, 512], F32, 3),
            "rd": Ring(nc, es, "rd", [64, 512], F32, 2),
            "hl": Ring(nc, es, "hl", [128, 2, 512], BF16, 2),
            "bt": Ring(nc, es, "bt", [64, 512], BF16, 2),
        }

    def phase_attn0():
        with ExitStack() as es:
            rings = attn_rings(es)
            pt_ring = Ring(nc, es, "pt", [128, 512], BF16, 4)
            sbanks = Rot([0, 1, 2, 3])
            abanks = Rot([4, 5])
            dbanks = Rot([6, 7])
            qt_ring = Ring(nc, es, "qt", [128, 512], BF16, 3)
            pend = []
            for hh in range(8):
                g = hh // 4
                for j, (t0, n, isc) in enumerate(tiles):
                    qt, qtb = qt_ring.next()
                    if isc:
                        S.dma("sp", qt[64:128, 0:n], QS[hh, 64:128, t0:t0 + n], r=[QSB[hh][j]], w=[qtb])
                    else:
                        S.dma("sp", qt[:, 0:n], QS[hh, :, t0:t0 + n], r=[QSB[hh][j]], w=[qtb])
                    steps = []
                    for cc in range(CTX // 128):
                        steps.append((Kctx[64:128, g, cc * 128:(cc + 1) * 128], qt[64:128, 0:n],
                                      [KCB, qtb], NBL + cc, 0, n, []))
                    if not isc:
                        i4 = t0 // 128
                        for jb in range(i4 - 1, i4 + 5):
                            if jb < 0 or jb >= NBL:
                                continue
                            qb0, qb1 = max(jb - 1, i4), min(jb + 1, i4 + 3)
                            c0, c1 = (qb0 - i4) * 128, (qb1 - i4 + 1) * 128
                            masks = []
                            for qb in range(qb0, qb1 + 1):
                                if qb == jb - 1:
                                    masks.append(((qb - qb0) * 128, m1B))
                                elif qb == jb + 1:
                                    masks.append(((qb - qb0) * 128, m2B))
                            steps.append((Klat[0:64, g, jb * 128:(jb + 1) * 128], qt[0:64, c0:c1],
                                          [KLB[jb // 4], qtb], jb, c0, c1, masks))
                    acc = abanks.next()
                    for si, (lhsT, rhs, rdb_, vblk, c0, c1, masks) in enumerate(steps):
                        m = c1 - c0
                        sbk = sbanks.next()
                        MM(PS[:, sbk, 0:m], lhsT, rhs, True, True, rdb_, [PB[sbk]])
                        pt, ptb = pt_ring.next()
                        ACT(pt[:, 0:m], PS[:, sbk, 0:m], AF.Exp, [PB[sbk]], [ptb], scale=0.125)
                        for (mo, mk) in masks:
                            TTo("dve", pt[:, mo:mo + 128], pt[:, mo:mo + 128], mk[:], ALU.mult, [ptb, CB], [ptb])
                        while len(pend) >= 2:
                            pend.pop(0)()

                        def later(acc=acc, c0=c0, c1=c1, vblk=vblk, g=g, pt=pt, ptb=ptb, m=m, si=si, ns=len(steps), n=n, hh=hh, t0=t0, j=j):
                            MM(PS[0:65, acc, c0:c1], Vt[:, vblk, g, 0:65], pt[:, 0:m], si == 0, si == ns - 1,
                               [ptb, VB[min(vblk // 4, NT)]], [PB[acc]])
                            if si == ns - 1:
                                osb, ob = attn_finalize_a(rings, acc, n, "act")
                                pend.append(lambda: attn_finalize_b(rings, osb, ob, n, esink[0:64, hh:hh + 1], B0[hh * 64:(hh + 1) * 64, t0:t0 + n], B0B[j], dbanks.next(), act_recip=True))
                        pend.append(later)
            while pend:
                pend.pop(0)()
            S.barrier()

    def phase_wout(l, Wsrc, Asrc, ASB, Bsrc, BSB, tl):
        with ExitStack() as es:
            Wa = sb(es, "Wa", [128, 4, D], BF16)
            Wb = sb(es, "Wb", [64, 8, D], BF16)
            WB = Buf()
            cast_load(Wa[:], Wsrc[0:512, :].rearrange("(c p) n -> p c n", p=128), WB)
            cast_load(Wb[:], Wsrc[512:1024, :].rearrange("(h d) n -> d h n", d=64), WB)
            rings = norm_rings(es, with_h=False)
            x_ring = Ring(nc, es, "xt", [128, 8, 512], F32, 2)
            a_ring = Ring(nc, es, "at", [128, 4, 512], BF16, 2)
            b_ring = Ring(nc, es, "bt2", [64, 8, 512], BF16, 2)
            y_ring = Ring(nc, es, "ysb", [128, 8, 512], F32, 2)
            Av = Asrc.rearrange("(c p) t -> p c t", p=128)
            Bv = Bsrc.rearrange("(h d) t -> d h t", d=64)
            banks = Rot([0, 1, 2, 3])
            loads = {}

            def issue_load(ji):
                j = tl[ji]
                t0, n, isc = tiles[j]
                xt, xb = x_ring.next()
                S.dma("sp", xt[:, :, 0:n], XTv[:, :, t0:t0 + n], r=[XTB[j]], w=[xb])
                at, ab = a_ring.next()
                S.dma("sp", at[:, :, 0:n], Av[:, :, t0:t0 + n], r=[ASB[j]], w=[ab])
                bt, bb = b_ring.next()
                S.dma("sp", bt[:, :, 0:n], Bv[:, :, t0:t0 + n], r=[BSB[j]], w=[bb])
                loads[ji] = (xt, xb, at, ab, bt, bb)

            issue_load(0)
            for ji, j in enumerate(tl):
                t0, n, isc = tiles[j]
                if ji + 1 < len(tl):
                    issue_load(ji + 1)
                xt, xb, at, ab, bt, bb = loads.pop(ji)
                s = 1 if isc else 0
                ysb, yb = y_ring.next()
                sq, sqb = rings["sq"].next()
                for fc in range(8):
                    bk = banks.next()
                    for c in range(4):
                        MM(PS[:, bk, 0:n], Wa[:, c, fc * 128:(fc + 1) * 128], at[:, c, 0:n], c == 0, False, [WB, ab], [PB[bk]])
                    for hh in range(8):
                        MM(PS[:, bk, 0:n], Wb[0:64, hh, fc * 128:(fc + 1) * 128], bt[0:64, hh, 0:n], False, hh == 7, [WB, bb], [PB[bk]])
                    CP("act", ysb[:, fc, 0:n], PS[:, bk, 0:n], [PB[bk]], [yb])
                    TTo("dve", sq[:, fc, 0:n], ysb[:, fc, 0:n], ysb[:, fc, 0:n], ALU.mult, [yb], [sqb])
                postnorm_residual(rings, ysb, yb, xt, xb, n, G1[:, l, s, :], 7, sqpre=(sq, sqb))
                S.dma("pool", XTv[:, :, t0:t0 + n], xt[:, :, 0:n], r=[xb], w=[XTB[j]])
            S.barrier()

    def phase_ffna(l, tl):
        with ExitStack() as es:
            Wu = sb(es, "Wu", [128, 8, 2 * FFN], BF16)
            WB = [Buf() for _ in range(8)]
            wsrc = W["ffn_w_up"][l].rearrange("(kc p) n -> p kc n", p=128)
            for blk in range(8):
                c0 = blk * 704
                cast_load(Wu[:, :, c0:c0 + 704], wsrc[:, :, c0:c0 + 704], WB[blk])
            rings = norm_rings(es)
            x_ring = Ring(nc, es, "xt", [128, 8, 512], F32, 2)
            st_ring = Ring(nc, es, "gst", [128, 4, 512], BF16, 3)
            banks = Rot([0, 1, 2, 3, 4, 5])
            GSv = GS.rearrange("(c p) t -> p c t", p=128)
            USv = US.rearrange("(c p) t -> p c t", p=128)
            loads = {}

            def issue_load(ji):
                j = tl[ji]
                t0, n, isc = tiles[j]
                xt, xb = x_ring.next()
                S.dma("sp", xt[:, :, 0:n], XTv[:, :, t0:t0 + n], r=[XTB[j]], w=[xb])
                loads[ji] = (xt, xb)

            TL = tl
            ACOL, SH0, SPLIT = A2, 24, 4

            def body(ji, hcur_):
                j = tl[ji]
                t0, n, isc = tiles[j]
                xt, xb = loads[ji]
                h, hb = hcur_
                for part, (dstv, dstB) in enumerate(((GSv, GSB), (USv, USB))):
                    k = 0
                    while k < NJ:
                        m = min(4, NJ - k)
                        st, stb = st_ring.next()
                        for q in range(m):
                            fc = part * NJ + k + q
                            bk = banks.next()
                            wb = WB[(fc * 128) // 704]
                            wb2 = WB[(fc * 128 + 127) // 704]
                            for kc in range(8):
                                MM(PS[:, bk, 0:n], Wu[:, kc, fc * 128:(fc + 1) * 128], h[:, kc, 0:n], kc == 0, kc == 7, [hb, wb, wb2], [PB[bk]])
                            CP("act" if (q % 2) else "dve", st[:, q, 0:n], PS[:, bk, 0:n], [PB[bk]], [stb])
                        S.dma("pool", dstv[:, k:k + m, t0:t0 + n], st[:, 0:m, 0:n], r=[stb], w=[dstB[j]])
                        yield
                        k += m
            def pn_gen(jj):
                t0_, n_, isc_ = tiles[TL[jj]]
                s_ = 1 if isc_ else 0
                return prenorm_gen(rings, loads[jj][0], loads[jj][1], n_, ACOL[:, l, s_, :], MODT[:, l, s_, SH0:SH0 + 8], 7)

            def drain(g_):
                while True:
                    try:
                        next(g_)
                    except StopIteration as e_:
                        return e_.value

            issue_load(0)
            if len(TL) > 1:
                issue_load(1)
            hcur = drain(pn_gen(0))
            for ji in range(len(TL)):
                gen = body(ji, hcur)
                k = 0
                pn = None
                hnext = None
                for _ in gen:
                    k += 1
                    if k == SPLIT and ji + 1 < len(TL):
                        pn = pn_gen(ji + 1)
                    if pn is not None:
                        try:
                            next(pn)
                        except StopIteration as e_:
                            hnext = e_.value
                            pn = None
                            if ji + 2 < len(TL):
                                issue_load(ji + 2)
                if ji + 1 < len(TL) and hnext is None:
                    if pn is None:
                        pn = pn_gen(ji + 1)
                    hnext = drain(pn)
                    if ji + 2 < len(TL):
                        issue_load(ji + 2)
                hcur = hnext
                loads.pop(ji)
            S.barrier()

    def phase_ffnb(l, tl):
        with ExitStack() as es:
            Wd = sb(es, "Wd", [128, NJ, D], BF16)
            WB = [Buf() for _ in range(2)]
            wsrc = W["ffn_w_down"][l].rearrange("(j p) n -> p j n", p=128)
            cast_load(Wd[:, 0:11, :], wsrc[:, 0:11, :], WB[0])
            cast_load(Wd[:, 11:22, :], wsrc[:, 11:22, :], WB[1])
            Dg = sb(es, "DgF", [128, NJ, 3, 128], BF16)
            DgB = Buf()
            fcw = cols["fcw%d" % l]
            for jj in range(NJ):
                for k in range(3):
                    TS("dve", Dg[:, jj, k, :], identB[:], fcw[:, k * NJ + jj:k * NJ + jj + 1], None, ALU.mult, None, [CB], [DgB])
            rings = norm_rings(es, with_h=False, nsq=1)
            x_ring = Ring(nc, es, "xt", [128, 8, 512], F32, 1)
            gH = [sb(es, "gtH%d" % i, [128, 11, 514], BF16) for i in range(2)]
            uH = [sb(es, "utH%d" % i, [128, 11, 512], BF16) for i in range(2)]
            gHB = [Buf(), Buf()]
            uHB = [Buf(), Buf()]
            ga_ring = Ring(nc, es, "ga", [128, 512], BF16, 3)
            act_ring = Ring(nc, es, "actt", [128, NJ, 512], BF16, 1)
            y_ring = Ring(nc, es, "ysb", [128, 8, 512], F32, 2)
            GSv = GS.rearrange("(c p) t -> p c t", p=128)
            USv = US.rearrange("(c p) t -> p c t", p=128)
            cbanks = Rot([0, 1, 2, 3])
            dbanks = Rot([4, 5, 6])
            loads = {}

            def load_gu(ji):
                j = tl[ji]
                t0, n, isc = tiles[j]
                seg0, seg1 = (T, TT) if isc else (0, T)
                lo, hi = max(t0 - 1, seg0), min(t0 + n + 1, seg1)
                rd = [GSB[j]]
                if j > 0 and not isc:
                    rd.append(GSB[j - 1])
                if j + 1 < NT:
                    rd.append(GSB[j + 1])
                for half in range(2):
                    gt, gb = gH[half], gHB[half]
                    if lo > t0 - 1:
                        MSET("pool", gt[:, :, 0:1], 0.0, [gb])
                    if hi < t0 + n + 1:
                        MSET("pool", gt[:, :, n + 1:n + 2], 0.0, [gb])
                    S.dma("sp", gt[:, :, lo - (t0 - 1):hi - (t0 - 1)], GSv[:, half * 11:(half + 1) * 11, lo:hi], r=rd, w=[gb])
                    S.dma("sp", uH[half][:, :, 0:n], USv[:, half * 11:(half + 1) * 11, t0:t0 + n], r=[USB[j]], w=[uHB[half]])

            def load_x(ji):
                j = tl[ji]
                t0, n, isc = tiles[j]
                xt, xb = x_ring.next()
                S.dma("sp", xt[:, :, 0:n], XTv[:, :, t0:t0 + n], r=[XTB[j]], w=[xb])
                loads[ji] = (xt, xb)

            acts = {}

            def conv(ji):
                j = tl[ji]
                t0, n, isc = tiles[j]
                actt, actb = act_ring.next()
                for jj in range(NJ):
                    bk = cbanks.next()
                    gt, gb, ut, ub = gH[jj // 11], gHB[jj // 11], uH[jj // 11], uHB[jj // 11]
                    for k in range(3):
                        MM(PS[:, bk, 0:n], Dg[:, jj, k, :], gt[:, jj % 11, k:k + n], k == 0, k == 2, [gb, DgB], [PB[bk]])
                    ga, gab = ga_ring.next()
                    ACT(ga[:, 0:n], PS[:, bk, 0:n], AF.Gelu_apprx_tanh, [PB[bk], CB], [gab], bias=cols["fcb"][:, l * NJ + jj:l * NJ + jj + 1])
                    TTo("dve" if (jj % 2) else "pool", actt[:, jj, 0:n], ga[:, 0:n], ut[:, jj % 11, 0:n], ALU.mult, [gab, ub], [actb])
                acts[ji] = (actt, actb)
                if ji + 1 < len(tl):
                    load_gu(ji + 1)

            load_gu(0)
            load_x(0)
            conv(0)
            for ji, j in enumerate(tl):
                t0, n, isc = tiles[j]
                xt, xb = loads.pop(ji)
                s = 1 if isc else 0
                actt, actb = acts.pop(ji)
                ysb, yb = y_ring.next()
                for fc in range(8):
                    bk = dbanks.next()
                    for jj in range(NJ):
                        MM(PS[:, bk, 0:n], Wd[:, jj, fc * 128:(fc + 1) * 128], actt[:, jj, 0:n], jj == 0, jj == NJ - 1, [actb] + WB, [PB[bk]])
                    CP("act", ysb[:, fc, 0:n], PS[:, bk, 0:n], [PB[bk]], [yb])
                if ji + 1 < len(tl):
                    conv(ji + 1)
                postnorm_residual(rings, ysb, yb, xt, xb, n, G2[:, l, s, :], 7)
                S.dma("pool", XTv[:, :, t0:t0 + n], xt[:, :, 0:n], r=[xb], w=[XTB[j]])
                if ji + 1 < len(tl):
                    load_x(ji + 1)
            S.barrier()

    def phase_tout():
        with ExitStack() as es:
            x_ring = Ring(nc, es, "xt", [128, 8, 512], F32, 2)
            o_ring = Ring(nc, es, "ot", [128, D], F32, 3)
            banks = Rot([(0, 1), (2, 3), (4, 5), (6, 7)])
            for j, (t0, n, isc) in enumerate(lat_tiles):
                xt, xb = x_ring.next()
                S.dma("sp", xt[:, :, 0:n], XTv[:, :, t0:t0 + n], r=[XTB[j]], w=[xb])
                for b in range(n // 128):
                    b0, b1 = banks.next()
                    for fc in range(8):
                        bk = b0 if fc < 4 else b1
                        TR(PS[:, bk, (fc % 4) * 128:(fc % 4 + 1) * 128], xt[:, fc, b * 128:(b + 1) * 128], identF[:], [xb, CB], [PB[bk]])
                    ot, ob = o_ring.next()
                    CP("act", ot[:, 0:512], PS[:, b0, :], [PB[b0]], [ob])
                    CP("dve", ot[:, 512:1024], PS[:, b1, :], [PB[b1]], [ob])
                    S.dma("pool", y_out[t0 + b * 128:t0 + (b + 1) * 128, :], ot[:], r=[ob], w=[Buf()])
            S.barrier()

    L1 = ExitStack()
    L1T = {}

    def alloc_l1():
        L1T["CQN"] = sb(L1, "CQN", [128, 3, T], BF16)
        L1T["CKVN"] = sb(L1, "CKVN", [128, 2, TT], BF16)
        L1T["KRb"] = sb(L1, "KRb", [64, TT], BF16)

    CQB = [Buf() for _ in tiles]
    CKB = [Buf() for _ in tiles]
    KRB = [Buf() for _ in tiles]

    def phase_p1_l1():
        l = 1
        CQN, CKVN, KRb = L1T["CQN"], L1T["CKVN"], L1T["KRb"]
        with ExitStack() as es:
            Wt = sb(es, "Wt1", [128, 8, 1728], BF16)
            WB = [Buf() for _ in range(3)]
            wsrc = W["cd_w_in"][0].rearrange("(kc p) n -> p kc n", p=128)
            cast_load(Wt[:, :, 0:1024], wsrc[:, :, 0:1024], WB[0])
            cast_load(Wt[:, :, 1024:1664], wsrc[:, :, 1024:1664], WB[1])
            cast_load(Wt[:, :, 1664:1696], wsrc[:, :, 1664:1696], WB[2])
            cast_load(Wt[:, :, 1696:1728], wsrc[:, :, 1664:1696], WB[2])
            MSET("pool", KRb[32:64, 0:T], 0.0, KRB[:NT])
            MSET("pool", KRb[0:32, T:TT], 0.0, [KRB[NT]])
            rings = norm_rings(es)
            x_ring = Ring(nc, es, "xt", [128, 8, 512], F32, 2)
            xst_ring = Ring(nc, es, "xst", [128, 4, 512], F32, 1)
            gst_ring = Ring(nc, es, "gst1", [128, 4, 512], F32, 1)
            cqs_ring = Ring(nc, es, "cqs", [128, 3, 512], F32, 1)
            cs_ring = Ring(nc, es, "csm", [32, 2, 512], F32, 3)
            t1_ring = Ring(nc, es, "t1m", [32, 512], F32, 2)
            t2_ring = Ring(nc, es, "t2m", [32, 512], F32, 2)
            krs_ring = Ring(nc, es, "krs", [64, 512], BF16, 2)
            banks = Rot([0, 1, 2, 3, 4])
            XBv = XBS.rearrange("(c p) t -> p c t", p=128)
            GTv = GTS.rearrange("(c p) t -> p c t", p=128)
            loads = {}

            def issue_load(j):
                t0, n, isc = tiles[j]
                xt, xb = x_ring.next()
                S.dma("sp", xt[:, :, 0:n], XTv[:, :, t0:t0 + n], r=[XTB[j]], w=[xb])
                cs, csb = cs_ring.next()
                if not isc:
                    S.dma("sp", cs[:, 0, :], cmv[:, t0:t0 + n], w=[csb])
                    S.dma("sp", cs[:, 1, :], smv[:, t0:t0 + n], w=[csb])
                loads[j] = (xt, xb, cs, csb)

            TL = list(range(len(tiles)))
            ACOL, SH0, SPLIT = A1, 0, 2

            def body(j, hcur_):
                t0, n, isc = tiles[j]
                xt, xb, cs, csb = loads[j]
                h, hb = hcur_

                def proj(col0, bank, M=128):
                    wdep = [WB[0]] if col0 < 1024 else ([WB[1]] if col0 < 1664 else [WB[2]])
                    for kc in range(8):
                        MM(PS[0:M, bank, 0:n], Wt[:, kc, col0:col0 + M], h[:, kc, 0:n], kc == 0, kc == 7, [hb] + wdep, [PB[bank]])

                xst, xstb = xst_ring.next()
                for c in range(4):
                    bk = banks.next()
                    proj(128 * c, bk)
                    CP("act" if c % 2 else "dve", xst[:, c, 0:n], PS[:, bk, 0:n], [PB[bk]], [xstb])
                    yield
                S.dma("pool", XBv[:, :, t0:t0 + n], xst[:, :, 0:n], r=[xstb], w=[XBSB[j]])
                if not isc:
                    gst, gstb = gst_ring.next()
                    for c in range(4):
                        bk = banks.next()
                        proj(512 + 128 * c, bk)
                        ACT(gst[:, c, 0:n], PS[:, bk, 0:n], AF.Gelu_apprx_tanh, [PB[bk]], [gstb])
                        yield
                    S.dma("pool", GTv[:, :, t0:t0 + n], gst[:, :, 0:n], r=[gstb], w=[GTSB[j]])

                def lowrank_norm(col0, nch, dim, gcol, dst, dstb):
                    cqs, cqsb = cqs_ring.next()
                    for c in range(nch):
                        bk = banks.next()
                        proj(col0 + 128 * c, bk)
                        CP("act" if c % 2 else "dve", cqs[:, c, 0:n], PS[:, bk, 0:n], [PB[bk]], [cqsb])
                    sq, sqb = rings["sq"].next()
                    ACT(sq[:, 0:nch, 0:n], cqs[:, 0:nch, 0:n], AF.Square, [cqsb], [sqb])
                    rs, rsb = stat_rstd(rings, sq, sqb, nch, n, dim, 7)
                    TTo("dve", cqs[:, 0:nch, 0:n], cqs[:, 0:nch, 0:n], rs[:, 0:n].unsqueeze(1).to_broadcast([128, nch, n]), ALU.mult, [cqsb, rsb], [cqsb])
                    for c in range(nch):
                        ACT(dst[:, c, t0:t0 + n], cqs[:, c, 0:n], AF.Identity, [cqsb, CB], [dstb], scale=gcol[:, c:c + 1])

                if not isc:
                    lowrank_norm(1024, 3, 384, cols["qn"], CQN, CQB[j])
                lowrank_norm(1408, 2, 256, cols["kvn"], CKVN, CKB[j])
                bk = banks.next()
                proj(1664, bk, M=64)
                if isc:
                    CP("act", KRb[32:64, t0:t0 + n], PS[32:64, bk, 0:n], [PB[bk]], [KRB[j]])
                else:
                    krs, krsb = krs_ring.next()
                    CP("act", krs[32:64, 0:n], PS[32:64, bk, 0:n], [PB[bk]], [krsb])
                    rbk = 5
                    MM(PS[0:32, rbk, 0:n], r32B[32:64, :], krs[32:64, 0:n], True, True, [krsb, CB], [PB[rbk]])
                    t1, t1b = t1_ring.next()
                    t2, t2b = t2_ring.next()
                    TTo("dve", t1[:, 0:n], PS[0:32, bk, 0:n], cs[:, 0, 0:n], ALU.mult, [PB[bk], csb], [t1b])
                    TTo("dve", t2[:, 0:n], PS[0:32, rbk, 0:n], cs[:, 1, 0:n], ALU.mult, [PB[rbk], csb], [t2b])
                    TTo("pool", KRb[0:32, t0:t0 + n], t1[:, 0:n], t2[:, 0:n], ALU.add, [t1b, t2b], [KRB[j]])
            def pn_gen(jj):
                t0_, n_, isc_ = tiles[TL[jj]]
                s_ = 1 if isc_ else 0
                return prenorm_gen(rings, loads[jj][0], loads[jj][1], n_, ACOL[:, l, s_, :], MODT[:, l, s_, SH0:SH0 + 8], 7)

            def drain(g_):
                while True:
                    try:
                        next(g_)
                    except StopIteration as e_:
                        return e_.value

            issue_load(0)
            if len(TL) > 1:
                issue_load(1)
            hcur = drain(pn_gen(0))
            for ji in range(len(TL)):
                gen = body(ji, hcur)
                k = 0
                pn = None
                hnext = None
                for _ in gen:
                    k += 1
                    if k == SPLIT and ji + 1 < len(TL):
                        pn = pn_gen(ji + 1)
                    if pn is not None:
                        try:
                            next(pn)
                        except StopIteration as e_:
                            hnext = e_.value
                            pn = None
                            if ji + 2 < len(TL):
                                issue_load(ji + 2)
                if ji + 1 < len(TL) and hnext is None:
                    if pn is None:
                        pn = pn_gen(ji + 1)
                    hnext = drain(pn)
                    if ji + 2 < len(TL):
                        issue_load(ji + 2)
                hcur = hnext
                loads.pop(ji)
            S.barrier()

    def phase_lru():
        with ExitStack() as es:
            GW = sb(es, "GW", [128, 2, 2, 4, 128], BF16)
            GWB = Buf()
            MSET("pool", GW[:], 0.0, [GWB])
            for d in range(2):
                for gate in range(2):
                    for nb in range(8):
                        p0 = (nb % 2) * 64
                        cast_load(GW[p0:p0 + 64, d, gate, nb // 2, p0:p0 + 64], W["lru_gate_w"][0, d, gate, nb], GWB)
            ytmp = sb(es, "ytmp", [128, 8], F32)
            yb_ = Buf()
            ACT(ytmp[:], cols["lam"][:], AF.Exp, [CB], [yb_], scale=-1.0)
            ACT(ytmp[:], ytmp[:], AF.Ln, [yb_, CB], [yb_], bias=onesF[:, 0:1])
            TS("dve", cch[:, 0, :], ytmp[:], -8.0, None, ALU.mult, None, [yb_], [CB])
            TS("dve", cch[:, 1, :], ytmp[:], -16.0, None, ALU.mult, None, [yb_], [CB])
            xb_ring = Ring(nc, es, "xbt", [128, 4, 515], F32, 2)
            xc_ring = Ring(nc, es, "xc", [128, 4, 512], F32, 2)
            xcb_ring = Ring(nc, es, "xcb", [128, 4, 512], BF16, 2)
            rg_ring = Ring(nc, es, "rg", [128, 4, 512], F32, 2)
            ig_ring = Ring(nc, es, "ig", [128, 4, 512], F32, 2)
            av_ring = Ring(nc, es, "av", [128, 4, 512], F32, 2)
            e2_ring = Ring(nc, es, "e2", [128, 4, 512], F32, 2)
            hv_rings = [Ring(nc, es, "hv%d" % d_, [128, 4, 512], F32, 2) for d_ in range(2)]
            XBv = XBS.rearrange("(c p) t -> p c t", p=128)
            GTv = GTS.rearrange("(c p) t -> p c t", p=128)
            HFv = HFS.rearrange("(c p) t -> p c t", p=128)
            C1v = C1.rearrange("(c p) t -> p c t", p=128)
            banks = Rot([0, 1, 2, 3, 4, 5])
            HBv = HBS.rearrange("(c p) t -> p c t", p=128)
            orders = [[NT] + list(range(NT)), [NT] + list(range(NT - 1, -1, -1))]
            prevs = [None, None]

            def stageA(d, j):
                t0, n, isc = tiles[j]
                seg0, seg1 = (T, TT) if isc else (0, T)
                xbt, xbb = xb_ring.next()
                rd = [XBSB[j]]
                if d == 0:
                    lo, hi = max(t0 - 3, seg0), t0 + n
                    if lo > t0 - 3:
                        MSET("pool", xbt[:, :, 0:3], 0.0, [xbb])
                    elif j > 0:
                        rd.append(XBSB[j - 1])
                    S.dma("sp", xbt[:, :, lo - (t0 - 3):n + 3], XBv[:, :, lo:hi], r=rd, w=[xbb])
                else:
                    lo, hi = t0, min(t0 + n + 3, seg1)
                    if hi < t0 + n + 3:
                        MSET("pool", xbt[:, :, n:n + 3], 0.0, [xbb])
                    elif j + 1 < NT:
                        rd.append(XBSB[j + 1])
                    S.dma("sp", xbt[:, :, 0:hi - lo], XBv[:, :, lo:hi], r=rd, w=[xbb])
                xc, xcb_ = xc_ring.next()
                for c in range(4):
                    wc = lambda k: cols["lcw"][:, d * 16 + k * 4 + c:d * 16 + k * 4 + c + 1]
                    TS("dve", xc[:, c, 0:n], xbt[:, c, 0:n], wc(0), cols["lcb"][:, d * 4 + c:d * 4 + c + 1], ALU.mult, ALU.add, [xbb, CB], [xcb_])
                    for k in range(1, 4):
                        STT(xc[:, c, 0:n], xbt[:, c, k:k + n], wc(k), xc[:, c, 0:n], ALU.mult, ALU.add, [xbb, xcb_, CB], [xcb_])
                xcb, xcbb = xcb_ring.next()
                CP("act", xcb[:, :, 0:n], xc[:, :, 0:n], [xcb_], [xcbb])
                rg, rgb = rg_ring.next()
                ig, igb = ig_ring.next()
                av, avb = av_ring.next()
                e2, e2b = e2_ring.next()
                for c in range(4):
                    b0 = banks.next()
                    MM(PS[:, b0, 0:n], GW[:, d, 0, c, :], xcb[:, c, 0:n], True, True, [xcbb, GWB], [PB[b0]])
                    ACT(rg[:, c, 0:n], PS[:, b0, 0:n], AF.Sigmoid, [PB[b0], CB], [rgb], bias=cols["lgb"][:, d * 8 + c:d * 8 + c + 1])
                    b1 = banks.next()
                    MM(PS[:, b1, 0:n], GW[:, d, 1, c, :], xcb[:, c, 0:n], True, True, [xcbb, GWB], [PB[b1]])
                    ACT(ig[:, c, 0:n], PS[:, b1, 0:n], AF.Sigmoid, [PB[b1], CB], [igb], bias=cols["lgb"][:, d * 8 + 4 + c:d * 8 + 4 + c + 1])
                for c in range(4):
                    ACT(av[:, c, 0:n], rg[:, c, 0:n], AF.Exp, [rgb, CB], [avb], scale=cch[:, 0, d * 4 + c:d * 4 + c + 1])
                    ACT(e2[:, c, 0:n], rg[:, c, 0:n], AF.Exp, [rgb, CB], [e2b], scale=cch[:, 1, d * 4 + c:d * 4 + c + 1])
                ACT(e2[:, :, 0:n], e2[:, :, 0:n], AF.Sqrt, [e2b, CB], [e2b], bias=onesF[:, 0:1], scale=-1.0)
                return (d, j, xc, xcb_, ig, igb, av, avb, e2, e2b)

            def stageB(ctx_):
                d, j, xc, xcb_, ig, igb, av, avb, e2, e2b = ctx_
                t0, n, isc = tiles[j]
                prev = prevs[d]
                TTo("dve", e2[:, :, 0:n], e2[:, :, 0:n], ig[:, :, 0:n], ALU.mult, [e2b, igb], [e2b])
                TTo("dve", e2[:, :, 0:n], e2[:, :, 0:n], xc[:, :, 0:n], ALU.mult, [e2b, xcb_], [e2b])
                hv, hvb = hv_rings[d].next()
                for c in range(4):
                    if prev is None:
                        init, rdp = 0.0, []
                    else:
                        ph, phb, pn = prev
                        init = ph[:, c, pn - 1:pn] if d == 0 else ph[:, c, 0:1]
                        rdp = [phb]
                    if d == 0:
                        o_, a_, b_ = hv[:, c, 0:n], av[:, c, 0:n], e2[:, c, 0:n]
                    else:
                        o_, a_, b_ = hv[:, c, 0:n][:, ::-1], av[:, c, 0:n][:, ::-1], e2[:, c, 0:n][:, ::-1]
                    S.op("dve", lambda e, o_=o_, a_=a_, b_=b_, init=init: e.tensor_tensor_scan(o_, a_, b_, init, ALU.mult, ALU.add),
                         [avb, e2b] + rdp, [hvb])
                prevs[d] = (hv, hvb, n)
                if not isc:
                    if d == 0:
                        S.dma("pool", HFv[:, :, t0:t0 + n], hv[:, :, 0:n], r=[hvb], w=[HFSB[j]])
                    else:
                        S.dma("pool", HBv[:, :, t0:t0 + n], hv[:, :, 0:n], r=[hvb], w=[HBSB[j]])

            items = [(d, orders[d][step]) for step in range(NT + 1) for d in range(2)]
            pend_ctx = None
            for (d, j) in items:
                ctx_ = stageA(d, j)
                if pend_ctx is not None:
                    stageB(pend_ctx)
                pend_ctx = ctx_
            stageB(pend_ctx)
            S.barrier()

    def phase_lru_combine():
        with ExitStack() as es:
            hf_ring = Ring(nc, es, "hf", [128, 4, 512], F32, 2)
            hb_ring = Ring(nc, es, "hb", [128, 4, 512], F32, 2)
            gg_ring = Ring(nc, es, "gg", [128, 4, 512], F32, 2)
            cl_ring = Ring(nc, es, "cl", [128, 4, 512], BF16, 2)
            GTv = GTS.rearrange("(c p) t -> p c t", p=128)
            HFv = HFS.rearrange("(c p) t -> p c t", p=128)
            HBv = HBS.rearrange("(c p) t -> p c t", p=128)
            C1v = C1.rearrange("(c p) t -> p c t", p=128)
            for j, (t0, n, isc) in enumerate(lat_tiles):
                hf, hfb = hf_ring.next()
                S.dma("sp", hf[:, :, 0:n], HFv[:, :, t0:t0 + n], r=[HFSB[j]], w=[hfb])
                hb, hbb = hb_ring.next()
                S.dma("sp", hb[:, :, 0:n], HBv[:, :, t0:t0 + n], r=[HBSB[j]], w=[hbb])
                gg, ggb = gg_ring.next()
                S.dma("sp", gg[:, :, 0:n], GTv[:, :, t0:t0 + n], r=[GTSB[j]], w=[ggb])
                cl, clb = cl_ring.next()
                TTo("dve", hf[:, :, 0:n], hf[:, :, 0:n], hb[:, :, 0:n], ALU.add, [hfb, hbb], [hfb])
                TTo("pool", cl[:, :, 0:n], hf[:, :, 0:n], gg[:, :, 0:n], ALU.mult, [hfb, ggb], [clb])
                S.dma("pool", C1v[:, :, t0:t0 + n], cl[:, :, 0:n], r=[clb], w=[C1B[j]])
            S.barrier()

    def phase_mla():
        CQN, CKVN, KRb = L1T["CQN"], L1T["CKVN"], L1T["KRb"]
        with ExitStack() as es:
            Wq = sb(es, "Wq", [128, 3, 8, 128], BF16)
            Wk = sb(es, "Wk", [128, 2, 8, 64], BF16)
            Wv = sb(es, "Wv", [128, 2, 8, 64], BF16)
            WB = Buf()
            qsrc = W["mla_w_uq"][0].rearrange("(kc p) (h e) -> p kc h e", p=128, e=96)
            ksrc = W["mla_w_ukv"][0].rearrange("(kc p) (h e) -> p kc h e", p=128, e=128)
            for kc in range(3):
                cast_load(Wq[:, kc, :, 64:128], qsrc[:, kc, :, 0:64], WB)
                cast_load(Wq[:, kc, :, 0:32], qsrc[:, kc, :, 64:96], WB)
                cast_load(Wq[:, kc, :, 32:64], qsrc[:, kc, :, 64:96], WB)
            for kc in range(2):
                cast_load(Wk[:, kc, :, :], ksrc[:, kc, :, 0:64], WB)
                cast_load(Wv[:, kc, :, :], ksrc[:, kc, :, 64:128], WB)
            Va = sb(es, "Va", [128, NB, 8, 66], BF16)
            VaB = Buf()
            MSET("pool", Va[:, :, :, 64:66], 1.0, [VaB])
            rings = attn_rings(es)
            k_ring = Ring(nc, es, "Kh", [128, TT], BF16, 2)
            q_ring = Ring(nc, es, "Qh", [128, T], BF16, 2)
            pt_ring = Ring(nc, es, "ptm", [128, 2, 512], BF16, 4)
            cs_ring = Ring(nc, es, "csq", [32, 2, 512], F32, 2)
            t1_ring = Ring(nc, es, "t1q", [32, 512], F32, 2)
            t2_ring = Ring(nc, es, "t2q", [32, 512], F32, 2)
            mbanks = Rot([6, 7])
            sbanks = Rot([0, 2])
            abanks = Rot([4, 5])
            for blk in range(NB):
                bk = mbanks.next()
                for kc in range(2):
                    MM(PS[:, bk, 0:512], CKVN[:, kc, blk * 128:(blk + 1) * 128], Wv[:, kc, :, :].rearrange("p h d -> p (h d)"),
                       kc == 0, kc == 1, [CKB[min(blk // 4, NT)], WB], [PB[bk]])
                CP("act" if blk % 2 else "dve", Va[:, blk, :, 0:64], PS[:, bk, 0:512].rearrange("p (h d) -> p h d", d=64), [PB[bk]], [VaB])
            sc = float(96 ** -0.5)
            pend = []
            qst_ = [None]
            hbufs = {}

            def get_bufs(h_):
                if h_ not in hbufs:
                    Kh_, KhB_ = k_ring.next()
                    Qh_, QhB_ = q_ring.next()
                    CP("pool", Kh_[0:64, :], KRb[0:64, :], KRB, [KhB_])
                    hbufs[h_] = (Kh_, KhB_, Qh_, QhB_)
                return hbufs[h_]

            def prod_k(h_, j):
                Kh_, KhB_, Qh_, QhB_ = get_bufs(h_)
                t0, n, isc = tiles[j]
                bk = mbanks.next()
                for kc in range(2):
                    MM(PS[64:128, bk, 0:n], Wk[:, kc, h_, :], CKVN[:, kc, t0:t0 + n], kc == 0, kc == 1, [CKB[j], WB], [PB[bk]])
                CP("dve", Kh_[64:128, t0:t0 + n], PS[64:128, bk, 0:n], [PB[bk]], [KhB_])

            def prod_q_a(h_, j):
                Kh_, KhB_, Qh_, QhB_ = get_bufs(h_)
                t0, n, isc = tiles[j]
                bk = mbanks.next()
                for kc in range(3):
                    MM(PS[:, bk, 0:n], Wq[:, kc, h_, :], CQN[:, kc, t0:t0 + n], kc == 0, kc == 2, [CQB[j], WB], [PB[bk]])
                CP("dve", Qh_[32:64, t0:t0 + n], PS[32:64, bk, 0:n], [PB[bk]], [QhB_])
                CP("dve", Qh_[64:128, t0:t0 + n], PS[64:128, bk, 0:n], [PB[bk]], [QhB_])
                t1, t1b = t1_ring.next()
                cs, csb = cs_ring.next()
                S.dma("sp", cs[:, 0, :], cmv[:, t0:t0 + n], w=[csb])
                S.dma("sp", cs[:, 1, :], smv[:, t0:t0 + n], w=[csb])
                TTo("dve", t1[:, 0:n], PS[0:32, bk, 0:n], cs[:, 0, 0:n], ALU.mult, [PB[bk], csb], [t1b])
                return (t1, t1b, cs, csb)

            def prod_q_b(h_, j, st_):
                Kh_, KhB_, Qh_, QhB_ = get_bufs(h_)
                t0, n, isc = tiles[j]
                t1, t1b, cs, csb = st_
                rbk = mbanks.next()
                MM(PS[0:32, rbk, 0:n], r32B[32:64, :], Qh_[32:64, t0:t0 + n], True, True, [QhB_, CB], [PB[rbk]])
                t2, t2b = t2_ring.next()
                TTo("dve", t2[:, 0:n], PS[0:32, rbk, 0:n], cs[:, 1, 0:n], ALU.mult, [PB[rbk], csb], [t2b])
                TTo("pool", Qh_[0:32, t0:t0 + n], t1[:, 0:n], t2[:, 0:n], ALU.add, [t1b, t2b], [QhB_])

            def prod_q(h_, j):
                prod_q_b(h_, j, prod_q_a(h_, j))

            for j in range(len(tiles)):
                prod_k(0, j)
            for j in range(NT):
                prod_q(0, j)
            for hh in range(8):
                Kh, KhB, Qh, QhB = get_bufs(hh)
                for j, (t0, n, isc) in enumerate(lat_tiles):
                    acc = abanks.next()
                    ngrp = (NB + 1) // 2
                    for gi in range(ngrp):
                        kbs = [kb for kb in (2 * gi, 2 * gi + 1) if kb < NB]
                        sb0 = sbanks.next()
                        for qi, kb in enumerate(kbs):
                            MM(PS[:, sb0 + qi, 0:n], Kh[:, kb * 128:(kb + 1) * 128], Qh[:, t0:t0 + n], True, True, [KhB, QhB], [PB[sb0 + qi]])
                        pt, ptb = pt_ring.next()
                        m = len(kbs)
                        ACT(pt[:, 0:m, 0:n], PS[:, sb0:sb0 + m, 0:n], AF.Exp, [PB[sb0 + q_] for q_ in range(m)], [ptb], scale=sc)
                        while len(pend) >= 2:
                            pend.pop(0)()

                        def later(kbs=kbs, acc=acc, n=n, pt=pt, ptb=ptb, gi=gi, hh=hh, t0=t0, j=j, last=(gi == ngrp - 1)):
                            for qi, kb in enumerate(kbs):
                                MM(PS[0:65, acc, 0:n], Va[:, kb, hh, 0:65], pt[:, qi, 0:n], gi == 0 and qi == 0, kb == NB - 1, [ptb, VaB], [PB[acc]])
                            if last:
                                osb, ob = attn_finalize_a(rings, acc, n, "dve")
                                pend.append(lambda: attn_finalize_b(rings, osb, ob, n, None, D1[hh * 64:(hh + 1) * 64, t0:t0 + n], D1B[j], mbanks.next()))
                        pend.append(later)
                        if hh + 1 < 8:
                            if gi == ngrp // 5:
                                prod_k(hh + 1, j)
                            if gi == (2 * ngrp) // 5:
                                qst_[0] = prod_q_a(hh + 1, j)
                            if gi == (4 * ngrp) // 5:
                                prod_q_b(hh + 1, j, qst_[0])
                            if gi == ngrp - 1 and j == NT - 1:
                                prod_k(hh + 1, NT)
            while pend:
                pend.pop(0)()
            S.barrier()

    def layer1():
        alloc_l1()
        phase_p1_l1()
        if stop == "p1_l1":
            return
        phase_lru()
        phase_lru_combine()
        if stop == "lru":
            return
        phase_mla()
        if stop == "mla":
            return
        L1.close()
        phase_wout(1, W["cd_w_out"][0], C1, C1B, D1, D1B, lat_t)
        if stop == "wout1":
            return
        phase_ffna(1, lat_t)
        phase_ffnb(1, lat_t)

    all_t = list(range(len(tiles)))
    lat_t = list(range(NT))

    def dump_small(parts):
        off = 0
        for t, w in parts:
            S.dma("sp", DBGV[:, off:off + w], t, r=[CB], w=[Buf()])
            off += w
        S.barrier()

    def run():
        if stop == "mod0":
            phase_mod(0)
            dump_small([(MODT[:, 0, :, :].rearrange("p s m -> p (s m)"), 96), (A1[:, 0].rearrange("p s m -> p (s m)"), 16),
                        (G1[:, 0].rearrange("p s m -> p (s m)"), 16), (A2[:, 0].rearrange("p s m -> p (s m)"), 16),
                        (G2[:, 0].rearrange("p s m -> p (s m)"), 16)])
            return
        phase_tin(with_mod=True)
        if stop == "tin":
            return
        phase_p1_l0()
        if stop == "p1_l0":
            return
        phase_conva()
        if stop == "conva":
            return
        phase_attn0()
        if stop == "attn0":
            return
        L0.close()
        phase_wout(0, W["ab_w_out"][0], A0, A0B, B0, B0B, all_t)
        if stop == "wout0":
            return
        phase_ffna(0, all_t)
        if stop == "ffna0":
            return
        phase_ffnb(0, all_t)
        if stop == "ffnb0":
            return
        layer1()
        phase_tout()

    run()
    L1.close()
    L0.close()
    ges.close()
    build.stats = (S.nops, S.nwaits)
    return nc


def make_in_maps(inputs, T):
    consts = make_consts(T)
    f = lambda a: np.ascontiguousarray(np.asarray(a, dtype=np.float32))
    shared = {k: f(inputs[k]) for k in WEIGHT_SHAPES}
    shared.update(consts)
    shared["c_ctx"] = f(inputs["c_ctx"]).reshape(8, 128)
    x, c, ctx = f(inputs["x"]), f(inputs["c"]), f(inputs["ctx"])
    maps = []
    for b in range(x.shape[0]):
        m = dict(shared)
        m["x"] = np.ascontiguousarray(x[b])
        m["ctx"] = np.ascontiguousarray(ctx[b])
        m["c"] = np.ascontiguousarray(c[b]).reshape(8, 128)
        maps.append(m)
    return maps


def kernel(**inputs):
    T = int(np.asarray(inputs["x"]).shape[1])
    nc = build(T)
    in_maps = make_in_maps(inputs, T)
    res = run_bass_kernel_spmd(nc, in_maps, core_ids=list(range(len(in_maps))))
    return np.stack([np.asarray(r["y"], dtype=np.float32) for r in res.results], axis=0)
```
